# Optimizing a Trainium2 kernel written in Bass

```python
import math
import jax, jax.numpy as jnp
from jax import lax
import numpy as np

D_MODEL = 1024
BATCH = 2
SEQ = 8192
DEPTH = 1
DEC_BATCH = 8
DEC_SEQ = 8192
PAST_LEN = 128

D_ATTN = D_MODEL // 2
ATTN_HEAD_DIM = 64
ATTN_HEADS = D_ATTN // ATTN_HEAD_DIM
DILATED_BRANCHES = ((128, 1), (512, 4), (2048, 16))
REL_BUCKETS = 32
REL_MAX_DIST = 1024
D_SSD = D_MODEL - D_ATTN
SSD_HEAD_DIM = 64
SSD_HEADS = D_SSD // SSD_HEAD_DIM
SSD_GROUPS = 2
SSD_STATE = 128
SSD_CONV = 5
SSD_CHUNK = 128
D_FF = -(-8 * D_MODEL // (3 * 256)) * 256
D_IN_PROJ = 3 * D_ATTN + 2 * D_SSD + 2 * SSD_GROUPS * SSD_STATE + 2 * SSD_HEADS
DEEPNORM_ALPHA = (2 * DEPTH) ** 0.25
DEEPNORM_BETA = (8 * DEPTH) ** -0.25
NORM_EPS = 1e-5

kernel_name = 'hymba_ssd_dilated_encoder'


def layer_norm(x, g, b):
    xf = x.astype(jnp.float32)
    mu = jnp.mean(xf, axis=-1, keepdims=True)
    var = jnp.mean(jnp.square(xf - mu), axis=-1, keepdims=True)
    return ((xf - mu) * lax.rsqrt(var + NORM_EPS) * g.astype(jnp.float32)
            + b.astype(jnp.float32)).astype(x.dtype)


def rms_norm(x, g, out_dtype):
    xf = x.astype(jnp.float32)
    inv = lax.rsqrt(jnp.mean(jnp.square(xf), axis=-1, keepdims=True) + NORM_EPS)
    return (xf * inv * g.astype(jnp.float32)).astype(out_dtype)


def t5_bucket(rel):
    half = REL_BUCKETS // 2
    max_exact = half // 2
    ret = (rel > 0).astype(np.int32) * half
    n = np.abs(rel)
    large = max_exact + (np.log(np.maximum(n, 1) / max_exact)
                         / math.log(REL_MAX_DIST / max_exact) * (half - max_exact)).astype(np.int32)
    large = np.minimum(large, half - 1)
    return ret + np.where(n < max_exact, n, large)


def dilated_branch(q, k, v, rel_table, window, dilation):
    b_, s, h, dh = q.shape
    r = window // (2 * dilation)
    blk = r
    l = s // dilation
    nb = -(-l // blk)
    lp = nb * blk
    n = b_ * dilation

    def to_sub(t):
        t = t.reshape(b_, l, dilation, h, dh).transpose(0, 2, 1, 3, 4).reshape(n, l, h, dh)
        return jnp.pad(t, ((0, 0), (0, lp - l), (0, 0), (0, 0)))

    def neighbours(t):
        tp = jnp.pad(t, ((0, 0), (blk, blk), (0, 0), (0, 0))).reshape(n, nb + 2, blk, h, dh)
        return jnp.concatenate([tp[:, :-2], tp[:, 1:-1], tp[:, 2:]], axis=2)

    qb = to_sub(q).reshape(n, nb, blk, h, dh)
    kw = neighbours(to_sub(k))
    vw = neighbours(to_sub(v))

    qi = np.arange(blk)[:, None]
    ti = np.arange(3 * blk)[None, :]
    rel = ti - blk - qi
    kidx = np.arange(nb)[:, None, None] * blk - blk + ti[None]
    valid = (np.abs(rel) <= r)[None] & (kidx >= 0) & (kidx < l)
    bias = rel_table[t5_bucket(rel * dilation)]
    bias = jnp.transpose(bias, (2, 0, 1)).astype(jnp.float32)

    logits = jnp.einsum('nbqhd,nbkhd->nbhqk', qb, kw).astype(jnp.float32) + bias[None, None]
    logits = jnp.where(valid[None, :, None], logits, -jnp.inf)
    m = jnp.max(logits, axis=-1, keepdims=True)
    p = jnp.exp(logits - m)
    denom = jnp.sum(p, axis=-1, keepdims=True)
    o = jnp.einsum('nbhqk,nbkhd->nbqhd', p / denom, vw.astype(jnp.float32))
    lse = (m + jnp.log(denom))[..., 0]

    o = o.reshape(n, lp, h, dh)[:, :l].reshape(b_, dilation, l, h, dh)
    o = o.transpose(0, 2, 1, 3, 4).reshape(b_, s, h, dh)
    lse = lse.transpose(0, 1, 3, 2).reshape(n, lp, h)[:, :l].reshape(b_, dilation, l, h)
    lse = lse.transpose(0, 2, 1, 3).reshape(b_, s, h)
    return o, lse


def dilated_attention(q, k, v, rel_table):
    outs, lses = [], []
    for window, dilation in DILATED_BRANCHES:
        o, lse = dilated_branch(q, k, v, rel_table, window, dilation)
        outs.append(o)
        lses.append(lse)
    w = jax.nn.softmax(jnp.stack(lses, axis=0), axis=0)
    return jnp.sum(w[..., None] * jnp.stack(outs, axis=0), axis=0)


def centred_depthwise_conv(u, w, b):
    kw = w.shape[0]
    out = lax.conv_general_dilated(
        u, w[:, None, :].astype(u.dtype), window_strides=(1,),
        padding=[(kw // 2, kw // 2)], dimension_numbers=('NWC', 'WIO', 'NWC'),
        feature_group_count=u.shape[-1])
    return out + b


def ssd_chunked(xs, dt, a, bm, cm):
    b_, l, h, p = xs.shape
    g, n = bm.shape[2], bm.shape[3]
    e = h // g
    qc = SSD_CHUNK
    c = l // qc
    f32 = jnp.float32
    xdt = (xs.astype(f32) * dt[..., None]).reshape(b_, c, qc, g, e, p)
    da = (dt * a.astype(f32)).reshape(b_, c, qc, g, e)
    a_cum = jnp.cumsum(da, axis=2).transpose(0, 1, 3, 4, 2)
    bm = bm.astype(f32).reshape(b_, c, qc, g, n)
    cm = cm.astype(f32).reshape(b_, c, qc, g, n)

    lower = np.tril(np.ones((qc, qc), dtype=bool))
    seg = a_cum[..., :, None] - a_cum[..., None, :]
    decay = jnp.exp(jnp.where(lower, seg, -jnp.inf))
    cb = jnp.einsum('bcqgn,bcsgn->bcgqs', cm, bm)
    y_diag = jnp.einsum('bcgeqs,bcsgep->bcqgep', cb[:, :, :, None] * decay, xdt)

    decay_states = jnp.exp(a_cum[..., -1:] - a_cum)
    states = jnp.einsum('bcqgn,bcgeq,bcqgep->bcgepn', bm, decay_states, xdt)
    chunk_decay = jnp.exp(a_cum[..., -1])

    def step(carry, inp):
        st, dec = inp
        return dec[..., None, None] * carry + st, carry

    init = jnp.zeros((b_, g, e, p, n), f32)
    _, prev = lax.scan(step, init, (jnp.moveaxis(states, 1, 0), jnp.moveaxis(chunk_decay, 1, 0)))
    prev = jnp.moveaxis(prev, 0, 1)
    y_off = jnp.einsum('bcqgn,bcgepn,bcgeq->bcqgep', cm, prev, jnp.exp(a_cum))
    return (y_diag + y_off).reshape(b_, l, h, p)


def encoder_layer(x, rel_bias, w_in, conv_w, conv_b, dt_bias_fwd, dt_bias_bwd,
                  a_log_fwd, a_log_bwd, d_skip, attn_norm_g, ssd_norm_g, w_out,
                  ln1_g, ln1_b, w_gate, w_up, w_down, ln2_g, ln2_b):
    b_, s, _ = x.shape
    f32 = jnp.float32
    proj = jnp.einsum('bsd,de->bse', x, w_in)
    split_at = [D_ATTN, 2 * D_ATTN, 3 * D_ATTN, 3 * D_ATTN + D_SSD,
                3 * D_ATTN + 2 * D_SSD + 2 * SSD_GROUPS * SSD_STATE]
    q, k, v, z, xbc, dt_raw = jnp.split(proj, split_at, axis=-1)

    def heads(t):
        return t.reshape(b_, s, ATTN_HEADS, ATTN_HEAD_DIM)
    attn = dilated_attention(heads(q) * (ATTN_HEAD_DIM ** -0.5), heads(k), heads(v), rel_bias)
    attn = rms_norm(attn.reshape(b_, s, D_ATTN), attn_norm_g, x.dtype)

    xbc = jax.nn.silu(centred_depthwise_conv(xbc, conv_w, conv_b))
    xs, bm, cm = jnp.split(xbc, [D_SSD, D_SSD + SSD_GROUPS * SSD_STATE], axis=-1)
    xs = xs.reshape(b_, s, SSD_HEADS, SSD_HEAD_DIM)
    bm = bm.reshape(b_, s, SSD_GROUPS, SSD_STATE)
    cm = cm.reshape(b_, s, SSD_GROUPS, SSD_STATE)
    dt = jax.nn.softplus(dt_raw.astype(f32)
                         + jnp.concatenate([dt_bias_fwd, dt_bias_bwd]).astype(f32))
    dt_f, dt_b = dt[..., :SSD_HEADS], dt[..., SSD_HEADS:]

    def flip(t):
        return jnp.flip(t, axis=1)
    y_f = ssd_chunked(xs, dt_f, -jnp.exp(a_log_fwd.astype(f32)), bm, cm)
    y_b = flip(ssd_chunked(flip(xs), flip(dt_b), -jnp.exp(a_log_bwd.astype(f32)), flip(bm), flip(cm)))
    y = y_f + y_b + d_skip.astype(f32)[:, None] * xs.astype(f32)
    y = rms_norm(y.reshape(b_, s, D_SSD) * jax.nn.silu(z.astype(f32)), ssd_norm_g, x.dtype)

    mix = jnp.einsum('bse,ed->bsd', jnp.concatenate([attn, y], axis=-1), w_out)
    x = layer_norm(DEEPNORM_ALPHA * x + mix, ln1_g, ln1_b)

    hidden = jax.nn.silu(jnp.einsum('bsd,df->bsf', x, w_gate)) * jnp.einsum('bsd,df->bsf', x, w_up)
    ffn = jnp.einsum('bsf,fd->bsd', hidden, w_down)
    return layer_norm(DEEPNORM_ALPHA * x + ffn, ln2_g, ln2_b)


def encoder_trunk(x, rel_bias, layer_params):
    for layer in range(DEPTH):
        x = encoder_layer(x, rel_bias, *[prm[layer] for prm in layer_params])
    return x


def setup_inputs(seed: int = 0) -> dict:
    key = jax.random.key(seed)
    ks = jax.random.split(key, 32)
    f32 = jnp.float32

    def nrm(k, shape, scale):
        return jax.random.normal(k, shape, f32) * scale

    s_in = D_MODEL ** -0.5
    x_prompt = jax.random.normal(ks[0], (BATCH, SEQ, D_MODEL), f32)
    x_sample = jax.random.normal(ks[1], (DEC_BATCH, DEC_SEQ, D_MODEL), f32)
    rel_bias = nrm(ks[2], (REL_BUCKETS, ATTN_HEADS), 0.5)
    w_in = jnp.concatenate([
        nrm(ks[3], (DEPTH, D_MODEL, D_ATTN), s_in),
        nrm(ks[4], (DEPTH, D_MODEL, D_ATTN), s_in),
        nrm(ks[5], (DEPTH, D_MODEL, D_ATTN), s_in * DEEPNORM_BETA),
        nrm(ks[6], (DEPTH, D_MODEL, D_SSD), s_in),
        nrm(ks[7], (DEPTH, D_MODEL, D_SSD), s_in * DEEPNORM_BETA),
        nrm(ks[8], (DEPTH, D_MODEL, SSD_GROUPS * SSD_STATE), s_in),
        nrm(ks[9], (DEPTH, D_MODEL, SSD_GROUPS * SSD_STATE), s_in),
        nrm(ks[10], (DEPTH, D_MODEL, 2 * SSD_HEADS), s_in * 0.1),
    ], axis=-1)
    conv_ch = D_SSD + 2 * SSD_GROUPS * SSD_STATE
    conv_w = nrm(ks[11], (DEPTH, SSD_CONV, conv_ch), SSD_CONV ** -0.5)
    conv_b = nrm(ks[12], (DEPTH, conv_ch), 0.01)

    def dt_bias_init(k):
        dt0 = jnp.exp(jax.random.uniform(k, (DEPTH, SSD_HEADS), f32,
                                         minval=math.log(1e-3), maxval=math.log(1e-1)))
        return dt0 + jnp.log(-jnp.expm1(-dt0))

    dt_bias_fwd = dt_bias_init(ks[13])
    dt_bias_bwd = dt_bias_init(ks[14])
    a_log_fwd = jnp.log(jax.random.uniform(ks[15], (DEPTH, SSD_HEADS), f32, minval=1.0, maxval=16.0))
    a_log_bwd = jnp.log(jax.random.uniform(ks[16], (DEPTH, SSD_HEADS), f32, minval=1.0, maxval=16.0))
    d_skip = 1.0 + nrm(ks[17], (DEPTH, SSD_HEADS), 0.1)
    attn_norm_g = 1.0 + nrm(ks[18], (DEPTH, D_ATTN), 0.01)
    ssd_norm_g = 1.0 + nrm(ks[19], (DEPTH, D_SSD), 0.01)
    w_out = nrm(ks[20], (DEPTH, D_MODEL, D_MODEL), s_in * DEEPNORM_BETA)
    ln1_g = 1.0 + nrm(ks[21], (DEPTH, D_MODEL), 0.01)
    ln1_b = nrm(ks[22], (DEPTH, D_MODEL), 0.01)
    w_gate = nrm(ks[23], (DEPTH, D_MODEL, D_FF), s_in * DEEPNORM_BETA)
    w_up = nrm(ks[24], (DEPTH, D_MODEL, D_FF), s_in * DEEPNORM_BETA)
    w_down = nrm(ks[25], (DEPTH, D_FF, D_MODEL), D_FF ** -0.5 * DEEPNORM_BETA)
    ln2_g = 1.0 + nrm(ks[26], (DEPTH, D_MODEL), 0.01)
    ln2_b = nrm(ks[27], (DEPTH, D_MODEL), 0.01)
    return {'x_prompt': x_prompt, 'x_sample': x_sample, 'rel_bias': rel_bias,
            'w_in': w_in, 'conv_w': conv_w, 'conv_b': conv_b,
            'dt_bias_fwd': dt_bias_fwd, 'dt_bias_bwd': dt_bias_bwd,
            'a_log_fwd': a_log_fwd, 'a_log_bwd': a_log_bwd, 'd_skip': d_skip,
            'attn_norm_g': attn_norm_g, 'ssd_norm_g': ssd_norm_g, 'w_out': w_out,
            'ln1_g': ln1_g, 'ln1_b': ln1_b, 'w_gate': w_gate, 'w_up': w_up,
            'w_down': w_down, 'ln2_g': ln2_g, 'ln2_b': ln2_b}


def reference(x_prompt, x_sample, rel_bias, w_in, conv_w, conv_b, dt_bias_fwd, dt_bias_bwd,
              a_log_fwd, a_log_bwd, d_skip, attn_norm_g, ssd_norm_g, w_out,
              ln1_g, ln1_b, w_gate, w_up, w_down, ln2_g, ln2_b):
    layer_params = (w_in, conv_w, conv_b, dt_bias_fwd, dt_bias_bwd, a_log_fwd, a_log_bwd,
                    d_skip, attn_norm_g, ssd_norm_g, w_out, ln1_g, ln1_b,
                    w_gate, w_up, w_down, ln2_g, ln2_b)
    y_prompt = encoder_trunk(x_prompt, rel_bias, layer_params)
    y_sample = encoder_trunk(x_sample, rel_bias, layer_params)
    return (y_prompt, y_sample)
```

```python
import contextlib
import math
import numpy as np
import concourse.bass as bass
import concourse.mybir as mybir
from concourse.bass_utils import run_bass_kernel_spmd

F32 = mybir.dt.float32
BF16 = mybir.dt.bfloat16
U8 = mybir.dt.uint8
AF = mybir.ActivationFunctionType
ALU = mybir.AluOpType

D = 1024
DIN = 3088
DFF = 2816
NH = 8
COL_Q, COL_K, COL_V, COL_Z, COL_X, COL_DT = 0, 512, 1024, 1536, 2048, 3072
ALPHA = 2.0 ** 0.25
EPS = 1e-5
ENGS = ['pe', 'act', 'dve', 'pool', 'sp']
EPOCH = 20000
POOL_BYTES = 206 * 1024


def _dsize(dt):
    return {F32: 4, BF16: 2, U8: 1}[dt]


class Res:
    __slots__ = ('name', 'w', 'r')

    def __init__(self, name):
        self.name = name
        self.w = {}
        self.r = {}


class Chan:
    def __init__(self, name):
        self.name = name
        self.n = 0
        self.sem = None


class Prog:
    def __init__(self, nc):
        self.nc = nc
        self.streams = {e: [] for e in ENGS}
        self.last_op = {e: None for e in ENGS}
        self.chans = []
        self.stack = contextlib.ExitStack()
        self.nres = 0
        self.pool = self.stack.enter_context(nc.sbuf_tensor("pool", [128, POOL_BYTES], U8))
        self.off = 0
        self.peak = 0
        self.banks = [self.stack.enter_context(nc.psum_tensor(f"bank{i}", [128, 512], F32))
                      for i in range(8)]
        self.Rbank = [self.res(f"bank{i}") for i in range(8)]

    def res(self, name=None):
        self.nres += 1
        return Res(name or f"r{self.nres}")

    def chan(self, name):
        c = Chan(name)
        self.chans.append(c)
        return c

    def alloc(self, shape, dtype):
        n = 1
        for s in shape[1:]:
            n *= s
        nbytes = n * _dsize(dtype)
        self.off = (self.off + 63) // 64 * 64
        assert self.off + nbytes <= POOL_BYTES, f"SBUF overflow {self.off + nbytes}"
        ap = self.pool[0:shape[0], self.off:self.off + nbytes].bitcast(dtype)
        self.off += nbytes
        self.peak = max(self.peak, self.off)
        if len(shape) > 2:
            names = [f"d{i}" for i in range(len(shape) - 1)]
            pat = "p (" + " ".join(names) + ") -> p " + " ".join(names)
            ap = ap.rearrange(pat, **{names[i]: shape[i + 1] for i in range(len(names))})
        return ap

    def mark(self):
        return self.off

    bank_list = list(range(8))

    def next_bank(self):
        self.bank_i = getattr(self, 'bank_i', -1) + 1
        return self.bank_list[self.bank_i % len(self.bank_list)]

    def reset(self, m):
        self.off = m

    def _deps(self, reads, writes):
        deps = set()
        for r in reads:
            deps.update(r.w.values())
        for w in writes:
            deps.update(w.w.values())
            deps.update(w.r.values())
        return deps

    def op(self, eng, fn, reads=(), writes=()):
        st = self.streams[eng]
        idx = len(st)
        deps = self._deps(reads, writes)
        st.append(dict(fn=fn, deps=deps, signal=False, chan=None))
        ev = (eng, idx)
        self.last_op[eng] = idx
        for r in reads:
            r.r[eng] = ev
        for w in writes:
            w.w = {eng: ev}
            w.r = {}
        return ev

    def dma(self, q, chan, fn, reads=(), writes=()):
        st = self.streams[q]
        deps = self._deps(reads, writes)
        if chan.n > 0:
            deps.add((chan, chan.n - 1))
        st.append(dict(fn=fn, deps=deps, signal=True, chan=chan))
        ev = (chan, chan.n)
        chan.n += 1
        for r in reads:
            r.r[chan] = ev
        for w in writes:
            w.w = {chan: ev}
            w.r = {}
        return ev

    def barrier(self):
        evs = set()
        for e in ENGS:
            if self.last_op[e] is not None:
                evs.add((e, self.last_op[e]))
        for c in self.chans:
            if c.n > 0:
                evs.add((c, c.n - 1))
        for e in ENGS:
            self.streams[e].append(dict(fn=None, deps=set(d for d in evs if d[0] != e),
                                        signal=False, chan=None))

    def wait_all(self, eng, evs):
        self.streams[eng].append(dict(fn=None, deps=set(evs), signal=False, chan=None))

    def emit(self):
        nc = self.nc
        for e in ENGS:
            for ins in self.streams[e]:
                for (prod, idx) in ins['deps']:
                    if isinstance(prod, str):
                        if prod == 'pe' and e == 'pe' and ins['chan'] is None and ins['fn'] is not None:
                            continue
                        self.streams[prod][idx]['signal'] = True
        nsig = {}
        for e in ENGS:
            c = 0
            for ins in self.streams[e]:
                if ins['chan'] is None and ins['signal'] and ins['fn'] is not None:
                    c += 1
                    ins['cnt'] = c
            nsig[e] = c
        sems = {}
        for e in ENGS:
            ne = max(1, -(-nsig[e] // EPOCH))
            sems[e] = [self.stack.enter_context(nc.semaphore(f"s_{e}{i}")) for i in range(ne)]
        for c in self.chans:
            c.sem = self.stack.enter_context(nc.semaphore(f"c_{c.name}"))
        self.stats = {e: [len(self.streams[e]), nsig[e], 0] for e in ENGS}

        def emit_stream(e, h):
            waited = {}
            nw = 0
            for ins in self.streams[e]:
                need = {}
                for (prod, idx) in ins['deps']:
                    if isinstance(prod, str):
                        if prod == 'pe' and e == 'pe' and ins['chan'] is None and ins['fn'] is not None:
                            continue
                        cnt = self.streams[prod][idx]['cnt']
                        ep = (cnt - 1) // EPOCH
                        sem = sems[prod][ep]
                        val = cnt - ep * EPOCH
                    else:
                        sem = prod.sem
                        val = 16 * (idx + 1)
                    k = id(sem)
                    if waited.get(k, 0) >= val:
                        continue
                    if k not in need or need[k][1] < val:
                        need[k] = (sem, val)
                for k, (sem, val) in need.items():
                    h.wait_ge(sem, val)
                    waited[k] = val
                    nw += 1
                if ins['fn'] is None:
                    continue
                r = ins['fn'](h)
                if ins['chan'] is not None:
                    r.then_inc(ins['chan'].sem, 16)
                elif ins['signal']:
                    cnt = ins['cnt']
                    ep = (cnt - 1) // EPOCH
                    r.then_inc(sems[e][ep], 1)
            self.stats[e][2] = nw

        with nc.Block() as block:
            @block.tensor
            def _(h):
                emit_stream('pe', h)

            @block.scalar
            def _(h):
                emit_stream('act', h)

            @block.vector
            def _(h):
                emit_stream('dve', h)

            @block.gpsimd
            def _(h):
                emit_stream('pool', h)

            @block.sync
            def _(h):
                emit_stream('sp', h)
        self.stack.close()


def mm_group(P, out_ap, pairs, Rout, reads):
    n = len(pairs)
    for i, (l, r) in enumerate(pairs):
        P.op('pe', lambda h, l=l, r=r, i=i: h.matmul(out_ap, l, r, start=(i == 0), stop=(i == n - 1)),
             reads=reads, writes=[Rout])


def act(P, out, in_, func, reads, writes, scale=None, bias=None):
    kw = {}
    if scale is not None:
        kw['scale'] = scale
    if bias is not None:
        kw['bias'] = bias
    return P.op('act', lambda h: h.activation(out=out, in_=in_, func=func, **kw), reads=reads, writes=writes)


class Ring:
    def __init__(self, P, name, n, shape, dtype, chan=False):
        self.bufs = [P.alloc(shape, dtype) for _ in range(n)]
        self.res = [P.res(f"{name}{i}") for i in range(n)]
        self.ch = [P.chan(f"{name}{i}") for i in range(n)] if chan else None
        self.i = -1
        self.n = n

    def next(self):
        self.i = (self.i + 1) % self.n
        if self.ch:
            return self.bufs[self.i], self.res[self.i], self.ch[self.i]
        return self.bufs[self.i], self.res[self.i]


class Ctx:
    pass


def build(L, NS, debug=False, phases=('A', 'S', 'T', 'C')):
    nc = bass.Bass("TRN2", target_bir_lowering=False)
    P = Prog(nc)
    C = Ctx()
    C.L, C.NS, C.P, C.nc = L, NS, P, nc
    NT = L // 512

    def din(name, shape, dt=F32):
        return nc.dram_tensor(name, list(shape), dt, kind="ExternalInput").ap()

    def dscr(name, shape, dt):
        return nc.dram_tensor(name, list(shape), dt,
                              kind="ExternalOutput" if debug else "Internal").ap()

    C.xT = din("xT", [NS, D, L])
    C.w_in = din("w_in", [D, DIN])
    C.convw = din("convw", [128, 8, 5])
    C.convb = din("convb", [128, 8])
    C.dtb = din("dtb", [128, 16])
    C.alog = din("alog", [128, 16])
    C.dsk = din("dsk", [128, 8])
    C.ang = din("ang", [128, 4])
    C.sng = din("sng", [128, 512])
    C.relb = din("relb", [32, 8])
    C.w_out = din("w_out", [D, D])
    C.w_gate = din("w_gate", [D, DFF])
    C.w_up = din("w_up", [D, DFF])
    C.w_down = din("w_down", [DFF, D])
    C.cst = din("cst", [128, 6, 128])
    C.oh = din("oh", [32, 6, 256])
    C.eband = din("eband", [128, 2, 384])
    C.sel = din("sel", [65, 64])
    C.lnfm = din("lnfm", [128, 4, 8])
    C.yT = nc.dram_tensor("yT", [NS, D, L], F32, kind="ExternalOutput").ap()
    C.H1F = dscr("H1F", [NS, D, L], F32)
    C.H1B = dscr("H1B", [NS, D, L], BF16)

    C.QT = dscr("QT", [NS, 512, L], BF16)
    C.KT = dscr("KT", [NS, 512, L], BF16)
    C.V = dscr("V", [NS, L, 8 * 65], BF16)
    C.Z = dscr("Z", [NS, L, 512], F32)
    C.XS = dscr("XS", [NS, 512, L], F32)
    C.BC = dscr("BC", [NS, 512, L], BF16)
    C.DT = dscr("DT", [NS, L, 16], F32)

    outs = []
    C.YN = dscr("YN", [NS, 512, L], BF16)
    C.AT = dscr("AT", [NS, 512, L], F32)
    if 'A' in phases:
        phase_A(C)
        P.barrier()
    if 'S' in phases:
        phase_S(C)
        P.barrier()
    if 'T' in phases:
        phase_T(C)
        P.barrier()
    if 'C' in phases:
        phase_C1(C)
        P.barrier()
        outs = phase_C2(C)
        P.barrier()
    if debug:
        evs = [(c, c.n - 1) for c in P.chans if c.n > 0]
        P.wait_all('sp', evs)
    else:
        P.wait_all('sp', outs)
    P.emit()
    return nc, P


def phase_A(C):
    P, L, NS = C.P, C.L, C.NS
    NT = L // 512
    m0 = P.mark()
    win = P.alloc([128, 8, DIN], BF16)
    Rwin = P.res("win")
    cw = P.alloc([128, 8, 5], F32)
    cb = P.alloc([128, 8], F32)
    dtb = P.alloc([128, 16], F32)
    Rsm = P.res("small")
    ch_w = P.chan("w")
    w_v = C.w_in.rearrange("(kc p) c -> p kc c", p=128)
    for c0 in (0, 1544):
        P.dma('pool', ch_w, lambda h, c0=c0: h.dma_start(out=win[:, :, c0:c0 + 1544], in_=w_v[:, :, c0:c0 + 1544]),
              writes=[Rwin])
    ch_s = P.chan("small")
    P.dma('sp', ch_s, lambda h: h.dma_start(out=cw, in_=C.convw), writes=[Rsm])
    P.dma('sp', ch_s, lambda h: h.dma_start(out=cb, in_=C.convb), writes=[Rsm])
    P.dma('sp', ch_s, lambda h: h.dma_start(out=dtb, in_=C.dtb), writes=[Rsm])

    xtb = Ring(P, "xtb", 2, [128, 8, 512], BF16, chan=True)
    stq = Ring(P, "stq", 2, [128, 4, 512], BF16, chan=True)
    stk = Ring(P, "stk", 2, [128, 4, 512], BF16, chan=True)
    stx = Ring(P, "stx", 2, [128, 4, 512], F32, chan=True)
    stbc = Ring(P, "stbc", 2, [128, 4, 512], BF16, chan=True)
    stv = Ring(P, "stv", 2, [128, 4, 8, 65], BF16, chan=True)
    stz = Ring(P, "stz", 2, [128, 4, 512], F32, chan=True)
    stdt = Ring(P, "stdt", 2, [128, 4, 16], F32, chan=True)
    raw = [P.alloc([128, 520], F32) for _ in range(8)]
    Rraw = [P.res(f"raw{c}") for c in range(8)]
    acc = Ring(P, "acc", 4, [128, 512], F32)
    dtt = Ring(P, "dtt", 2, [128, 16], F32)
    for b, r in zip(stv.bufs, stv.res):
        P.op('pool', lambda h, b=b: h.memset(b, 1.0), writes=[r])
    fm_banks = [0, 1, 2]
    tm_banks = [3, 4, 5, 6]
    dt_bank = 7
    fmi = [0]
    tmi = [0]

    def next_fm():
        b = fm_banks[fmi[0] % len(fm_banks)]
        fmi[0] += 1
        return b

    def next_tm():
        b = tm_banks[tmi[0] % len(tm_banks)]
        tmi[0] += 1
        return b

    def load_x(s, T):
        buf, r, ch = xtb.next()
        src = C.xT[s].rearrange("(kc p) t -> p kc t", p=128)[:, :, T * 512:(T + 1) * 512]
        P.dma('pool', ch, lambda h: h.dma_start(out=buf, in_=src), writes=[r])
        return buf, r

    def conv_chunk_ops(c, width, accb, Racc):
        ops = []
        rb = raw[c]
        ops.append(lambda: P.op('dve', lambda h: h.tensor_scalar(
            out=accb[:, 0:width], in0=rb[:, 0:width], scalar1=cw[:, c, 0:1], scalar2=cb[:, c:c + 1],
            op0=ALU.mult, op1=ALU.add), reads=[Rraw[c], Rsm], writes=[Racc]))
        for j in range(1, 5):
            ops.append(lambda j=j: P.op('dve', lambda h: h.scalar_tensor_tensor(
                out=accb[:, 0:width], in0=rb[:, j:j + width], scalar=cw[:, c, j:j + 1], in1=accb[:, 0:width],
                op0=ALU.mult, op1=ALU.add), reads=[Rraw[c], Rsm, Racc], writes=[Racc]))
        return ops

    def conv_finish(s, c, width, accb, Racc, xbuf, xr, bcbuf, bcr):
        if c < 4:
            act(P, xbuf[:, c, 0:width], accb[:, 0:width], AF.Silu, [Racc], [xr])
        else:
            act(P, bcbuf[:, c - 4, 0:width], accb[:, 0:width], AF.Silu, [Racc], [bcr])

    for s in range(NS):
        for c in range(8):
            P.op('pool', lambda h, c=c: h.memset(raw[c][:, 0:4], 0.0), writes=[Rraw[c]])
        nxt = load_x(s, 0)
        for T in range(NT):
            xb, xr_ = nxt
            if T + 1 < NT:
                nxt = load_x(s, T + 1)
            t0 = T * 512
            qb, qr, qch = stq.next()
            kb, kr, kch = stk.next()
            for c in range(4):
                b = next_fm()
                mm_group(P, P.banks[b][:, :], [(win[:, kc, COL_Q + c * 128:COL_Q + (c + 1) * 128], xb[:, kc, :])
                                               for kc in range(8)], P.Rbank[b], [Rwin, xr_])
                act(P, qb[:, c, :], P.banks[b][:, :], AF.Identity, [P.Rbank[b]], [qr], scale=0.125)
            P.dma('sp', qch, lambda h, qb=qb, s=s, t0=t0: h.dma_start(
                out=C.QT[s].rearrange("(c p) t -> p c t", p=128)[:, :, t0:t0 + 512], in_=qb), reads=[qr])
            for c in range(4):
                b = next_fm()
                mm_group(P, P.banks[b][:, :], [(win[:, kc, COL_K + c * 128:COL_K + (c + 1) * 128], xb[:, kc, :])
                                               for kc in range(8)], P.Rbank[b], [Rwin, xr_])
                P.op('dve', lambda h, b=b, c=c, kb=kb: h.tensor_copy(out=kb[:, c, :], in_=P.banks[b][:, :]),
                     reads=[P.Rbank[b]], writes=[kr])
            P.dma('sp', kch, lambda h, kb=kb, s=s, t0=t0: h.dma_start(
                out=C.KT[s].rearrange("(c p) t -> p c t", p=128)[:, :, t0:t0 + 512], in_=kb), reads=[kr])
            sxb, sxr, sxch = stx.next()
            sbb, sbr, sbch = stbc.next()
            for cp in range(4):
                chains = []
                accs = []
                for c in (2 * cp, 2 * cp + 1):
                    b = next_fm()
                    mm_group(P, P.banks[b][:, :],
                             [(win[:, kc, COL_X + c * 128:COL_X + (c + 1) * 128], xb[:, kc, :]) for kc in range(8)],
                             P.Rbank[b], [Rwin, xr_])
                    act(P, raw[c][:, 4:516], P.banks[b][:, :], AF.Identity, [P.Rbank[b]], [Rraw[c]])
                    ab, ar = acc.next()
                    accs.append((c, ab, ar))
                    chains.append(conv_chunk_ops(c, 512, ab, ar))
                for j in range(5):
                    for ch_ in chains:
                        ch_[j]()
                for (c, ab, ar) in accs:
                    conv_finish(s, c, 512, ab, ar, sxb, sxr, sbb, sbr)
                    P.op('pool', lambda h, c=c: h.tensor_copy(out=raw[c][:, 0:4], in_=raw[c][:, 512:516]),
                         reads=[Rraw[c]], writes=[Rraw[c]])
            lo = 2 if T == 0 else 0
            P.dma('sp', sxch, lambda h, sxb=sxb, s=s, t0=t0, lo=lo: h.dma_start(
                out=C.XS[s].rearrange("(c p) t -> p c t", p=128)[:, :, t0 - 2 + lo:t0 + 510],
                in_=sxb[:, :, lo:512]), reads=[sxr])
            P.dma('sp', sbch, lambda h, sbb=sbb, s=s, t0=t0, lo=lo: h.dma_start(
                out=C.BC[s].rearrange("(c p) t -> p c t", p=128)[:, :, t0 - 2 + lo:t0 + 510],
                in_=sbb[:, :, lo:512]), reads=[sbr])
            vb, vr, vch = stv.next()
            zb, zr, zch = stz.next()
            db, dr, dch = stdt.next()
            for u in range(4):
                lw = [xb[:, kc, u * 128:(u + 1) * 128] for kc in range(8)]
                b = next_tm()
                mm_group(P, P.banks[b][:, :], [(lw[kc], win[:, kc, COL_V:COL_V + 512]) for kc in range(8)],
                         P.Rbank[b], [Rwin, xr_])
                act(P, vb[:, u, :, 0:64], P.banks[b][:, :].rearrange("p (h d) -> p h d", d=64), AF.Identity,
                    [P.Rbank[b]], [vr])
                b = next_tm()
                mm_group(P, P.banks[b][:, :], [(lw[kc], win[:, kc, COL_Z:COL_Z + 512]) for kc in range(8)],
                         P.Rbank[b], [Rwin, xr_])
                act(P, zb[:, u, :], P.banks[b][:, :], AF.Silu, [P.Rbank[b]], [zr])
                b = dt_bank
                mm_group(P, P.banks[b][:, 0:16], [(lw[kc], win[:, kc, COL_DT:COL_DT + 16]) for kc in range(8)],
                         P.Rbank[b], [Rwin, xr_])
                tb, tr = dtt.next()
                P.op('dve', lambda h, b=b, tb=tb: h.tensor_tensor(out=tb, in0=P.banks[b][:, 0:16], in1=dtb, op=ALU.add),
                     reads=[P.Rbank[b], Rsm], writes=[tr])
                act(P, tb, tb, AF.Exp, [tr], [tr])
                act(P, db[:, u, :], tb, AF.Ln, [tr], [dr], bias=1.0)
            P.dma('sp', vch, lambda h, vb=vb, s=s, t0=t0: h.dma_start(
                out=C.V[s][t0:t0 + 512, :].rearrange("(u p) f -> p u f", p=128),
                in_=vb.rearrange("p u h e -> p u (h e)")), reads=[vr])
            P.dma('sp', zch, lambda h, zb=zb, s=s, t0=t0: h.dma_start(
                out=C.Z[s][t0:t0 + 512, :].rearrange("(u p) f -> p u f", p=128), in_=zb), reads=[zr])
            P.dma('sp', dch, lambda h, db=db, s=s, t0=t0: h.dma_start(
                out=C.DT[s][t0:t0 + 512, :].rearrange("(u p) f -> p u f", p=128), in_=db), reads=[dr])
        sxb, sxr, sxch = stx.next()
        sbb, sbr, sbch = stbc.next()
        for c in range(8):
            P.op('pool', lambda h, c=c: h.memset(raw[c][:, 4:8], 0.0), reads=[Rraw[c]], writes=[Rraw[c]])
            ab, ar = acc.next()
            for o in conv_chunk_ops(c, 2, ab, ar):
                o()
            conv_finish(s, c, 2, ab, ar, sxb, sxr, sbb, sbr)
        P.dma('sp', sxch, lambda h, sxb=sxb, s=s: h.dma_start(
            out=C.XS[s].rearrange("(c p) t -> p c t", p=128)[:, :, L - 2:L], in_=sxb[:, :, 0:2]),
            reads=[sxr])
        P.dma('sp', sbch, lambda h, sbb=sbb, s=s: h.dma_start(
            out=C.BC[s].rearrange("(c p) t -> p c t", p=128)[:, :, L - 2:L], in_=sbb[:, :, 0:2]),
            reads=[sbr])
    P.reset(m0)


def phase_S(C):
    P, L, NS = C.P, C.L, C.NS
    NC = L // 128
    NG = NC // 4
    m0 = P.mark()
    cst = P.alloc([128, 6, 128], F32)
    idb = P.alloc([128, 128], BF16)
    dsk = P.alloc([128, 8], F32)
    Aneg = P.alloc([128, 16], F32)
    sng = P.alloc([128, 512], F32)
    Rc = P.res("s_const")
    chc = P.chan("s_const")
    P.dma('sp', chc, lambda h: h.dma_start(out=cst, in_=C.cst), writes=[Rc])
    P.dma('sp', chc, lambda h: h.dma_start(out=dsk, in_=C.dsk), writes=[Rc])
    P.dma('sp', chc, lambda h: h.dma_start(out=Aneg, in_=C.alog), writes=[Rc])
    P.dma('sp', chc, lambda h: h.dma_start(out=sng, in_=C.sng), writes=[Rc])
    act(P, Aneg, Aneg, AF.Exp, [Rc], [Rc])
    P.op('dve', lambda h: h.tensor_scalar(out=Aneg, in0=Aneg, scalar1=-1.0, scalar2=None, op0=ALU.mult),
         reads=[Rc], writes=[Rc])
    P.op('dve', lambda h: h.tensor_copy(out=idb, in_=cst[:, 5, :]), reads=[Rc], writes=[Rc])
    U_, SL_, LO_, SU_, ON_, ID_ = [cst[:, i, :] for i in range(6)]

    SbAll = P.alloc([128, NC, 512], BF16)
    RSb = [P.res(f"sb{c}") for c in range(NC)]
    gx = Ring(P, "gx", 2, [128, 4, 512], F32, chan=True)
    gbc = Ring(P, "gbc", 2, [128, 4, 512], BF16, chan=True)
    gdt = Ring(P, "gdt", 2, [128, 4, 16], F32, chan=True)
    gz = Ring(P, "gz", 2, [128, 4, 512], F32, chan=True)
    syn = Ring(P, "syn", 2, [128, 4, 512], BF16, chan=True)
    da_r = Ring(P, "da", 3, [128, 16], F32)
    ew_r = Ring(P, "ew", 3, [128, 64], F32)
    sc_r = Ring(P, "sc", 3, [128, 16], F32)
    xdt_r = Ring(P, "xdt", 6, [128, 512], BF16)
    xsd_r = Ring(P, "xsd", 2, [128, 512], F32)
    btok_r = Ring(P, "btok", 2, [128, 256], BF16)
    cbm_r = Ring(P, "cbm", 4, [128, 256], F32)
    L_r = Ring(P, "Lr", 2, [128, 8, 128], F32)
    dec_r = Ring(P, "dec", 2, [128, 8, 128], F32)
    M_r = Ring(P, "Mr", 4, [128, 8, 128], BF16)
    t_r = Ring(P, "tr", 4, [128, 512], F32)
    yn_r = Ring(P, "yn", 2, [128, 512], BF16)
    sm_r = Ring(P, "sm", 4, [128, 4], F32)
    Sf = P.alloc([128, 512], F32)
    Sfb = P.alloc([128, 512], BF16)
    Sb = P.alloc([128, 512], F32)
    RSf, RSfb, RSbr = P.res("Sf"), P.res("Sfb"), P.res("Sbr")

    def bc8(ap8):
        return ap8.unsqueeze(2).to_broadcast([128, 8, 64])

    def v3(ap):
        return ap.rearrange("p (h d) -> p h d", d=64)

    def load_group(s, g, with_z):
        t0 = g * 512
        xb, xr, xch = gx.next()
        P.dma('sp', xch, lambda h: h.dma_start(
            out=xb, in_=C.XS[s].rearrange("(c p) t -> p c t", p=128)[:, :, t0:t0 + 512]), writes=[xr])
        bb, br, bch = gbc.next()
        P.dma('sp', bch, lambda h: h.dma_start(
            out=bb, in_=C.BC[s].rearrange("(c p) t -> p c t", p=128)[:, :, t0:t0 + 512]), writes=[br])
        db, dr, dch = gdt.next()
        P.dma('sp', dch, lambda h: h.dma_start(
            out=db, in_=C.DT[s][t0:t0 + 512, :].rearrange("(u p) f -> p u f", p=128)), writes=[dr])
        zz = None
        if with_z:
            zb, zr, zch = gz.next()
            P.dma('sp', zch, lambda h: h.dma_start(
                out=zb, in_=C.Z[s][t0:t0 + 512, :].rearrange("(u p) f -> p u f", p=128)), writes=[zr])
            zz = (zb, zr)
        return (xb, xr), (bb, br), (db, dr), zz

    def small_mms(da, Rda, mats):
        b = P.next_bank()
        for i, m_ in enumerate(mats):
            P.op('pe', lambda h, i=i, m_=m_, b=b: h.matmul(P.banks[b][:, 16 * i:16 * i + 16], m_, da, start=True, stop=True),
                 reads=[Rc, Rda], writes=[P.Rbank[b]])
        ew, Rew = ew_r.next()
        n = 16 * len(mats)
        act(P, ew[:, 0:n], P.banks[b][:, 0:n], AF.Exp, [P.Rbank[b]], [Rew])
        return ew, Rew

    def xs_transpose(xb, xr, u):
        b = P.next_bank()
        for fc in range(4):
            P.op('pe', lambda h, fc=fc, b=b: h.transpose(P.banks[b][:, fc * 128:(fc + 1) * 128],
                                                        xb[:, fc, u * 128:(u + 1) * 128], ID_),
                 reads=[xr, Rc], writes=[P.Rbank[b]])
        return b

    def b_transpose(bb, br, u):
        b = P.next_bank()
        pb = P.banks[b][:, :].bitcast(BF16)
        for g in range(2):
            P.op('pe', lambda h, g=g, pb=pb: h.transpose(pb[:, g * 128:(g + 1) * 128],
                                                        bb[:, g, u * 128:(u + 1) * 128], idb),
                 reads=[br, Rc], writes=[P.Rbank[b]])
        bt, Rbt = btok_r.next()
        act(P, bt, pb[:, 0:256], AF.Identity, [P.Rbank[b]], [Rbt])
        return bt, Rbt

    def state_mm(bt, Rbt, xw, Rxw):
        b = P.next_bank()
        for g in range(2):
            P.op('pe', lambda h, g=g, b=b: h.matmul(P.banks[b][:, g * 256:(g + 1) * 256], bt[:, g * 128:(g + 1) * 128],
                                                   xw[:, g * 256:(g + 1) * 256], start=True, stop=True),
                 reads=[Rbt, Rxw], writes=[P.Rbank[b]])
        return b

    for s in range(NS):
        P.op('pool', lambda h: h.memset(Sb, 0.0), writes=[RSbr])
        P.op('pool', lambda h: h.memset(SbAll[:, NC - 1, :], 0.0), writes=[RSb[NC - 1]])
        nxt = load_group(s, NG - 1, False)
        for g in range(NG - 1, -1, -1):
            (xb, xr), (bb, br), (db, dr), _ = nxt
            if g > 0:
                nxt = load_group(s, g - 1, False)
            for u in range(3, -1, -1):
                c = g * 4 + u
                if c == 0:
                    break
                da, Rda = da_r.next()
                P.op('dve', lambda h, da=da, db=db, u=u: h.tensor_tensor(out=da, in0=db[:, u, :], in1=Aneg, op=ALU.mult),
                     reads=[dr, Rc], writes=[Rda])
                ew, Rew = small_mms(da, Rda, [SU_, ON_])
                sc, Rsc = sc_r.next()
                P.op('dve', lambda h, sc=sc, db=db, u=u, ew=ew: h.tensor_tensor(
                    out=sc[:, 0:8], in0=db[:, u, 8:16], in1=ew[:, 8:16], op=ALU.mult), reads=[dr, Rew], writes=[Rsc])
                bx = xs_transpose(xb, xr, u)
                xw, Rxw = xdt_r.next()
                P.op('dve', lambda h, xw=xw, bx=bx, sc=sc: h.tensor_tensor(
                    out=v3(xw), in0=v3(P.banks[bx][:, :]), in1=bc8(sc[:, 0:8]), op=ALU.mult),
                    reads=[P.Rbank[bx], Rsc], writes=[Rxw])
                bt, Rbt = b_transpose(bb, br, u)
                bs = state_mm(bt, Rbt, xw, Rxw)
                P.op('dve', lambda h, ew=ew: h.tensor_tensor(out=v3(Sb), in0=v3(Sb), in1=bc8(ew[:, 24:32]), op=ALU.mult),
                     reads=[RSbr, Rew], writes=[RSbr])
                P.op('dve', lambda h, bs=bs: h.tensor_tensor(out=Sb, in0=Sb, in1=P.banks[bs][:, :], op=ALU.add),
                     reads=[RSbr, P.Rbank[bs]], writes=[RSbr])
                act(P, SbAll[:, c - 1, :], Sb, AF.Identity, [RSbr], [RSb[c - 1]])
        P.op('pool', lambda h: h.memset(Sf, 0.0), writes=[RSf])
        P.op('pool', lambda h: h.memset(Sfb, 0.0), writes=[RSfb])
        nxt = load_group(s, 0, True)
        for g in range(NG):
            (xb, xr), (bb, br), (db, dr), (zb, zr) = nxt
            if g + 1 < NG:
                nxt = load_group(s, g + 1, True)
            yb, yr, ych = syn.next()
            for u in range(4):
                c = g * 4 + u
                da, Rda = da_r.next()
                P.op('dve', lambda h, da=da, db=db, u=u: h.tensor_tensor(out=da, in0=db[:, u, :], in1=Aneg, op=ALU.mult),
                     reads=[dr, Rc], writes=[Rda])
                ew, Rew = small_mms(da, Rda, [U_, SL_, LO_, ON_])
                sc, Rsc = sc_r.next()
                P.op('dve', lambda h, sc=sc, db=db, u=u, ew=ew: h.tensor_tensor(
                    out=sc[:, 0:8], in0=db[:, u, 0:8], in1=ew[:, 16:24], op=ALU.mult), reads=[dr, Rew], writes=[Rsc])
                bx = xs_transpose(xb, xr, u)
                xsP = v3(P.banks[bx][:, :])
                xf, Rxf = xdt_r.next()
                xbw, Rxbw = xdt_r.next()
                xw, Rxw = xdt_r.next()
                xsd, Rxsd = xsd_r.next()
                P.op('dve', lambda h, xf=xf, xsP=xsP, db=db, u=u: h.tensor_tensor(
                    out=v3(xf), in0=xsP, in1=bc8(db[:, u, 0:8]), op=ALU.mult), reads=[P.Rbank[bx], dr], writes=[Rxf])
                P.op('dve', lambda h, xbw=xbw, xsP=xsP, db=db, u=u: h.tensor_tensor(
                    out=v3(xbw), in0=xsP, in1=bc8(db[:, u, 8:16]), op=ALU.mult), reads=[P.Rbank[bx], dr], writes=[Rxbw])
                P.op('dve', lambda h, xw=xw, xsP=xsP, sc=sc: h.tensor_tensor(
                    out=v3(xw), in0=xsP, in1=bc8(sc[:, 0:8]), op=ALU.mult), reads=[P.Rbank[bx], Rsc], writes=[Rxw])
                P.op('dve', lambda h, xsd=xsd, xsP=xsP: h.tensor_tensor(
                    out=v3(xsd), in0=xsP, in1=bc8(dsk), op=ALU.mult), reads=[P.Rbank[bx], Rc], writes=[Rxsd])
                bt, Rbt = b_transpose(bb, br, u)
                bcb = P.next_bank()
                for gg in range(2):
                    P.op('pe', lambda h, gg=gg, bcb=bcb, bb=bb, u=u: h.matmul(
                        P.banks[bcb][:, gg * 128:(gg + 1) * 128], bb[:, gg, u * 128:(u + 1) * 128],
                        bb[:, 2 + gg, u * 128:(u + 1) * 128], start=True, stop=True), reads=[br], writes=[P.Rbank[bcb]])
                cbU, RcbU = cbm_r.next()
                cbL, RcbL = cbm_r.next()
                cbP = P.banks[bcb][:, 0:256].rearrange("p (g q) -> p g q", g=2)
                P.op('dve', lambda h, cbU=cbU, cbP=cbP: h.tensor_tensor(
                    out=cbU.rearrange("p (g q) -> p g q", g=2), in0=cbP,
                    in1=U_.unsqueeze(1).to_broadcast([128, 2, 128]), op=ALU.mult),
                    reads=[P.Rbank[bcb], Rc], writes=[RcbU])
                P.op('dve', lambda h, cbL=cbL, cbP=cbP: h.tensor_tensor(
                    out=cbL.rearrange("p (g q) -> p g q", g=2), in0=cbP,
                    in1=LO_.unsqueeze(1).to_broadcast([128, 2, 128]), op=ALU.mult),
                    reads=[P.Rbank[bcb], Rc], writes=[RcbL])
                Ms = []
                for di, (tri_l, tri_r, cbm, Rcbm, c0) in enumerate(((SL_, U_, cbU, RcbU, 0), (SU_, LO_, cbL, RcbL, 8))):
                    Lt, RLt = L_r.next()
                    P.op('pool', lambda h, Lt=Lt, tri_l=tri_l, da=da, c0=c0: h.tensor_tensor(
                        out=Lt, in0=tri_l.unsqueeze(1).to_broadcast([128, 8, 128]),
                        in1=da[:, c0:c0 + 8].unsqueeze(2).to_broadcast([128, 8, 128]), op=ALU.mult),
                        reads=[Rc, Rda], writes=[RLt])
                    dec, Rdec = dec_r.next()
                    for hh in range(2):
                        b = P.next_bank()
                        for h4 in range(4):
                            hd = hh * 4 + h4
                            P.op('pe', lambda h, b=b, h4=h4, hd=hd, Lt=Lt, tri_r=tri_r: h.matmul(
                                P.banks[b][:, h4 * 128:(h4 + 1) * 128], Lt[:, hd, :], tri_r, start=True, stop=True),
                                reads=[RLt, Rc], writes=[P.Rbank[b]])
                        act(P, dec[:, hh * 4:(hh + 1) * 4, :], P.banks[b][:, :].rearrange("p (a q) -> p a q", a=4),
                            AF.Exp, [P.Rbank[b]], [Rdec])
                    Mt, RMt = M_r.next()
                    P.op('dve', lambda h, Mt=Mt, dec=dec, cbm=cbm: h.tensor_tensor(
                        out=Mt.rearrange("p (g e) q -> p g e q", g=2), in0=dec.rearrange("p (g e) q -> p g e q", g=2),
                        in1=cbm.rearrange("p (g q) -> p g q", g=2).unsqueeze(2).to_broadcast([128, 2, 4, 128]),
                        op=ALU.mult), reads=[Rdec, Rcbm], writes=[RMt])
                    Ms.append((Mt, RMt))
                by = P.next_bank()
                for hd in range(8):
                    P.op('pe', lambda h, by=by, hd=hd, M0=Ms[0][0], xf=xf: h.matmul(
                        P.banks[by][:, hd * 64:(hd + 1) * 64], M0[:, hd, :], xf[:, hd * 64:(hd + 1) * 64],
                        start=True, stop=False), reads=[Ms[0][1], Rxf], writes=[P.Rbank[by]])
                    P.op('pe', lambda h, by=by, hd=hd, M1=Ms[1][0], xbw=xbw: h.matmul(
                        P.banks[by][:, hd * 64:(hd + 1) * 64], M1[:, hd, :], xbw[:, hd * 64:(hd + 1) * 64],
                        start=False, stop=True), reads=[Ms[1][1], Rxbw], writes=[P.Rbank[by]])
                bof = P.next_bank()
                bob = P.next_bank()
                for gg in range(2):
                    P.op('pe', lambda h, gg=gg, bof=bof, bb=bb, u=u: h.matmul(
                        P.banks[bof][:, gg * 256:(gg + 1) * 256], bb[:, 2 + gg, u * 128:(u + 1) * 128],
                        Sfb[:, gg * 256:(gg + 1) * 256], start=True, stop=True), reads=[br, RSfb], writes=[P.Rbank[bof]])
                for gg in range(2):
                    P.op('pe', lambda h, gg=gg, bob=bob, bb=bb, u=u, c=c: h.matmul(
                        P.banks[bob][:, gg * 256:(gg + 1) * 256], bb[:, 2 + gg, u * 128:(u + 1) * 128],
                        SbAll[:, c, gg * 256:(gg + 1) * 256], start=True, stop=True), reads=[br, RSb[c]], writes=[P.Rbank[bob]])
                t1, Rt1 = t_r.next()
                t2, Rt2 = t_r.next()
                P.op('dve', lambda h, t1=t1, bof=bof, ew=ew: h.tensor_tensor(
                    out=v3(t1), in0=v3(P.banks[bof][:, :]), in1=bc8(ew[:, 0:8]), op=ALU.mult),
                    reads=[P.Rbank[bof], Rew], writes=[Rt1])
                P.op('dve', lambda h, t2=t2, bob=bob, ew=ew: h.tensor_tensor(
                    out=v3(t2), in0=v3(P.banks[bob][:, :]), in1=bc8(ew[:, 40:48]), op=ALU.mult),
                    reads=[P.Rbank[bob], Rew], writes=[Rt2])
                P.op('pool', lambda h, t1=t1, t2=t2: h.tensor_tensor(out=t1, in0=t1, in1=t2, op=ALU.add),
                     reads=[Rt1, Rt2], writes=[Rt1])
                P.op('pool', lambda h, t1=t1, xsd=xsd: h.tensor_tensor(out=t1, in0=t1, in1=xsd, op=ALU.add),
                     reads=[Rt1, Rxsd], writes=[Rt1])
                P.op('dve', lambda h, t1=t1, by=by: h.tensor_tensor(out=t1, in0=t1, in1=P.banks[by][:, :], op=ALU.add),
                     reads=[Rt1, P.Rbank[by]], writes=[Rt1])
                P.op('dve', lambda h, t1=t1, zb=zb, u=u: h.tensor_tensor(out=t1, in0=t1, in1=zb[:, u, :], op=ALU.mult),
                     reads=[Rt1, zr], writes=[Rt1])
                sm, Rsm_ = sm_r.next()
                P.op('act', lambda h, t2=t2, t1=t1, sm=sm: h.activation(out=t2, in_=t1, func=AF.Square, accum_out=sm[:, 0:1]),
                     reads=[Rt1, Rt2], writes=[Rt2, Rsm_])
                act(P, sm[:, 1:2], sm[:, 0:1], AF.Ln, [Rsm_], [Rsm_], scale=1.0 / 512, bias=EPS)
                act(P, sm[:, 2:3], sm[:, 1:2], AF.Exp, [Rsm_], [Rsm_], scale=-0.5)
                yn, Ryn = yn_r.next()
                P.op('dve', lambda h, yn=yn, t1=t1, sm=sm: h.scalar_tensor_tensor(
                    out=yn, in0=t1, scalar=sm[:, 2:3], in1=sng, op0=ALU.mult, op1=ALU.mult),
                    reads=[Rt1, Rsm_, Rc], writes=[Ryn])
                bt_ = P.next_bank()
                pbt = P.banks[bt_][:, :].bitcast(BF16)
                for fc in range(4):
                    P.op('pe', lambda h, fc=fc, pbt=pbt, yn=yn: h.transpose(
                        pbt[:, fc * 128:(fc + 1) * 128], yn[:, fc * 128:(fc + 1) * 128], idb),
                        reads=[Ryn, Rc], writes=[P.Rbank[bt_]])
                act(P, yb[:, :, u * 128:(u + 1) * 128], pbt[:, 0:512].rearrange("p (c t) -> p c t", c=4), AF.Identity,
                    [P.Rbank[bt_]], [yr])
                bs = state_mm(bt, Rbt, xw, Rxw)
                P.op('dve', lambda h, ew=ew: h.tensor_tensor(out=v3(Sf), in0=v3(Sf), in1=bc8(ew[:, 48:56]), op=ALU.mult),
                     reads=[RSf, Rew], writes=[RSf])
                P.op('dve', lambda h, bs=bs: h.tensor_tensor(out=Sf, in0=Sf, in1=P.banks[bs][:, :], op=ALU.add),
                     reads=[RSf, P.Rbank[bs]], writes=[RSf])
                act(P, Sfb, Sf, AF.Identity, [RSf], [RSfb])
            P.dma('sp', ych, lambda h, yb=yb, s=s, g=g: h.dma_start(
                out=C.YN[s].rearrange("(c p) t -> p c t", p=128)[:, :, g * 512:(g + 1) * 512], in_=yb), reads=[yr])
    P.reset(m0)


BRANCH_DIL = (1, 4, 16)


def phase_T(C):
    P, L, NS = C.P, C.L, C.NS
    m0 = P.mark()
    cst = P.alloc([128, 6, 128], F32)
    oh = P.alloc([32, 6, 256], F32)
    eband = P.alloc([128, 2, 384], F32)
    relb = P.alloc([32, 8], F32)
    sel = P.alloc([65, 64], F32)
    Rc = P.res("t_const")
    chc = P.chan("t_const")
    P.dma('sp', chc, lambda h: h.dma_start(out=cst, in_=C.cst), writes=[Rc])
    P.dma('sp', chc, lambda h: h.dma_start(out=oh, in_=C.oh), writes=[Rc])
    P.dma('sp', chc, lambda h: h.dma_start(out=eband, in_=C.eband), writes=[Rc])
    P.dma('sp', chc, lambda h: h.dma_start(out=relb, in_=C.relb), writes=[Rc])
    P.dma('sp', chc, lambda h: h.dma_start(out=sel, in_=C.sel), writes=[Rc])
    U_, LO_ = cst[:, 0, :], cst[:, 2, :]
    EB = P.alloc([128, 4, 3, 2, 256], F32)
    REB = P.res("EB")
    gv_r = Ring(P, "gv", 2, [128, 2, 8], F32)
    tmp_r = Ring(P, "ebtmp", 2, [128, 128], F32)
    for bt in range(6):
        bi, ty = bt // 2, bt % 2
        b = P.next_bank()
        for ck in range(2):
            P.op('pe', lambda h, b=b, ck=ck, bt=bt: h.matmul(P.banks[b][:, ck * 8:(ck + 1) * 8],
                                                             oh[:, bt, ck * 128:(ck + 1) * 128], relb, start=True, stop=True),
                 reads=[Rc], writes=[P.Rbank[b]])
        gv, Rgv = gv_r.next()
        P.op('dve', lambda h, gv=gv, b=b: h.tensor_copy(out=gv.rearrange("p c h -> p (c h)"), in_=P.banks[b][:, 0:16]),
             reads=[P.Rbank[b]], writes=[Rgv])
        by = [P.next_bank(), P.next_bank()]
        for bb in range(128):
            bk = by[bb // 64]
            o = P.banks[bk][:, (bb % 64) * 8:(bb % 64) * 8 + 8]
            for ck in range(2):
                P.op('pe', lambda h, o=o, ck=ck, bb=bb, gv=gv: h.matmul(
                    o, eband[:, ck, 127 - bb:255 - bb], gv[:, ck, :], start=(ck == 0), stop=(ck == 1)),
                    reads=[Rc, Rgv], writes=[P.Rbank[bk]])
        for hd in range(8):
            tmp, Rtmp = tmp_r.next()
            for half in range(2):
                act(P, tmp[:, half * 64:(half + 1) * 64],
                    P.banks[by[half]][:, :].rearrange("p (b h) -> p b h", h=8)[:, :, hd], AF.Exp,
                    [P.Rbank[by[half]]], [Rtmp])
            msk = U_ if ty == 0 else LO_
            P.op('dve', lambda h, tmp=tmp, msk=msk, hd=hd, bi=bi, ty=ty: h.tensor_tensor(
                out=EB[:, hd // 2, bi, hd % 2, ty * 128:(ty + 1) * 128], in0=tmp, in1=msk, op=ALU.mult),
                reads=[Rtmp, Rc], writes=[REB])

    PADK = 1024
    Qbd = P.alloc([128, 2, L], BF16)
    KTp = P.alloc([128, L + 2 * PADK], BF16)
    OT = P.alloc([128, 2, L], F32)
    NTmax = L // 128 + 16
    Vb = P.alloc([128, NTmax, 130], BF16)
    RQ, RK, ROT, RV = P.res("Qbd"), P.res("KTp"), P.res("OT"), P.res("Vb")
    chq, chk, chv = P.chan("q"), P.chan("k"), P.chan("v")
    chv2 = [P.chan(f"v{i}") for i in range(4)]
    E_r = Ring(P, "E", 3, [128, 512], F32)
    P_r = Ring(P, "Pt", 3, [128, 2, 256], BF16)
    rc_r = Ring(P, "rc", 2, [64, 512], F32)
    so_r = Ring(P, "so", 4, [64, 512], F32, chan=True)
    P.op('pool', lambda h: h.memset(Qbd, 0.0), writes=[RQ])
    P.op('pool', lambda h: h.memset(KTp, 0.0), writes=[RK])
    S_banks = [0, 1, 2, 3]
    si = [0]
    O_bank = {(0, 0): 4, (0, 1): 5, (1, 0): 6, (1, 1): 7}

    for s in range(NS):
        for hp in range(4):
            r0 = hp * 128
            P.dma('sp', chq, lambda h, s=s, r0=r0: h.dma_start(out=Qbd[0:64, 0, :], in_=C.QT[s][r0:r0 + 64, :]), writes=[RQ])
            P.dma('sp', chq, lambda h, s=s, r0=r0: h.dma_start(out=Qbd[64:128, 1, :], in_=C.QT[s][r0 + 64:r0 + 128, :]), writes=[RQ])
            P.dma('sp', chk, lambda h, s=s, r0=r0: h.dma_start(out=KTp[:, PADK:PADK + L], in_=C.KT[s][r0:r0 + 128, :]), writes=[RK])
            P.op('pool', lambda h: h.memset(OT, 0.0), writes=[ROT])
            for bi, dl in enumerate(BRANCH_DIL):
                Ld = L // dl
                NQ = Ld // 128
                NJ = NQ + 1
                vi = 0
                for rho in range(dl):
                    tb = rho * NJ
                    P.op('pool', lambda h, tb=tb: h.memset(Vb[0:64, tb, :], 0.0), writes=[RV])
                    P.op('pool', lambda h, tb=tb, NJ=NJ: h.memset(Vb[64:128, tb + NJ - 1, :], 0.0), writes=[RV])
                    Vs = C.V[s]

                    def rows(pos0, n, rho=rho, dl=dl, hp=hp, Vs=Vs):
                        t0 = rho + dl * pos0
                        return Vs[t0:t0 + dl * (n - 1) + 1:dl, hp * 130:(hp + 1) * 130]
                    r_first = rows(0, 64)
                    r_last = rows(Ld - 64, 64)
                    ch_ = chv2[vi % 4]
                    vi += 1
                    if NJ > 2:
                        src = Vs[rho + dl * 64:rho + dl * 64 + dl * 128 * (NJ - 2):dl, hp * 130:(hp + 1) * 130]
                        P.dma('sp', ch_, lambda h, tb=tb, NJ=NJ, src=src: h.dma_start(
                            out=Vb[:, tb + 1:tb + NJ - 1, :], in_=src.rearrange("(j a) f -> a j f", a=128)), writes=[RV])
                    P.dma('sp', ch_, lambda h, tb=tb, r_first=r_first: h.dma_start(out=Vb[64:128, tb, :], in_=r_first), writes=[RV])
                    P.dma('sp', ch_, lambda h, tb=tb, NJ=NJ, r_last=r_last: h.dma_start(
                        out=Vb[0:64, tb + NJ - 1, :], in_=r_last), writes=[RV])
                for rho in range(dl):
                    for j in range(NJ):
                        lo_q = 128 * (j - 1)
                        halves = [hf for hf in (0, 1) if 0 <= j - 1 + hf < NQ]
                        h0, h1 = halves[0], halves[-1] + 1
                        nq = (h1 - h0) * 128
                        qs = rho + dl * (lo_q + h0 * 128)
                        ks = PADK + rho + dl * (128 * j - 64)
                        bS = S_banks[si[0] % 4]
                        si[0] += 1
                        P.op('pe', lambda h, bS=bS, ks=ks, dl=dl, qs=qs, nq=nq, h0=h0, h1=h1: h.matmul(
                            P.banks[bS][:, :].rearrange("p (h q) -> p h q", h=2)[:, :, h0 * 128:h1 * 128],
                            KTp[:, ks:ks + dl * 127 + 1:dl],
                            Qbd[:, :, qs:qs + dl * (nq - 1) + 1:dl], start=True, stop=True),
                            reads=[RK, RQ], writes=[P.Rbank[bS]])
                        Et, REt = E_r.next()
                        Pt, RPt = P_r.next()
                        Ev = Et.rearrange("p (h q) -> p h q", h=2)[:, :, h0 * 128:h1 * 128]
                        act(P, Ev, P.banks[bS][:, :].rearrange("p (h q) -> p h q", h=2)[:, :, h0 * 128:h1 * 128],
                            AF.Exp, [P.Rbank[bS]], [REt])
                        P.op('dve', lambda h, Pt=Pt, Ev=Ev, hp=hp, bi=bi, h0=h0, h1=h1: h.tensor_tensor(
                            out=Pt[:, :, h0 * 128:h1 * 128], in0=Ev, in1=EB[:, hp, bi, :, h0 * 128:h1 * 128], op=ALU.mult),
                            reads=[REt, REB], writes=[RPt])
                        for hd in range(2):
                            for hf in halves:
                                m = j - 1 + hf
                                bO = O_bank[(hd, m % 2)]
                                P.op('pe', lambda h, bO=bO, hd=hd, hf=hf, Pt=Pt, tile=rho * NJ + j: h.matmul(
                                    P.banks[bO][0:65, 0:128], Vb[:, tile, hd * 65:(hd + 1) * 65],
                                    Pt[:, hd, hf * 128:(hf + 1) * 128], start=(hf == 1), stop=(hf == 0)),
                                    reads=[RV, RPt], writes=[P.Rbank[bO]])
                                if hf == 0:
                                    t0 = rho + dl * 128 * m
                                    ov = OT[0:65, hd, t0:t0 + dl * 127 + 1:dl]
                                    P.op('dve', lambda h, ov=ov, bO=bO: h.tensor_tensor(
                                        out=ov, in0=ov, in1=P.banks[bO][0:65, 0:128], op=ALU.add),
                                        reads=[ROT, P.Rbank[bO]], writes=[ROT])
            for hd in range(2):
                for ct in range(L // 512):
                    b = S_banks[si[0] % 4]
                    si[0] += 1
                    P.op('pe', lambda h, b=b, hd=hd, ct=ct: h.matmul(
                        P.banks[b][0:64, :], sel, OT[0:65, hd, ct * 512:(ct + 1) * 512], start=True, stop=True),
                        reads=[Rc, ROT], writes=[P.Rbank[b]])
                    rc, Rrc = rc_r.next()
                    P.op('dve', lambda h, rc=rc, b=b: h.reciprocal(out=rc, in_=P.banks[b][0:64, :]),
                         reads=[P.Rbank[b]], writes=[Rrc])
                    so, Rso, soch = so_r.next()
                    P.op('dve', lambda h, so=so, rc=rc, hd=hd, ct=ct: h.tensor_tensor(
                        out=so, in0=OT[0:64, hd, ct * 512:(ct + 1) * 512], in1=rc, op=ALU.mult),
                        reads=[ROT, Rrc], writes=[Rso])
                    rr = hp * 128 + hd * 64
                    P.dma('sp', soch, lambda h, so=so, s=s, rr=rr, ct=ct: h.dma_start(
                        out=C.AT[s][rr:rr + 64, ct * 512:(ct + 1) * 512], in_=so), reads=[Rso])
    P.reset(m0)


def _ln_feature_major(P, C, tt, Rtt, nd, T, S1, S2, cst_ones, Rc, mk_out):
    mean = C.ln_mean
    m2 = C.ln_m2
    rstd = C.ln_rstd
    Rst = C.ln_Rst
    act(P, mean[:, 0:T], P.banks[S1][:, 0:T], AF.Identity, [P.Rbank[S1]], [Rst], scale=1.0 / D)
    P.op('dve', lambda h: h.tensor_tensor(out=m2[:, 0:T], in0=mean[:, 0:T], in1=mean[:, 0:T], op=ALU.mult),
         reads=[Rst], writes=[Rst])
    P.op('dve', lambda h: h.scalar_tensor_tensor(out=m2[:, 0:T], in0=P.banks[S2][:, 0:T], scalar=1.0 / D,
                                                 in1=m2[:, 0:T], op0=ALU.mult, op1=ALU.subtract),
         reads=[P.Rbank[S2], Rst], writes=[Rst])
    act(P, m2[:, 0:T], m2[:, 0:T], AF.Ln, [Rst], [Rst], bias=EPS)
    act(P, rstd[:, 0:T], m2[:, 0:T], AF.Exp, [Rst], [Rst], scale=-0.5)
    for dc in range(nd):
        u1, Ru1 = C.ln_u.next()
        P.op('dve', lambda h, u1=u1, dc=dc: h.tensor_tensor(out=u1[:, 0:T], in0=tt[:, dc, 0:T], in1=mean[:, 0:T], op=ALU.subtract),
             reads=[Rtt, Rst], writes=[Ru1])
        P.op('dve', lambda h, u1=u1: h.tensor_tensor(out=u1[:, 0:T], in0=u1[:, 0:T], in1=rstd[:, 0:T], op=ALU.mult),
             reads=[Ru1, Rst], writes=[Ru1])
        mk_out(dc, u1, Ru1)


def phase_C1(C):
    P, L, NS = C.P, C.L, C.NS
    T = 512
    m0 = P.mark()
    P.bank_list = [0, 1, 2, 3, 4]
    SA, S1, S2 = 5, 6, 7
    wout = P.alloc([128, 8, D], BF16)
    cst = P.alloc([128, 6, 128], F32)
    ang = P.alloc([128, 4], F32)
    lnfm = P.alloc([128, 4, 8], F32)
    Rc = P.res("c1_const")
    chc = P.chan("c1_const")
    P.dma('pool', chc, lambda h: h.dma_start(out=wout, in_=C.w_out.rearrange("(kc p) c -> p kc c", p=128)), writes=[Rc])
    P.dma('sp', chc, lambda h: h.dma_start(out=cst, in_=C.cst), writes=[Rc])
    P.dma('sp', chc, lambda h: h.dma_start(out=ang, in_=C.ang), writes=[Rc])
    P.dma('sp', chc, lambda h: h.dma_start(out=lnfm, in_=C.lnfm), writes=[Rc])
    ONES = cst[:, 4, :]
    xt_r = Ring(P, "c1x", 2, [128, 8, T], F32, chan=True)
    at_r = Ring(P, "c1a", 2, [128, 4, T], F32, chan=True)
    yn_r = Ring(P, "c1y", 2, [128, 4, T], BF16, chan=True)
    an = P.alloc([128, 4, T], BF16)
    Ran = P.res("an")
    sq_r = Ring(P, "c1sq", 3, [128, T], F32)
    rsa = P.alloc([128, T], F32)
    Rrsa = P.res("rsa")
    tt = P.alloc([128, 8, T], F32)
    Rtt = P.res("tt")
    C.ln_mean = P.alloc([128, T], F32)
    C.ln_m2 = P.alloc([128, T], F32)
    C.ln_rstd = P.alloc([128, T], F32)
    C.ln_Rst = P.res("lnst")
    C.ln_u = Ring(P, "lnu", 3, [128, T], F32)
    hf_r = Ring(P, "c1hf", 2, [128, 8, T], F32, chan=True)
    hb_r = Ring(P, "c1hb", 2, [128, 8, T], BF16, chan=True)

    def load(s, t0):
        xb, xr, xch = xt_r.next()
        P.dma('sp', xch, lambda h: h.dma_start(out=xb, in_=C.xT[s].rearrange("(c p) t -> p c t", p=128)[:, :, t0:t0 + T]), writes=[xr])
        ab, ar, ach = at_r.next()
        P.dma('sp', ach, lambda h: h.dma_start(out=ab, in_=C.AT[s].rearrange("(c p) t -> p c t", p=128)[:, :, t0:t0 + T]), writes=[ar])
        yb, yr, ych = yn_r.next()
        P.dma('sp', ych, lambda h: h.dma_start(out=yb, in_=C.YN[s].rearrange("(c p) t -> p c t", p=128)[:, :, t0:t0 + T]), writes=[yr])
        return (xb, xr), (ab, ar), (yb, yr)

    tiles = [(s, t0) for s in range(NS) for t0 in range(0, L, T)]
    nxt = load(*tiles[0])
    for ti, (s, t0) in enumerate(tiles):
        (xb, xr), (ab, ar), (yb, yr) = nxt
        if ti + 1 < len(tiles):
            nxt = load(*tiles[ti + 1])
        for fc in range(4):
            sq, Rsq = sq_r.next()
            act(P, sq, ab[:, fc, :], AF.Square, [ar], [Rsq])
            P.op('pe', lambda h, sq=sq, fc=fc: h.matmul(P.banks[SA][:, :], ONES, sq, start=(fc == 0), stop=(fc == 3)),
                 reads=[Rc, Rsq], writes=[P.Rbank[SA]])
        act(P, rsa, P.banks[SA][:, :], AF.Ln, [P.Rbank[SA]], [Rrsa], scale=1.0 / 512, bias=EPS)
        act(P, rsa, rsa, AF.Exp, [Rrsa], [Rrsa], scale=-0.5)
        for fc in range(4):
            P.op('dve', lambda h, fc=fc, ab=ab: h.scalar_tensor_tensor(
                out=an[:, fc, :], in0=ab[:, fc, :], scalar=ang[:, fc:fc + 1], in1=rsa, op0=ALU.mult, op1=ALU.mult),
                reads=[ar, Rc, Rrsa], writes=[Ran])
        for dc in range(8):
            b = P.next_bank()
            mm_group(P, P.banks[b][:, :],
                     [(wout[:, kc, dc * 128:(dc + 1) * 128], an[:, kc, :] if kc < 4 else yb[:, kc - 4, :]) for kc in range(8)],
                     P.Rbank[b], [Rc, Ran, yr])
            P.op('dve', lambda h, dc=dc, b=b, xb=xb: h.scalar_tensor_tensor(
                out=tt[:, dc, :], in0=xb[:, dc, :], scalar=ALPHA, in1=P.banks[b][:, :], op0=ALU.mult, op1=ALU.add),
                reads=[xr, P.Rbank[b]], writes=[Rtt])
            sq, Rsq = sq_r.next()
            act(P, sq, tt[:, dc, :], AF.Square, [Rtt], [Rsq])
            P.op('pe', lambda h, dc=dc: h.matmul(P.banks[S1][:, :], ONES, tt[:, dc, :], start=(dc == 0), stop=(dc == 7)),
                 reads=[Rc, Rtt], writes=[P.Rbank[S1]])
            P.op('pe', lambda h, dc=dc, sq=sq: h.matmul(P.banks[S2][:, :], ONES, sq, start=(dc == 0), stop=(dc == 7)),
                 reads=[Rc, Rsq], writes=[P.Rbank[S2]])
        hf, Rhf, hfch = hf_r.next()
        hb, Rhb, hbch = hb_r.next()

        def mk_out(dc, u1, Ru1, hf=hf, Rhf=Rhf, hb=hb, Rhb=Rhb):
            act(P, hf[:, dc, :], u1, AF.Identity, [Ru1, Rc], [Rhf], scale=lnfm[:, 0, dc:dc + 1], bias=lnfm[:, 1, dc:dc + 1])
            act(P, hb[:, dc, :], u1, AF.Identity, [Ru1, Rc], [Rhb], scale=lnfm[:, 0, dc:dc + 1], bias=lnfm[:, 1, dc:dc + 1])
        _ln_feature_major(P, C, tt, Rtt, 8, T, S1, S2, ONES, Rc, mk_out)
        P.dma('sp', hfch, lambda h, hf=hf, s=s, t0=t0: h.dma_start(
            out=C.H1F[s].rearrange("(c p) t -> p c t", p=128)[:, :, t0:t0 + T], in_=hf), reads=[Rhf])
        P.dma('sp', hbch, lambda h, hb=hb, s=s, t0=t0: h.dma_start(
            out=C.H1B[s].rearrange("(c p) t -> p c t", p=128)[:, :, t0:t0 + T], in_=hb), reads=[Rhb])
    P.bank_list = list(range(8))
    P.reset(m0)


def phase_C2(C):
    P, L, NS = C.P, C.L, C.NS
    T = 256
    NF = DFF // 128
    m0 = P.mark()
    P.bank_list = [0, 1, 2, 3, 4, 5]
    S1, S2 = 6, 7
    wg = P.alloc([128, 8, DFF], BF16)
    wu = P.alloc([128, 8, DFF], BF16)
    wd = P.alloc([128, NF, D], BF16)
    cst = P.alloc([128, 6, 128], F32)
    lnfm = P.alloc([128, 4, 8], F32)
    Rc = P.res("c2_const")
    chc = P.chan("c2_const")
    chw = [P.chan(f"c2w{i}") for i in range(3)]
    for c0 in (0, 1408):
        P.dma('pool', chw[0], lambda h, c0=c0: h.dma_start(
            out=wg[:, :, c0:c0 + 1408], in_=C.w_gate.rearrange("(kc p) c -> p kc c", p=128)[:, :, c0:c0 + 1408]), writes=[Rc])
        P.dma('pool', chw[1], lambda h, c0=c0: h.dma_start(
            out=wu[:, :, c0:c0 + 1408], in_=C.w_up.rearrange("(kc p) c -> p kc c", p=128)[:, :, c0:c0 + 1408]), writes=[Rc])
    P.dma('pool', chw[2], lambda h: h.dma_start(out=wd, in_=C.w_down.rearrange("(kc p) c -> p kc c", p=128)), writes=[Rc])
    P.dma('sp', chc, lambda h: h.dma_start(out=cst, in_=C.cst), writes=[Rc])
    P.dma('sp', chc, lambda h: h.dma_start(out=lnfm, in_=C.lnfm), writes=[Rc])
    ONES = cst[:, 4, :]
    hb_r = Ring(P, "c2hb", 2, [128, 8, T], BF16, chan=True)
    hf_r = Ring(P, "c2hf", 4, [128, T], F32, chan=True)
    hid = P.alloc([128, NF, T], BF16)
    Rhid = P.res("hid")
    sg_r = Ring(P, "c2sg", 3, [128, T], F32)
    sq_r = Ring(P, "c2sq", 3, [128, T], F32)
    tt = P.alloc([128, 8, T], F32)
    Rtt = P.res("tt2")
    C.ln_mean = P.alloc([128, T], F32)
    C.ln_m2 = P.alloc([128, T], F32)
    C.ln_rstd = P.alloc([128, T], F32)
    C.ln_Rst = P.res("lnst2")
    C.ln_u = Ring(P, "lnu2", 3, [128, T], F32)
    yo_r = Ring(P, "c2yo", 4, [128, T], F32, chan=True)
    outs = []

    def load(s, t0):
        hb, hr, hch = hb_r.next()
        P.dma('sp', hch, lambda h: h.dma_start(out=hb, in_=C.H1B[s].rearrange("(c p) t -> p c t", p=128)[:, :, t0:t0 + T]), writes=[hr])
        return hb, hr

    tiles = [(s, t0) for s in range(NS) for t0 in range(0, L, T)]
    nxt = load(*tiles[0])
    for ti, (s, t0) in enumerate(tiles):
        hb, hr = nxt
        if ti + 1 < len(tiles):
            nxt = load(*tiles[ti + 1])
        for fc in range(NF):
            bg = P.next_bank()
            mm_group(P, P.banks[bg][:, 0:T], [(wg[:, kc, fc * 128:(fc + 1) * 128], hb[:, kc, :]) for kc in range(8)],
                     P.Rbank[bg], [Rc, hr])
            bu = P.next_bank()
            mm_group(P, P.banks[bu][:, 0:T], [(wu[:, kc, fc * 128:(fc + 1) * 128], hb[:, kc, :]) for kc in range(8)],
                     P.Rbank[bu], [Rc, hr])
            sg, Rsg = sg_r.next()
            act(P, sg, P.banks[bg][:, 0:T], AF.Silu, [P.Rbank[bg]], [Rsg])
            P.op('dve', lambda h, fc=fc, sg=sg, bu=bu: h.tensor_tensor(out=hid[:, fc, :], in0=sg, in1=P.banks[bu][:, 0:T], op=ALU.mult),
                 reads=[Rsg, P.Rbank[bu]], writes=[Rhid])
        for dc in range(8):
            hf, Rhf, hfch = hf_r.next()
            P.dma('sp', hfch, lambda h, hf=hf, s=s, t0=t0, dc=dc: h.dma_start(
                out=hf, in_=C.H1F[s][dc * 128:(dc + 1) * 128, t0:t0 + T]), writes=[Rhf])
            b = P.next_bank()
            mm_group(P, P.banks[b][:, 0:T], [(wd[:, fc, dc * 128:(dc + 1) * 128], hid[:, fc, :]) for fc in range(NF)],
                     P.Rbank[b], [Rc, Rhid])
            P.op('dve', lambda h, dc=dc, b=b, hf=hf: h.scalar_tensor_tensor(
                out=tt[:, dc, :], in0=hf, scalar=ALPHA, in1=P.banks[b][:, 0:T], op0=ALU.mult, op1=ALU.add),
                reads=[Rhf, P.Rbank[b]], writes=[Rtt])
            sq, Rsq = sq_r.next()
            act(P, sq, tt[:, dc, :], AF.Square, [Rtt], [Rsq])
            P.op('pe', lambda h, dc=dc: h.matmul(P.banks[S1][:, 0:T], ONES, tt[:, dc, :], start=(dc == 0), stop=(dc == 7)),
                 reads=[Rc, Rtt], writes=[P.Rbank[S1]])
            P.op('pe', lambda h, dc=dc, sq=sq: h.matmul(P.banks[S2][:, 0:T], ONES, sq, start=(dc == 0), stop=(dc == 7)),
                 reads=[Rc, Rsq], writes=[P.Rbank[S2]])

        def mk_out(dc, u1, Ru1, s=s, t0=t0):
            yo, Ryo, yoch = yo_r.next()
            act(P, yo, u1[:, 0:T], AF.Identity, [Ru1, Rc], [Ryo], scale=lnfm[:, 2, dc:dc + 1], bias=lnfm[:, 3, dc:dc + 1])
            outs.append(P.dma('sp', yoch, lambda h, yo=yo, dc=dc: h.dma_start(
                out=C.yT[s][dc * 128:(dc + 1) * 128, t0:t0 + T], in_=yo), reads=[Ryo]))
        _ln_feature_major(P, C, tt, Rtt, 8, T, S1, S2, ONES, Rc, mk_out)
    P.bank_list = list(range(8))
    P.reset(m0)
    return outs

def make_cst():
    i = np.arange(128)
    U = (i[:, None] <= i[None, :]).astype(np.float32)
    SL = (i[:, None] > i[None, :]).astype(np.float32)
    Lo = (i[:, None] >= i[None, :]).astype(np.float32)
    SU = (i[:, None] < i[None, :]).astype(np.float32)
    ones = np.ones((128, 128), np.float32)
    ident = np.eye(128, dtype=np.float32)
    return np.ascontiguousarray(np.stack([U, SL, Lo, SU, ones, ident], axis=1))


def shared_inputs(inp):
    f = np.float32
    g = lambda k: np.asarray(inp[k], dtype=f)
    bc = lambda v, n=128: np.ascontiguousarray(np.broadcast_to(v[None, :], (n, v.shape[0])))
    m = {}
    m["w_in"] = np.ascontiguousarray(g("w_in")[0])
    m["convw"] = np.ascontiguousarray(g("conv_w")[0].reshape(5, 8, 128).transpose(2, 1, 0))
    m["convb"] = np.ascontiguousarray(g("conv_b")[0].reshape(8, 128).T)
    m["dtb"] = bc(np.concatenate([g("dt_bias_fwd")[0], g("dt_bias_bwd")[0]]))
    m["alog"] = bc(np.concatenate([g("a_log_fwd")[0], g("a_log_bwd")[0]]))
    m["dsk"] = bc(g("d_skip")[0])
    m["ang"] = np.ascontiguousarray(g("attn_norm_g")[0].reshape(4, 128).T)
    m["sng"] = bc(g("ssd_norm_g")[0])
    m["relb"] = np.ascontiguousarray(g("rel_bias"))
    m["lnfm"] = np.ascontiguousarray(np.stack([g(k)[0].reshape(8, 128).T for k in ("ln1_g", "ln1_b", "ln2_g", "ln2_b")], axis=1))
    m["w_out"] = np.ascontiguousarray(g("w_out")[0])
    m["w_gate"] = np.ascontiguousarray(g("w_gate")[0])
    m["w_up"] = np.ascontiguousarray(g("w_up")[0])
    m["w_down"] = np.ascontiguousarray(g("w_down")[0])
    m["cst"] = make_cst()
    m["oh"], m["eband"], m["sel"] = make_att_consts()
    return m


def t5_bucket(rel):
    half = 16
    max_exact = 8
    ret = (rel > 0).astype(np.int32) * half
    n = np.abs(rel)
    large = max_exact + (np.log(np.maximum(n, 1) / max_exact)
                         / math.log(1024 / max_exact) * (half - max_exact)).astype(np.int32)
    large = np.minimum(large, half - 1)
    return ret + np.where(n < max_exact, n, large)


def make_att_consts():
    oh = np.zeros((32, 6, 256), np.float32)
    i = np.arange(255)
    for bi, dl in enumerate(BRANCH_DIL):
        for ty in range(2):
            rel = (i - 63) if ty == 0 else (i - 191)
            bk = t5_bucket(rel * dl)
            oh[bk, bi * 2 + ty, i] = 1.0
    eb = np.zeros((128, 2, 384), np.float32)
    r = np.arange(128)
    eb[r, 0, r] = 1.0
    eb[r, 1, r + 128] = 1.0
    sel = np.zeros((65, 64), np.float32)
    sel[64, :] = 1.0
    return oh, eb, sel


SEQ_LEN = 8192
N_CORES = 8
SLOTS = 2
_CACHE = {}


def kernel(**inputs):
    xp = np.asarray(inputs["x_prompt"], dtype=np.float32)
    xs = np.asarray(inputs["x_sample"], dtype=np.float32)
    seqs = [xp[i] for i in range(xp.shape[0])] + [xs[i] for i in range(xs.shape[0])]
    nseq = len(seqs)
    L = seqs[0].shape[0]
    shared = shared_inputs(inputs)
    in_maps = []
    for c in range(N_CORES):
        xT = np.zeros((SLOTS, D, L), np.float32)
        for sl in range(SLOTS):
            i = c * SLOTS + sl
            if i < nseq:
                xT[sl] = seqs[i].T
        m = dict(shared)
        m["xT"] = xT
        in_maps.append(m)
    key = (L, SLOTS)
    if key not in _CACHE:
        _CACHE[key] = build(L, SLOTS)
    nc, _ = _CACHE[key]
    res = run_bass_kernel_spmd(nc, in_maps, core_ids=list(range(N_CORES)))
    outs = []
    for i in range(nseq):
        c, sl = divmod(i, SLOTS)
        outs.append(np.ascontiguousarray(np.asarray(res.results[c]["yT"][sl]).T))
    y_prompt = np.stack(outs[:xp.shape[0]]).astype(np.float32)
    y_sample = np.stack(outs[xp.shape[0]:]).astype(np.float32)
    return (y_prompt, y_sample)
```

```python
import contextlib
import math
import numpy as np
import concourse.bass as bass
import concourse.mybir as mybir
from concourse.bass_utils import run_bass_kernel_spmd

F32 = mybir.dt.float32
BF16 = mybir.dt.bfloat16
U8 = mybir.dt.uint8
AF = mybir.ActivationFunctionType
ALU = mybir.AluOpType

D = 1024
DIN = 3088
DFF = 2816
NH = 8
COL_Q, COL_K, COL_V, COL_Z, COL_X, COL_DT = 0, 512, 1024, 1536, 2048, 3072
ALPHA = 2.0 ** 0.25
EPS = 1e-5
ENGS = ['pe', 'act', 'dve', 'pool', 'sp']
EPOCH = 20000
POOL_BYTES = 206 * 1024


def _dsize(dt):
    return {F32: 4, BF16: 2, U8: 1}[dt]


class Res:
    __slots__ = ('name', 'w', 'r')

    def __init__(self, name):
        self.name = name
        self.w = {}
        self.r = {}


class Chan:
    def __init__(self, name):
        self.name = name
        self.n = 0
        self.sem = None


class Prog:
    def __init__(self, nc):
        self.nc = nc
        self.streams = {e: [] for e in ENGS}
        self.last_op = {e: None for e in ENGS}
        self.chans = []
        self.stack = contextlib.ExitStack()
        self.nres = 0
        self.pool = self.stack.enter_context(nc.sbuf_tensor("pool", [128, POOL_BYTES], U8))
        self.off = 0
        self.peak = 0
        self.banks = [self.stack.enter_context(nc.psum_tensor(f"bank{i}", [128, 512], F32))
                      for i in range(8)]
        self.Rbank = [self.res(f"bank{i}") for i in range(8)]

    def res(self, name=None):
        self.nres += 1
        return Res(name or f"r{self.nres}")

    def chan(self, name):
        c = Chan(name)
        self.chans.append(c)
        return c

    def alloc(self, shape, dtype):
        n = 1
        for s in shape[1:]:
            n *= s
        nbytes = n * _dsize(dtype)
        self.off = (self.off + 63) // 64 * 64
        assert self.off + nbytes <= POOL_BYTES, f"SBUF overflow {self.off + nbytes}"
        ap = self.pool[0:shape[0], self.off:self.off + nbytes].bitcast(dtype)
        self.off += nbytes
        self.peak = max(self.peak, self.off)
        if len(shape) > 2:
            names = [f"d{i}" for i in range(len(shape) - 1)]
            pat = "p (" + " ".join(names) + ") -> p " + " ".join(names)
            ap = ap.rearrange(pat, **{names[i]: shape[i + 1] for i in range(len(names))})
        return ap

    def mark(self):
        return self.off

    bank_list = list(range(8))

    def next_bank(self):
        self.bank_i = getattr(self, 'bank_i', -1) + 1
        return self.bank_list[self.bank_i % len(self.bank_list)]

    def reset(self, m):
        self.off = m

    def _deps(self, reads, writes):
        deps = set()
        for r in reads:
            deps.update(r.w.values())
        for w in writes:
            deps.update(w.w.values())
            deps.update(w.r.values())
        return deps

    def op(self, eng, fn, reads=(), writes=()):
        st = self.streams[eng]
        idx = len(st)
        deps = self._deps(reads, writes)
        st.append(dict(fn=fn, deps=deps, signal=False, chan=None))
        ev = (eng, idx)
        self.last_op[eng] = idx
        for r in reads:
            r.r[eng] = ev
        for w in writes:
            w.w = {eng: ev}
            w.r = {}
        return ev

    def dma(self, q, chan, fn, reads=(), writes=()):
        st = self.streams[q]
        deps = self._deps(reads, writes)
        if chan.n > 0:
            deps.add((chan, chan.n - 1))
        st.append(dict(fn=fn, deps=deps, signal=True, chan=chan))
        ev = (chan, chan.n)
        chan.n += 1
        for r in reads:
            r.r[chan] = ev
        for w in writes:
            w.w = {chan: ev}
            w.r = {}
        return ev

    def barrier(self):
        evs = set()
        for e in ENGS:
            if self.last_op[e] is not None:
                evs.add((e, self.last_op[e]))
        for c in self.chans:
            if c.n > 0:
                evs.add((c, c.n - 1))
        for e in ENGS:
            self.streams[e].append(dict(fn=None, deps=set(d for d in evs if d[0] != e),
                                        signal=False, chan=None))

    def wait_all(self, eng, evs):
        self.streams[eng].append(dict(fn=None, deps=set(evs), signal=False, chan=None))

    def emit(self):
        nc = self.nc
        for e in ENGS:
            for ins in self.streams[e]:
                for (prod, idx) in ins['deps']:
                    if isinstance(prod, str):
                        if prod == 'pe' and e == 'pe' and ins['chan'] is None and ins['fn'] is not None:
                            continue
                        self.streams[prod][idx]['signal'] = True
        nsig = {}
        for e in ENGS:
            c = 0
            for ins in self.streams[e]:
                if ins['chan'] is None and ins['signal'] and ins['fn'] is not None:
                    c += 1
                    ins['cnt'] = c
            nsig[e] = c
        sems = {}
        for e in ENGS:
            ne = max(1, -(-nsig[e] // EPOCH))
            sems[e] = [self.stack.enter_context(nc.semaphore(f"s_{e}{i}")) for i in range(ne)]
        for c in self.chans:
            c.sem = self.stack.enter_context(nc.semaphore(f"c_{c.name}"))
        self.stats = {e: [len(self.streams[e]), nsig[e], 0] for e in ENGS}

        def emit_stream(e, h):
            waited = {}
            nw = 0
            for ins in self.streams[e]:
                need = {}
                for (prod, idx) in ins['deps']:
                    if isinstance(prod, str):
                        if prod == 'pe' and e == 'pe' and ins['chan'] is None and ins['fn'] is not None:
                            continue
                        cnt = self.streams[prod][idx]['cnt']
                        ep = (cnt - 1) // EPOCH
                        sem = sems[prod][ep]
                        val = cnt - ep * EPOCH
                    else:
                        sem = prod.sem
                        val = 16 * (idx + 1)
                    k = id(sem)
                    if waited.get(k, 0) >= val:
                        continue
                    if k not in need or need[k][1] < val:
                        need[k] = (sem, val)
                for k, (sem, val) in need.items():
                    h.wait_ge(sem, val)
                    waited[k] = val
                    nw += 1
                if ins['fn'] is None:
                    continue
                r = ins['fn'](h)
                if ins['chan'] is not None:
                    r.then_inc(ins['chan'].sem, 16)
                elif ins['signal']:
                    cnt = ins['cnt']
                    ep = (cnt - 1) // EPOCH
                    r.then_inc(sems[e][ep], 1)
            self.stats[e][2] = nw

        with nc.Block() as block:
            @block.tensor
            def _(h):
                emit_stream('pe', h)

            @block.scalar
            def _(h):
                emit_stream('act', h)

            @block.vector
            def _(h):
                emit_stream('dve', h)

            @block.gpsimd
            def _(h):
                emit_stream('pool', h)

            @block.sync
            def _(h):
                emit_stream('sp', h)
        self.stack.close()


def mm_group(P, out_ap, pairs, Rout, reads):
    n = len(pairs)
    for i, (l, r) in enumerate(pairs):
        P.op('pe', lambda h, l=l, r=r, i=i: h.matmul(out_ap, l, r, start=(i == 0), stop=(i == n - 1)),
             reads=reads, writes=[Rout])


def act(P, out, in_, func, reads, writes, scale=None, bias=None):
    kw = {}
    if scale is not None:
        kw['scale'] = scale
    if bias is not None:
        kw['bias'] = bias
    return P.op('act', lambda h: h.activation(out=out, in_=in_, func=func, **kw), reads=reads, writes=writes)


class Ring:
    def __init__(self, P, name, n, shape, dtype, chan=False):
        self.bufs = [P.alloc(shape, dtype) for _ in range(n)]
        self.res = [P.res(f"{name}{i}") for i in range(n)]
        self.ch = [P.chan(f"{name}{i}") for i in range(n)] if chan else None
        self.i = -1
        self.n = n

    def next(self):
        self.i = (self.i + 1) % self.n
        if self.ch:
            return self.bufs[self.i], self.res[self.i], self.ch[self.i]
        return self.bufs[self.i], self.res[self.i]


class Ctx:
    pass


def build(L, NS, debug=False, phases=('A', 'S', 'T', 'C')):
    nc = bass.Bass("TRN2", target_bir_lowering=False)
    P = Prog(nc)
    C = Ctx()
    C.L, C.NS, C.P, C.nc = L, NS, P, nc
    NT = L // 512

    def din(name, shape, dt=F32):
        return nc.dram_tensor(name, list(shape), dt, kind="ExternalInput").ap()

    def dscr(name, shape, dt):
        return nc.dram_tensor(name, list(shape), dt,
                              kind="ExternalOutput" if debug else "Internal").ap()

    C.xT = din("xT", [NS, D, L])
    C.w_in = din("w_in", [D, DIN])
    C.convw = din("convw", [128, 8, 5])
    C.convb = din("convb", [128, 8])
    C.dtb = din("dtb", [128, 16])
    C.alog = din("alog", [128, 16])
    C.dsk = din("dsk", [128, 8])
    C.ang = din("ang", [128, 4])
    C.sng = din("sng", [128, 512])
    C.relb = din("relb", [32, 8])
    C.w_out = din("w_out", [D, D])
    C.w_gate = din("w_gate", [D, DFF])
    C.w_up = din("w_up", [D, DFF])
    C.w_down = din("w_down", [DFF, D])
    C.cst = din("cst", [128, 6, 128])
    C.oh = din("oh", [32, 6, 256])
    C.jmat = din("jmat", [128, 128])
    C.sel = din("sel", [65, 64])
    C.GV = dscr("GV", [6, 8, 256], F32)
    C.lnfm = din("lnfm", [128, 4, 8])
    C.yT = nc.dram_tensor("yT", [NS, D, L], F32, kind="ExternalOutput").ap()
    C.H1F = dscr("H1F", [NS, D, L], F32)
    C.H1B = dscr("H1B", [NS, D, L], BF16)

    C.QT = dscr("QT", [NS, 512, L], BF16)
    C.KT = dscr("KT", [NS, 512, L], BF16)
    C.V = dscr("V", [NS, L, 8 * 65], BF16)
    C.Z = dscr("Z", [NS, L, 512], F32)
    C.XS = dscr("XS", [NS, 512, L], F32)
    C.BC = dscr("BC", [NS, 512, L], BF16)
    C.DT = dscr("DT", [NS, L, 16], F32)

    outs = []
    C.YN = dscr("YN", [NS, 512, L], BF16)
    C.AT = dscr("AT", [NS, 512, L], F32)
    if 'A' in phases:
        phase_A(C)
        P.barrier()
    if 'S' in phases:
        phase_S(C)
        P.barrier()
    if 'T' in phases:
        phase_T(C)
        P.barrier()
    if 'C' in phases:
        phase_C1(C)
        P.barrier()
        outs = phase_C2(C)
        P.barrier()
    if debug:
        evs = [(c, c.n - 1) for c in P.chans if c.n > 0]
        P.wait_all('sp', evs)
    else:
        P.wait_all('sp', outs)
    P.emit()
    return nc, P


def phase_A(C):
    P, L, NS = C.P, C.L, C.NS
    NT = L // 512
    m0 = P.mark()
    win = P.alloc([128, 8, DIN], BF16)
    Rwin = P.res("win")
    cw = P.alloc([128, 8, 5], F32)
    cb = P.alloc([128, 8], F32)
    dtb = P.alloc([128, 16], F32)
    Rsm = P.res("small")
    ch_w = P.chan("w")
    w_v = C.w_in.rearrange("(kc p) c -> p kc c", p=128)
    for c0 in (0, 1544):
        P.dma('pool', ch_w, lambda h, c0=c0: h.dma_start(out=win[:, :, c0:c0 + 1544], in_=w_v[:, :, c0:c0 + 1544]),
              writes=[Rwin])
    ch_s = P.chan("small")
    P.dma('sp', ch_s, lambda h: h.dma_start(out=cw, in_=C.convw), writes=[Rsm])
    P.dma('sp', ch_s, lambda h: h.dma_start(out=cb, in_=C.convb), writes=[Rsm])
    P.dma('sp', ch_s, lambda h: h.dma_start(out=dtb, in_=C.dtb), writes=[Rsm])

    xtb = Ring(P, "xtb", 2, [128, 8, 512], BF16, chan=True)
    stq = Ring(P, "stq", 2, [128, 4, 512], BF16, chan=True)
    stk = Ring(P, "stk", 2, [128, 4, 512], BF16, chan=True)
    stx = Ring(P, "stx", 2, [128, 4, 512], F32, chan=True)
    stbc = Ring(P, "stbc", 2, [128, 4, 512], BF16, chan=True)
    stv = Ring(P, "stv", 2, [128, 4, 8, 65], BF16, chan=True)
    stz = Ring(P, "stz", 2, [128, 4, 512], F32, chan=True)
    stdt = Ring(P, "stdt", 2, [128, 4, 16], F32, chan=True)
    raw = [P.alloc([128, 520], F32) for _ in range(8)]
    Rraw = [P.res(f"raw{c}") for c in range(8)]
    acc = Ring(P, "acc", 4, [128, 512], F32)
    dtt = Ring(P, "dtt", 2, [128, 16], F32)
    for b, r in zip(stv.bufs, stv.res):
        P.op('pool', lambda h, b=b: h.memset(b, 1.0), writes=[r])
    fm_banks = [0, 1, 2]
    tm_banks = [3, 4, 5, 6]
    dt_bank = 7
    fmi = [0]
    tmi = [0]

    def next_fm():
        b = fm_banks[fmi[0] % len(fm_banks)]
        fmi[0] += 1
        return b

    def next_tm():
        b = tm_banks[tmi[0] % len(tm_banks)]
        tmi[0] += 1
        return b

    def load_x(s, T):
        buf, r, ch = xtb.next()
        src = C.xT[s].rearrange("(kc p) t -> p kc t", p=128)[:, :, T * 512:(T + 1) * 512]
        P.dma('pool', ch, lambda h: h.dma_start(out=buf, in_=src), writes=[r])
        return buf, r

    def conv_chunk_ops(c, width, accb, Racc):
        ops = []
        rb = raw[c]
        ops.append(lambda: P.op('dve', lambda h: h.tensor_scalar(
            out=accb[:, 0:width], in0=rb[:, 0:width], scalar1=cw[:, c, 0:1], scalar2=cb[:, c:c + 1],
            op0=ALU.mult, op1=ALU.add), reads=[Rraw[c], Rsm], writes=[Racc]))
        for j in range(1, 5):
            ops.append(lambda j=j: P.op('dve', lambda h: h.scalar_tensor_tensor(
                out=accb[:, 0:width], in0=rb[:, j:j + width], scalar=cw[:, c, j:j + 1], in1=accb[:, 0:width],
                op0=ALU.mult, op1=ALU.add), reads=[Rraw[c], Rsm, Racc], writes=[Racc]))
        return ops

    def conv_finish(s, c, width, accb, Racc, xbuf, xr, bcbuf, bcr):
        if c < 4:
            act(P, xbuf[:, c, 0:width], accb[:, 0:width], AF.Silu, [Racc], [xr])
        else:
            act(P, bcbuf[:, c - 4, 0:width], accb[:, 0:width], AF.Silu, [Racc], [bcr])

    for s in range(NS):
        for c in range(8):
            P.op('pool', lambda h, c=c: h.memset(raw[c][:, 0:4], 0.0), writes=[Rraw[c]])
        nxt = load_x(s, 0)
        for T in range(NT):
            xb, xr_ = nxt
            if T + 1 < NT:
                nxt = load_x(s, T + 1)
            t0 = T * 512
            qb, qr, qch = stq.next()
            kb, kr, kch = stk.next()
            for c in range(4):
                b = next_fm()
                mm_group(P, P.banks[b][:, :], [(win[:, kc, COL_Q + c * 128:COL_Q + (c + 1) * 128], xb[:, kc, :])
                                               for kc in range(8)], P.Rbank[b], [Rwin, xr_])
                act(P, qb[:, c, :], P.banks[b][:, :], AF.Identity, [P.Rbank[b]], [qr], scale=0.125)
            P.dma('sp', qch, lambda h, qb=qb, s=s, t0=t0: h.dma_start(
                out=C.QT[s].rearrange("(c p) t -> p c t", p=128)[:, :, t0:t0 + 512], in_=qb), reads=[qr])
            for c in range(4):
                b = next_fm()
                mm_group(P, P.banks[b][:, :], [(win[:, kc, COL_K + c * 128:COL_K + (c + 1) * 128], xb[:, kc, :])
                                               for kc in range(8)], P.Rbank[b], [Rwin, xr_])
                P.op('dve', lambda h, b=b, c=c, kb=kb: h.tensor_copy(out=kb[:, c, :], in_=P.banks[b][:, :]),
                     reads=[P.Rbank[b]], writes=[kr])
            P.dma('sp', kch, lambda h, kb=kb, s=s, t0=t0: h.dma_start(
                out=C.KT[s].rearrange("(c p) t -> p c t", p=128)[:, :, t0:t0 + 512], in_=kb), reads=[kr])
            sxb, sxr, sxch = stx.next()
            sbb, sbr, sbch = stbc.next()
            for cp in range(4):
                chains = []
                accs = []
                for c in (2 * cp, 2 * cp + 1):
                    b = next_fm()
                    mm_group(P, P.banks[b][:, :],
                             [(win[:, kc, COL_X + c * 128:COL_X + (c + 1) * 128], xb[:, kc, :]) for kc in range(8)],
                             P.Rbank[b], [Rwin, xr_])
                    act(P, raw[c][:, 4:516], P.banks[b][:, :], AF.Identity, [P.Rbank[b]], [Rraw[c]])
                    ab, ar = acc.next()
                    accs.append((c, ab, ar))
                    chains.append(conv_chunk_ops(c, 512, ab, ar))
                for j in range(5):
                    for ch_ in chains:
                        ch_[j]()
                for (c, ab, ar) in accs:
                    conv_finish(s, c, 512, ab, ar, sxb, sxr, sbb, sbr)
                    P.op('pool', lambda h, c=c: h.tensor_copy(out=raw[c][:, 0:4], in_=raw[c][:, 512:516]),
                         reads=[Rraw[c]], writes=[Rraw[c]])
            lo = 2 if T == 0 else 0
            P.dma('sp', sxch, lambda h, sxb=sxb, s=s, t0=t0, lo=lo: h.dma_start(
                out=C.XS[s].rearrange("(c p) t -> p c t", p=128)[:, :, t0 - 2 + lo:t0 + 510],
                in_=sxb[:, :, lo:512]), reads=[sxr])
            P.dma('sp', sbch, lambda h, sbb=sbb, s=s, t0=t0, lo=lo: h.dma_start(
                out=C.BC[s].rearrange("(c p) t -> p c t", p=128)[:, :, t0 - 2 + lo:t0 + 510],
                in_=sbb[:, :, lo:512]), reads=[sbr])
            vb, vr, vch = stv.next()
            zb, zr, zch = stz.next()
            db, dr, dch = stdt.next()
            for u in range(4):
                lw = [xb[:, kc, u * 128:(u + 1) * 128] for kc in range(8)]
                b = next_tm()
                mm_group(P, P.banks[b][:, :], [(lw[kc], win[:, kc, COL_V:COL_V + 512]) for kc in range(8)],
                         P.Rbank[b], [Rwin, xr_])
                act(P, vb[:, u, :, 0:64], P.banks[b][:, :].rearrange("p (h d) -> p h d", d=64), AF.Identity,
                    [P.Rbank[b]], [vr])
                b = next_tm()
                mm_group(P, P.banks[b][:, :], [(lw[kc], win[:, kc, COL_Z:COL_Z + 512]) for kc in range(8)],
                         P.Rbank[b], [Rwin, xr_])
                act(P, zb[:, u, :], P.banks[b][:, :], AF.Silu, [P.Rbank[b]], [zr])
                b = dt_bank
                mm_group(P, P.banks[b][:, 0:16], [(lw[kc], win[:, kc, COL_DT:COL_DT + 16]) for kc in range(8)],
                         P.Rbank[b], [Rwin, xr_])
                tb, tr = dtt.next()
                P.op('dve', lambda h, b=b, tb=tb: h.tensor_tensor(out=tb, in0=P.banks[b][:, 0:16], in1=dtb, op=ALU.add),
                     reads=[P.Rbank[b], Rsm], writes=[tr])
                act(P, tb, tb, AF.Exp, [tr], [tr])
                act(P, db[:, u, :], tb, AF.Ln, [tr], [dr], bias=1.0)
            P.dma('sp', vch, lambda h, vb=vb, s=s, t0=t0: h.dma_start(
                out=C.V[s][t0:t0 + 512, :].rearrange("(u p) f -> p u f", p=128),
                in_=vb.rearrange("p u h e -> p u (h e)")), reads=[vr])
            P.dma('sp', zch, lambda h, zb=zb, s=s, t0=t0: h.dma_start(
                out=C.Z[s][t0:t0 + 512, :].rearrange("(u p) f -> p u f", p=128), in_=zb), reads=[zr])
            P.dma('sp', dch, lambda h, db=db, s=s, t0=t0: h.dma_start(
                out=C.DT[s][t0:t0 + 512, :].rearrange("(u p) f -> p u f", p=128), in_=db), reads=[dr])
        sxb, sxr, sxch = stx.next()
        sbb, sbr, sbch = stbc.next()
        for c in range(8):
            P.op('pool', lambda h, c=c: h.memset(raw[c][:, 4:8], 0.0), reads=[Rraw[c]], writes=[Rraw[c]])
            ab, ar = acc.next()
            for o in conv_chunk_ops(c, 2, ab, ar):
                o()
            conv_finish(s, c, 2, ab, ar, sxb, sxr, sbb, sbr)
        P.dma('sp', sxch, lambda h, sxb=sxb, s=s: h.dma_start(
            out=C.XS[s].rearrange("(c p) t -> p c t", p=128)[:, :, L - 2:L], in_=sxb[:, :, 0:2]),
            reads=[sxr])
        P.dma('sp', sbch, lambda h, sbb=sbb, s=s: h.dma_start(
            out=C.BC[s].rearrange("(c p) t -> p c t", p=128)[:, :, L - 2:L], in_=sbb[:, :, 0:2]),
            reads=[sbr])
    P.reset(m0)


def phase_S(C):
    P, L, NS = C.P, C.L, C.NS
    NC = L // 128
    NG = NC // 4
    m0 = P.mark()
    cst = P.alloc([128, 6, 128], F32)
    idb = P.alloc([128, 128], BF16)
    dsk = P.alloc([128, 8], F32)
    Aneg = P.alloc([128, 16], F32)
    sng = P.alloc([128, 512], F32)
    Rc = P.res("s_const")
    chc = P.chan("s_const")
    P.dma('sp', chc, lambda h: h.dma_start(out=cst, in_=C.cst), writes=[Rc])
    P.dma('sp', chc, lambda h: h.dma_start(out=dsk, in_=C.dsk), writes=[Rc])
    P.dma('sp', chc, lambda h: h.dma_start(out=Aneg, in_=C.alog), writes=[Rc])
    P.dma('sp', chc, lambda h: h.dma_start(out=sng, in_=C.sng), writes=[Rc])
    act(P, Aneg, Aneg, AF.Exp, [Rc], [Rc])
    P.op('dve', lambda h: h.tensor_scalar(out=Aneg, in0=Aneg, scalar1=-1.0, scalar2=None, op0=ALU.mult),
         reads=[Rc], writes=[Rc])
    P.op('dve', lambda h: h.tensor_copy(out=idb, in_=cst[:, 5, :]), reads=[Rc], writes=[Rc])
    U_, SL_, LO_, SU_, ON_, ID_ = [cst[:, i, :] for i in range(6)]

    SbAll = P.alloc([128, NC, 512], BF16)
    RSb = [P.res(f"sb{c}") for c in range(NC)]
    gx = Ring(P, "gx", 2, [128, 4, 512], F32, chan=True)
    gbc = Ring(P, "gbc", 2, [128, 4, 512], BF16, chan=True)
    gdt = Ring(P, "gdt", 2, [128, 4, 16], F32, chan=True)
    gz = Ring(P, "gz", 2, [128, 4, 512], F32, chan=True)
    syn = Ring(P, "syn", 2, [128, 4, 512], BF16, chan=True)
    da_r = Ring(P, "da", 3, [128, 16], F32)
    ew_r = Ring(P, "ew", 3, [128, 64], F32)
    sc_r = Ring(P, "sc", 3, [128, 16], F32)
    xdt_r = Ring(P, "xdt", 6, [128, 512], BF16)
    xsd_r = Ring(P, "xsd", 2, [128, 512], F32)
    btok_r = Ring(P, "btok", 2, [128, 256], BF16)
    cbm_r = Ring(P, "cbm", 4, [128, 256], F32)
    L_r = Ring(P, "Lr", 2, [128, 8, 128], F32)
    dec_r = Ring(P, "dec", 2, [128, 8, 128], F32)
    M_r = Ring(P, "Mr", 4, [128, 8, 128], BF16)
    t_r = Ring(P, "tr", 4, [128, 512], F32)
    yn_r = Ring(P, "yn", 2, [128, 512], BF16)
    sm_r = Ring(P, "sm", 4, [128, 4], F32)
    Sf = P.alloc([128, 512], F32)
    Sfb = P.alloc([128, 512], BF16)
    Sb = P.alloc([128, 512], F32)
    RSf, RSfb, RSbr = P.res("Sf"), P.res("Sfb"), P.res("Sbr")

    def bc8(ap8):
        return ap8.unsqueeze(2).to_broadcast([128, 8, 64])

    def v3(ap):
        return ap.rearrange("p (h d) -> p h d", d=64)

    def load_group(s, g, with_z):
        t0 = g * 512
        xb, xr, xch = gx.next()
        P.dma('sp', xch, lambda h: h.dma_start(
            out=xb, in_=C.XS[s].rearrange("(c p) t -> p c t", p=128)[:, :, t0:t0 + 512]), writes=[xr])
        bb, br, bch = gbc.next()
        P.dma('sp', bch, lambda h: h.dma_start(
            out=bb, in_=C.BC[s].rearrange("(c p) t -> p c t", p=128)[:, :, t0:t0 + 512]), writes=[br])
        db, dr, dch = gdt.next()
        P.dma('sp', dch, lambda h: h.dma_start(
            out=db, in_=C.DT[s][t0:t0 + 512, :].rearrange("(u p) f -> p u f", p=128)), writes=[dr])
        zz = None
        if with_z:
            zb, zr, zch = gz.next()
            P.dma('sp', zch, lambda h: h.dma_start(
                out=zb, in_=C.Z[s][t0:t0 + 512, :].rearrange("(u p) f -> p u f", p=128)), writes=[zr])
            zz = (zb, zr)
        return (xb, xr), (bb, br), (db, dr), zz

    def small_mms(da, Rda, mats):
        b = P.next_bank()
        for i, m_ in enumerate(mats):
            P.op('pe', lambda h, i=i, m_=m_, b=b: h.matmul(P.banks[b][:, 16 * i:16 * i + 16], m_, da, start=True, stop=True),
                 reads=[Rc, Rda], writes=[P.Rbank[b]])
        ew, Rew = ew_r.next()
        n = 16 * len(mats)
        act(P, ew[:, 0:n], P.banks[b][:, 0:n], AF.Exp, [P.Rbank[b]], [Rew])
        return ew, Rew

    def xs_transpose(xb, xr, u):
        b = P.next_bank()
        for fc in range(4):
            P.op('pe', lambda h, fc=fc, b=b: h.transpose(P.banks[b][:, fc * 128:(fc + 1) * 128],
                                                        xb[:, fc, u * 128:(u + 1) * 128], ID_),
                 reads=[xr, Rc], writes=[P.Rbank[b]])
        return b

    def b_transpose(bb, br, u):
        b = P.next_bank()
        pb = P.banks[b][:, :].bitcast(BF16)
        for g in range(2):
            P.op('pe', lambda h, g=g, pb=pb: h.transpose(pb[:, g * 128:(g + 1) * 128],
                                                        bb[:, g, u * 128:(u + 1) * 128], idb),
                 reads=[br, Rc], writes=[P.Rbank[b]])
        bt, Rbt = btok_r.next()
        act(P, bt, pb[:, 0:256], AF.Identity, [P.Rbank[b]], [Rbt])
        return bt, Rbt

    def state_mm(bt, Rbt, xw, Rxw):
        b = P.next_bank()
        for g in range(2):
            P.op('pe', lambda h, g=g, b=b: h.matmul(P.banks[b][:, g * 256:(g + 1) * 256], bt[:, g * 128:(g + 1) * 128],
                                                   xw[:, g * 256:(g + 1) * 256], start=True, stop=True),
                 reads=[Rbt, Rxw], writes=[P.Rbank[b]])
        return b

    for s in range(NS):
        P.op('pool', lambda h: h.memset(Sb, 0.0), writes=[RSbr])
        P.op('pool', lambda h: h.memset(SbAll[:, NC - 1, :], 0.0), writes=[RSb[NC - 1]])
        nxt = load_group(s, NG - 1, False)
        for g in range(NG - 1, -1, -1):
            (xb, xr), (bb, br), (db, dr), _ = nxt
            if g > 0:
                nxt = load_group(s, g - 1, False)
            for u in range(3, -1, -1):
                c = g * 4 + u
                if c == 0:
                    break
                da, Rda = da_r.next()
                P.op('dve', lambda h, da=da, db=db, u=u: h.tensor_tensor(out=da, in0=db[:, u, :], in1=Aneg, op=ALU.mult),
                     reads=[dr, Rc], writes=[Rda])
                ew, Rew = small_mms(da, Rda, [SU_, ON_])
                sc, Rsc = sc_r.next()
                P.op('dve', lambda h, sc=sc, db=db, u=u, ew=ew: h.tensor_tensor(
                    out=sc[:, 0:8], in0=db[:, u, 8:16], in1=ew[:, 8:16], op=ALU.mult), reads=[dr, Rew], writes=[Rsc])
                bx = xs_transpose(xb, xr, u)
                xw, Rxw = xdt_r.next()
                P.op('dve', lambda h, xw=xw, bx=bx, sc=sc: h.tensor_tensor(
                    out=v3(xw), in0=v3(P.banks[bx][:, :]), in1=bc8(sc[:, 0:8]), op=ALU.mult),
                    reads=[P.Rbank[bx], Rsc], writes=[Rxw])
                bt, Rbt = b_transpose(bb, br, u)
                bs = state_mm(bt, Rbt, xw, Rxw)
                P.op('dve', lambda h, ew=ew: h.tensor_tensor(out=v3(Sb), in0=v3(Sb), in1=bc8(ew[:, 24:32]), op=ALU.mult),
                     reads=[RSbr, Rew], writes=[RSbr])
                P.op('dve', lambda h, bs=bs: h.tensor_tensor(out=Sb, in0=Sb, in1=P.banks[bs][:, :], op=ALU.add),
                     reads=[RSbr, P.Rbank[bs]], writes=[RSbr])
                act(P, SbAll[:, c - 1, :], Sb, AF.Identity, [RSbr], [RSb[c - 1]])
        P.op('pool', lambda h: h.memset(Sf, 0.0), writes=[RSf])
        P.op('pool', lambda h: h.memset(Sfb, 0.0), writes=[RSfb])
        nxt = load_group(s, 0, True)
        for g in range(NG):
            (xb, xr), (bb, br), (db, dr), (zb, zr) = nxt
            if g + 1 < NG:
                nxt = load_group(s, g + 1, True)
            yb, yr, ych = syn.next()
            for u in range(4):
                c = g * 4 + u
                da, Rda = da_r.next()
                P.op('dve', lambda h, da=da, db=db, u=u: h.tensor_tensor(out=da, in0=db[:, u, :], in1=Aneg, op=ALU.mult),
                     reads=[dr, Rc], writes=[Rda])
                ew, Rew = small_mms(da, Rda, [U_, SL_, LO_, ON_])
                sc, Rsc = sc_r.next()
                P.op('dve', lambda h, sc=sc, db=db, u=u, ew=ew: h.tensor_tensor(
                    out=sc[:, 0:8], in0=db[:, u, 0:8], in1=ew[:, 16:24], op=ALU.mult), reads=[dr, Rew], writes=[Rsc])
                bx = xs_transpose(xb, xr, u)
                xsP = v3(P.banks[bx][:, :])
                xf, Rxf = xdt_r.next()
                xbw, Rxbw = xdt_r.next()
                xw, Rxw = xdt_r.next()
                xsd, Rxsd = xsd_r.next()
                P.op('dve', lambda h, xf=xf, xsP=xsP, db=db, u=u: h.tensor_tensor(
                    out=v3(xf), in0=xsP, in1=bc8(db[:, u, 0:8]), op=ALU.mult), reads=[P.Rbank[bx], dr], writes=[Rxf])
                P.op('dve', lambda h, xbw=xbw, xsP=xsP, db=db, u=u: h.tensor_tensor(
                    out=v3(xbw), in0=xsP, in1=bc8(db[:, u, 8:16]), op=ALU.mult), reads=[P.Rbank[bx], dr], writes=[Rxbw])
                P.op('dve', lambda h, xw=xw, xsP=xsP, sc=sc: h.tensor_tensor(
                    out=v3(xw), in0=xsP, in1=bc8(sc[:, 0:8]), op=ALU.mult), reads=[P.Rbank[bx], Rsc], writes=[Rxw])
                P.op('dve', lambda h, xsd=xsd, xsP=xsP: h.tensor_tensor(
                    out=v3(xsd), in0=xsP, in1=bc8(dsk), op=ALU.mult), reads=[P.Rbank[bx], Rc], writes=[Rxsd])
                bt, Rbt = b_transpose(bb, br, u)
                bcb = P.next_bank()
                for gg in range(2):
                    P.op('pe', lambda h, gg=gg, bcb=bcb, bb=bb, u=u: h.matmul(
                        P.banks[bcb][:, gg * 128:(gg + 1) * 128], bb[:, gg, u * 128:(u + 1) * 128],
                        bb[:, 2 + gg, u * 128:(u + 1) * 128], start=True, stop=True), reads=[br], writes=[P.Rbank[bcb]])
                cbU, RcbU = cbm_r.next()
                cbL, RcbL = cbm_r.next()
                cbP = P.banks[bcb][:, 0:256].rearrange("p (g q) -> p g q", g=2)
                P.op('dve', lambda h, cbU=cbU, cbP=cbP: h.tensor_tensor(
                    out=cbU.rearrange("p (g q) -> p g q", g=2), in0=cbP,
                    in1=U_.unsqueeze(1).to_broadcast([128, 2, 128]), op=ALU.mult),
                    reads=[P.Rbank[bcb], Rc], writes=[RcbU])
                P.op('dve', lambda h, cbL=cbL, cbP=cbP: h.tensor_tensor(
                    out=cbL.rearrange("p (g q) -> p g q", g=2), in0=cbP,
                    in1=LO_.unsqueeze(1).to_broadcast([128, 2, 128]), op=ALU.mult),
                    reads=[P.Rbank[bcb], Rc], writes=[RcbL])
                Ms = []
                for di, (tri_l, tri_r, cbm, Rcbm, c0) in enumerate(((SL_, U_, cbU, RcbU, 0), (SU_, LO_, cbL, RcbL, 8))):
                    Lt, RLt = L_r.next()
                    P.op('pool', lambda h, Lt=Lt, tri_l=tri_l, da=da, c0=c0: h.tensor_tensor(
                        out=Lt, in0=tri_l.unsqueeze(1).to_broadcast([128, 8, 128]),
                        in1=da[:, c0:c0 + 8].unsqueeze(2).to_broadcast([128, 8, 128]), op=ALU.mult),
                        reads=[Rc, Rda], writes=[RLt])
                    dec, Rdec = dec_r.next()
                    for hh in range(2):
                        b = P.next_bank()
                        for h4 in range(4):
                            hd = hh * 4 + h4
                            P.op('pe', lambda h, b=b, h4=h4, hd=hd, Lt=Lt, tri_r=tri_r: h.matmul(
                                P.banks[b][:, h4 * 128:(h4 + 1) * 128], Lt[:, hd, :], tri_r, start=True, stop=True),
                                reads=[RLt, Rc], writes=[P.Rbank[b]])
                        act(P, dec[:, hh * 4:(hh + 1) * 4, :], P.banks[b][:, :].rearrange("p (a q) -> p a q", a=4),
                            AF.Exp, [P.Rbank[b]], [Rdec])
                    Mt, RMt = M_r.next()
                    P.op('dve', lambda h, Mt=Mt, dec=dec, cbm=cbm: h.tensor_tensor(
                        out=Mt.rearrange("p (g e) q -> p g e q", g=2), in0=dec.rearrange("p (g e) q -> p g e q", g=2),
                        in1=cbm.rearrange("p (g q) -> p g q", g=2).unsqueeze(2).to_broadcast([128, 2, 4, 128]),
                        op=ALU.mult), reads=[Rdec, Rcbm], writes=[RMt])
                    Ms.append((Mt, RMt))
                by = P.next_bank()
                for hd in range(8):
                    P.op('pe', lambda h, by=by, hd=hd, M0=Ms[0][0], xf=xf: h.matmul(
                        P.banks[by][:, hd * 64:(hd + 1) * 64], M0[:, hd, :], xf[:, hd * 64:(hd + 1) * 64],
                        start=True, stop=False), reads=[Ms[0][1], Rxf], writes=[P.Rbank[by]])
                    P.op('pe', lambda h, by=by, hd=hd, M1=Ms[1][0], xbw=xbw: h.matmul(
                        P.banks[by][:, hd * 64:(hd + 1) * 64], M1[:, hd, :], xbw[:, hd * 64:(hd + 1) * 64],
                        start=False, stop=True), reads=[Ms[1][1], Rxbw], writes=[P.Rbank[by]])
                bof = P.next_bank()
                bob = P.next_bank()
                for gg in range(2):
                    P.op('pe', lambda h, gg=gg, bof=bof, bb=bb, u=u: h.matmul(
                        P.banks[bof][:, gg * 256:(gg + 1) * 256], bb[:, 2 + gg, u * 128:(u + 1) * 128],
                        Sfb[:, gg * 256:(gg + 1) * 256], start=True, stop=True), reads=[br, RSfb], writes=[P.Rbank[bof]])
                for gg in range(2):
                    P.op('pe', lambda h, gg=gg, bob=bob, bb=bb, u=u, c=c: h.matmul(
                        P.banks[bob][:, gg * 256:(gg + 1) * 256], bb[:, 2 + gg, u * 128:(u + 1) * 128],
                        SbAll[:, c, gg * 256:(gg + 1) * 256], start=True, stop=True), reads=[br, RSb[c]], writes=[P.Rbank[bob]])
                t1, Rt1 = t_r.next()
                t2, Rt2 = t_r.next()
                P.op('dve', lambda h, t1=t1, bof=bof, ew=ew: h.tensor_tensor(
                    out=v3(t1), in0=v3(P.banks[bof][:, :]), in1=bc8(ew[:, 0:8]), op=ALU.mult),
                    reads=[P.Rbank[bof], Rew], writes=[Rt1])
                P.op('dve', lambda h, t2=t2, bob=bob, ew=ew: h.tensor_tensor(
                    out=v3(t2), in0=v3(P.banks[bob][:, :]), in1=bc8(ew[:, 40:48]), op=ALU.mult),
                    reads=[P.Rbank[bob], Rew], writes=[Rt2])
                P.op('pool', lambda h, t1=t1, t2=t2: h.tensor_tensor(out=t1, in0=t1, in1=t2, op=ALU.add),
                     reads=[Rt1, Rt2], writes=[Rt1])
                P.op('pool', lambda h, t1=t1, xsd=xsd: h.tensor_tensor(out=t1, in0=t1, in1=xsd, op=ALU.add),
                     reads=[Rt1, Rxsd], writes=[Rt1])
                P.op('dve', lambda h, t1=t1, by=by: h.tensor_tensor(out=t1, in0=t1, in1=P.banks[by][:, :], op=ALU.add),
                     reads=[Rt1, P.Rbank[by]], writes=[Rt1])
                P.op('dve', lambda h, t1=t1, zb=zb, u=u: h.tensor_tensor(out=t1, in0=t1, in1=zb[:, u, :], op=ALU.mult),
                     reads=[Rt1, zr], writes=[Rt1])
                sm, Rsm_ = sm_r.next()
                P.op('act', lambda h, t2=t2, t1=t1, sm=sm: h.activation(out=t2, in_=t1, func=AF.Square, accum_out=sm[:, 0:1]),
                     reads=[Rt1, Rt2], writes=[Rt2, Rsm_])
                act(P, sm[:, 1:2], sm[:, 0:1], AF.Ln, [Rsm_], [Rsm_], scale=1.0 / 512, bias=EPS)
                act(P, sm[:, 2:3], sm[:, 1:2], AF.Exp, [Rsm_], [Rsm_], scale=-0.5)
                yn, Ryn = yn_r.next()
                P.op('dve', lambda h, yn=yn, t1=t1, sm=sm: h.scalar_tensor_tensor(
                    out=yn, in0=t1, scalar=sm[:, 2:3], in1=sng, op0=ALU.mult, op1=ALU.mult),
                    reads=[Rt1, Rsm_, Rc], writes=[Ryn])
                bt_ = P.next_bank()
                pbt = P.banks[bt_][:, :].bitcast(BF16)
                for fc in range(4):
                    P.op('pe', lambda h, fc=fc, pbt=pbt, yn=yn: h.transpose(
                        pbt[:, fc * 128:(fc + 1) * 128], yn[:, fc * 128:(fc + 1) * 128], idb),
                        reads=[Ryn, Rc], writes=[P.Rbank[bt_]])
                act(P, yb[:, :, u * 128:(u + 1) * 128], pbt[:, 0:512].rearrange("p (c t) -> p c t", c=4), AF.Identity,
                    [P.Rbank[bt_]], [yr])
                bs = state_mm(bt, Rbt, xw, Rxw)
                P.op('dve', lambda h, ew=ew: h.tensor_tensor(out=v3(Sf), in0=v3(Sf), in1=bc8(ew[:, 48:56]), op=ALU.mult),
                     reads=[RSf, Rew], writes=[RSf])
                P.op('dve', lambda h, bs=bs: h.tensor_tensor(out=Sf, in0=Sf, in1=P.banks[bs][:, :], op=ALU.add),
                     reads=[RSf, P.Rbank[bs]], writes=[RSf])
                act(P, Sfb, Sf, AF.Identity, [RSf], [RSfb])
            P.dma('sp', ych, lambda h, yb=yb, s=s, g=g: h.dma_start(
                out=C.YN[s].rearrange("(c p) t -> p c t", p=128)[:, :, g * 512:(g + 1) * 512], in_=yb), reads=[yr])
    P.reset(m0)


BRANCH_DIL = (1, 4, 16)


def phase_T(C):
    P, L, NS = C.P, C.L, C.NS
    m0 = P.mark()
    cst = P.alloc([128, 6, 128], F32)
    jmat = P.alloc([128, 128], F32)
    sel = P.alloc([65, 64], F32)
    EB = P.alloc([128, 4, 3, 2, 256], F32)
    Rc = P.res("t_const")
    REB = P.res("EB")
    chc = P.chan("t_const")
    P.dma('sp', chc, lambda h: h.dma_start(out=cst, in_=C.cst), writes=[Rc])
    P.dma('sp', chc, lambda h: h.dma_start(out=jmat, in_=C.jmat), writes=[Rc])
    P.dma('sp', chc, lambda h: h.dma_start(out=sel, in_=C.sel), writes=[Rc])
    U_, LO_ = cst[:, 0, :], cst[:, 2, :]
    m1 = P.mark()
    oh = P.alloc([32, 6, 256], F32)
    relb = P.alloc([32, 8], F32)
    P.dma('sp', chc, lambda h: h.dma_start(out=oh, in_=C.oh), writes=[Rc])
    P.dma('sp', chc, lambda h: h.dma_start(out=relb, in_=C.relb), writes=[Rc])
    gs_r = Ring(P, "gs", 2, [8, 256], F32, chan=True)
    hk_r = Ring(P, "hk", 4, [128, 128], F32, chan=True)
    tmp_r = Ring(P, "ebtmp", 3, [128, 128], F32)
    RGV = [P.res(f"gv{i}") for i in range(6)]
    for bt in range(6):
        b = P.next_bank()
        P.op('pe', lambda h, b=b, bt=bt: h.matmul(P.banks[b][0:8, 0:256], relb, oh[:, bt, :], start=True, stop=True),
             reads=[Rc], writes=[P.Rbank[b]])
        gs, Rgs, gch = gs_r.next()
        P.op('dve', lambda h, gs=gs, b=b: h.tensor_copy(out=gs, in_=P.banks[b][0:8, 0:256]), reads=[P.Rbank[b]], writes=[Rgs])
        P.dma('sp', gch, lambda h, gs=gs, bt=bt: h.dma_start(out=C.GV[bt], in_=gs), reads=[Rgs], writes=[RGV[bt]])
    for bt in range(6):
        bi, ty = bt // 2, bt % 2
        for hd in range(8):
            hk, Rhk, hch = hk_r.next()
            src = bass.AP(C.GV.tensor, (bt * 8 + hd) * 256, [[1, 128], [1, 128]])
            P.dma('sp', hch, lambda h, hk=hk, src=src: h.dma_start(out=hk, in_=src), reads=[RGV[bt]], writes=[Rhk])
            b = P.next_bank()
            P.op('pe', lambda h, b=b, hk=hk: h.matmul(P.banks[b][:, 0:128], hk, jmat, start=True, stop=True),
                 reads=[Rhk, Rc], writes=[P.Rbank[b]])
            tmp, Rtmp = tmp_r.next()
            act(P, tmp, P.banks[b][:, 0:128], AF.Exp, [P.Rbank[b]], [Rtmp])
            msk = U_ if ty == 0 else LO_
            P.op('dve', lambda h, tmp=tmp, msk=msk, hd=hd, bi=bi, ty=ty: h.tensor_tensor(
                out=EB[:, hd // 2, bi, hd % 2, ty * 128:(ty + 1) * 128], in0=tmp, in1=msk, op=ALU.mult),
                reads=[Rtmp, Rc], writes=[REB])
    P.barrier()
    P.reset(m1)

    PADK = 1024
    Qbd = P.alloc([128, 2, L], BF16)
    KTp = P.alloc([128, L + 2 * PADK], BF16)
    OT = P.alloc([128, 2, L], F32)
    NTmax = L // 128 + 16
    Vbs = [P.alloc([128, NTmax, 130], BF16) for _ in range(2)]
    RVs = [P.res("Vb0"), P.res("Vb1")]
    RQ, RK, ROT = P.res("Qbd"), P.res("KTp"), P.res("OT")
    chq, chk = P.chan("q"), P.chan("k")
    chv2 = [[P.chan(f"v{i}_{k}") for k in range(4)] for i in range(2)]
    E_r = Ring(P, "E", 3, [128, 512], F32)
    P_r = Ring(P, "Pt", 4, [128, 2, 256], BF16)
    sg_r = Ring(P, "osg", 4, [65, 128], F32)
    rc_r = Ring(P, "rc", 2, [64, 512], F32)
    so_r = Ring(P, "so", 2, [64, 512], F32, chan=True)
    dummy = P.alloc([128, 16], F32)
    P.op('pool', lambda h: h.memset(Qbd, 0.0), writes=[RQ])
    P.op('pool', lambda h: h.memset(KTp, 0.0), writes=[RK])
    S_banks = [0, 1, 2, 3]
    si = [0]
    O_bank = {(0, 0): 4, (0, 1): 5, (1, 0): 6, (1, 1): 7}

    def load_V(s, hp, bi, slot):
        dl = BRANCH_DIL[bi]
        Ld = L // dl
        NJ = Ld // 128 + 1
        Vb, RV = Vbs[slot], RVs[slot]
        Vs = C.V[s]
        c0, c1 = hp * 130, (hp + 1) * 130
        for rho in range(dl):
            tb = rho * NJ
            P.op('pool', lambda h, tb=tb, Vb=Vb: h.memset(Vb[0:64, tb, :], 0.0), writes=[RV])
            P.op('pool', lambda h, tb=tb, NJ=NJ, Vb=Vb: h.memset(Vb[64:128, tb + NJ - 1, :], 0.0), writes=[RV])
            ch_ = chv2[slot][rho % 4]
            if NJ > 2:
                src = Vs[rho + dl * 64:rho + dl * 64 + dl * 128 * (NJ - 2):dl, c0:c1]
                P.dma('sp', ch_, lambda h, tb=tb, NJ=NJ, src=src, Vb=Vb: h.dma_start(
                    out=Vb[:, tb + 1:tb + NJ - 1, :], in_=src.rearrange("(j a) f -> a j f", a=128)), writes=[RV])
            r_first = Vs[rho:rho + dl * 63 + 1:dl, c0:c1]
            t_l = rho + dl * (Ld - 64)
            r_last = Vs[t_l:t_l + dl * 63 + 1:dl, c0:c1]
            P.dma('sp', ch_, lambda h, tb=tb, r_first=r_first, Vb=Vb: h.dma_start(out=Vb[64:128, tb, :], in_=r_first), writes=[RV])
            P.dma('sp', ch_, lambda h, tb=tb, NJ=NJ, r_last=r_last, Vb=Vb: h.dma_start(
                out=Vb[0:64, tb + NJ - 1, :], in_=r_last), writes=[RV])

    work = [(s, hp, bi) for s in range(NS) for hp in range(4) for bi in range(3)]
    load_V(*work[0], 0)
    def do_work(wi, s, hp, bi):
        slot = wi % 2
        Vb, RV = Vbs[slot], RVs[slot]
        dl = BRANCH_DIL[bi]
        Ld = L // dl
        NQ = Ld // 128
        NJ = NQ + 1
        if bi == 0:
            r0 = hp * 128
            P.dma('sp', chq, lambda h, s=s, r0=r0: h.dma_start(out=Qbd[0:64, 0, :], in_=C.QT[s][r0:r0 + 64, :]), writes=[RQ])
            P.dma('sp', chq, lambda h, s=s, r0=r0: h.dma_start(out=Qbd[64:128, 1, :], in_=C.QT[s][r0 + 64:r0 + 128, :]), writes=[RQ])
            P.dma('sp', chk, lambda h, s=s, r0=r0: h.dma_start(out=KTp[:, PADK:PADK + L], in_=C.KT[s][r0:r0 + 128, :]), writes=[RK])
            P.op('pool', lambda h: h.memset(OT, 0.0), writes=[ROT])
        else:
            P.op('pool', lambda h: h.memset(dummy, 0.0), writes=[ROT])
        if wi + 1 < len(work):
            load_V(*work[wi + 1], 1 - slot)
        tiles = [(rho, j) for rho in range(dl) for j in range(NJ)]
        st = {}

        def front(t):
            rho, j = tiles[t]
            halves = [hf for hf in (0, 1) if 0 <= j - 1 + hf < NQ]
            h0, h1 = halves[0], halves[-1] + 1
            nq = (h1 - h0) * 128
            qs = rho + dl * (128 * (j - 1) + h0 * 128)
            ks = PADK + rho + dl * (128 * j - 64)
            bS = S_banks[si[0] % 4]
            si[0] += 1
            P.op('pe', lambda h: h.matmul(
                P.banks[bS][:, :].rearrange("p (h q) -> p h q", h=2)[:, :, h0 * 128:h1 * 128],
                KTp[:, ks:ks + dl * 127 + 1:dl],
                Qbd[:, :, qs:qs + dl * (nq - 1) + 1:dl], start=True, stop=True),
                reads=[RK, RQ], writes=[P.Rbank[bS]])
            Et, REt = E_r.next()
            Pt, RPt = P_r.next()
            Ev = Et.rearrange("p (h q) -> p h q", h=2)[:, :, h0 * 128:h1 * 128]
            act(P, Ev, P.banks[bS][:, :].rearrange("p (h q) -> p h q", h=2)[:, :, h0 * 128:h1 * 128],
                AF.Exp, [P.Rbank[bS]], [REt])
            P.op('dve', lambda h: h.tensor_tensor(
                out=Pt[:, :, h0 * 128:h1 * 128], in0=Ev, in1=EB[:, hp, bi, :, h0 * 128:h1 * 128], op=ALU.mult),
                reads=[REt, REB], writes=[RPt])
            st[t] = (Pt, RPt, halves)

        def back(t):
            rho, j = tiles[t]
            Pt, RPt, halves = st.pop(t)
            tile = rho * NJ + j
            for hd in range(2):
                for hf in halves:
                    m = j - 1 + hf
                    bO = O_bank[(hd, m % 2)]
                    P.op('pe', lambda h, bO=bO, hd=hd, hf=hf: h.matmul(
                        P.banks[bO][0:65, 0:128], Vb[:, tile, hd * 65:(hd + 1) * 65],
                        Pt[:, hd, hf * 128:(hf + 1) * 128], start=(hf == 1), stop=(hf == 0)),
                        reads=[RV, RPt], writes=[P.Rbank[bO]])
                    if hf == 0:
                        t0 = rho + dl * 128 * m
                        ov = OT[0:65, hd, t0:t0 + dl * 127 + 1:dl]
                        sg, Rsg = sg_r.next()
                        act(P, sg, P.banks[bO][0:65, 0:128], AF.Identity, [P.Rbank[bO]], [Rsg])
                        P.op('pool', lambda h, ov=ov, sg=sg: h.tensor_tensor(out=ov, in0=ov, in1=sg, op=ALU.add),
                             reads=[ROT, Rsg], writes=[])
        n = len(tiles)
        for t in range(n + 2):
            if t < n:
                front(t)
            if t >= 2:
                back(t - 2)
        if bi == 2:
            P.op('pool', lambda h: h.memset(dummy, 0.0), writes=[ROT])
            for hd in range(2):
                for ct in range(L // 512):
                    b = S_banks[si[0] % 4]
                    si[0] += 1
                    P.op('pe', lambda h, b=b, hd=hd, ct=ct: h.matmul(
                        P.banks[b][0:64, :], sel, OT[0:65, hd, ct * 512:(ct + 1) * 512], start=True, stop=True),
                        reads=[Rc, ROT], writes=[P.Rbank[b]])
                    rc, Rrc = rc_r.next()
                    P.op('dve', lambda h, rc=rc, b=b: h.reciprocal(out=rc, in_=P.banks[b][0:64, :]),
                         reads=[P.Rbank[b]], writes=[Rrc])
                    so, Rso, soch = so_r.next()
                    P.op('dve', lambda h, so=so, rc=rc, hd=hd, ct=ct: h.tensor_tensor(
                        out=so, in0=OT[0:64, hd, ct * 512:(ct + 1) * 512], in1=rc, op=ALU.mult),
                        reads=[ROT, Rrc], writes=[Rso])
                    rr = hp * 128 + hd * 64
                    P.dma('sp', soch, lambda h, so=so, s=s, rr=rr, ct=ct: h.dma_start(
                        out=C.AT[s][rr:rr + 64, ct * 512:(ct + 1) * 512], in_=so), reads=[Rso])

    for wi, (s, hp, bi) in enumerate(work):
        do_work(wi, s, hp, bi)
    P.reset(m0)


def _ln_feature_major(P, C, tt, Rtt, nd, T, S1, S2, cst_ones, Rc, mk_out):
    mean = C.ln_mean
    m2 = C.ln_m2
    rstd = C.ln_rstd
    Rst = C.ln_Rst
    act(P, mean[:, 0:T], P.banks[S1][:, 0:T], AF.Identity, [P.Rbank[S1]], [Rst], scale=1.0 / D)
    P.op('dve', lambda h: h.tensor_tensor(out=m2[:, 0:T], in0=mean[:, 0:T], in1=mean[:, 0:T], op=ALU.mult),
         reads=[Rst], writes=[Rst])
    P.op('dve', lambda h: h.scalar_tensor_tensor(out=m2[:, 0:T], in0=P.banks[S2][:, 0:T], scalar=1.0 / D,
                                                 in1=m2[:, 0:T], op0=ALU.mult, op1=ALU.subtract),
         reads=[P.Rbank[S2], Rst], writes=[Rst])
    act(P, m2[:, 0:T], m2[:, 0:T], AF.Ln, [Rst], [Rst], bias=EPS)
    act(P, rstd[:, 0:T], m2[:, 0:T], AF.Exp, [Rst], [Rst], scale=-0.5)
    for dc in range(nd):
        u1, Ru1 = C.ln_u.next()
        P.op('dve', lambda h, u1=u1, dc=dc: h.tensor_tensor(out=u1[:, 0:T], in0=tt[:, dc, 0:T], in1=mean[:, 0:T], op=ALU.subtract),
             reads=[Rtt, Rst], writes=[Ru1])
        P.op('dve', lambda h, u1=u1: h.tensor_tensor(out=u1[:, 0:T], in0=u1[:, 0:T], in1=rstd[:, 0:T], op=ALU.mult),
             reads=[Ru1, Rst], writes=[Ru1])
        mk_out(dc, u1, Ru1)


def phase_C1(C):
    P, L, NS = C.P, C.L, C.NS
    T = 512
    m0 = P.mark()
    P.bank_list = [0, 1, 2, 3, 4]
    SA, S1, S2 = 5, 6, 7
    wout = P.alloc([128, 8, D], BF16)
    cst = P.alloc([128, 6, 128], F32)
    ang = P.alloc([128, 4], F32)
    lnfm = P.alloc([128, 4, 8], F32)
    Rc = P.res("c1_const")
    chc = P.chan("c1_const")
    P.dma('pool', chc, lambda h: h.dma_start(out=wout, in_=C.w_out.rearrange("(kc p) c -> p kc c", p=128)), writes=[Rc])
    P.dma('sp', chc, lambda h: h.dma_start(out=cst, in_=C.cst), writes=[Rc])
    P.dma('sp', chc, lambda h: h.dma_start(out=ang, in_=C.ang), writes=[Rc])
    P.dma('sp', chc, lambda h: h.dma_start(out=lnfm, in_=C.lnfm), writes=[Rc])
    ONES = cst[:, 4, :]
    xt_r = Ring(P, "c1x", 2, [128, 8, T], F32, chan=True)
    at_r = Ring(P, "c1a", 2, [128, 4, T], F32, chan=True)
    yn_r = Ring(P, "c1y", 2, [128, 4, T], BF16, chan=True)
    an = P.alloc([128, 4, T], BF16)
    Ran = P.res("an")
    sq_r = Ring(P, "c1sq", 3, [128, T], F32)
    rsa = P.alloc([128, T], F32)
    Rrsa = P.res("rsa")
    tt = P.alloc([128, 8, T], F32)
    Rtt = P.res("tt")
    C.ln_mean = P.alloc([128, T], F32)
    C.ln_m2 = P.alloc([128, T], F32)
    C.ln_rstd = P.alloc([128, T], F32)
    C.ln_Rst = P.res("lnst")
    C.ln_u = Ring(P, "lnu", 3, [128, T], F32)
    hf_r = Ring(P, "c1hf", 2, [128, 8, T], F32, chan=True)
    hb_r = Ring(P, "c1hb", 2, [128, 8, T], BF16, chan=True)

    def load(s, t0):
        xb, xr, xch = xt_r.next()
        P.dma('sp', xch, lambda h: h.dma_start(out=xb, in_=C.xT[s].rearrange("(c p) t -> p c t", p=128)[:, :, t0:t0 + T]), writes=[xr])
        ab, ar, ach = at_r.next()
        P.dma('sp', ach, lambda h: h.dma_start(out=ab, in_=C.AT[s].rearrange("(c p) t -> p c t", p=128)[:, :, t0:t0 + T]), writes=[ar])
        yb, yr, ych = yn_r.next()
        P.dma('sp', ych, lambda h: h.dma_start(out=yb, in_=C.YN[s].rearrange("(c p) t -> p c t", p=128)[:, :, t0:t0 + T]), writes=[yr])
        return (xb, xr), (ab, ar), (yb, yr)

    tiles = [(s, t0) for s in range(NS) for t0 in range(0, L, T)]
    nxt = load(*tiles[0])
    for ti, (s, t0) in enumerate(tiles):
        (xb, xr), (ab, ar), (yb, yr) = nxt
        if ti + 1 < len(tiles):
            nxt = load(*tiles[ti + 1])
        for fc in range(4):
            sq, Rsq = sq_r.next()
            act(P, sq, ab[:, fc, :], AF.Square, [ar], [Rsq])
            P.op('pe', lambda h, sq=sq, fc=fc: h.matmul(P.banks[SA][:, :], ONES, sq, start=(fc == 0), stop=(fc == 3)),
                 reads=[Rc, Rsq], writes=[P.Rbank[SA]])
        act(P, rsa, P.banks[SA][:, :], AF.Ln, [P.Rbank[SA]], [Rrsa], scale=1.0 / 512, bias=EPS)
        act(P, rsa, rsa, AF.Exp, [Rrsa], [Rrsa], scale=-0.5)
        for fc in range(4):
            P.op('dve', lambda h, fc=fc, ab=ab: h.scalar_tensor_tensor(
                out=an[:, fc, :], in0=ab[:, fc, :], scalar=ang[:, fc:fc + 1], in1=rsa, op0=ALU.mult, op1=ALU.mult),
                reads=[ar, Rc, Rrsa], writes=[Ran])
        for dc in range(8):
            b = P.next_bank()
            mm_group(P, P.banks[b][:, :],
                     [(wout[:, kc, dc * 128:(dc + 1) * 128], an[:, kc, :] if kc < 4 else yb[:, kc - 4, :]) for kc in range(8)],
                     P.Rbank[b], [Rc, Ran, yr])
            P.op('dve', lambda h, dc=dc, b=b, xb=xb: h.scalar_tensor_tensor(
                out=tt[:, dc, :], in0=xb[:, dc, :], scalar=ALPHA, in1=P.banks[b][:, :], op0=ALU.mult, op1=ALU.add),
                reads=[xr, P.Rbank[b]], writes=[Rtt])
            sq, Rsq = sq_r.next()
            act(P, sq, tt[:, dc, :], AF.Square, [Rtt], [Rsq])
            P.op('pe', lambda h, dc=dc: h.matmul(P.banks[S1][:, :], ONES, tt[:, dc, :], start=(dc == 0), stop=(dc == 7)),
                 reads=[Rc, Rtt], writes=[P.Rbank[S1]])
            P.op('pe', lambda h, dc=dc, sq=sq: h.matmul(P.banks[S2][:, :], ONES, sq, start=(dc == 0), stop=(dc == 7)),
                 reads=[Rc, Rsq], writes=[P.Rbank[S2]])
        hf, Rhf, hfch = hf_r.next()
        hb, Rhb, hbch = hb_r.next()

        def mk_out(dc, u1, Ru1, hf=hf, Rhf=Rhf, hb=hb, Rhb=Rhb):
            act(P, hf[:, dc, :], u1, AF.Identity, [Ru1, Rc], [Rhf], scale=lnfm[:, 0, dc:dc + 1], bias=lnfm[:, 1, dc:dc + 1])
            act(P, hb[:, dc, :], u1, AF.Identity, [Ru1, Rc], [Rhb], scale=lnfm[:, 0, dc:dc + 1], bias=lnfm[:, 1, dc:dc + 1])
        _ln_feature_major(P, C, tt, Rtt, 8, T, S1, S2, ONES, Rc, mk_out)
        P.dma('sp', hfch, lambda h, hf=hf, s=s, t0=t0: h.dma_start(
            out=C.H1F[s].rearrange("(c p) t -> p c t", p=128)[:, :, t0:t0 + T], in_=hf), reads=[Rhf])
        P.dma('sp', hbch, lambda h, hb=hb, s=s, t0=t0: h.dma_start(
            out=C.H1B[s].rearrange("(c p) t -> p c t", p=128)[:, :, t0:t0 + T], in_=hb), reads=[Rhb])
    P.bank_list = list(range(8))
    P.reset(m0)


def phase_C2(C):
    P, L, NS = C.P, C.L, C.NS
    T = 256
    NF = DFF // 128
    m0 = P.mark()
    P.bank_list = [0, 1, 2, 3, 4, 5]
    S1, S2 = 6, 7
    wg = P.alloc([128, 8, DFF], BF16)
    wu = P.alloc([128, 8, DFF], BF16)
    wd = P.alloc([128, NF, D], BF16)
    cst = P.alloc([128, 6, 128], F32)
    lnfm = P.alloc([128, 4, 8], F32)
    Rc = P.res("c2_const")
    chc = P.chan("c2_const")
    chw = [P.chan(f"c2w{i}") for i in range(3)]
    for c0 in (0, 1408):
        P.dma('pool', chw[0], lambda h, c0=c0: h.dma_start(
            out=wg[:, :, c0:c0 + 1408], in_=C.w_gate.rearrange("(kc p) c -> p kc c", p=128)[:, :, c0:c0 + 1408]), writes=[Rc])
        P.dma('pool', chw[1], lambda h, c0=c0: h.dma_start(
            out=wu[:, :, c0:c0 + 1408], in_=C.w_up.rearrange("(kc p) c -> p kc c", p=128)[:, :, c0:c0 + 1408]), writes=[Rc])
    P.dma('pool', chw[2], lambda h: h.dma_start(out=wd, in_=C.w_down.rearrange("(kc p) c -> p kc c", p=128)), writes=[Rc])
    P.dma('sp', chc, lambda h: h.dma_start(out=cst, in_=C.cst), writes=[Rc])
    P.dma('sp', chc, lambda h: h.dma_start(out=lnfm, in_=C.lnfm), writes=[Rc])
    ONES = cst[:, 4, :]
    hb_r = Ring(P, "c2hb", 2, [128, 8, T], BF16, chan=True)
    hf_r = Ring(P, "c2hf", 4, [128, T], F32, chan=True)
    hid = P.alloc([128, NF, T], BF16)
    Rhid = P.res("hid")
    sg_r = Ring(P, "c2sg", 3, [128, T], F32)
    sq_r = Ring(P, "c2sq", 3, [128, T], F32)
    tt = P.alloc([128, 8, T], F32)
    Rtt = P.res("tt2")
    C.ln_mean = P.alloc([128, T], F32)
    C.ln_m2 = P.alloc([128, T], F32)
    C.ln_rstd = P.alloc([128, T], F32)
    C.ln_Rst = P.res("lnst2")
    C.ln_u = Ring(P, "lnu2", 3, [128, T], F32)
    yo_r = Ring(P, "c2yo", 4, [128, T], F32, chan=True)
    outs = []

    def load(s, t0):
        hb, hr, hch = hb_r.next()
        P.dma('sp', hch, lambda h: h.dma_start(out=hb, in_=C.H1B[s].rearrange("(c p) t -> p c t", p=128)[:, :, t0:t0 + T]), writes=[hr])
        return hb, hr

    tiles = [(s, t0) for s in range(NS) for t0 in range(0, L, T)]
    nxt = load(*tiles[0])
    for ti, (s, t0) in enumerate(tiles):
        hb, hr = nxt
        if ti + 1 < len(tiles):
            nxt = load(*tiles[ti + 1])
        for fc in range(NF):
            bg = P.next_bank()
            mm_group(P, P.banks[bg][:, 0:T], [(wg[:, kc, fc * 128:(fc + 1) * 128], hb[:, kc, :]) for kc in range(8)],
                     P.Rbank[bg], [Rc, hr])
            bu = P.next_bank()
            mm_group(P, P.banks[bu][:, 0:T], [(wu[:, kc, fc * 128:(fc + 1) * 128], hb[:, kc, :]) for kc in range(8)],
                     P.Rbank[bu], [Rc, hr])
            sg, Rsg = sg_r.next()
            act(P, sg, P.banks[bg][:, 0:T], AF.Silu, [P.Rbank[bg]], [Rsg])
            P.op('dve', lambda h, fc=fc, sg=sg, bu=bu: h.tensor_tensor(out=hid[:, fc, :], in0=sg, in1=P.banks[bu][:, 0:T], op=ALU.mult),
                 reads=[Rsg, P.Rbank[bu]], writes=[Rhid])
        for dc in range(8):
            hf, Rhf, hfch = hf_r.next()
            P.dma('sp', hfch, lambda h, hf=hf, s=s, t0=t0, dc=dc: h.dma_start(
                out=hf, in_=C.H1F[s][dc * 128:(dc + 1) * 128, t0:t0 + T]), writes=[Rhf])
            b = P.next_bank()
            mm_group(P, P.banks[b][:, 0:T], [(wd[:, fc, dc * 128:(dc + 1) * 128], hid[:, fc, :]) for fc in range(NF)],
                     P.Rbank[b], [Rc, Rhid])
            P.op('dve', lambda h, dc=dc, b=b, hf=hf: h.scalar_tensor_tensor(
                out=tt[:, dc, :], in0=hf, scalar=ALPHA, in1=P.banks[b][:, 0:T], op0=ALU.mult, op1=ALU.add),
                reads=[Rhf, P.Rbank[b]], writes=[Rtt])
            sq, Rsq = sq_r.next()
            act(P, sq, tt[:, dc, :], AF.Square, [Rtt], [Rsq])
            P.op('pe', lambda h, dc=dc: h.matmul(P.banks[S1][:, 0:T], ONES, tt[:, dc, :], start=(dc == 0), stop=(dc == 7)),
                 reads=[Rc, Rtt], writes=[P.Rbank[S1]])
            P.op('pe', lambda h, dc=dc, sq=sq: h.matmul(P.banks[S2][:, 0:T], ONES, sq, start=(dc == 0), stop=(dc == 7)),
                 reads=[Rc, Rsq], writes=[P.Rbank[S2]])

        def mk_out(dc, u1, Ru1, s=s, t0=t0):
            yo, Ryo, yoch = yo_r.next()
            act(P, yo, u1[:, 0:T], AF.Identity, [Ru1, Rc], [Ryo], scale=lnfm[:, 2, dc:dc + 1], bias=lnfm[:, 3, dc:dc + 1])
            outs.append(P.dma('sp', yoch, lambda h, yo=yo, dc=dc: h.dma_start(
                out=C.yT[s][dc * 128:(dc + 1) * 128, t0:t0 + T], in_=yo), reads=[Ryo]))
        _ln_feature_major(P, C, tt, Rtt, 8, T, S1, S2, ONES, Rc, mk_out)
    P.bank_list = list(range(8))
    P.reset(m0)
    return outs

def make_cst():
    i = np.arange(128)
    U = (i[:, None] <= i[None, :]).astype(np.float32)
    SL = (i[:, None] > i[None, :]).astype(np.float32)
    Lo = (i[:, None] >= i[None, :]).astype(np.float32)
    SU = (i[:, None] < i[None, :]).astype(np.float32)
    ones = np.ones((128, 128), np.float32)
    ident = np.eye(128, dtype=np.float32)
    return np.ascontiguousarray(np.stack([U, SL, Lo, SU, ones, ident], axis=1))


def shared_inputs(inp):
    f = np.float32
    g = lambda k: np.asarray(inp[k], dtype=f)
    bc = lambda v, n=128: np.ascontiguousarray(np.broadcast_to(v[None, :], (n, v.shape[0])))
    m = {}
    m["w_in"] = np.ascontiguousarray(g("w_in")[0])
    m["convw"] = np.ascontiguousarray(g("conv_w")[0].reshape(5, 8, 128).transpose(2, 1, 0))
    m["convb"] = np.ascontiguousarray(g("conv_b")[0].reshape(8, 128).T)
    m["dtb"] = bc(np.concatenate([g("dt_bias_fwd")[0], g("dt_bias_bwd")[0]]))
    m["alog"] = bc(np.concatenate([g("a_log_fwd")[0], g("a_log_bwd")[0]]))
    m["dsk"] = bc(g("d_skip")[0])
    m["ang"] = np.ascontiguousarray(g("attn_norm_g")[0].reshape(4, 128).T)
    m["sng"] = bc(g("ssd_norm_g")[0])
    m["relb"] = np.ascontiguousarray(g("rel_bias"))
    m["lnfm"] = np.ascontiguousarray(np.stack([g(k)[0].reshape(8, 128).T for k in ("ln1_g", "ln1_b", "ln2_g", "ln2_b")], axis=1))
    m["w_out"] = np.ascontiguousarray(g("w_out")[0])
    m["w_gate"] = np.ascontiguousarray(g("w_gate")[0])
    m["w_up"] = np.ascontiguousarray(g("w_up")[0])
    m["w_down"] = np.ascontiguousarray(g("w_down")[0])
    m["cst"] = make_cst()
    m["oh"], m["jmat"], m["sel"] = make_att_consts()
    return m


def t5_bucket(rel):
    half = 16
    max_exact = 8
    ret = (rel > 0).astype(np.int32) * half
    n = np.abs(rel)
    large = max_exact + (np.log(np.maximum(n, 1) / max_exact)
                         / math.log(1024 / max_exact) * (half - max_exact)).astype(np.int32)
    large = np.minimum(large, half - 1)
    return ret + np.where(n < max_exact, n, large)


def make_att_consts():
    oh = np.zeros((32, 6, 256), np.float32)
    i = np.arange(255)
    for bi, dl in enumerate(BRANCH_DIL):
        for ty in range(2):
            rel = (i - 63) if ty == 0 else (i - 191)
            bk = t5_bucket(rel * dl)
            oh[bk, bi * 2 + ty, i] = 1.0
    eb = np.ascontiguousarray(np.eye(128, dtype=np.float32)[::-1])
    sel = np.zeros((65, 64), np.float32)
    sel[64, :] = 1.0
    return oh, eb, sel


SEQ_LEN = 8192
N_CORES = 8
SLOTS = 2
_CACHE = {}


def kernel(**inputs):
    xp = np.asarray(inputs["x_prompt"], dtype=np.float32)
    xs = np.asarray(inputs["x_sample"], dtype=np.float32)
    seqs = [xp[i] for i in range(xp.shape[0])] + [xs[i] for i in range(xs.shape[0])]
    nseq = len(seqs)
    L = seqs[0].shape[0]
    shared = shared_inputs(inputs)
    in_maps = []
    for c in range(N_CORES):
        xT = np.zeros((SLOTS, D, L), np.float32)
        for sl in range(SLOTS):
            i = c * SLOTS + sl
            if i < nseq:
                xT[sl] = seqs[i].T
        m = dict(shared)
        m["xT"] = xT
        in_maps.append(m)
    key = (L, SLOTS)
    if key not in _CACHE:
        _CACHE[key] = build(L, SLOTS)
    nc, _ = _CACHE[key]
    res = run_bass_kernel_spmd(nc, in_maps, core_ids=list(range(N_CORES)))
    outs = []
    for i in range(nseq):
        c, sl = divmod(i, SLOTS)
        outs.append(np.ascontiguousarray(np.asarray(res.results[c]["yT"][sl]).T))
    y_prompt = np.stack(outs[:xp.shape[0]]).astype(np.float32)
    y_sample = np.stack(outs[xp.shape[0]:]).astype(np.float32)
    return (y_prompt, y_sample)
```

```python
import contextlib
import math
import numpy as np
import concourse.bass as bass
import concourse.mybir as mybir
from concourse.bass_utils import run_bass_kernel_spmd

F32 = mybir.dt.float32
BF16 = mybir.dt.bfloat16
U8 = mybir.dt.uint8
AF = mybir.ActivationFunctionType
ALU = mybir.AluOpType

D = 1024
DIN = 3088
DFF = 2816
NH = 8
COL_Q, COL_K, COL_V, COL_Z, COL_X, COL_DT = 0, 512, 1024, 1536, 2048, 3072
ALPHA = 2.0 ** 0.25
EPS = 1e-5
ENGS = ['pe', 'act', 'dve', 'pool', 'sp']
EPOCH = 20000
POOL_BYTES = 206 * 1024


def _dsize(dt):
    return {F32: 4, BF16: 2, U8: 1}[dt]


class Res:
    __slots__ = ('name', 'w', 'r')

    def __init__(self, name):
        self.name = name
        self.w = {}
        self.r = {}


class Chan:
    def __init__(self, name):
        self.name = name
        self.n = 0
        self.sem = None


class Prog:
    def __init__(self, nc):
        self.nc = nc
        self.streams = {e: [] for e in ENGS}
        self.last_op = {e: None for e in ENGS}
        self.chans = []
        self.stack = contextlib.ExitStack()
        self.nres = 0
        self.pool = self.stack.enter_context(nc.sbuf_tensor("pool", [128, POOL_BYTES], U8))
        self.off = 0
        self.peak = 0
        self.banks = [self.stack.enter_context(nc.psum_tensor(f"bank{i}", [128, 512], F32))
                      for i in range(8)]
        self.Rbank = [self.res(f"bank{i}") for i in range(8)]

    def res(self, name=None):
        self.nres += 1
        return Res(name or f"r{self.nres}")

    def chan(self, name):
        c = Chan(name)
        self.chans.append(c)
        return c

    def alloc(self, shape, dtype):
        n = 1
        for s in shape[1:]:
            n *= s
        nbytes = n * _dsize(dtype)
        self.off = (self.off + 63) // 64 * 64
        assert self.off + nbytes <= POOL_BYTES, f"SBUF overflow {self.off + nbytes}"
        ap = self.pool[0:shape[0], self.off:self.off + nbytes].bitcast(dtype)
        self.off += nbytes
        self.peak = max(self.peak, self.off)
        if len(shape) > 2:
            names = [f"d{i}" for i in range(len(shape) - 1)]
            pat = "p (" + " ".join(names) + ") -> p " + " ".join(names)
            ap = ap.rearrange(pat, **{names[i]: shape[i + 1] for i in range(len(names))})
        return ap

    def mark(self):
        return self.off

    bank_list = list(range(8))

    def next_bank(self):
        self.bank_i = getattr(self, 'bank_i', -1) + 1
        return self.bank_list[self.bank_i % len(self.bank_list)]

    def reset(self, m):
        self.off = m

    def _deps(self, reads, writes):
        deps = set()
        for r in reads:
            deps.update(r.w.values())
        for w in writes:
            deps.update(w.w.values())
            deps.update(w.r.values())
        return deps

    def op(self, eng, fn, reads=(), writes=()):
        st = self.streams[eng]
        idx = len(st)
        deps = self._deps(reads, writes)
        st.append(dict(fn=fn, deps=deps, signal=False, chan=None))
        ev = (eng, idx)
        self.last_op[eng] = idx
        for r in reads:
            r.r[eng] = ev
        for w in writes:
            w.w = {eng: ev}
            w.r = {}
        return ev

    def dma(self, q, chan, fn, reads=(), writes=()):
        st = self.streams[q]
        deps = self._deps(reads, writes)
        if chan.n > 0:
            deps.add((chan, chan.n - 1))
        st.append(dict(fn=fn, deps=deps, signal=True, chan=chan))
        ev = (chan, chan.n)
        chan.n += 1
        for r in reads:
            r.r[chan] = ev
        for w in writes:
            w.w = {chan: ev}
            w.r = {}
        return ev

    def barrier(self):
        evs = set()
        for e in ENGS:
            if self.last_op[e] is not None:
                evs.add((e, self.last_op[e]))
        for c in self.chans:
            if c.n > 0:
                evs.add((c, c.n - 1))
        for e in ENGS:
            self.streams[e].append(dict(fn=None, deps=set(d for d in evs if d[0] != e),
                                        signal=False, chan=None))

    def wait_all(self, eng, evs):
        self.streams[eng].append(dict(fn=None, deps=set(evs), signal=False, chan=None))

    def emit(self):
        nc = self.nc
        for e in ENGS:
            for ins in self.streams[e]:
                for (prod, idx) in ins['deps']:
                    if isinstance(prod, str):
                        if prod == 'pe' and e == 'pe' and ins['chan'] is None and ins['fn'] is not None:
                            continue
                        self.streams[prod][idx]['signal'] = True
        nsig = {}
        for e in ENGS:
            c = 0
            for ins in self.streams[e]:
                if ins['chan'] is None and ins['signal'] and ins['fn'] is not None:
                    c += 1
                    ins['cnt'] = c
            nsig[e] = c
        sems = {}
        for e in ENGS:
            ne = max(1, -(-nsig[e] // EPOCH))
            sems[e] = [self.stack.enter_context(nc.semaphore(f"s_{e}{i}")) for i in range(ne)]
        for c in self.chans:
            c.sem = self.stack.enter_context(nc.semaphore(f"c_{c.name}"))
        self.stats = {e: [len(self.streams[e]), nsig[e], 0] for e in ENGS}

        def emit_stream(e, h):
            waited = {}
            nw = 0
            for ins in self.streams[e]:
                need = {}
                for (prod, idx) in ins['deps']:
                    if isinstance(prod, str):
                        if prod == 'pe' and e == 'pe' and ins['chan'] is None and ins['fn'] is not None:
                            continue
                        cnt = self.streams[prod][idx]['cnt']
                        ep = (cnt - 1) // EPOCH
                        sem = sems[prod][ep]
                        val = cnt - ep * EPOCH
                    else:
                        sem = prod.sem
                        val = 16 * (idx + 1)
                    k = id(sem)
                    if waited.get(k, 0) >= val:
                        continue
                    if k not in need or need[k][1] < val:
                        need[k] = (sem, val)
                for k, (sem, val) in need.items():
                    h.wait_ge(sem, val)
                    waited[k] = val
                    nw += 1
                if ins['fn'] is None:
                    continue
                r = ins['fn'](h)
                if ins['chan'] is not None:
                    r.then_inc(ins['chan'].sem, 16)
                elif ins['signal']:
                    cnt = ins['cnt']
                    ep = (cnt - 1) // EPOCH
                    r.then_inc(sems[e][ep], 1)
            self.stats[e][2] = nw

        with nc.Block() as block:
            @block.tensor
            def _(h):
                emit_stream('pe', h)

            @block.scalar
            def _(h):
                emit_stream('act', h)

            @block.vector
            def _(h):
                emit_stream('dve', h)

            @block.gpsimd
            def _(h):
                emit_stream('pool', h)

            @block.sync
            def _(h):
                emit_stream('sp', h)
        self.stack.close()


def mm_group(P, out_ap, pairs, Rout, reads):
    n = len(pairs)
    for i, (l, r) in enumerate(pairs):
        P.op('pe', lambda h, l=l, r=r, i=i: h.matmul(out_ap, l, r, start=(i == 0), stop=(i == n - 1)),
             reads=reads, writes=[Rout])


def act(P, out, in_, func, reads, writes, scale=None, bias=None):
    kw = {}
    if scale is not None:
        kw['scale'] = scale
    if bias is not None:
        kw['bias'] = bias
    return P.op('act', lambda h: h.activation(out=out, in_=in_, func=func, **kw), reads=reads, writes=writes)


class Ring:
    def __init__(self, P, name, n, shape, dtype, chan=False):
        self.bufs = [P.alloc(shape, dtype) for _ in range(n)]
        self.res = [P.res(f"{name}{i}") for i in range(n)]
        self.ch = [P.chan(f"{name}{i}") for i in range(n)] if chan else None
        self.i = -1
        self.n = n

    def next(self):
        self.i = (self.i + 1) % self.n
        if self.ch:
            return self.bufs[self.i], self.res[self.i], self.ch[self.i]
        return self.bufs[self.i], self.res[self.i]


class Ctx:
    pass


def build(L, NS, debug=False, phases=('A', 'S', 'T', 'C')):
    nc = bass.Bass("TRN2", target_bir_lowering=False)
    P = Prog(nc)
    C = Ctx()
    C.L, C.NS, C.P, C.nc = L, NS, P, nc
    NT = L // 512

    def din(name, shape, dt=F32):
        return nc.dram_tensor(name, list(shape), dt, kind="ExternalInput").ap()

    def dscr(name, shape, dt):
        return nc.dram_tensor(name, list(shape), dt,
                              kind="ExternalOutput" if debug else "Internal").ap()

    C.xT = din("xT", [NS, D, L])
    C.w_in = din("w_in", [D, DIN])
    C.convw = din("convw", [128, 8, 5])
    C.convb = din("convb", [128, 8])
    C.dtb = din("dtb", [128, 16])
    C.alog = din("alog", [128, 16])
    C.dsk = din("dsk", [128, 8])
    C.ang = din("ang", [128, 4])
    C.sng = din("sng", [128, 512])
    C.relb = din("relb", [32, 8])
    C.w_out = din("w_out", [D, D])
    C.w_gate = din("w_gate", [D, DFF])
    C.w_up = din("w_up", [D, DFF])
    C.w_down = din("w_down", [DFF, D])
    C.cst = din("cst", [128, 6, 128])
    C.oh = din("oh", [32, 6, 256])
    C.jmat = din("jmat", [128, 128])
    C.sel = din("sel", [65, 64])
    C.GV = dscr("GV", [6, 8, 256], F32)
    C.lnfm = din("lnfm", [128, 4, 8])
    C.yT = nc.dram_tensor("yT", [NS, D, L], F32, kind="ExternalOutput").ap()
    C.H1F = dscr("H1F", [NS, D, L], F32)
    C.H1B = dscr("H1B", [NS, D, L], BF16)

    C.QT = dscr("QT", [NS, 512, L], BF16)
    C.KT = dscr("KT", [NS, 512, L], BF16)
    C.V = dscr("V", [NS, L, 8 * 65], BF16)
    C.Z = dscr("Z", [NS, L, 512], F32)
    C.XS = dscr("XS", [NS, 512, L], F32)
    C.BC = dscr("BC", [NS, 512, L], BF16)
    C.DT = dscr("DT", [NS, L, 16], F32)

    outs = []
    C.YN = dscr("YN", [NS, 512, L], BF16)
    C.AT = dscr("AT", [NS, 512, L], F32)
    if 'A' in phases:
        phase_A(C)
        P.barrier()
    if 'S' in phases:
        phase_S(C)
        P.barrier()
    if 'T' in phases:
        phase_T(C)
        P.barrier()
    if 'C' in phases:
        phase_C1(C)
        P.barrier()
        outs = phase_C2(C)
        P.barrier()
    if debug:
        evs = [(c, c.n - 1) for c in P.chans if c.n > 0]
        P.wait_all('sp', evs)
    else:
        P.wait_all('sp', outs)
    P.emit()
    return nc, P


def phase_A(C):
    P, L, NS = C.P, C.L, C.NS
    NT = L // 512
    m0 = P.mark()
    win = P.alloc([128, 8, DIN], BF16)
    Rwin = P.res("win")
    cw = P.alloc([128, 8, 5], F32)
    cb = P.alloc([128, 8], F32)
    dtb = P.alloc([128, 16], F32)
    Rsm = P.res("small")
    ch_w = P.chan("w")
    w_v = C.w_in.rearrange("(kc p) c -> p kc c", p=128)
    for c0 in (0, 1544):
        P.dma('pool', ch_w, lambda h, c0=c0: h.dma_start(out=win[:, :, c0:c0 + 1544], in_=w_v[:, :, c0:c0 + 1544]),
              writes=[Rwin])
    ch_s = P.chan("small")
    P.dma('sp', ch_s, lambda h: h.dma_start(out=cw, in_=C.convw), writes=[Rsm])
    P.dma('sp', ch_s, lambda h: h.dma_start(out=cb, in_=C.convb), writes=[Rsm])
    P.dma('sp', ch_s, lambda h: h.dma_start(out=dtb, in_=C.dtb), writes=[Rsm])

    xtb = Ring(P, "xtb", 2, [128, 8, 512], BF16, chan=True)
    stq = Ring(P, "stq", 2, [128, 4, 512], BF16, chan=True)
    stk = Ring(P, "stk", 2, [128, 4, 512], BF16, chan=True)
    stx = Ring(P, "stx", 2, [128, 4, 512], F32, chan=True)
    stbc = Ring(P, "stbc", 2, [128, 4, 512], BF16, chan=True)
    stv = Ring(P, "stv", 2, [128, 4, 8, 65], BF16, chan=True)
    stz = Ring(P, "stz", 2, [128, 4, 512], F32, chan=True)
    stdt = Ring(P, "stdt", 2, [128, 4, 16], F32, chan=True)
    raw = [P.alloc([128, 520], F32) for _ in range(8)]
    Rraw = [P.res(f"raw{c}") for c in range(8)]
    acc = Ring(P, "acc", 4, [128, 512], F32)
    dtt = Ring(P, "dtt", 2, [128, 16], F32)
    for b, r in zip(stv.bufs, stv.res):
        P.op('pool', lambda h, b=b: h.memset(b, 1.0), writes=[r])
    fm_banks = [0, 1, 2]
    tm_banks = [3, 4, 5, 6]
    dt_bank = 7
    fmi = [0]
    tmi = [0]

    def next_fm():
        b = fm_banks[fmi[0] % len(fm_banks)]
        fmi[0] += 1
        return b

    def next_tm():
        b = tm_banks[tmi[0] % len(tm_banks)]
        tmi[0] += 1
        return b

    def load_x(s, T):
        buf, r, ch = xtb.next()
        src = C.xT[s].rearrange("(kc p) t -> p kc t", p=128)[:, :, T * 512:(T + 1) * 512]
        P.dma('pool', ch, lambda h: h.dma_start(out=buf, in_=src), writes=[r])
        return buf, r

    def conv_chunk_ops(c, width, accb, Racc):
        ops = []
        rb = raw[c]
        ops.append(lambda: P.op('dve', lambda h: h.tensor_scalar(
            out=accb[:, 0:width], in0=rb[:, 0:width], scalar1=cw[:, c, 0:1], scalar2=cb[:, c:c + 1],
            op0=ALU.mult, op1=ALU.add), reads=[Rraw[c], Rsm], writes=[Racc]))
        for j in range(1, 5):
            ops.append(lambda j=j: P.op('dve', lambda h: h.scalar_tensor_tensor(
                out=accb[:, 0:width], in0=rb[:, j:j + width], scalar=cw[:, c, j:j + 1], in1=accb[:, 0:width],
                op0=ALU.mult, op1=ALU.add), reads=[Rraw[c], Rsm, Racc], writes=[Racc]))
        return ops

    def conv_finish(s, c, width, accb, Racc, xbuf, xr, bcbuf, bcr):
        if c < 4:
            act(P, xbuf[:, c, 0:width], accb[:, 0:width], AF.Silu, [Racc], [xr])
        else:
            act(P, bcbuf[:, c - 4, 0:width], accb[:, 0:width], AF.Silu, [Racc], [bcr])

    for s in range(NS):
        for c in range(8):
            P.op('pool', lambda h, c=c: h.memset(raw[c][:, 0:4], 0.0), writes=[Rraw[c]])
        nxt = load_x(s, 0)
        for T in range(NT):
            xb, xr_ = nxt
            if T + 1 < NT:
                nxt = load_x(s, T + 1)
            t0 = T * 512
            qb, qr, qch = stq.next()
            kb, kr, kch = stk.next()
            for c in range(4):
                b = next_fm()
                mm_group(P, P.banks[b][:, :], [(win[:, kc, COL_Q + c * 128:COL_Q + (c + 1) * 128], xb[:, kc, :])
                                               for kc in range(8)], P.Rbank[b], [Rwin, xr_])
                act(P, qb[:, c, :], P.banks[b][:, :], AF.Identity, [P.Rbank[b]], [qr], scale=0.125)
            P.dma('sp', qch, lambda h, qb=qb, s=s, t0=t0: h.dma_start(
                out=C.QT[s].rearrange("(c p) t -> p c t", p=128)[:, :, t0:t0 + 512], in_=qb), reads=[qr])
            for c in range(4):
                b = next_fm()
                mm_group(P, P.banks[b][:, :], [(win[:, kc, COL_K + c * 128:COL_K + (c + 1) * 128], xb[:, kc, :])
                                               for kc in range(8)], P.Rbank[b], [Rwin, xr_])
                P.op('dve', lambda h, b=b, c=c, kb=kb: h.tensor_copy(out=kb[:, c, :], in_=P.banks[b][:, :]),
                     reads=[P.Rbank[b]], writes=[kr])
            P.dma('sp', kch, lambda h, kb=kb, s=s, t0=t0: h.dma_start(
                out=C.KT[s].rearrange("(c p) t -> p c t", p=128)[:, :, t0:t0 + 512], in_=kb), reads=[kr])
            sxb, sxr, sxch = stx.next()
            sbb, sbr, sbch = stbc.next()
            for cp in range(4):
                chains = []
                accs = []
                for c in (2 * cp, 2 * cp + 1):
                    b = next_fm()
                    mm_group(P, P.banks[b][:, :],
                             [(win[:, kc, COL_X + c * 128:COL_X + (c + 1) * 128], xb[:, kc, :]) for kc in range(8)],
                             P.Rbank[b], [Rwin, xr_])
                    act(P, raw[c][:, 4:516], P.banks[b][:, :], AF.Identity, [P.Rbank[b]], [Rraw[c]])
                    ab, ar = acc.next()
                    accs.append((c, ab, ar))
                    chains.append(conv_chunk_ops(c, 512, ab, ar))
                for j in range(5):
                    for ch_ in chains:
                        ch_[j]()
                for (c, ab, ar) in accs:
                    conv_finish(s, c, 512, ab, ar, sxb, sxr, sbb, sbr)
                    P.op('pool', lambda h, c=c: h.tensor_copy(out=raw[c][:, 0:4], in_=raw[c][:, 512:516]),
                         reads=[Rraw[c]], writes=[Rraw[c]])
            lo = 2 if T == 0 else 0
            P.dma('sp', sxch, lambda h, sxb=sxb, s=s, t0=t0, lo=lo: h.dma_start(
                out=C.XS[s].rearrange("(c p) t -> p c t", p=128)[:, :, t0 - 2 + lo:t0 + 510],
                in_=sxb[:, :, lo:512]), reads=[sxr])
            P.dma('sp', sbch, lambda h, sbb=sbb, s=s, t0=t0, lo=lo: h.dma_start(
                out=C.BC[s].rearrange("(c p) t -> p c t", p=128)[:, :, t0 - 2 + lo:t0 + 510],
                in_=sbb[:, :, lo:512]), reads=[sbr])
            vb, vr, vch = stv.next()
            zb, zr, zch = stz.next()
            db, dr, dch = stdt.next()
            for u in range(4):
                lw = [xb[:, kc, u * 128:(u + 1) * 128] for kc in range(8)]
                b = next_tm()
                mm_group(P, P.banks[b][:, :], [(lw[kc], win[:, kc, COL_V:COL_V + 512]) for kc in range(8)],
                         P.Rbank[b], [Rwin, xr_])
                act(P, vb[:, u, :, 0:64], P.banks[b][:, :].rearrange("p (h d) -> p h d", d=64), AF.Identity,
                    [P.Rbank[b]], [vr])
                b = next_tm()
                mm_group(P, P.banks[b][:, :], [(lw[kc], win[:, kc, COL_Z:COL_Z + 512]) for kc in range(8)],
                         P.Rbank[b], [Rwin, xr_])
                act(P, zb[:, u, :], P.banks[b][:, :], AF.Silu, [P.Rbank[b]], [zr])
                b = dt_bank
                mm_group(P, P.banks[b][:, 0:16], [(lw[kc], win[:, kc, COL_DT:COL_DT + 16]) for kc in range(8)],
                         P.Rbank[b], [Rwin, xr_])
                tb, tr = dtt.next()
                P.op('dve', lambda h, b=b, tb=tb: h.tensor_tensor(out=tb, in0=P.banks[b][:, 0:16], in1=dtb, op=ALU.add),
                     reads=[P.Rbank[b], Rsm], writes=[tr])
                act(P, tb, tb, AF.Exp, [tr], [tr])
                act(P, db[:, u, :], tb, AF.Ln, [tr], [dr], bias=1.0)
            P.dma('sp', vch, lambda h, vb=vb, s=s, t0=t0: h.dma_start(
                out=C.V[s][t0:t0 + 512, :].rearrange("(u p) f -> p u f", p=128),
                in_=vb.rearrange("p u h e -> p u (h e)")), reads=[vr])
            P.dma('sp', zch, lambda h, zb=zb, s=s, t0=t0: h.dma_start(
                out=C.Z[s][t0:t0 + 512, :].rearrange("(u p) f -> p u f", p=128), in_=zb), reads=[zr])
            P.dma('sp', dch, lambda h, db=db, s=s, t0=t0: h.dma_start(
                out=C.DT[s][t0:t0 + 512, :].rearrange("(u p) f -> p u f", p=128), in_=db), reads=[dr])
        sxb, sxr, sxch = stx.next()
        sbb, sbr, sbch = stbc.next()
        for c in range(8):
            P.op('pool', lambda h, c=c: h.memset(raw[c][:, 4:8], 0.0), reads=[Rraw[c]], writes=[Rraw[c]])
            ab, ar = acc.next()
            for o in conv_chunk_ops(c, 2, ab, ar):
                o()
            conv_finish(s, c, 2, ab, ar, sxb, sxr, sbb, sbr)
        P.dma('sp', sxch, lambda h, sxb=sxb, s=s: h.dma_start(
            out=C.XS[s].rearrange("(c p) t -> p c t", p=128)[:, :, L - 2:L], in_=sxb[:, :, 0:2]),
            reads=[sxr])
        P.dma('sp', sbch, lambda h, sbb=sbb, s=s: h.dma_start(
            out=C.BC[s].rearrange("(c p) t -> p c t", p=128)[:, :, L - 2:L], in_=sbb[:, :, 0:2]),
            reads=[sbr])
    P.reset(m0)


def phase_S(C):
    P, L, NS = C.P, C.L, C.NS
    NC = L // 128
    NG = NC // 4
    m0 = P.mark()
    cst = P.alloc([128, 6, 128], F32)
    idb = P.alloc([128, 128], BF16)
    dsk = P.alloc([128, 8], F32)
    Aneg = P.alloc([128, 16], F32)
    sng = P.alloc([128, 512], F32)
    Rc = P.res("s_const")
    chc = P.chan("s_const")
    P.dma('sp', chc, lambda h: h.dma_start(out=cst, in_=C.cst), writes=[Rc])
    P.dma('sp', chc, lambda h: h.dma_start(out=dsk, in_=C.dsk), writes=[Rc])
    P.dma('sp', chc, lambda h: h.dma_start(out=Aneg, in_=C.alog), writes=[Rc])
    P.dma('sp', chc, lambda h: h.dma_start(out=sng, in_=C.sng), writes=[Rc])
    act(P, Aneg, Aneg, AF.Exp, [Rc], [Rc])
    P.op('dve', lambda h: h.tensor_scalar(out=Aneg, in0=Aneg, scalar1=-1.0, scalar2=None, op0=ALU.mult),
         reads=[Rc], writes=[Rc])
    P.op('dve', lambda h: h.tensor_copy(out=idb, in_=cst[:, 5, :]), reads=[Rc], writes=[Rc])
    U_, SL_, LO_, SU_, ON_, ID_ = [cst[:, i, :] for i in range(6)]

    SbAll = P.alloc([128, NC, 512], BF16)
    RSb = [P.res(f"sb{c}") for c in range(NC)]
    gx = Ring(P, "gx", 2, [128, 4, 512], F32, chan=True)
    gbc = Ring(P, "gbc", 2, [128, 4, 512], BF16, chan=True)
    gdt = Ring(P, "gdt", 2, [128, 4, 16], F32, chan=True)
    gz = Ring(P, "gz", 2, [128, 4, 512], F32, chan=True)
    syn = Ring(P, "syn", 2, [128, 4, 512], BF16, chan=True)
    da_r = Ring(P, "da", 3, [128, 16], F32)
    ew_r = Ring(P, "ew", 3, [128, 64], F32)
    sc_r = Ring(P, "sc", 3, [128, 16], F32)
    xdt_r = Ring(P, "xdt", 6, [128, 512], BF16)
    xsd_r = Ring(P, "xsd", 2, [128, 512], F32)
    btok_r = Ring(P, "btok", 2, [128, 256], BF16)
    cbm_r = Ring(P, "cbm", 4, [128, 256], F32)
    L_r = Ring(P, "Lr", 2, [128, 8, 128], F32)
    dec_r = Ring(P, "dec", 2, [128, 8, 128], F32)
    M_r = Ring(P, "Mr", 4, [128, 8, 128], BF16)
    t_r = Ring(P, "tr", 4, [128, 512], F32)
    yn_r = Ring(P, "yn", 2, [128, 512], BF16)
    sm_r = Ring(P, "sm", 4, [128, 4], F32)
    Sf = P.alloc([128, 512], F32)
    Sfb = P.alloc([128, 512], BF16)
    Sb = P.alloc([128, 512], F32)
    RSf, RSfb, RSbr = P.res("Sf"), P.res("Sfb"), P.res("Sbr")

    def bc8(ap8):
        return ap8.unsqueeze(2).to_broadcast([128, 8, 64])

    def v3(ap):
        return ap.rearrange("p (h d) -> p h d", d=64)

    def load_group(s, g, with_z):
        t0 = g * 512
        xb, xr, xch = gx.next()
        P.dma('sp', xch, lambda h: h.dma_start(
            out=xb, in_=C.XS[s].rearrange("(c p) t -> p c t", p=128)[:, :, t0:t0 + 512]), writes=[xr])
        bb, br, bch = gbc.next()
        P.dma('sp', bch, lambda h: h.dma_start(
            out=bb, in_=C.BC[s].rearrange("(c p) t -> p c t", p=128)[:, :, t0:t0 + 512]), writes=[br])
        db, dr, dch = gdt.next()
        P.dma('sp', dch, lambda h: h.dma_start(
            out=db, in_=C.DT[s][t0:t0 + 512, :].rearrange("(u p) f -> p u f", p=128)), writes=[dr])
        zz = None
        if with_z:
            zb, zr, zch = gz.next()
            P.dma('sp', zch, lambda h: h.dma_start(
                out=zb, in_=C.Z[s][t0:t0 + 512, :].rearrange("(u p) f -> p u f", p=128)), writes=[zr])
            zz = (zb, zr)
        return (xb, xr), (bb, br), (db, dr), zz

    def small_mms(da, Rda, mats):
        b = P.next_bank()
        for i, m_ in enumerate(mats):
            P.op('pe', lambda h, i=i, m_=m_, b=b: h.matmul(P.banks[b][:, 16 * i:16 * i + 16], m_, da, start=True, stop=True),
                 reads=[Rc, Rda], writes=[P.Rbank[b]])
        ew, Rew = ew_r.next()
        n = 16 * len(mats)
        act(P, ew[:, 0:n], P.banks[b][:, 0:n], AF.Exp, [P.Rbank[b]], [Rew])
        return ew, Rew

    def xs_transpose(xb, xr, u):
        b = P.next_bank()
        for fc in range(4):
            P.op('pe', lambda h, fc=fc, b=b: h.transpose(P.banks[b][:, fc * 128:(fc + 1) * 128],
                                                        xb[:, fc, u * 128:(u + 1) * 128], ID_),
                 reads=[xr, Rc], writes=[P.Rbank[b]])
        return b

    def b_transpose(bb, br, u):
        b = P.next_bank()
        pb = P.banks[b][:, :].bitcast(BF16)
        for g in range(2):
            P.op('pe', lambda h, g=g, pb=pb: h.transpose(pb[:, g * 128:(g + 1) * 128],
                                                        bb[:, g, u * 128:(u + 1) * 128], idb),
                 reads=[br, Rc], writes=[P.Rbank[b]])
        bt, Rbt = btok_r.next()
        act(P, bt, pb[:, 0:256], AF.Identity, [P.Rbank[b]], [Rbt])
        return bt, Rbt

    def state_mm(bt, Rbt, xw, Rxw):
        b = P.next_bank()
        for g in range(2):
            P.op('pe', lambda h, g=g, b=b: h.matmul(P.banks[b][:, g * 256:(g + 1) * 256], bt[:, g * 128:(g + 1) * 128],
                                                   xw[:, g * 256:(g + 1) * 256], start=True, stop=True),
                 reads=[Rbt, Rxw], writes=[P.Rbank[b]])
        return b

    for s in range(NS):
        P.op('pool', lambda h: h.memset(Sb, 0.0), writes=[RSbr])
        P.op('pool', lambda h: h.memset(SbAll[:, NC - 1, :], 0.0), writes=[RSb[NC - 1]])
        nxt = load_group(s, NG - 1, False)
        for g in range(NG - 1, -1, -1):
            (xb, xr), (bb, br), (db, dr), _ = nxt
            if g > 0:
                nxt = load_group(s, g - 1, False)
            for u in range(3, -1, -1):
                c = g * 4 + u
                if c == 0:
                    break
                da, Rda = da_r.next()
                P.op('dve', lambda h, da=da, db=db, u=u: h.tensor_tensor(out=da, in0=db[:, u, :], in1=Aneg, op=ALU.mult),
                     reads=[dr, Rc], writes=[Rda])
                ew, Rew = small_mms(da, Rda, [SU_, ON_])
                sc, Rsc = sc_r.next()
                P.op('dve', lambda h, sc=sc, db=db, u=u, ew=ew: h.tensor_tensor(
                    out=sc[:, 0:8], in0=db[:, u, 8:16], in1=ew[:, 8:16], op=ALU.mult), reads=[dr, Rew], writes=[Rsc])
                bx = xs_transpose(xb, xr, u)
                xw, Rxw = xdt_r.next()
                P.op('dve', lambda h, xw=xw, bx=bx, sc=sc: h.tensor_tensor(
                    out=v3(xw), in0=v3(P.banks[bx][:, :]), in1=bc8(sc[:, 0:8]), op=ALU.mult),
                    reads=[P.Rbank[bx], Rsc], writes=[Rxw])
                bt, Rbt = b_transpose(bb, br, u)
                bs = state_mm(bt, Rbt, xw, Rxw)
                P.op('dve', lambda h, ew=ew: h.tensor_tensor(out=v3(Sb), in0=v3(Sb), in1=bc8(ew[:, 24:32]), op=ALU.mult),
                     reads=[RSbr, Rew], writes=[RSbr])
                P.op('dve', lambda h, bs=bs: h.tensor_tensor(out=Sb, in0=Sb, in1=P.banks[bs][:, :], op=ALU.add),
                     reads=[RSbr, P.Rbank[bs]], writes=[RSbr])
                act(P, SbAll[:, c - 1, :], Sb, AF.Identity, [RSbr], [RSb[c - 1]])
        P.op('pool', lambda h: h.memset(Sf, 0.0), writes=[RSf])
        P.op('pool', lambda h: h.memset(Sfb, 0.0), writes=[RSfb])
        nxt = load_group(s, 0, True)
        for g in range(NG):
            (xb, xr), (bb, br), (db, dr), (zb, zr) = nxt
            if g + 1 < NG:
                nxt = load_group(s, g + 1, True)
            yb, yr, ych = syn.next()
            for u in range(4):
                c = g * 4 + u
                da, Rda = da_r.next()
                P.op('dve', lambda h, da=da, db=db, u=u: h.tensor_tensor(out=da, in0=db[:, u, :], in1=Aneg, op=ALU.mult),
                     reads=[dr, Rc], writes=[Rda])
                ew, Rew = small_mms(da, Rda, [U_, SL_, LO_, ON_])
                sc, Rsc = sc_r.next()
                P.op('dve', lambda h, sc=sc, db=db, u=u, ew=ew: h.tensor_tensor(
                    out=sc[:, 0:8], in0=db[:, u, 0:8], in1=ew[:, 16:24], op=ALU.mult), reads=[dr, Rew], writes=[Rsc])
                bx = xs_transpose(xb, xr, u)
                xsP = v3(P.banks[bx][:, :])
                xf, Rxf = xdt_r.next()
                xbw, Rxbw = xdt_r.next()
                xw, Rxw = xdt_r.next()
                xsd, Rxsd = xsd_r.next()
                P.op('dve', lambda h, xf=xf, xsP=xsP, db=db, u=u: h.tensor_tensor(
                    out=v3(xf), in0=xsP, in1=bc8(db[:, u, 0:8]), op=ALU.mult), reads=[P.Rbank[bx], dr], writes=[Rxf])
                P.op('dve', lambda h, xbw=xbw, xsP=xsP, db=db, u=u: h.tensor_tensor(
                    out=v3(xbw), in0=xsP, in1=bc8(db[:, u, 8:16]), op=ALU.mult), reads=[P.Rbank[bx], dr], writes=[Rxbw])
                P.op('dve', lambda h, xw=xw, xsP=xsP, sc=sc: h.tensor_tensor(
                    out=v3(xw), in0=xsP, in1=bc8(sc[:, 0:8]), op=ALU.mult), reads=[P.Rbank[bx], Rsc], writes=[Rxw])
                P.op('dve', lambda h, xsd=xsd, xsP=xsP: h.tensor_tensor(
                    out=v3(xsd), in0=xsP, in1=bc8(dsk), op=ALU.mult), reads=[P.Rbank[bx], Rc], writes=[Rxsd])
                bt, Rbt = b_transpose(bb, br, u)
                bcb = P.next_bank()
                for gg in range(2):
                    P.op('pe', lambda h, gg=gg, bcb=bcb, bb=bb, u=u: h.matmul(
                        P.banks[bcb][:, gg * 128:(gg + 1) * 128], bb[:, gg, u * 128:(u + 1) * 128],
                        bb[:, 2 + gg, u * 128:(u + 1) * 128], start=True, stop=True), reads=[br], writes=[P.Rbank[bcb]])
                cbU, RcbU = cbm_r.next()
                cbL, RcbL = cbm_r.next()
                cbP = P.banks[bcb][:, 0:256].rearrange("p (g q) -> p g q", g=2)
                P.op('dve', lambda h, cbU=cbU, cbP=cbP: h.tensor_tensor(
                    out=cbU.rearrange("p (g q) -> p g q", g=2), in0=cbP,
                    in1=U_.unsqueeze(1).to_broadcast([128, 2, 128]), op=ALU.mult),
                    reads=[P.Rbank[bcb], Rc], writes=[RcbU])
                P.op('dve', lambda h, cbL=cbL, cbP=cbP: h.tensor_tensor(
                    out=cbL.rearrange("p (g q) -> p g q", g=2), in0=cbP,
                    in1=LO_.unsqueeze(1).to_broadcast([128, 2, 128]), op=ALU.mult),
                    reads=[P.Rbank[bcb], Rc], writes=[RcbL])
                Ms = []
                for di, (tri_l, tri_r, cbm, Rcbm, c0) in enumerate(((SL_, U_, cbU, RcbU, 0), (SU_, LO_, cbL, RcbL, 8))):
                    Lt, RLt = L_r.next()
                    P.op('pool', lambda h, Lt=Lt, tri_l=tri_l, da=da, c0=c0: h.tensor_tensor(
                        out=Lt, in0=tri_l.unsqueeze(1).to_broadcast([128, 8, 128]),
                        in1=da[:, c0:c0 + 8].unsqueeze(2).to_broadcast([128, 8, 128]), op=ALU.mult),
                        reads=[Rc, Rda], writes=[RLt])
                    dec, Rdec = dec_r.next()
                    for hh in range(2):
                        b = P.next_bank()
                        for h4 in range(4):
                            hd = hh * 4 + h4
                            P.op('pe', lambda h, b=b, h4=h4, hd=hd, Lt=Lt, tri_r=tri_r: h.matmul(
                                P.banks[b][:, h4 * 128:(h4 + 1) * 128], Lt[:, hd, :], tri_r, start=True, stop=True),
                                reads=[RLt, Rc], writes=[P.Rbank[b]])
                        act(P, dec[:, hh * 4:(hh + 1) * 4, :], P.banks[b][:, :].rearrange("p (a q) -> p a q", a=4),
                            AF.Exp, [P.Rbank[b]], [Rdec])
                    Mt, RMt = M_r.next()
                    P.op('dve', lambda h, Mt=Mt, dec=dec, cbm=cbm: h.tensor_tensor(
                        out=Mt.rearrange("p (g e) q -> p g e q", g=2), in0=dec.rearrange("p (g e) q -> p g e q", g=2),
                        in1=cbm.rearrange("p (g q) -> p g q", g=2).unsqueeze(2).to_broadcast([128, 2, 4, 128]),
                        op=ALU.mult), reads=[Rdec, Rcbm], writes=[RMt])
                    Ms.append((Mt, RMt))
                by = P.next_bank()
                for hd in range(8):
                    P.op('pe', lambda h, by=by, hd=hd, M0=Ms[0][0], xf=xf: h.matmul(
                        P.banks[by][:, hd * 64:(hd + 1) * 64], M0[:, hd, :], xf[:, hd * 64:(hd + 1) * 64],
                        start=True, stop=False), reads=[Ms[0][1], Rxf], writes=[P.Rbank[by]])
                    P.op('pe', lambda h, by=by, hd=hd, M1=Ms[1][0], xbw=xbw: h.matmul(
                        P.banks[by][:, hd * 64:(hd + 1) * 64], M1[:, hd, :], xbw[:, hd * 64:(hd + 1) * 64],
                        start=False, stop=True), reads=[Ms[1][1], Rxbw], writes=[P.Rbank[by]])
                bof = P.next_bank()
                bob = P.next_bank()
                for gg in range(2):
                    P.op('pe', lambda h, gg=gg, bof=bof, bb=bb, u=u: h.matmul(
                        P.banks[bof][:, gg * 256:(gg + 1) * 256], bb[:, 2 + gg, u * 128:(u + 1) * 128],
                        Sfb[:, gg * 256:(gg + 1) * 256], start=True, stop=True), reads=[br, RSfb], writes=[P.Rbank[bof]])
                for gg in range(2):
                    P.op('pe', lambda h, gg=gg, bob=bob, bb=bb, u=u, c=c: h.matmul(
                        P.banks[bob][:, gg * 256:(gg + 1) * 256], bb[:, 2 + gg, u * 128:(u + 1) * 128],
                        SbAll[:, c, gg * 256:(gg + 1) * 256], start=True, stop=True), reads=[br, RSb[c]], writes=[P.Rbank[bob]])
                t1, Rt1 = t_r.next()
                t2, Rt2 = t_r.next()
                P.op('dve', lambda h, t1=t1, bof=bof, ew=ew: h.tensor_tensor(
                    out=v3(t1), in0=v3(P.banks[bof][:, :]), in1=bc8(ew[:, 0:8]), op=ALU.mult),
                    reads=[P.Rbank[bof], Rew], writes=[Rt1])
                P.op('dve', lambda h, t2=t2, bob=bob, ew=ew: h.tensor_tensor(
                    out=v3(t2), in0=v3(P.banks[bob][:, :]), in1=bc8(ew[:, 40:48]), op=ALU.mult),
                    reads=[P.Rbank[bob], Rew], writes=[Rt2])
                P.op('pool', lambda h, t1=t1, t2=t2: h.tensor_tensor(out=t1, in0=t1, in1=t2, op=ALU.add),
                     reads=[Rt1, Rt2], writes=[Rt1])
                P.op('pool', lambda h, t1=t1, xsd=xsd: h.tensor_tensor(out=t1, in0=t1, in1=xsd, op=ALU.add),
                     reads=[Rt1, Rxsd], writes=[Rt1])
                P.op('dve', lambda h, t1=t1, by=by: h.tensor_tensor(out=t1, in0=t1, in1=P.banks[by][:, :], op=ALU.add),
                     reads=[Rt1, P.Rbank[by]], writes=[Rt1])
                P.op('dve', lambda h, t1=t1, zb=zb, u=u: h.tensor_tensor(out=t1, in0=t1, in1=zb[:, u, :], op=ALU.mult),
                     reads=[Rt1, zr], writes=[Rt1])
                sm, Rsm_ = sm_r.next()
                P.op('act', lambda h, t2=t2, t1=t1, sm=sm: h.activation(out=t2, in_=t1, func=AF.Square, accum_out=sm[:, 0:1]),
                     reads=[Rt1, Rt2], writes=[Rt2, Rsm_])
                act(P, sm[:, 1:2], sm[:, 0:1], AF.Ln, [Rsm_], [Rsm_], scale=1.0 / 512, bias=EPS)
                act(P, sm[:, 2:3], sm[:, 1:2], AF.Exp, [Rsm_], [Rsm_], scale=-0.5)
                yn, Ryn = yn_r.next()
                P.op('dve', lambda h, yn=yn, t1=t1, sm=sm: h.scalar_tensor_tensor(
                    out=yn, in0=t1, scalar=sm[:, 2:3], in1=sng, op0=ALU.mult, op1=ALU.mult),
                    reads=[Rt1, Rsm_, Rc], writes=[Ryn])
                bt_ = P.next_bank()
                pbt = P.banks[bt_][:, :].bitcast(BF16)
                for fc in range(4):
                    P.op('pe', lambda h, fc=fc, pbt=pbt, yn=yn: h.transpose(
                        pbt[:, fc * 128:(fc + 1) * 128], yn[:, fc * 128:(fc + 1) * 128], idb),
                        reads=[Ryn, Rc], writes=[P.Rbank[bt_]])
                act(P, yb[:, :, u * 128:(u + 1) * 128], pbt[:, 0:512].rearrange("p (c t) -> p c t", c=4), AF.Identity,
                    [P.Rbank[bt_]], [yr])
                bs = state_mm(bt, Rbt, xw, Rxw)
                P.op('dve', lambda h, ew=ew: h.tensor_tensor(out=v3(Sf), in0=v3(Sf), in1=bc8(ew[:, 48:56]), op=ALU.mult),
                     reads=[RSf, Rew], writes=[RSf])
                P.op('dve', lambda h, bs=bs: h.tensor_tensor(out=Sf, in0=Sf, in1=P.banks[bs][:, :], op=ALU.add),
                     reads=[RSf, P.Rbank[bs]], writes=[RSf])
                act(P, Sfb, Sf, AF.Identity, [RSf], [RSfb])
            P.dma('sp', ych, lambda h, yb=yb, s=s, g=g: h.dma_start(
                out=C.YN[s].rearrange("(c p) t -> p c t", p=128)[:, :, g * 512:(g + 1) * 512], in_=yb), reads=[yr])
    P.reset(m0)


BRANCH_DIL = (1, 4, 16)


def phase_T(C):
    P, L, NS = C.P, C.L, C.NS
    m0 = P.mark()
    cst = P.alloc([128, 6, 128], F32)
    jmat = P.alloc([128, 128], F32)
    sel = P.alloc([65, 64], F32)
    EB = P.alloc([128, 4, 3, 2, 256], F32)
    Rc = P.res("t_const")
    REB = P.res("EB")
    chc = P.chan("t_const")
    P.dma('sp', chc, lambda h: h.dma_start(out=cst, in_=C.cst), writes=[Rc])
    P.dma('sp', chc, lambda h: h.dma_start(out=jmat, in_=C.jmat), writes=[Rc])
    P.dma('sp', chc, lambda h: h.dma_start(out=sel, in_=C.sel), writes=[Rc])
    U_, LO_ = cst[:, 0, :], cst[:, 2, :]
    m1 = P.mark()
    oh = P.alloc([32, 6, 256], F32)
    relb = P.alloc([32, 8], F32)
    P.dma('sp', chc, lambda h: h.dma_start(out=oh, in_=C.oh), writes=[Rc])
    P.dma('sp', chc, lambda h: h.dma_start(out=relb, in_=C.relb), writes=[Rc])
    gs_r = Ring(P, "gs", 2, [8, 256], F32, chan=True)
    hk_r = Ring(P, "hk", 4, [128, 128], F32, chan=True)
    tmp_r = Ring(P, "ebtmp", 3, [128, 128], F32)
    RGV = [P.res(f"gv{i}") for i in range(6)]
    for bt in range(6):
        b = P.next_bank()
        P.op('pe', lambda h, b=b, bt=bt: h.matmul(P.banks[b][0:8, 0:256], relb, oh[:, bt, :], start=True, stop=True),
             reads=[Rc], writes=[P.Rbank[b]])
        gs, Rgs, gch = gs_r.next()
        P.op('dve', lambda h, gs=gs, b=b: h.tensor_copy(out=gs, in_=P.banks[b][0:8, 0:256]), reads=[P.Rbank[b]], writes=[Rgs])
        P.dma('sp', gch, lambda h, gs=gs, bt=bt: h.dma_start(out=C.GV[bt], in_=gs), reads=[Rgs], writes=[RGV[bt]])
    for bt in range(6):
        bi, ty = bt // 2, bt % 2
        for hd in range(8):
            hk, Rhk, hch = hk_r.next()
            src = bass.AP(C.GV.tensor, (bt * 8 + hd) * 256, [[1, 128], [1, 128]])
            P.dma('sp', hch, lambda h, hk=hk, src=src: h.dma_start(out=hk, in_=src), reads=[RGV[bt]], writes=[Rhk])
            b = P.next_bank()
            P.op('pe', lambda h, b=b, hk=hk: h.matmul(P.banks[b][:, 0:128], hk, jmat, start=True, stop=True),
                 reads=[Rhk, Rc], writes=[P.Rbank[b]])
            tmp, Rtmp = tmp_r.next()
            act(P, tmp, P.banks[b][:, 0:128], AF.Exp, [P.Rbank[b]], [Rtmp])
            msk = U_ if ty == 0 else LO_
            P.op('dve', lambda h, tmp=tmp, msk=msk, hd=hd, bi=bi, ty=ty: h.tensor_tensor(
                out=EB[:, hd // 2, bi, hd % 2, ty * 128:(ty + 1) * 128], in0=tmp, in1=msk, op=ALU.mult),
                reads=[Rtmp, Rc], writes=[REB])
    P.barrier()
    P.reset(m1)

    PADK = 1024
    Qbd = P.alloc([128, 2, L], BF16)
    KTp = P.alloc([128, L + 2 * PADK], BF16)
    OT = P.alloc([128, 2, L], F32)
    NTmax = L // 128 + 16
    Vbs = [P.alloc([128, NTmax, 130], BF16) for _ in range(2)]
    RVs = [P.res("Vb0"), P.res("Vb1")]
    RQ, RK, ROT = P.res("Qbd"), P.res("KTp"), P.res("OT")
    chq, chk = P.chan("q"), P.chan("k")
    chv2 = [[P.chan(f"v{i}_{k}") for k in range(4)] for i in range(2)]
    E_r = Ring(P, "E", 3, [128, 512], F32)
    P_r = Ring(P, "Pt", 4, [128, 2, 256], BF16)
    sg_r = Ring(P, "osg", 4, [65, 128], F32)
    rc_r = Ring(P, "rc", 2, [64, 512], F32)
    so_r = Ring(P, "so", 2, [64, 512], F32, chan=True)
    dummy = P.alloc([128, 16], F32)
    P.op('pool', lambda h: h.memset(Qbd, 0.0), writes=[RQ])
    P.op('pool', lambda h: h.memset(KTp, 0.0), writes=[RK])
    S_banks = [0, 1, 2, 3]
    si = [0]
    O_bank = {(0, 0): 4, (0, 1): 5, (1, 0): 6, (1, 1): 7}

    def load_V(s, hp, bi, slot):
        dl = BRANCH_DIL[bi]
        Ld = L // dl
        NJ = Ld // 128 + 1
        Vb, RV = Vbs[slot], RVs[slot]
        Vs = C.V[s]
        c0, c1 = hp * 130, (hp + 1) * 130
        for rho in range(dl):
            tb = rho * NJ
            P.op('pool', lambda h, tb=tb, Vb=Vb: h.memset(Vb[0:64, tb, :], 0.0), writes=[RV])
            P.op('pool', lambda h, tb=tb, NJ=NJ, Vb=Vb: h.memset(Vb[64:128, tb + NJ - 1, :], 0.0), writes=[RV])
            ch_ = chv2[slot][rho % 4]
            if NJ > 2:
                src = Vs[rho + dl * 64:rho + dl * 64 + dl * 128 * (NJ - 2):dl, c0:c1]
                P.dma('sp', ch_, lambda h, tb=tb, NJ=NJ, src=src, Vb=Vb: h.dma_start(
                    out=Vb[:, tb + 1:tb + NJ - 1, :], in_=src.rearrange("(j a) f -> a j f", a=128)), writes=[RV])
            r_first = Vs[rho:rho + dl * 63 + 1:dl, c0:c1]
            t_l = rho + dl * (Ld - 64)
            r_last = Vs[t_l:t_l + dl * 63 + 1:dl, c0:c1]
            P.dma('sp', ch_, lambda h, tb=tb, r_first=r_first, Vb=Vb: h.dma_start(out=Vb[64:128, tb, :], in_=r_first), writes=[RV])
            P.dma('sp', ch_, lambda h, tb=tb, NJ=NJ, r_last=r_last, Vb=Vb: h.dma_start(
                out=Vb[0:64, tb + NJ - 1, :], in_=r_last), writes=[RV])

    work = [(s, hp, bi) for s in range(NS) for hp in range(4) for bi in range(3)]
    load_V(*work[0], 0)
    def do_work(wi, s, hp, bi):
        slot = wi % 2
        Vb, RV = Vbs[slot], RVs[slot]
        dl = BRANCH_DIL[bi]
        Ld = L // dl
        NQ = Ld // 128
        NJ = NQ + 1
        if bi == 0:
            r0 = hp * 128
            P.dma('sp', chq, lambda h, s=s, r0=r0: h.dma_start(out=Qbd[0:64, 0, :], in_=C.QT[s][r0:r0 + 64, :]), writes=[RQ])
            P.dma('sp', chq, lambda h, s=s, r0=r0: h.dma_start(out=Qbd[64:128, 1, :], in_=C.QT[s][r0 + 64:r0 + 128, :]), writes=[RQ])
            P.dma('sp', chk, lambda h, s=s, r0=r0: h.dma_start(out=KTp[:, PADK:PADK + L], in_=C.KT[s][r0:r0 + 128, :]), writes=[RK])
            P.op('pool', lambda h: h.memset(OT, 0.0), writes=[ROT])
        else:
            P.op('pool', lambda h: h.memset(dummy, 0.0), writes=[ROT])
        if wi + 1 < len(work):
            load_V(*work[wi + 1], 1 - slot)
        tiles = [(rho, j) for rho in range(dl) for j in range(NJ)]
        st = {}

        def front(t):
            rho, j = tiles[t]
            halves = [hf for hf in (0, 1) if 0 <= j - 1 + hf < NQ]
            h0, h1 = halves[0], halves[-1] + 1
            nq = (h1 - h0) * 128
            qs = rho + dl * (128 * (j - 1) + h0 * 128)
            ks = PADK + rho + dl * (128 * j - 64)
            bS = S_banks[si[0] % 4]
            si[0] += 1
            P.op('pe', lambda h: h.matmul(
                P.banks[bS][:, :].rearrange("p (h q) -> p h q", h=2)[:, :, h0 * 128:h1 * 128],
                KTp[:, ks:ks + dl * 127 + 1:dl],
                Qbd[:, :, qs:qs + dl * (nq - 1) + 1:dl], start=True, stop=True),
                reads=[RK, RQ], writes=[P.Rbank[bS]])
            Et, REt = E_r.next()
            Pt, RPt = P_r.next()
            Ev = Et.rearrange("p (h q) -> p h q", h=2)[:, :, h0 * 128:h1 * 128]
            act(P, Ev, P.banks[bS][:, :].rearrange("p (h q) -> p h q", h=2)[:, :, h0 * 128:h1 * 128],
                AF.Exp, [P.Rbank[bS]], [REt])
            P.op('dve', lambda h: h.tensor_tensor(
                out=Pt[:, :, h0 * 128:h1 * 128], in0=Ev, in1=EB[:, hp, bi, :, h0 * 128:h1 * 128], op=ALU.mult),
                reads=[REt, REB], writes=[RPt])
            st[t] = (Pt, RPt, halves)

        def back(t):
            rho, j = tiles[t]
            Pt, RPt, halves = st.pop(t)
            tile = rho * NJ + j
            for hd in range(2):
                for hf in halves:
                    m = j - 1 + hf
                    bO = O_bank[(hd, m % 2)]
                    P.op('pe', lambda h, bO=bO, hd=hd, hf=hf: h.matmul(
                        P.banks[bO][0:65, 0:128], Vb[:, tile, hd * 65:(hd + 1) * 65],
                        Pt[:, hd, hf * 128:(hf + 1) * 128], start=(hf == 1), stop=(hf == 0)),
                        reads=[RV, RPt], writes=[P.Rbank[bO]])
                    if hf == 0:
                        t0 = rho + dl * 128 * m
                        ov = OT[0:65, hd, t0:t0 + dl * 127 + 1:dl]
                        sg, Rsg = sg_r.next()
                        act(P, sg, P.banks[bO][0:65, 0:128], AF.Identity, [P.Rbank[bO]], [Rsg])
                        P.op('pool', lambda h, ov=ov, sg=sg: h.tensor_tensor(out=ov, in0=ov, in1=sg, op=ALU.add),
                             reads=[ROT, Rsg], writes=[])
        n = len(tiles)
        for t in range(n + 2):
            if t < n:
                front(t)
            if t >= 2:
                back(t - 2)
        if bi == 2:
            P.op('pool', lambda h: h.memset(dummy, 0.0), writes=[ROT])
            for hd in range(2):
                for ct in range(L // 512):
                    b = S_banks[si[0] % 4]
                    si[0] += 1
                    P.op('pe', lambda h, b=b, hd=hd, ct=ct: h.matmul(
                        P.banks[b][0:64, :], sel, OT[0:65, hd, ct * 512:(ct + 1) * 512], start=True, stop=True),
                        reads=[Rc, ROT], writes=[P.Rbank[b]])
                    rc, Rrc = rc_r.next()
                    act(P, rc, P.banks[b][0:64, :], AF.Ln, [P.Rbank[b]], [Rrc])
                    act(P, rc, rc, AF.Exp, [Rrc], [Rrc], scale=-1.0)
                    so, Rso, soch = so_r.next()
                    P.op('dve', lambda h, so=so, rc=rc, hd=hd, ct=ct: h.tensor_tensor(
                        out=so, in0=OT[0:64, hd, ct * 512:(ct + 1) * 512], in1=rc, op=ALU.mult),
                        reads=[ROT, Rrc], writes=[Rso])
                    rr = hp * 128 + hd * 64
                    P.dma('sp', soch, lambda h, so=so, s=s, rr=rr, ct=ct: h.dma_start(
                        out=C.AT[s][rr:rr + 64, ct * 512:(ct + 1) * 512], in_=so), reads=[Rso])

    for wi, (s, hp, bi) in enumerate(work):
        do_work(wi, s, hp, bi)
    P.reset(m0)


def _ln_feature_major(P, C, tt, Rtt, nd, T, S1, S2, cst_ones, Rc, mk_out):
    mean = C.ln_mean
    m2 = C.ln_m2
    rstd = C.ln_rstd
    Rst = C.ln_Rst
    act(P, mean[:, 0:T], P.banks[S1][:, 0:T], AF.Identity, [P.Rbank[S1]], [Rst], scale=1.0 / D)
    P.op('dve', lambda h: h.tensor_tensor(out=m2[:, 0:T], in0=mean[:, 0:T], in1=mean[:, 0:T], op=ALU.mult),
         reads=[Rst], writes=[Rst])
    P.op('dve', lambda h: h.scalar_tensor_tensor(out=m2[:, 0:T], in0=P.banks[S2][:, 0:T], scalar=1.0 / D,
                                                 in1=m2[:, 0:T], op0=ALU.mult, op1=ALU.subtract),
         reads=[P.Rbank[S2], Rst], writes=[Rst])
    act(P, m2[:, 0:T], m2[:, 0:T], AF.Ln, [Rst], [Rst], bias=EPS)
    act(P, rstd[:, 0:T], m2[:, 0:T], AF.Exp, [Rst], [Rst], scale=-0.5)
    for dc in range(nd):
        u1, Ru1 = C.ln_u.next()
        P.op('dve', lambda h, u1=u1, dc=dc: h.tensor_tensor(out=u1[:, 0:T], in0=tt[:, dc, 0:T], in1=mean[:, 0:T], op=ALU.subtract),
             reads=[Rtt[dc], Rst], writes=[Ru1])
        P.op('dve', lambda h, u1=u1: h.tensor_tensor(out=u1[:, 0:T], in0=u1[:, 0:T], in1=rstd[:, 0:T], op=ALU.mult),
             reads=[Ru1, Rst], writes=[Ru1])
        mk_out(dc, u1, Ru1)


def phase_C1(C):
    P, L, NS = C.P, C.L, C.NS
    T = 512
    m0 = P.mark()
    P.bank_list = [0, 1, 2, 3, 4]
    SA, S1, S2 = 5, 6, 7
    wout = P.alloc([128, 8, D], BF16)
    cst = P.alloc([128, 6, 128], F32)
    ang = P.alloc([128, 4], F32)
    lnfm = P.alloc([128, 4, 8], F32)
    Rc = P.res("c1_const")
    chc = P.chan("c1_const")
    P.dma('pool', chc, lambda h: h.dma_start(out=wout, in_=C.w_out.rearrange("(kc p) c -> p kc c", p=128)), writes=[Rc])
    P.dma('sp', chc, lambda h: h.dma_start(out=cst, in_=C.cst), writes=[Rc])
    P.dma('sp', chc, lambda h: h.dma_start(out=ang, in_=C.ang), writes=[Rc])
    P.dma('sp', chc, lambda h: h.dma_start(out=lnfm, in_=C.lnfm), writes=[Rc])
    ONES = cst[:, 4, :]
    xt_r = Ring(P, "c1x", 3, [128, 8, T], F32, chan=True)
    at_r = Ring(P, "c1a", 2, [128, 4, T], F32, chan=True)
    yn_r = Ring(P, "c1y", 3, [128, 4, T], BF16, chan=True)
    an_r = Ring(P, "c1an", 2, [128, 4, T], BF16)
    sq_r = Ring(P, "c1sq", 3, [128, T], F32)
    sqx_r = Ring(P, "c1sqx", 2, [128, T], F32)
    rsa_r = Ring(P, "c1rsa", 2, [128, T], F32)
    tts = [P.alloc([128, 8, T], F32) for _ in range(2)]
    Rtts = [[P.res(f"tt{k}_{i}") for i in range(8)] for k in range(2)]
    C.ln_mean = P.alloc([128, T], F32)
    C.ln_m2 = P.alloc([128, T], F32)
    C.ln_rstd = P.alloc([128, T], F32)
    C.ln_Rst = P.res("lnst")
    C.ln_u = Ring(P, "lnu", 3, [128, T], F32)
    hf_r = Ring(P, "c1hf", 1, [128, 8, T], F32, chan=True)
    hb_r = Ring(P, "c1hb", 1, [128, 8, T], BF16, chan=True)
    tiles = [(s, t0) for s in range(NS) for t0 in range(0, L, T)]

    def load(s, t0):
        xb, xr, xch = xt_r.next()
        P.dma('sp', xch, lambda h: h.dma_start(out=xb, in_=C.xT[s].rearrange("(c p) t -> p c t", p=128)[:, :, t0:t0 + T]), writes=[xr])
        ab, ar, ach = at_r.next()
        P.dma('sp', ach, lambda h: h.dma_start(out=ab, in_=C.AT[s].rearrange("(c p) t -> p c t", p=128)[:, :, t0:t0 + T]), writes=[ar])
        yb, yr, ych = yn_r.next()
        P.dma('sp', ych, lambda h: h.dma_start(out=yb, in_=C.YN[s].rearrange("(c p) t -> p c t", p=128)[:, :, t0:t0 + T]), writes=[yr])
        return (xb, xr), (ab, ar), (yb, yr)

    def stage_X(ld):
        (xb, xr), (ab, ar), (yb, yr) = ld
        for fc in range(4):
            sq, Rsq = sqx_r.next()
            act(P, sq, ab[:, fc, :], AF.Square, [ar], [Rsq])
            P.op('pe', lambda h, sq=sq, fc=fc: h.matmul(P.banks[SA][:, :], ONES, sq, start=(fc == 0), stop=(fc == 3)),
                 reads=[Rc, Rsq], writes=[P.Rbank[SA]])
        rsa, Rrsa = rsa_r.next()
        act(P, rsa, P.banks[SA][:, :], AF.Ln, [P.Rbank[SA]], [Rrsa], scale=1.0 / 512, bias=EPS)
        act(P, rsa, rsa, AF.Exp, [Rrsa], [Rrsa], scale=-0.5)
        an, Ran = an_r.next()
        for fc in range(4):
            P.op('dve', lambda h, fc=fc: h.scalar_tensor_tensor(
                out=an[:, fc, :], in0=ab[:, fc, :], scalar=ang[:, fc:fc + 1], in1=rsa, op0=ALU.mult, op1=ALU.mult),
                reads=[ar, Rc, Rrsa], writes=[Ran])
        return (xb, xr), (yb, yr), (an, Ran)

    def stage_YZ(ti, xs_, inject):
        s, t0 = tiles[ti]
        (xb, xr), (yb, yr), (an, Ran) = xs_
        tt, Rtt = tts[ti % 2], Rtts[ti % 2]
        pend = None
        nxt_x = None
        for dc in range(8):
            if dc == 4 and inject is not None:
                nxt_x = inject()
            b = P.next_bank()
            mm_group(P, P.banks[b][:, :],
                     [(wout[:, kc, dc * 128:(dc + 1) * 128], an[:, kc, :] if kc < 4 else yb[:, kc - 4, :]) for kc in range(8)],
                     P.Rbank[b], [Rc, Ran, yr])
            P.op('dve', lambda h, dc=dc, b=b: h.scalar_tensor_tensor(
                out=tt[:, dc, :], in0=xb[:, dc, :], scalar=ALPHA, in1=P.banks[b][:, :], op0=ALU.mult, op1=ALU.add),
                reads=[xr, P.Rbank[b]], writes=[Rtt[dc]])
            sq, Rsq = sq_r.next()
            act(P, sq, tt[:, dc, :], AF.Square, [Rtt[dc]], [Rsq])

            def stats(dc=dc, sq=sq, Rsq=Rsq):
                P.op('pe', lambda h: h.matmul(P.banks[S1][:, :], ONES, tt[:, dc, :], start=(dc == 0), stop=(dc == 7)),
                     reads=[Rc, Rtt[dc]], writes=[P.Rbank[S1]])
                P.op('pe', lambda h: h.matmul(P.banks[S2][:, :], ONES, sq, start=(dc == 0), stop=(dc == 7)),
                     reads=[Rc, Rsq], writes=[P.Rbank[S2]])
            if pend is not None:
                pend()
            pend = stats
        pend()
        hf, Rhf, hfch = hf_r.next()
        hb, Rhb, hbch = hb_r.next()

        def mk_out(dc, u1, Ru1):
            act(P, hf[:, dc, :], u1, AF.Identity, [Ru1, Rc], [Rhf], scale=lnfm[:, 0, dc:dc + 1], bias=lnfm[:, 1, dc:dc + 1])
            act(P, hb[:, dc, :], u1, AF.Identity, [Ru1, Rc], [Rhb], scale=lnfm[:, 0, dc:dc + 1], bias=lnfm[:, 1, dc:dc + 1])
        _ln_feature_major(P, C, tt, Rtt, 8, T, S1, S2, ONES, Rc, mk_out)
        P.dma('sp', hfch, lambda h: h.dma_start(
            out=C.H1F[s].rearrange("(c p) t -> p c t", p=128)[:, :, t0:t0 + T], in_=hf), reads=[Rhf])
        P.dma('sp', hbch, lambda h: h.dma_start(
            out=C.H1B[s].rearrange("(c p) t -> p c t", p=128)[:, :, t0:t0 + T], in_=hb), reads=[Rhb])
        return nxt_x

    ld = [None] * (len(tiles) + 2)
    ld[0] = load(*tiles[0])
    if len(tiles) > 1:
        ld[1] = load(*tiles[1])
    xs_next = stage_X(ld[0])
    for ti in range(len(tiles)):
        xs_cur = xs_next
        if ti + 2 < len(tiles):
            ld[ti + 2] = load(*tiles[ti + 2])
        inj = (lambda ti=ti: stage_X(ld[ti + 1])) if ti + 1 < len(tiles) else None
        xs_next = stage_YZ(ti, xs_cur, inj)
    P.bank_list = list(range(8))
    P.reset(m0)


def phase_C2(C):
    P, L, NS = C.P, C.L, C.NS
    T = 256
    NF = DFF // 128
    m0 = P.mark()
    P.bank_list = [0, 1, 2, 3, 4, 5]
    S1, S2 = 6, 7
    wg = P.alloc([128, 8, DFF], BF16)
    wu = P.alloc([128, 8, DFF], BF16)
    wd = P.alloc([128, NF, D], BF16)
    cst = P.alloc([128, 6, 128], F32)
    lnfm = P.alloc([128, 4, 8], F32)
    Rc = P.res("c2_const")
    chc = P.chan("c2_const")
    chw = [P.chan(f"c2w{i}") for i in range(3)]
    for c0 in (0, 1408):
        P.dma('pool', chw[0], lambda h, c0=c0: h.dma_start(
            out=wg[:, :, c0:c0 + 1408], in_=C.w_gate.rearrange("(kc p) c -> p kc c", p=128)[:, :, c0:c0 + 1408]), writes=[Rc])
        P.dma('pool', chw[1], lambda h, c0=c0: h.dma_start(
            out=wu[:, :, c0:c0 + 1408], in_=C.w_up.rearrange("(kc p) c -> p kc c", p=128)[:, :, c0:c0 + 1408]), writes=[Rc])
    P.dma('pool', chw[2], lambda h: h.dma_start(out=wd, in_=C.w_down.rearrange("(kc p) c -> p kc c", p=128)), writes=[Rc])
    P.dma('sp', chc, lambda h: h.dma_start(out=cst, in_=C.cst), writes=[Rc])
    P.dma('sp', chc, lambda h: h.dma_start(out=lnfm, in_=C.lnfm), writes=[Rc])
    ONES = cst[:, 4, :]
    hb_r = Ring(P, "c2hb", 2, [128, 8, T], BF16, chan=True)
    hf_r = Ring(P, "c2hf", 4, [128, T], F32, chan=True)
    hid = P.alloc([128, NF, T], BF16)
    Rhid = P.res("hid")
    sg_r = Ring(P, "c2sg", 3, [128, T], F32)
    sq_r = Ring(P, "c2sq", 4, [128, T], F32)
    tt = P.alloc([128, 8, T], F32)
    Rtt = [P.res(f"tt2_{i}") for i in range(8)]
    C.ln_mean = P.alloc([128, T], F32)
    C.ln_m2 = P.alloc([128, T], F32)
    C.ln_rstd = P.alloc([128, T], F32)
    C.ln_Rst = P.res("lnst2")
    C.ln_u = Ring(P, "lnu2", 3, [128, T], F32)
    yo_r = Ring(P, "c2yo", 4, [128, T], F32, chan=True)
    outs = []

    def load(s, t0):
        hb, hr, hch = hb_r.next()
        P.dma('sp', hch, lambda h: h.dma_start(out=hb, in_=C.H1B[s].rearrange("(c p) t -> p c t", p=128)[:, :, t0:t0 + T]), writes=[hr])
        return hb, hr

    tiles = [(s, t0) for s in range(NS) for t0 in range(0, L, T)]
    nxt = load(*tiles[0])
    for ti, (s, t0) in enumerate(tiles):
        hb, hr = nxt
        if ti + 1 < len(tiles):
            nxt = load(*tiles[ti + 1])
        for fc in range(NF):
            bg = P.next_bank()
            mm_group(P, P.banks[bg][:, 0:T], [(wg[:, kc, fc * 128:(fc + 1) * 128], hb[:, kc, :]) for kc in range(8)],
                     P.Rbank[bg], [Rc, hr])
            bu = P.next_bank()
            mm_group(P, P.banks[bu][:, 0:T], [(wu[:, kc, fc * 128:(fc + 1) * 128], hb[:, kc, :]) for kc in range(8)],
                     P.Rbank[bu], [Rc, hr])
            sg, Rsg = sg_r.next()
            act(P, sg, P.banks[bg][:, 0:T], AF.Silu, [P.Rbank[bg]], [Rsg])
            P.op('dve', lambda h, fc=fc, sg=sg, bu=bu: h.tensor_tensor(out=hid[:, fc, :], in0=sg, in1=P.banks[bu][:, 0:T], op=ALU.mult),
                 reads=[Rsg, P.Rbank[bu]], writes=[Rhid])
        pend = None
        for dc in range(8):
            hf, Rhf, hfch = hf_r.next()
            P.dma('sp', hfch, lambda h, hf=hf, s=s, t0=t0, dc=dc: h.dma_start(
                out=hf, in_=C.H1F[s][dc * 128:(dc + 1) * 128, t0:t0 + T]), writes=[Rhf])
            b = P.next_bank()
            mm_group(P, P.banks[b][:, 0:T], [(wd[:, fc, dc * 128:(dc + 1) * 128], hid[:, fc, :]) for fc in range(NF)],
                     P.Rbank[b], [Rc, Rhid])
            P.op('dve', lambda h, dc=dc, b=b, hf=hf: h.scalar_tensor_tensor(
                out=tt[:, dc, :], in0=hf, scalar=ALPHA, in1=P.banks[b][:, 0:T], op0=ALU.mult, op1=ALU.add),
                reads=[Rhf, P.Rbank[b]], writes=[Rtt[dc]])
            sq, Rsq = sq_r.next()
            act(P, sq, tt[:, dc, :], AF.Square, [Rtt[dc]], [Rsq])

            def stats(dc=dc, sq=sq, Rsq=Rsq):
                P.op('pe', lambda h: h.matmul(P.banks[S1][:, 0:T], ONES, tt[:, dc, :], start=(dc == 0), stop=(dc == 7)),
                     reads=[Rc, Rtt[dc]], writes=[P.Rbank[S1]])
                P.op('pe', lambda h: h.matmul(P.banks[S2][:, 0:T], ONES, sq, start=(dc == 0), stop=(dc == 7)),
                     reads=[Rc, Rsq], writes=[P.Rbank[S2]])
            if pend is not None:
                pend()
            pend = stats
        pend()
        pend = None

        def mk_out(dc, u1, Ru1, s=s, t0=t0):
            yo, Ryo, yoch = yo_r.next()
            act(P, yo, u1[:, 0:T], AF.Identity, [Ru1, Rc], [Ryo], scale=lnfm[:, 2, dc:dc + 1], bias=lnfm[:, 3, dc:dc + 1])
            outs.append(P.dma('sp', yoch, lambda h, yo=yo, dc=dc: h.dma_start(
                out=C.yT[s][dc * 128:(dc + 1) * 128, t0:t0 + T], in_=yo), reads=[Ryo]))
        _ln_feature_major(P, C, tt, Rtt, 8, T, S1, S2, ONES, Rc, mk_out)
    P.bank_list = list(range(8))
    P.reset(m0)
    return outs

def make_cst():
    i = np.arange(128)
    U = (i[:, None] <= i[None, :]).astype(np.float32)
    SL = (i[:, None] > i[None, :]).astype(np.float32)
    Lo = (i[:, None] >= i[None, :]).astype(np.float32)
    SU = (i[:, None] < i[None, :]).astype(np.float32)
    ones = np.ones((128, 128), np.float32)
    ident = np.eye(128, dtype=np.float32)
    return np.ascontiguousarray(np.stack([U, SL, Lo, SU, ones, ident], axis=1))


def shared_inputs(inp):
    f = np.float32
    g = lambda k: np.asarray(inp[k], dtype=f)
    bc = lambda v, n=128: np.ascontiguousarray(np.broadcast_to(v[None, :], (n, v.shape[0])))
    m = {}
    m["w_in"] = np.ascontiguousarray(g("w_in")[0])
    m["convw"] = np.ascontiguousarray(g("conv_w")[0].reshape(5, 8, 128).transpose(2, 1, 0))
    m["convb"] = np.ascontiguousarray(g("conv_b")[0].reshape(8, 128).T)
    m["dtb"] = bc(np.concatenate([g("dt_bias_fwd")[0], g("dt_bias_bwd")[0]]))
    m["alog"] = bc(np.concatenate([g("a_log_fwd")[0], g("a_log_bwd")[0]]))
    m["dsk"] = bc(g("d_skip")[0])
    m["ang"] = np.ascontiguousarray(g("attn_norm_g")[0].reshape(4, 128).T)
    m["sng"] = bc(g("ssd_norm_g")[0])
    m["relb"] = np.ascontiguousarray(g("rel_bias"))
    m["lnfm"] = np.ascontiguousarray(np.stack([g(k)[0].reshape(8, 128).T for k in ("ln1_g", "ln1_b", "ln2_g", "ln2_b")], axis=1))
    m["w_out"] = np.ascontiguousarray(g("w_out")[0])
    m["w_gate"] = np.ascontiguousarray(g("w_gate")[0])
    m["w_up"] = np.ascontiguousarray(g("w_up")[0])
    m["w_down"] = np.ascontiguousarray(g("w_down")[0])
    m["cst"] = make_cst()
    m["oh"], m["jmat"], m["sel"] = make_att_consts()
    return m


def t5_bucket(rel):
    half = 16
    max_exact = 8
    ret = (rel > 0).astype(np.int32) * half
    n = np.abs(rel)
    large = max_exact + (np.log(np.maximum(n, 1) / max_exact)
                         / math.log(1024 / max_exact) * (half - max_exact)).astype(np.int32)
    large = np.minimum(large, half - 1)
    return ret + np.where(n < max_exact, n, large)


def make_att_consts():
    oh = np.zeros((32, 6, 256), np.float32)
    i = np.arange(255)
    for bi, dl in enumerate(BRANCH_DIL):
        for ty in range(2):
            rel = (i - 63) if ty == 0 else (i - 191)
            bk = t5_bucket(rel * dl)
            oh[bk, bi * 2 + ty, i] = 1.0
    eb = np.ascontiguousarray(np.eye(128, dtype=np.float32)[::-1])
    sel = np.zeros((65, 64), np.float32)
    sel[64, :] = 1.0
    return oh, eb, sel


SEQ_LEN = 8192
N_CORES = 8
SLOTS = 2
_CACHE = {}


def kernel(**inputs):
    xp = np.asarray(inputs["x_prompt"], dtype=np.float32)
    xs = np.asarray(inputs["x_sample"], dtype=np.float32)
    seqs = [xp[i] for i in range(xp.shape[0])] + [xs[i] for i in range(xs.shape[0])]
    nseq = len(seqs)
    L = seqs[0].shape[0]
    shared = shared_inputs(inputs)
    in_maps = []
    for c in range(N_CORES):
        xT = np.zeros((SLOTS, D, L), np.float32)
        for sl in range(SLOTS):
            i = c * SLOTS + sl
            if i < nseq:
                xT[sl] = seqs[i].T
        m = dict(shared)
        m["xT"] = xT
        in_maps.append(m)
    key = (L, SLOTS)
    if key not in _CACHE:
        _CACHE[key] = build(L, SLOTS)
    nc, _ = _CACHE[key]
    res = run_bass_kernel_spmd(nc, in_maps, core_ids=list(range(N_CORES)))
    outs = []
    for i in range(nseq):
        c, sl = divmod(i, SLOTS)
        outs.append(np.ascontiguousarray(np.asarray(res.results[c]["yT"][sl]).T))
    y_prompt = np.stack(outs[:xp.shape[0]]).astype(np.float32)
    y_sample = np.stack(outs[xp.shape[0]:]).astype(np.float32)
    return (y_prompt, y_sample)
```

```python
import contextlib
import math
import numpy as np
import concourse.bass as bass
import concourse.mybir as mybir
from concourse.bass_utils import run_bass_kernel_spmd

F32 = mybir.dt.float32
BF16 = mybir.dt.bfloat16
U8 = mybir.dt.uint8
AF = mybir.ActivationFunctionType
ALU = mybir.AluOpType

D = 1024
DIN = 3088
DFF = 2816
NH = 8
COL_Q, COL_K, COL_V, COL_Z, COL_X, COL_DT = 0, 512, 1024, 1536, 2048, 3072
ALPHA = 2.0 ** 0.25
EPS = 1e-5
ENGS = ['pe', 'act', 'dve', 'pool', 'sp']
EPOCH = 20000
POOL_BYTES = 206 * 1024


def _dsize(dt):
    return {F32: 4, BF16: 2, U8: 1}[dt]


class Res:
    __slots__ = ('name', 'w', 'r')

    def __init__(self, name):
        self.name = name
        self.w = {}
        self.r = {}


class Chan:
    def __init__(self, name):
        self.name = name
        self.n = 0
        self.sem = None
        self.last = None


class Prog:
    def __init__(self, nc):
        self.nc = nc
        self.streams = {e: [] for e in ENGS}
        self.last_op = {e: None for e in ENGS}
        self.chans = []
        self.stack = contextlib.ExitStack()
        self.nres = 0
        self.nseq = 0
        self.pool = self.stack.enter_context(nc.sbuf_tensor("pool", [128, POOL_BYTES], U8))
        self.off = 0
        self.peak = 0
        self.banks = [self.stack.enter_context(nc.psum_tensor(f"bank{i}", [128, 512], F32))
                      for i in range(8)]
        self.Rbank = [self.res(f"bank{i}") for i in range(8)]

    def res(self, name=None):
        self.nres += 1
        return Res(name or f"r{self.nres}")

    def chan(self, name):
        c = Chan(name)
        self.chans.append(c)
        return c

    def alloc(self, shape, dtype):
        n = 1
        for s in shape[1:]:
            n *= s
        nbytes = n * _dsize(dtype)
        self.off = (self.off + 63) // 64 * 64
        assert self.off + nbytes <= POOL_BYTES, f"SBUF overflow {self.off + nbytes}"
        ap = self.pool[0:shape[0], self.off:self.off + nbytes].bitcast(dtype)
        self.off += nbytes
        self.peak = max(self.peak, self.off)
        if len(shape) > 2:
            names = [f"d{i}" for i in range(len(shape) - 1)]
            pat = "p (" + " ".join(names) + ") -> p " + " ".join(names)
            ap = ap.rearrange(pat, **{names[i]: shape[i + 1] for i in range(len(names))})
        return ap

    def mark(self):
        return self.off

    bank_list = list(range(8))

    def next_bank(self):
        self.bank_i = getattr(self, 'bank_i', -1) + 1
        return self.bank_list[self.bank_i % len(self.bank_list)]

    def reset(self, m):
        self.off = m

    def _deps(self, reads, writes):
        deps = {}
        for r in reads:
            for v in r.w.values():
                deps[id(v)] = v
        for w in writes:
            for v in w.w.values():
                deps[id(v)] = v
            for v in w.r.values():
                deps[id(v)] = v
        return list(deps.values())

    def op(self, eng, fn, reads=(), writes=()):
        ins = dict(fn=fn, deps=self._deps(reads, writes), signal=False, chan=None, eng=eng, seq=self.nseq)
        self.nseq += 1
        self.streams[eng].append(ins)
        self.last_op[eng] = ins
        for r in reads:
            r.r[id(ins)] = ins
        for w in writes:
            w.w = {eng: ins}
            w.r = {}
        return ins

    def dma(self, q, chan, fn, reads=(), writes=()):
        deps = self._deps(reads, writes)
        if chan.n > 0:
            deps.append(chan.last)
        ins = dict(fn=fn, deps=deps, signal=True, chan=chan, eng=q, seq=self.nseq, n=chan.n)
        self.nseq += 1
        self.streams[q].append(ins)
        chan.n += 1
        chan.last = ins
        for r in reads:
            r.r[id(ins)] = ins
        for w in writes:
            w.w = {chan: ins}
            w.r = {}
        return ins

    def barrier(self):
        evs = []
        for e in ENGS:
            if self.last_op[e] is not None:
                evs.append(self.last_op[e])
        for c in self.chans:
            if c.n > 0:
                evs.append(c.last)
        for e in ENGS:
            self.streams[e].append(dict(fn=None, deps=[d for d in evs if not (d['chan'] is None and d['eng'] == e)],
                                        signal=False, chan=None, eng=e, seq=self.nseq, barrier=True))
        self.nseq += 1

    def wait_all(self, eng, evs):
        self.streams[eng].append(dict(fn=None, deps=list(evs), signal=False, chan=None, eng=eng, seq=self.nseq,
                                      barrier=True))
        self.nseq += 1

    def _cost(self, ins):
        rec = _Fake()
        try:
            ins['fn'](rec)
        except Exception:
            return 300.0
        name, args, kw = rec.call
        try:
            if name == 'dma_start':
                o = kw.get('out')
                n = 1
                for d in o.shape:
                    n *= d
                ins['dma_ns'] = 2000.0 + n * _dsize(o.dtype) / 150.0
                return 60.0
            if name in ('matmul', 'transpose'):
                rhs = kw.get('rhs', args[2] if len(args) > 2 else None)
                n = 1
                for d in rhs.shape[1:]:
                    n *= d
                st = 1
                try:
                    st = max(1, abs(rhs.ap[-1][0]))
                except Exception:
                    pass
                c = max(64, n) / 2.4
                if rhs.dtype == F32 and name == 'matmul':
                    c *= 4
                elif st > 1:
                    c *= min(8, st) / 2.0 + 0.5
                return c + 8
            o = kw.get('out', args[0] if args else None)
            n = 1
            for d in o.shape[1:]:
                n *= d
            e = ins['eng']
            if e == 'act':
                return 70 + n * 0.96 + (90 if 'accum_out' in kw else 0)
            if e == 'dve':
                return 65 + n * 1.04
            return 110 + n * (0.6 if name == 'memset' else 2.3)
        except Exception:
            return 300.0

    def schedule(self, window=32):
        LAT = 120.0
        new_streams = {e: [] for e in ENGS}
        pos = {e: 0 for e in ENGS}
        free = {e: 0.0 for e in ENGS}
        tnow = 0.0
        while any(pos[e] < len(self.streams[e]) for e in ENGS):
            seg = {}
            for e in ENGS:
                st = self.streams[e]
                i = pos[e]
                j = i
                while j < len(st) and not st[j].get('barrier'):
                    j += 1
                seg[e] = st[i:j]
                bar = st[j] if j < len(st) else None
                pos[e] = j + 1 if j < len(st) else j
                seg[e + '_bar'] = bar
            for e in ENGS:
                for ins in seg[e]:
                    ins['cost'] = self._cost(ins)
                    ins['fin'] = None
                free[e] = tnow
            pend = {e: list(seg[e]) for e in ENGS}
            nleft = sum(len(v) for v in pend.values())
            while nleft:
                progressed = False
                for e in sorted(ENGS, key=lambda x: free[x]):
                    pl = pend[e]
                    if not pl:
                        continue
                    best = None
                    bstart = None
                    for k in range(min(window if e != 'sp' else 6, len(pl))):
                        ins = pl[k]
                        rd = ins.get('ready')
                        if rd is None:
                            rd = 0.0
                            ok = True
                            for d in ins['deps']:
                                f = d.get('fin', 0.0)
                                if f is None:
                                    ok = False
                                    break
                                if d['chan'] is not None:
                                    f = d.get('dfin', f)
                                if d['eng'] == e and d['chan'] is None and e == 'pe':
                                    f = f - d['cost'] * 0.5
                                else:
                                    f = f + LAT
                                if f > rd:
                                    rd = f
                            if not ok:
                                continue
                            ins['ready'] = rd
                        stt = rd if rd > free[e] else free[e]
                        if bstart is None or stt < bstart - 1e-9:
                            best, bstart = k, stt
                            if stt <= free[e]:
                                break
                    if best is None:
                        continue
                    ins = pl.pop(best)
                    ins['fin'] = bstart + ins['cost']
                    if ins['chan'] is not None:
                        ins['dfin'] = bstart + ins.get('dma_ns', 2000.0)
                    free[e] = ins['fin']
                    new_streams[e].append(ins)
                    nleft -= 1
                    progressed = True
                    break
                assert progressed, "scheduler deadlock"
            tnow = max(free.values())
            for e in ENGS:
                for ins in seg[e]:
                    if ins['chan'] is not None and ins.get('dfin', 0) > tnow:
                        tnow = ins['dfin']
            for e in ENGS:
                if seg[e + '_bar'] is not None:
                    seg[e + '_bar']['fin'] = tnow
                    new_streams[e].append(seg[e + '_bar'])
        self.streams = new_streams
        self.est_ns = tnow

    def emit(self, sched=True):
        nc = self.nc
        import os
        if sched and os.environ.get('NOSCHED') != '1':
            self.schedule()
        for e in ENGS:
            for ins in self.streams[e]:
                for d in ins['deps']:
                    if d['chan'] is None:
                        if d['eng'] == 'pe' and e == 'pe' and ins['chan'] is None and ins['fn'] is not None:
                            continue
                        d['signal'] = True
        nsig = {}
        for e in ENGS:
            c = 0
            for ins in self.streams[e]:
                if ins['chan'] is None and ins['signal'] and ins['fn'] is not None:
                    c += 1
                    ins['cnt'] = c
            nsig[e] = c
        sems = {}
        for e in ENGS:
            ne = max(1, -(-nsig[e] // EPOCH))
            sems[e] = [self.stack.enter_context(nc.semaphore(f"s_{e}{i}")) for i in range(ne)]
        for c in self.chans:
            c.sem = self.stack.enter_context(nc.semaphore(f"c_{c.name}"))
        self.stats = {e: [len(self.streams[e]), nsig[e], 0] for e in ENGS}

        def emit_stream(e, h):
            waited = {}
            nw = 0
            for ins in self.streams[e]:
                need = {}
                for d in ins['deps']:
                    if d['chan'] is None:
                        if d['eng'] == 'pe' and e == 'pe' and ins['chan'] is None and ins['fn'] is not None:
                            continue
                        cnt = d['cnt']
                        ep = (cnt - 1) // EPOCH
                        sem = sems[d['eng']][ep]
                        val = cnt - ep * EPOCH
                    else:
                        sem = d['chan'].sem
                        val = 16 * (d['n'] + 1)
                    k = id(sem)
                    if waited.get(k, 0) >= val:
                        continue
                    if k not in need or need[k][1] < val:
                        need[k] = (sem, val)
                for k, (sem, val) in need.items():
                    h.wait_ge(sem, val)
                    waited[k] = val
                    nw += 1
                if ins['fn'] is None:
                    continue
                r = ins['fn'](h)
                if ins['chan'] is not None:
                    r.then_inc(ins['chan'].sem, 16)
                elif ins['signal']:
                    cnt = ins['cnt']
                    ep = (cnt - 1) // EPOCH
                    r.then_inc(sems[e][ep], 1)
            self.stats[e][2] = nw

        with nc.Block() as block:
            @block.tensor
            def _(h):
                emit_stream('pe', h)

            @block.scalar
            def _(h):
                emit_stream('act', h)

            @block.vector
            def _(h):
                emit_stream('dve', h)

            @block.gpsimd
            def _(h):
                emit_stream('pool', h)

            @block.sync
            def _(h):
                emit_stream('sp', h)
        self.stack.close()


class _Fake:
    def __init__(self):
        self.call = None

    def __getattr__(self, name):
        def f(*a, **k):
            self.call = (name, a, k)
            return self
        return f


def mm_group(P, out_ap, pairs, Rout, reads):
    n = len(pairs)
    for i, (l, r) in enumerate(pairs):
        P.op('pe', lambda h, l=l, r=r, i=i: h.matmul(out_ap, l, r, start=(i == 0), stop=(i == n - 1)),
             reads=reads, writes=[Rout])


def act(P, out, in_, func, reads, writes, scale=None, bias=None):
    kw = {}
    if scale is not None:
        kw['scale'] = scale
    if bias is not None:
        kw['bias'] = bias
    return P.op('act', lambda h: h.activation(out=out, in_=in_, func=func, **kw), reads=reads, writes=writes)


class Ring:
    def __init__(self, P, name, n, shape, dtype, chan=False):
        self.bufs = [P.alloc(shape, dtype) for _ in range(n)]
        self.res = [P.res(f"{name}{i}") for i in range(n)]
        self.ch = [P.chan(f"{name}{i}") for i in range(n)] if chan else None
        self.i = -1
        self.n = n

    def next(self):
        self.i = (self.i + 1) % self.n
        if self.ch:
            return self.bufs[self.i], self.res[self.i], self.ch[self.i]
        return self.bufs[self.i], self.res[self.i]


class Ctx:
    pass


def build(L, NS, debug=False, phases=('A', 'S', 'T', 'C')):
    nc = bass.Bass("TRN2", target_bir_lowering=False)
    P = Prog(nc)
    C = Ctx()
    C.L, C.NS, C.P, C.nc = L, NS, P, nc
    NT = L // 512

    def din(name, shape, dt=F32):
        return nc.dram_tensor(name, list(shape), dt, kind="ExternalInput").ap()

    def dscr(name, shape, dt):
        return nc.dram_tensor(name, list(shape), dt,
                              kind="ExternalOutput" if debug else "Internal").ap()

    C.xT = din("xT", [NS, D, L])
    C.w_in = din("w_in", [D, DIN])
    C.convw = din("convw", [128, 8, 5])
    C.convb = din("convb", [128, 8])
    C.dtb = din("dtb", [128, 16])
    C.alog = din("alog", [128, 16])
    C.dsk = din("dsk", [128, 8])
    C.ang = din("ang", [128, 4])
    C.sng = din("sng", [128, 512])
    C.relb = din("relb", [32, 8])
    C.w_out = din("w_out", [D, D])
    C.w_gate = din("w_gate", [D, DFF])
    C.w_up = din("w_up", [D, DFF])
    C.w_down = din("w_down", [DFF, D])
    C.cst = din("cst", [128, 6, 128])
    C.oh = din("oh", [32, 6, 256])
    C.jmat = din("jmat", [128, 128])
    C.sel = din("sel", [65, 64])
    C.GV = dscr("GV", [6, 8, 256], F32)
    C.lnfm = din("lnfm", [128, 4, 8])
    C.yT = nc.dram_tensor("yT", [NS, D, L], F32, kind="ExternalOutput").ap()
    C.H1F = dscr("H1F", [NS, D, L], F32)
    C.H1B = dscr("H1B", [NS, D, L], BF16)

    C.QT = dscr("QT", [NS, 512, L], BF16)
    C.KT = dscr("KT", [NS, 512, L], BF16)
    C.V = dscr("V", [NS, L, 8 * 65], BF16)
    C.Z = dscr("Z", [NS, L, 512], F32)
    C.XS = dscr("XS", [NS, 512, L], F32)
    C.BC = dscr("BC", [NS, 512, L], BF16)
    C.DT = dscr("DT", [NS, L, 16], F32)

    outs = []
    C.YN = dscr("YN", [NS, 512, L], BF16)
    C.AT = dscr("AT", [NS, 512, L], F32)
    if 'A' in phases:
        phase_A(C)
        P.barrier()
    if 'S' in phases:
        phase_S(C)
        P.barrier()
    if 'T' in phases:
        phase_T(C)
        P.barrier()
    if 'C' in phases:
        phase_C1(C)
        P.barrier()
        outs = phase_C2(C)
        P.barrier()
    if debug:
        evs = [c.last for c in P.chans if c.n > 0]
        P.wait_all('sp', evs)
    else:
        P.wait_all('sp', outs)
    P.emit()
    return nc, P


def phase_A(C):
    P, L, NS = C.P, C.L, C.NS
    NT = L // 512
    m0 = P.mark()
    win = P.alloc([128, 8, DIN], BF16)
    Rwin = P.res("win")
    cw = P.alloc([128, 8, 5], F32)
    cb = P.alloc([128, 8], F32)
    dtb = P.alloc([128, 16], F32)
    Rsm = P.res("small")
    ch_w = P.chan("w")
    w_v = C.w_in.rearrange("(kc p) c -> p kc c", p=128)
    for c0 in (0, 1544):
        P.dma('pool', ch_w, lambda h, c0=c0: h.dma_start(out=win[:, :, c0:c0 + 1544], in_=w_v[:, :, c0:c0 + 1544]),
              writes=[Rwin])
    ch_s = P.chan("small")
    P.dma('sp', ch_s, lambda h: h.dma_start(out=cw, in_=C.convw), writes=[Rsm])
    P.dma('sp', ch_s, lambda h: h.dma_start(out=cb, in_=C.convb), writes=[Rsm])
    P.dma('sp', ch_s, lambda h: h.dma_start(out=dtb, in_=C.dtb), writes=[Rsm])

    xtb = Ring(P, "xtb", 2, [128, 8, 512], BF16, chan=True)
    stq = Ring(P, "stq", 2, [128, 4, 512], BF16, chan=True)
    stk = Ring(P, "stk", 2, [128, 4, 512], BF16, chan=True)
    stx = Ring(P, "stx", 2, [128, 4, 512], F32, chan=True)
    stbc = Ring(P, "stbc", 2, [128, 4, 512], BF16, chan=True)
    stv = Ring(P, "stv", 2, [128, 4, 8, 65], BF16, chan=True)
    stz = Ring(P, "stz", 2, [128, 4, 512], F32, chan=True)
    stdt = Ring(P, "stdt", 2, [128, 4, 16], F32, chan=True)
    raw = [P.alloc([128, 520], F32) for _ in range(8)]
    Rraw = [P.res(f"raw{c}") for c in range(8)]
    acc = Ring(P, "acc", 4, [128, 512], F32)
    dtt = Ring(P, "dtt", 2, [128, 16], F32)
    for b, r in zip(stv.bufs, stv.res):
        P.op('pool', lambda h, b=b: h.memset(b, 1.0), writes=[r])
    fm_banks = [0, 1, 2]
    tm_banks = [3, 4, 5, 6]
    dt_bank = 7
    fmi = [0]
    tmi = [0]

    def next_fm():
        b = fm_banks[fmi[0] % len(fm_banks)]
        fmi[0] += 1
        return b

    def next_tm():
        b = tm_banks[tmi[0] % len(tm_banks)]
        tmi[0] += 1
        return b

    def load_x(s, T):
        buf, r, ch = xtb.next()
        src = C.xT[s].rearrange("(kc p) t -> p kc t", p=128)[:, :, T * 512:(T + 1) * 512]
        P.dma('pool', ch, lambda h: h.dma_start(out=buf, in_=src), writes=[r])
        return buf, r

    def conv_chunk_ops(c, width, accb, Racc):
        ops = []
        rb = raw[c]
        ops.append(lambda: P.op('dve', lambda h: h.tensor_scalar(
            out=accb[:, 0:width], in0=rb[:, 0:width], scalar1=cw[:, c, 0:1], scalar2=cb[:, c:c + 1],
            op0=ALU.mult, op1=ALU.add), reads=[Rraw[c], Rsm], writes=[Racc]))
        for j in range(1, 5):
            ops.append(lambda j=j: P.op('dve', lambda h: h.scalar_tensor_tensor(
                out=accb[:, 0:width], in0=rb[:, j:j + width], scalar=cw[:, c, j:j + 1], in1=accb[:, 0:width],
                op0=ALU.mult, op1=ALU.add), reads=[Rraw[c], Rsm, Racc], writes=[Racc]))
        return ops

    def conv_finish(s, c, width, accb, Racc, xbuf, xr, bcbuf, bcr):
        if c < 4:
            act(P, xbuf[:, c, 0:width], accb[:, 0:width], AF.Silu, [Racc], [xr])
        else:
            act(P, bcbuf[:, c - 4, 0:width], accb[:, 0:width], AF.Silu, [Racc], [bcr])

    for s in range(NS):
        for c in range(8):
            P.op('pool', lambda h, c=c: h.memset(raw[c][:, 0:4], 0.0), writes=[Rraw[c]])
        nxt = load_x(s, 0)
        for T in range(NT):
            xb, xr_ = nxt
            if T + 1 < NT:
                nxt = load_x(s, T + 1)
            t0 = T * 512
            qb, qr, qch = stq.next()
            kb, kr, kch = stk.next()
            for c in range(4):
                b = next_fm()
                mm_group(P, P.banks[b][:, :], [(win[:, kc, COL_Q + c * 128:COL_Q + (c + 1) * 128], xb[:, kc, :])
                                               for kc in range(8)], P.Rbank[b], [Rwin, xr_])
                act(P, qb[:, c, :], P.banks[b][:, :], AF.Identity, [P.Rbank[b]], [qr], scale=0.125)
            P.dma('sp', qch, lambda h, qb=qb, s=s, t0=t0: h.dma_start(
                out=C.QT[s].rearrange("(c p) t -> p c t", p=128)[:, :, t0:t0 + 512], in_=qb), reads=[qr])
            for c in range(4):
                b = next_fm()
                mm_group(P, P.banks[b][:, :], [(win[:, kc, COL_K + c * 128:COL_K + (c + 1) * 128], xb[:, kc, :])
                                               for kc in range(8)], P.Rbank[b], [Rwin, xr_])
                P.op('dve', lambda h, b=b, c=c, kb=kb: h.tensor_copy(out=kb[:, c, :], in_=P.banks[b][:, :]),
                     reads=[P.Rbank[b]], writes=[kr])
            P.dma('sp', kch, lambda h, kb=kb, s=s, t0=t0: h.dma_start(
                out=C.KT[s].rearrange("(c p) t -> p c t", p=128)[:, :, t0:t0 + 512], in_=kb), reads=[kr])
            sxb, sxr, sxch = stx.next()
            sbb, sbr, sbch = stbc.next()
            for cp in range(4):
                chains = []
                accs = []
                for c in (2 * cp, 2 * cp + 1):
                    b = next_fm()
                    mm_group(P, P.banks[b][:, :],
                             [(win[:, kc, COL_X + c * 128:COL_X + (c + 1) * 128], xb[:, kc, :]) for kc in range(8)],
                             P.Rbank[b], [Rwin, xr_])
                    act(P, raw[c][:, 4:516], P.banks[b][:, :], AF.Identity, [P.Rbank[b]], [Rraw[c]])
                    ab, ar = acc.next()
                    accs.append((c, ab, ar))
                    chains.append(conv_chunk_ops(c, 512, ab, ar))
                for j in range(5):
                    for ch_ in chains:
                        ch_[j]()
                for (c, ab, ar) in accs:
                    conv_finish(s, c, 512, ab, ar, sxb, sxr, sbb, sbr)
                    P.op('pool', lambda h, c=c: h.tensor_copy(out=raw[c][:, 0:4], in_=raw[c][:, 512:516]),
                         reads=[Rraw[c]], writes=[Rraw[c]])
            lo = 2 if T == 0 else 0
            P.dma('sp', sxch, lambda h, sxb=sxb, s=s, t0=t0, lo=lo: h.dma_start(
                out=C.XS[s].rearrange("(c p) t -> p c t", p=128)[:, :, t0 - 2 + lo:t0 + 510],
                in_=sxb[:, :, lo:512]), reads=[sxr])
            P.dma('sp', sbch, lambda h, sbb=sbb, s=s, t0=t0, lo=lo: h.dma_start(
                out=C.BC[s].rearrange("(c p) t -> p c t", p=128)[:, :, t0 - 2 + lo:t0 + 510],
                in_=sbb[:, :, lo:512]), reads=[sbr])
            vb, vr, vch = stv.next()
            zb, zr, zch = stz.next()
            db, dr, dch = stdt.next()
            for u in range(4):
                lw = [xb[:, kc, u * 128:(u + 1) * 128] for kc in range(8)]
                b = next_tm()
                mm_group(P, P.banks[b][:, :], [(lw[kc], win[:, kc, COL_V:COL_V + 512]) for kc in range(8)],
                         P.Rbank[b], [Rwin, xr_])
                act(P, vb[:, u, :, 0:64], P.banks[b][:, :].rearrange("p (h d) -> p h d", d=64), AF.Identity,
                    [P.Rbank[b]], [vr])
                b = next_tm()
                mm_group(P, P.banks[b][:, :], [(lw[kc], win[:, kc, COL_Z:COL_Z + 512]) for kc in range(8)],
                         P.Rbank[b], [Rwin, xr_])
                act(P, zb[:, u, :], P.banks[b][:, :], AF.Silu, [P.Rbank[b]], [zr])
                b = dt_bank
                mm_group(P, P.banks[b][:, 0:16], [(lw[kc], win[:, kc, COL_DT:COL_DT + 16]) for kc in range(8)],
                         P.Rbank[b], [Rwin, xr_])
                tb, tr = dtt.next()
                P.op('dve', lambda h, b=b, tb=tb: h.tensor_tensor(out=tb, in0=P.banks[b][:, 0:16], in1=dtb, op=ALU.add),
                     reads=[P.Rbank[b], Rsm], writes=[tr])
                act(P, tb, tb, AF.Exp, [tr], [tr])
                act(P, db[:, u, :], tb, AF.Ln, [tr], [dr], bias=1.0)
            P.dma('sp', vch, lambda h, vb=vb, s=s, t0=t0: h.dma_start(
                out=C.V[s][t0:t0 + 512, :].rearrange("(u p) f -> p u f", p=128),
                in_=vb.rearrange("p u h e -> p u (h e)")), reads=[vr])
            P.dma('sp', zch, lambda h, zb=zb, s=s, t0=t0: h.dma_start(
                out=C.Z[s][t0:t0 + 512, :].rearrange("(u p) f -> p u f", p=128), in_=zb), reads=[zr])
            P.dma('sp', dch, lambda h, db=db, s=s, t0=t0: h.dma_start(
                out=C.DT[s][t0:t0 + 512, :].rearrange("(u p) f -> p u f", p=128), in_=db), reads=[dr])
        sxb, sxr, sxch = stx.next()
        sbb, sbr, sbch = stbc.next()
        for c in range(8):
            P.op('pool', lambda h, c=c: h.memset(raw[c][:, 4:8], 0.0), reads=[Rraw[c]], writes=[Rraw[c]])
            ab, ar = acc.next()
            for o in conv_chunk_ops(c, 2, ab, ar):
                o()
            conv_finish(s, c, 2, ab, ar, sxb, sxr, sbb, sbr)
        P.dma('sp', sxch, lambda h, sxb=sxb, s=s: h.dma_start(
            out=C.XS[s].rearrange("(c p) t -> p c t", p=128)[:, :, L - 2:L], in_=sxb[:, :, 0:2]),
            reads=[sxr])
        P.dma('sp', sbch, lambda h, sbb=sbb, s=s: h.dma_start(
            out=C.BC[s].rearrange("(c p) t -> p c t", p=128)[:, :, L - 2:L], in_=sbb[:, :, 0:2]),
            reads=[sbr])
    P.reset(m0)


def phase_S(C):
    P, L, NS = C.P, C.L, C.NS
    NC = L // 128
    NG = NC // 4
    m0 = P.mark()
    cst = P.alloc([128, 6, 128], F32)
    idb = P.alloc([128, 128], BF16)
    dsk = P.alloc([128, 8], F32)
    Aneg = P.alloc([128, 16], F32)
    sng = P.alloc([128, 512], F32)
    Rc = P.res("s_const")
    chc = P.chan("s_const")
    P.dma('sp', chc, lambda h: h.dma_start(out=cst, in_=C.cst), writes=[Rc])
    P.dma('sp', chc, lambda h: h.dma_start(out=dsk, in_=C.dsk), writes=[Rc])
    P.dma('sp', chc, lambda h: h.dma_start(out=Aneg, in_=C.alog), writes=[Rc])
    P.dma('sp', chc, lambda h: h.dma_start(out=sng, in_=C.sng), writes=[Rc])
    act(P, Aneg, Aneg, AF.Exp, [Rc], [Rc])
    P.op('dve', lambda h: h.tensor_scalar(out=Aneg, in0=Aneg, scalar1=-1.0, scalar2=None, op0=ALU.mult),
         reads=[Rc], writes=[Rc])
    P.op('dve', lambda h: h.tensor_copy(out=idb, in_=cst[:, 5, :]), reads=[Rc], writes=[Rc])
    U_, SL_, LO_, SU_, ON_, ID_ = [cst[:, i, :] for i in range(6)]

    SbAll = P.alloc([128, NC, 512], BF16)
    RSb = [P.res(f"sb{c}") for c in range(NC)]
    gx = Ring(P, "gx", 3, [128, 4, 512], F32, chan=True)
    gbc = Ring(P, "gbc", 3, [128, 4, 512], BF16, chan=True)
    gdt = Ring(P, "gdt", 3, [128, 4, 16], F32, chan=True)
    gz = Ring(P, "gz", 3, [128, 4, 512], F32, chan=True)
    syn = Ring(P, "syn", 2, [128, 4, 512], BF16, chan=True)
    da_r = Ring(P, "da", 3, [128, 16], F32)
    ew_r = Ring(P, "ew", 4, [128, 64], F32)
    sc_r = Ring(P, "sc", 4, [128, 16], F32)
    xdt_r = Ring(P, "xdt", 6, [128, 512], BF16)
    xsd_r = Ring(P, "xsd", 3, [128, 512], F32)
    btok_r = Ring(P, "btok", 3, [128, 256], BF16)
    cbm_r = Ring(P, "cbm", 4, [128, 256], F32)
    L_r = Ring(P, "Lr", 2, [128, 8, 128], F32)
    dec_r = Ring(P, "dec", 2, [128, 8, 128], F32)
    M_r = Ring(P, "Mr", 4, [128, 8, 128], BF16)
    t_r = Ring(P, "tr", 4, [128, 512], F32)
    yn_r = Ring(P, "yn", 2, [128, 512], BF16)
    sm_r = Ring(P, "sm", 4, [128, 4], F32)
    Sf = P.alloc([128, 512], F32)
    Sfb = P.alloc([128, 512], BF16)
    Sb = P.alloc([128, 512], F32)
    RSf, RSfb, RSbr = P.res("Sf"), P.res("Sfb"), P.res("Sbr")

    def bc8(ap8):
        return ap8.unsqueeze(2).to_broadcast([128, 8, 64])

    def v3(ap):
        return ap.rearrange("p (h d) -> p h d", d=64)

    def load_group(s, g, with_z):
        t0 = g * 512
        xb, xr, xch = gx.next()
        P.dma('sp', xch, lambda h: h.dma_start(
            out=xb, in_=C.XS[s].rearrange("(c p) t -> p c t", p=128)[:, :, t0:t0 + 512]), writes=[xr])
        bb, br, bch = gbc.next()
        P.dma('sp', bch, lambda h: h.dma_start(
            out=bb, in_=C.BC[s].rearrange("(c p) t -> p c t", p=128)[:, :, t0:t0 + 512]), writes=[br])
        db, dr, dch = gdt.next()
        P.dma('sp', dch, lambda h: h.dma_start(
            out=db, in_=C.DT[s][t0:t0 + 512, :].rearrange("(u p) f -> p u f", p=128)), writes=[dr])
        zz = None
        if with_z:
            zb, zr, zch = gz.next()
            P.dma('sp', zch, lambda h: h.dma_start(
                out=zb, in_=C.Z[s][t0:t0 + 512, :].rearrange("(u p) f -> p u f", p=128)), writes=[zr])
            zz = (zb, zr)
        return (xb, xr), (bb, br), (db, dr), zz

    def small_mms(da, Rda, mats):
        b = P.next_bank()
        for i, m_ in enumerate(mats):
            P.op('pe', lambda h, i=i, m_=m_, b=b: h.matmul(P.banks[b][:, 16 * i:16 * i + 16], m_, da, start=True, stop=True),
                 reads=[Rc, Rda], writes=[P.Rbank[b]])
        ew, Rew = ew_r.next()
        n = 16 * len(mats)
        act(P, ew[:, 0:n], P.banks[b][:, 0:n], AF.Exp, [P.Rbank[b]], [Rew])
        return ew, Rew

    def xs_transpose(xb, xr, u):
        b = P.next_bank()
        for fc in range(4):
            P.op('pe', lambda h, fc=fc, b=b: h.transpose(P.banks[b][:, fc * 128:(fc + 1) * 128],
                                                        xb[:, fc, u * 128:(u + 1) * 128], ID_),
                 reads=[xr, Rc], writes=[P.Rbank[b]])
        return b

    def b_transpose(bb, br, u):
        b = P.next_bank()
        pb = P.banks[b][:, :].bitcast(BF16)
        for g in range(2):
            P.op('pe', lambda h, g=g, pb=pb: h.transpose(pb[:, g * 128:(g + 1) * 128],
                                                        bb[:, g, u * 128:(u + 1) * 128], idb),
                 reads=[br, Rc], writes=[P.Rbank[b]])
        bt, Rbt = btok_r.next()
        act(P, bt, pb[:, 0:256], AF.Identity, [P.Rbank[b]], [Rbt])
        return bt, Rbt

    def state_mm(bt, Rbt, xw, Rxw):
        b = P.next_bank()
        for g in range(2):
            P.op('pe', lambda h, g=g, b=b: h.matmul(P.banks[b][:, g * 256:(g + 1) * 256], bt[:, g * 128:(g + 1) * 128],
                                                   xw[:, g * 256:(g + 1) * 256], start=True, stop=True),
                 reads=[Rbt, Rxw], writes=[P.Rbank[b]])
        return b

    def s1_front(db, dr, xb, xr, bb, br, u):
        da, Rda = da_r.next()
        P.op('dve', lambda h: h.tensor_tensor(out=da, in0=db[:, u, :], in1=Aneg, op=ALU.mult),
             reads=[dr, Rc], writes=[Rda])
        ew, Rew = small_mms(da, Rda, [SU_, ON_])
        sc, Rsc = sc_r.next()
        P.op('dve', lambda h: h.tensor_tensor(out=sc[:, 0:8], in0=db[:, u, 8:16], in1=ew[:, 8:16], op=ALU.mult),
             reads=[dr, Rew], writes=[Rsc])
        bx = xs_transpose(xb, xr, u)
        xw, Rxw = xdt_r.next()
        P.op('dve', lambda h: h.tensor_tensor(out=v3(xw), in0=v3(P.banks[bx][:, :]), in1=bc8(sc[:, 0:8]), op=ALU.mult),
             reads=[P.Rbank[bx], Rsc], writes=[Rxw])
        bt, Rbt = b_transpose(bb, br, u)
        bs = state_mm(bt, Rbt, xw, Rxw)
        return ew, Rew, bs

    def s1_back(c, ew, Rew, bs):
        P.op('dve', lambda h: h.tensor_tensor(out=v3(Sb), in0=v3(Sb), in1=bc8(ew[:, 24:32]), op=ALU.mult),
             reads=[RSbr, Rew], writes=[RSbr])
        P.op('dve', lambda h: h.tensor_tensor(out=Sb, in0=Sb, in1=P.banks[bs][:, :], op=ALU.add),
             reads=[RSbr, P.Rbank[bs]], writes=[RSbr])
        act(P, SbAll[:, c - 1, :], Sb, AF.Identity, [RSbr], [RSb[c - 1]])

    def s2_front(db, dr, xb, xr, bb, br, u):
        da, Rda = da_r.next()
        P.op('dve', lambda h: h.tensor_tensor(out=da, in0=db[:, u, :], in1=Aneg, op=ALU.mult),
             reads=[dr, Rc], writes=[Rda])
        ew, Rew = small_mms(da, Rda, [U_, SL_, LO_, ON_])
        sc, Rsc = sc_r.next()
        P.op('dve', lambda h: h.tensor_tensor(out=sc[:, 0:8], in0=db[:, u, 0:8], in1=ew[:, 16:24], op=ALU.mult),
             reads=[dr, Rew], writes=[Rsc])
        Ms = []
        Ls = []
        for (tri_l, c0) in ((SL_, 0), (SU_, 8)):
            Lt, RLt = L_r.next()
            P.op('pool', lambda h, Lt=Lt, tri_l=tri_l, c0=c0: h.tensor_tensor(
                out=Lt, in0=tri_l.unsqueeze(1).to_broadcast([128, 8, 128]),
                in1=da[:, c0:c0 + 8].unsqueeze(2).to_broadcast([128, 8, 128]), op=ALU.mult),
                reads=[Rc, Rda], writes=[RLt])
            Ls.append((Lt, RLt))
        bx = xs_transpose(xb, xr, u)
        xsP = v3(P.banks[bx][:, :])
        xf, Rxf = xdt_r.next()
        xbw, Rxbw = xdt_r.next()
        xw, Rxw = xdt_r.next()
        xsd, Rxsd = xsd_r.next()
        P.op('dve', lambda h: h.tensor_tensor(out=v3(xf), in0=xsP, in1=bc8(db[:, u, 0:8]), op=ALU.mult),
             reads=[P.Rbank[bx], dr], writes=[Rxf])
        P.op('dve', lambda h: h.tensor_tensor(out=v3(xbw), in0=xsP, in1=bc8(db[:, u, 8:16]), op=ALU.mult),
             reads=[P.Rbank[bx], dr], writes=[Rxbw])
        P.op('dve', lambda h: h.tensor_tensor(out=v3(xw), in0=xsP, in1=bc8(sc[:, 0:8]), op=ALU.mult),
             reads=[P.Rbank[bx], Rsc], writes=[Rxw])
        P.op('dve', lambda h: h.tensor_tensor(out=v3(xsd), in0=xsP, in1=bc8(dsk), op=ALU.mult),
             reads=[P.Rbank[bx], Rc], writes=[Rxsd])
        bt, Rbt = b_transpose(bb, br, u)
        bcb = P.next_bank()
        for gg in range(2):
            P.op('pe', lambda h, gg=gg: h.matmul(
                P.banks[bcb][:, gg * 128:(gg + 1) * 128], bb[:, gg, u * 128:(u + 1) * 128],
                bb[:, 2 + gg, u * 128:(u + 1) * 128], start=True, stop=True), reads=[br], writes=[P.Rbank[bcb]])
        cbU, RcbU = cbm_r.next()
        cbL, RcbL = cbm_r.next()
        cbP = P.banks[bcb][:, 0:256].rearrange("p (g q) -> p g q", g=2)
        P.op('dve', lambda h: h.tensor_tensor(
            out=cbU.rearrange("p (g q) -> p g q", g=2), in0=cbP,
            in1=U_.unsqueeze(1).to_broadcast([128, 2, 128]), op=ALU.mult),
            reads=[P.Rbank[bcb], Rc], writes=[RcbU])
        P.op('dve', lambda h: h.tensor_tensor(
            out=cbL.rearrange("p (g q) -> p g q", g=2), in0=cbP,
            in1=LO_.unsqueeze(1).to_broadcast([128, 2, 128]), op=ALU.mult),
            reads=[P.Rbank[bcb], Rc], writes=[RcbL])
        for di, (tri_r, cbm, Rcbm) in enumerate(((U_, cbU, RcbU), (LO_, cbL, RcbL))):
            Lt, RLt = Ls[di]
            dec, Rdec = dec_r.next()
            for hh in range(2):
                b = P.next_bank()
                for h4 in range(4):
                    hd = hh * 4 + h4
                    P.op('pe', lambda h, b=b, h4=h4, hd=hd, Lt=Lt, tri_r=tri_r: h.matmul(
                        P.banks[b][:, h4 * 128:(h4 + 1) * 128], Lt[:, hd, :], tri_r, start=True, stop=True),
                        reads=[RLt, Rc], writes=[P.Rbank[b]])
                act(P, dec[:, hh * 4:(hh + 1) * 4, :], P.banks[b][:, :].rearrange("p (a q) -> p a q", a=4),
                    AF.Exp, [P.Rbank[b]], [Rdec])
            Mt, RMt = M_r.next()
            P.op('dve', lambda h, Mt=Mt, dec=dec, cbm=cbm: h.tensor_tensor(
                out=Mt.rearrange("p (g e) q -> p g e q", g=2), in0=dec.rearrange("p (g e) q -> p g e q", g=2),
                in1=cbm.rearrange("p (g q) -> p g q", g=2).unsqueeze(2).to_broadcast([128, 2, 4, 128]),
                op=ALU.mult), reads=[Rdec, Rcbm], writes=[RMt])
            Ms.append((Mt, RMt))
        return dict(ew=ew, Rew=Rew, xf=xf, Rxf=Rxf, xbw=xbw, Rxbw=Rxbw, xw=xw, Rxw=Rxw, xsd=xsd, Rxsd=Rxsd,
                    bt=bt, Rbt=Rbt, Ms=Ms)

    def s2_back(c, u, f, bb, br, zb, zr, yb, yr):
        ew, Rew, xf, Rxf, xbw, Rxbw, xw, Rxw = f['ew'], f['Rew'], f['xf'], f['Rxf'], f['xbw'], f['Rxbw'], f['xw'], f['Rxw']
        xsd, Rxsd, bt, Rbt, Ms = f['xsd'], f['Rxsd'], f['bt'], f['Rbt'], f['Ms']
        by = P.next_bank()
        for hd in range(8):
            P.op('pe', lambda h, hd=hd: h.matmul(
                P.banks[by][:, hd * 64:(hd + 1) * 64], Ms[0][0][:, hd, :], xf[:, hd * 64:(hd + 1) * 64],
                start=True, stop=False), reads=[Ms[0][1], Rxf], writes=[P.Rbank[by]])
            P.op('pe', lambda h, hd=hd: h.matmul(
                P.banks[by][:, hd * 64:(hd + 1) * 64], Ms[1][0][:, hd, :], xbw[:, hd * 64:(hd + 1) * 64],
                start=False, stop=True), reads=[Ms[1][1], Rxbw], writes=[P.Rbank[by]])
        bof = P.next_bank()
        bob = P.next_bank()
        for gg in range(2):
            P.op('pe', lambda h, gg=gg: h.matmul(
                P.banks[bof][:, gg * 256:(gg + 1) * 256], bb[:, 2 + gg, u * 128:(u + 1) * 128],
                Sfb[:, gg * 256:(gg + 1) * 256], start=True, stop=True), reads=[br, RSfb], writes=[P.Rbank[bof]])
        for gg in range(2):
            P.op('pe', lambda h, gg=gg: h.matmul(
                P.banks[bob][:, gg * 256:(gg + 1) * 256], bb[:, 2 + gg, u * 128:(u + 1) * 128],
                SbAll[:, c, gg * 256:(gg + 1) * 256], start=True, stop=True), reads=[br, RSb[c]], writes=[P.Rbank[bob]])
        bs = state_mm(bt, Rbt, xw, Rxw)
        P.op('dve', lambda h: h.tensor_tensor(out=v3(Sf), in0=v3(Sf), in1=bc8(ew[:, 48:56]), op=ALU.mult),
             reads=[RSf, Rew], writes=[RSf])
        P.op('dve', lambda h: h.tensor_tensor(out=Sf, in0=Sf, in1=P.banks[bs][:, :], op=ALU.add),
             reads=[RSf, P.Rbank[bs]], writes=[RSf])
        act(P, Sfb, Sf, AF.Identity, [RSf], [RSfb])
        t1, Rt1 = t_r.next()
        t2, Rt2 = t_r.next()
        P.op('dve', lambda h: h.tensor_tensor(out=v3(t1), in0=v3(P.banks[bof][:, :]), in1=bc8(ew[:, 0:8]), op=ALU.mult),
             reads=[P.Rbank[bof], Rew], writes=[Rt1])
        P.op('dve', lambda h: h.tensor_tensor(out=v3(t2), in0=v3(P.banks[bob][:, :]), in1=bc8(ew[:, 40:48]), op=ALU.mult),
             reads=[P.Rbank[bob], Rew], writes=[Rt2])
        P.op('pool', lambda h: h.tensor_tensor(out=t1, in0=t1, in1=t2, op=ALU.add), reads=[Rt1, Rt2], writes=[Rt1])
        P.op('pool', lambda h: h.tensor_tensor(out=t1, in0=t1, in1=xsd, op=ALU.add), reads=[Rt1, Rxsd], writes=[Rt1])
        P.op('dve', lambda h: h.tensor_tensor(out=t1, in0=t1, in1=P.banks[by][:, :], op=ALU.add),
             reads=[Rt1, P.Rbank[by]], writes=[Rt1])
        P.op('dve', lambda h: h.tensor_tensor(out=t1, in0=t1, in1=zb[:, u, :], op=ALU.mult),
             reads=[Rt1, zr], writes=[Rt1])
        sm, Rsm_ = sm_r.next()
        P.op('act', lambda h: h.activation(out=t2, in_=t1, func=AF.Square, accum_out=sm[:, 0:1]),
             reads=[Rt1, Rt2], writes=[Rt2, Rsm_])
        act(P, sm[:, 1:2], sm[:, 0:1], AF.Ln, [Rsm_], [Rsm_], scale=1.0 / 512, bias=EPS)
        act(P, sm[:, 2:3], sm[:, 1:2], AF.Exp, [Rsm_], [Rsm_], scale=-0.5)
        yn, Ryn = yn_r.next()
        P.op('dve', lambda h: h.scalar_tensor_tensor(
            out=yn, in0=t1, scalar=sm[:, 2:3], in1=sng, op0=ALU.mult, op1=ALU.mult),
            reads=[Rt1, Rsm_, Rc], writes=[Ryn])
        bt_ = P.next_bank()
        pbt = P.banks[bt_][:, :].bitcast(BF16)
        for fc in range(4):
            P.op('pe', lambda h, fc=fc: h.transpose(
                pbt[:, fc * 128:(fc + 1) * 128], yn[:, fc * 128:(fc + 1) * 128], idb),
                reads=[Ryn, Rc], writes=[P.Rbank[bt_]])
        act(P, yb[:, :, u * 128:(u + 1) * 128], pbt[:, 0:512].rearrange("p (c t) -> p c t", c=4), AF.Identity,
            [P.Rbank[bt_]], [yr])

    ybuf = {}

    def finish_back(p):
        c, u, g, fr, bb, br, zb, zr = p
        if u == 0:
            ybuf['cur'] = syn.next()
        yb, yr, ych = ybuf['cur']
        s2_back(c, u, fr, bb, br, zb, zr, yb, yr)
        if u == 3:
            s_ = ybuf['s']
            P.dma('sp', ych, lambda h: h.dma_start(
                out=C.YN[s_].rearrange("(c p) t -> p c t", p=128)[:, :, g * 512:(g + 1) * 512], in_=yb), reads=[yr])

    for s in range(NS):
        ybuf['s'] = s
        P.op('pool', lambda h: h.memset(Sb, 0.0), writes=[RSbr])
        P.op('pool', lambda h: h.memset(SbAll[:, NC - 1, :], 0.0), writes=[RSb[NC - 1]])
        groups = {}
        groups[NG - 1] = load_group(s, NG - 1, False)
        if NG > 1:
            groups[NG - 2] = load_group(s, NG - 2, False)
        pend = None
        for c in range(NC - 1, 0, -1):
            g, u = divmod(c, 4)
            (xb, xr), (bb, br), (db, dr), _ = groups[g]
            fr = s1_front(db, dr, xb, xr, bb, br, u)
            if pend is not None:
                s1_back(*pend)
            pend = (c,) + fr
            if u == 3 and g - 2 >= 0:
                groups[g - 2] = load_group(s, g - 2, False)
        if pend is not None:
            s1_back(*pend)
        P.op('pool', lambda h: h.memset(Sf, 0.0), writes=[RSf])
        P.op('pool', lambda h: h.memset(Sfb, 0.0), writes=[RSfb])
        groups = {0: load_group(s, 0, True)}
        if NG > 1:
            groups[1] = load_group(s, 1, True)
        ybs = {}
        pend = None
        for c in range(NC):
            g, u = divmod(c, 4)
            (xb, xr), (bb, br), (db, dr), (zb, zr) = groups[g]
            fr = s2_front(db, dr, xb, xr, bb, br, u)
            if pend is not None:
                finish_back(pend)
            pend = (c, u, g, fr, bb, br, zb, zr)
            if u == 0 and g + 2 < NG:
                groups[g + 2] = load_group(s, g + 2, True)
        finish_back(pend)
    P.reset(m0)


BRANCH_DIL = (1, 4, 16)


def phase_T(C):
    P, L, NS = C.P, C.L, C.NS
    m0 = P.mark()
    cst = P.alloc([128, 6, 128], F32)
    jmat = P.alloc([128, 128], F32)
    sel = P.alloc([65, 64], F32)
    EB = P.alloc([128, 4, 3, 2, 256], F32)
    Rc = P.res("t_const")
    REB = P.res("EB")
    chc = P.chan("t_const")
    P.dma('sp', chc, lambda h: h.dma_start(out=cst, in_=C.cst), writes=[Rc])
    P.dma('sp', chc, lambda h: h.dma_start(out=jmat, in_=C.jmat), writes=[Rc])
    P.dma('sp', chc, lambda h: h.dma_start(out=sel, in_=C.sel), writes=[Rc])
    U_, LO_ = cst[:, 0, :], cst[:, 2, :]
    m1 = P.mark()
    oh = P.alloc([32, 6, 256], F32)
    relb = P.alloc([32, 8], F32)
    P.dma('sp', chc, lambda h: h.dma_start(out=oh, in_=C.oh), writes=[Rc])
    P.dma('sp', chc, lambda h: h.dma_start(out=relb, in_=C.relb), writes=[Rc])
    gs_r = Ring(P, "gs", 2, [8, 256], F32, chan=True)
    hk_r = Ring(P, "hk", 4, [128, 128], F32, chan=True)
    tmp_r = Ring(P, "ebtmp", 3, [128, 128], F32)
    RGV = [P.res(f"gv{i}") for i in range(6)]
    for bt in range(6):
        b = P.next_bank()
        P.op('pe', lambda h, b=b, bt=bt: h.matmul(P.banks[b][0:8, 0:256], relb, oh[:, bt, :], start=True, stop=True),
             reads=[Rc], writes=[P.Rbank[b]])
        gs, Rgs, gch = gs_r.next()
        P.op('dve', lambda h, gs=gs, b=b: h.tensor_copy(out=gs, in_=P.banks[b][0:8, 0:256]), reads=[P.Rbank[b]], writes=[Rgs])
        P.dma('sp', gch, lambda h, gs=gs, bt=bt: h.dma_start(out=C.GV[bt], in_=gs), reads=[Rgs], writes=[RGV[bt]])
    for bt in range(6):
        bi, ty = bt // 2, bt % 2
        for hd in range(8):
            hk, Rhk, hch = hk_r.next()
            src = bass.AP(C.GV.tensor, (bt * 8 + hd) * 256, [[1, 128], [1, 128]])
            P.dma('sp', hch, lambda h, hk=hk, src=src: h.dma_start(out=hk, in_=src), reads=[RGV[bt]], writes=[Rhk])
            b = P.next_bank()
            P.op('pe', lambda h, b=b, hk=hk: h.matmul(P.banks[b][:, 0:128], hk, jmat, start=True, stop=True),
                 reads=[Rhk, Rc], writes=[P.Rbank[b]])
            tmp, Rtmp = tmp_r.next()
            act(P, tmp, P.banks[b][:, 0:128], AF.Exp, [P.Rbank[b]], [Rtmp])
            msk = U_ if ty == 0 else LO_
            P.op('dve', lambda h, tmp=tmp, msk=msk, hd=hd, bi=bi, ty=ty: h.tensor_tensor(
                out=EB[:, hd // 2, bi, hd % 2, ty * 128:(ty + 1) * 128], in0=tmp, in1=msk, op=ALU.mult),
                reads=[Rtmp, Rc], writes=[REB])
    P.barrier()
    P.reset(m1)

    PADK = 1024
    Qbd = P.alloc([128, 2, L], BF16)
    KTp = P.alloc([128, L + 2 * PADK], BF16)
    OT = P.alloc([128, 2, L], F32)
    NTmax = L // 128 + 16
    Vbs = [P.alloc([128, NTmax, 130], BF16) for _ in range(2)]
    RVs = [P.res("Vb0"), P.res("Vb1")]
    RQ, RK, ROT = P.res("Qbd"), P.res("KTp"), P.res("OT")
    chq, chk = P.chan("q"), P.chan("k")
    chv2 = [[P.chan(f"v{i}_{k}") for k in range(4)] for i in range(2)]
    E_r = Ring(P, "E", 3, [128, 512], F32)
    P_r = Ring(P, "Pt", 4, [128, 2, 256], BF16)
    sg_r = Ring(P, "osg", 4, [65, 128], F32)
    rc_r = Ring(P, "rc", 2, [64, 512], F32)
    so_r = Ring(P, "so", 2, [64, 512], F32, chan=True)
    dummy = P.alloc([128, 16], F32)
    P.op('pool', lambda h: h.memset(Qbd, 0.0), writes=[RQ])
    P.op('pool', lambda h: h.memset(KTp, 0.0), writes=[RK])
    S_banks = [0, 1, 2, 3]
    si = [0]
    O_bank = {(0, 0): 4, (0, 1): 5, (1, 0): 6, (1, 1): 7}

    def load_V(s, hp, bi, slot):
        dl = BRANCH_DIL[bi]
        Ld = L // dl
        NJ = Ld // 128 + 1
        Vb, RV = Vbs[slot], RVs[slot]
        Vs = C.V[s]
        c0, c1 = hp * 130, (hp + 1) * 130
        for rho in range(dl):
            tb = rho * NJ
            P.op('pool', lambda h, tb=tb, Vb=Vb: h.memset(Vb[0:64, tb, :], 0.0), writes=[RV])
            P.op('pool', lambda h, tb=tb, NJ=NJ, Vb=Vb: h.memset(Vb[64:128, tb + NJ - 1, :], 0.0), writes=[RV])
            ch_ = chv2[slot][rho % 4]
            if NJ > 2:
                src = Vs[rho + dl * 64:rho + dl * 64 + dl * 128 * (NJ - 2):dl, c0:c1]
                P.dma('sp', ch_, lambda h, tb=tb, NJ=NJ, src=src, Vb=Vb: h.dma_start(
                    out=Vb[:, tb + 1:tb + NJ - 1, :], in_=src.rearrange("(j a) f -> a j f", a=128)), writes=[RV])
            r_first = Vs[rho:rho + dl * 63 + 1:dl, c0:c1]
            t_l = rho + dl * (Ld - 64)
            r_last = Vs[t_l:t_l + dl * 63 + 1:dl, c0:c1]
            P.dma('sp', ch_, lambda h, tb=tb, r_first=r_first, Vb=Vb: h.dma_start(out=Vb[64:128, tb, :], in_=r_first), writes=[RV])
            P.dma('sp', ch_, lambda h, tb=tb, NJ=NJ, r_last=r_last, Vb=Vb: h.dma_start(
                out=Vb[0:64, tb + NJ - 1, :], in_=r_last), writes=[RV])

    work = [(s, hp, bi) for s in range(NS) for hp in range(4) for bi in range(3)]
    load_V(*work[0], 0)
    def do_work(wi, s, hp, bi):
        slot = wi % 2
        Vb, RV = Vbs[slot], RVs[slot]
        dl = BRANCH_DIL[bi]
        Ld = L // dl
        NQ = Ld // 128
        NJ = NQ + 1
        if bi == 0:
            r0 = hp * 128
            P.dma('sp', chq, lambda h, s=s, r0=r0: h.dma_start(out=Qbd[0:64, 0, :], in_=C.QT[s][r0:r0 + 64, :]), writes=[RQ])
            P.dma('sp', chq, lambda h, s=s, r0=r0: h.dma_start(out=Qbd[64:128, 1, :], in_=C.QT[s][r0 + 64:r0 + 128, :]), writes=[RQ])
            P.dma('sp', chk, lambda h, s=s, r0=r0: h.dma_start(out=KTp[:, PADK:PADK + L], in_=C.KT[s][r0:r0 + 128, :]), writes=[RK])
            P.op('pool', lambda h: h.memset(OT, 0.0), writes=[ROT])
        else:
            P.op('pool', lambda h: h.memset(dummy, 0.0), writes=[ROT])
        if wi + 1 < len(work):
            load_V(*work[wi + 1], 1 - slot)
        tiles = [(rho, j) for rho in range(dl) for j in range(NJ)]
        st = {}

        def front(t):
            rho, j = tiles[t]
            halves = [hf for hf in (0, 1) if 0 <= j - 1 + hf < NQ]
            h0, h1 = halves[0], halves[-1] + 1
            nq = (h1 - h0) * 128
            qs = rho + dl * (128 * (j - 1) + h0 * 128)
            ks = PADK + rho + dl * (128 * j - 64)
            bS = S_banks[si[0] % 4]
            si[0] += 1
            P.op('pe', lambda h: h.matmul(
                P.banks[bS][:, :].rearrange("p (h q) -> p h q", h=2)[:, :, h0 * 128:h1 * 128],
                KTp[:, ks:ks + dl * 127 + 1:dl],
                Qbd[:, :, qs:qs + dl * (nq - 1) + 1:dl], start=True, stop=True),
                reads=[RK, RQ], writes=[P.Rbank[bS]])
            Et, REt = E_r.next()
            Pt, RPt = P_r.next()
            Ev = Et.rearrange("p (h q) -> p h q", h=2)[:, :, h0 * 128:h1 * 128]
            act(P, Ev, P.banks[bS][:, :].rearrange("p (h q) -> p h q", h=2)[:, :, h0 * 128:h1 * 128],
                AF.Exp, [P.Rbank[bS]], [REt])
            P.op('dve', lambda h: h.tensor_tensor(
                out=Pt[:, :, h0 * 128:h1 * 128], in0=Ev, in1=EB[:, hp, bi, :, h0 * 128:h1 * 128], op=ALU.mult),
                reads=[REt, REB], writes=[RPt])
            st[t] = (Pt, RPt, halves)

        def back(t):
            rho, j = tiles[t]
            Pt, RPt, halves = st.pop(t)
            tile = rho * NJ + j
            for hd in range(2):
                for hf in halves:
                    m = j - 1 + hf
                    bO = O_bank[(hd, m % 2)]
                    P.op('pe', lambda h, bO=bO, hd=hd, hf=hf: h.matmul(
                        P.banks[bO][0:65, 0:128], Vb[:, tile, hd * 65:(hd + 1) * 65],
                        Pt[:, hd, hf * 128:(hf + 1) * 128], start=(hf == 1), stop=(hf == 0)),
                        reads=[RV, RPt], writes=[P.Rbank[bO]])
                    if hf == 0:
                        t0 = rho + dl * 128 * m
                        ov = OT[0:65, hd, t0:t0 + dl * 127 + 1:dl]
                        sg, Rsg = sg_r.next()
                        act(P, sg, P.banks[bO][0:65, 0:128], AF.Identity, [P.Rbank[bO]], [Rsg])
                        P.op('pool', lambda h, ov=ov, sg=sg: h.tensor_tensor(out=ov, in0=ov, in1=sg, op=ALU.add),
                             reads=[ROT, Rsg], writes=[])
        n = len(tiles)
        for t in range(n + 2):
            if t < n:
                front(t)
            if t >= 2:
                back(t - 2)
        if bi == 2:
            P.op('pool', lambda h: h.memset(dummy, 0.0), writes=[ROT])
            for hd in range(2):
                for ct in range(L // 512):
                    b = S_banks[si[0] % 4]
                    si[0] += 1
                    P.op('pe', lambda h, b=b, hd=hd, ct=ct: h.matmul(
                        P.banks[b][0:64, :], sel, OT[0:65, hd, ct * 512:(ct + 1) * 512], start=True, stop=True),
                        reads=[Rc, ROT], writes=[P.Rbank[b]])
                    rc, Rrc = rc_r.next()
                    act(P, rc, P.banks[b][0:64, :], AF.Ln, [P.Rbank[b]], [Rrc])
                    act(P, rc, rc, AF.Exp, [Rrc], [Rrc], scale=-1.0)
                    so, Rso, soch = so_r.next()
                    P.op('dve', lambda h, so=so, rc=rc, hd=hd, ct=ct: h.tensor_tensor(
                        out=so, in0=OT[0:64, hd, ct * 512:(ct + 1) * 512], in1=rc, op=ALU.mult),
                        reads=[ROT, Rrc], writes=[Rso])
                    rr = hp * 128 + hd * 64
                    P.dma('sp', soch, lambda h, so=so, s=s, rr=rr, ct=ct: h.dma_start(
                        out=C.AT[s][rr:rr + 64, ct * 512:(ct + 1) * 512], in_=so), reads=[Rso])

    for wi, (s, hp, bi) in enumerate(work):
        do_work(wi, s, hp, bi)
    P.reset(m0)


def _ln_feature_major(P, C, tt, Rtt, nd, T, S1, S2, cst_ones, Rc, mk_out):
    mean = C.ln_mean
    m2 = C.ln_m2
    rstd = C.ln_rstd
    Rst = C.ln_Rst
    act(P, mean[:, 0:T], P.banks[S1][:, 0:T], AF.Identity, [P.Rbank[S1]], [Rst], scale=1.0 / D)
    P.op('dve', lambda h: h.tensor_tensor(out=m2[:, 0:T], in0=mean[:, 0:T], in1=mean[:, 0:T], op=ALU.mult),
         reads=[Rst], writes=[Rst])
    P.op('dve', lambda h: h.scalar_tensor_tensor(out=m2[:, 0:T], in0=P.banks[S2][:, 0:T], scalar=1.0 / D,
                                                 in1=m2[:, 0:T], op0=ALU.mult, op1=ALU.subtract),
         reads=[P.Rbank[S2], Rst], writes=[Rst])
    act(P, m2[:, 0:T], m2[:, 0:T], AF.Ln, [Rst], [Rst], bias=EPS)
    act(P, rstd[:, 0:T], m2[:, 0:T], AF.Exp, [Rst], [Rst], scale=-0.5)
    for dc in range(nd):
        u1, Ru1 = C.ln_u.next()
        P.op('dve', lambda h, u1=u1, dc=dc: h.tensor_tensor(out=u1[:, 0:T], in0=tt[:, dc, 0:T], in1=mean[:, 0:T], op=ALU.subtract),
             reads=[Rtt[dc], Rst], writes=[Ru1])
        P.op('dve', lambda h, u1=u1: h.tensor_tensor(out=u1[:, 0:T], in0=u1[:, 0:T], in1=rstd[:, 0:T], op=ALU.mult),
             reads=[Ru1, Rst], writes=[Ru1])
        mk_out(dc, u1, Ru1)


def phase_C1(C):
    P, L, NS = C.P, C.L, C.NS
    T = 512
    m0 = P.mark()
    P.bank_list = [0, 1, 2, 3, 4]
    SA, S1, S2 = 5, 6, 7
    wout = P.alloc([128, 8, D], BF16)
    cst = P.alloc([128, 6, 128], F32)
    ang = P.alloc([128, 4], F32)
    lnfm = P.alloc([128, 4, 8], F32)
    Rc = P.res("c1_const")
    chc = P.chan("c1_const")
    P.dma('pool', chc, lambda h: h.dma_start(out=wout, in_=C.w_out.rearrange("(kc p) c -> p kc c", p=128)), writes=[Rc])
    P.dma('sp', chc, lambda h: h.dma_start(out=cst, in_=C.cst), writes=[Rc])
    P.dma('sp', chc, lambda h: h.dma_start(out=ang, in_=C.ang), writes=[Rc])
    P.dma('sp', chc, lambda h: h.dma_start(out=lnfm, in_=C.lnfm), writes=[Rc])
    ONES = cst[:, 4, :]
    xt_r = Ring(P, "c1x", 3, [128, 8, T], F32, chan=True)
    at_r = Ring(P, "c1a", 2, [128, 4, T], F32, chan=True)
    yn_r = Ring(P, "c1y", 3, [128, 4, T], BF16, chan=True)
    an_r = Ring(P, "c1an", 2, [128, 4, T], BF16)
    sq_r = Ring(P, "c1sq", 3, [128, T], F32)
    sqx_r = Ring(P, "c1sqx", 2, [128, T], F32)
    rsa_r = Ring(P, "c1rsa", 2, [128, T], F32)
    tts = [P.alloc([128, 8, T], F32) for _ in range(2)]
    Rtts = [[P.res(f"tt{k}_{i}") for i in range(8)] for k in range(2)]
    C.ln_mean = P.alloc([128, T], F32)
    C.ln_m2 = P.alloc([128, T], F32)
    C.ln_rstd = P.alloc([128, T], F32)
    C.ln_Rst = P.res("lnst")
    C.ln_u = Ring(P, "lnu", 3, [128, T], F32)
    hf_r = Ring(P, "c1hf", 1, [128, 8, T], F32, chan=True)
    hb_r = Ring(P, "c1hb", 1, [128, 8, T], BF16, chan=True)
    tiles = [(s, t0) for s in range(NS) for t0 in range(0, L, T)]

    def load(s, t0):
        xb, xr, xch = xt_r.next()
        P.dma('sp', xch, lambda h: h.dma_start(out=xb, in_=C.xT[s].rearrange("(c p) t -> p c t", p=128)[:, :, t0:t0 + T]), writes=[xr])
        ab, ar, ach = at_r.next()
        P.dma('sp', ach, lambda h: h.dma_start(out=ab, in_=C.AT[s].rearrange("(c p) t -> p c t", p=128)[:, :, t0:t0 + T]), writes=[ar])
        yb, yr, ych = yn_r.next()
        P.dma('sp', ych, lambda h: h.dma_start(out=yb, in_=C.YN[s].rearrange("(c p) t -> p c t", p=128)[:, :, t0:t0 + T]), writes=[yr])
        return (xb, xr), (ab, ar), (yb, yr)

    def stage_X(ld):
        (xb, xr), (ab, ar), (yb, yr) = ld
        for fc in range(4):
            sq, Rsq = sqx_r.next()
            act(P, sq, ab[:, fc, :], AF.Square, [ar], [Rsq])
            P.op('pe', lambda h, sq=sq, fc=fc: h.matmul(P.banks[SA][:, :], ONES, sq, start=(fc == 0), stop=(fc == 3)),
                 reads=[Rc, Rsq], writes=[P.Rbank[SA]])
        rsa, Rrsa = rsa_r.next()
        act(P, rsa, P.banks[SA][:, :], AF.Ln, [P.Rbank[SA]], [Rrsa], scale=1.0 / 512, bias=EPS)
        act(P, rsa, rsa, AF.Exp, [Rrsa], [Rrsa], scale=-0.5)
        an, Ran = an_r.next()
        for fc in range(4):
            P.op('dve', lambda h, fc=fc: h.scalar_tensor_tensor(
                out=an[:, fc, :], in0=ab[:, fc, :], scalar=ang[:, fc:fc + 1], in1=rsa, op0=ALU.mult, op1=ALU.mult),
                reads=[ar, Rc, Rrsa], writes=[Ran])
        return (xb, xr), (yb, yr), (an, Ran)

    def stage_YZ(ti, xs_, inject):
        s, t0 = tiles[ti]
        (xb, xr), (yb, yr), (an, Ran) = xs_
        tt, Rtt = tts[ti % 2], Rtts[ti % 2]
        pend = None
        nxt_x = None
        for dc in range(8):
            if dc == 4 and inject is not None:
                nxt_x = inject()
            b = P.next_bank()
            mm_group(P, P.banks[b][:, :],
                     [(wout[:, kc, dc * 128:(dc + 1) * 128], an[:, kc, :] if kc < 4 else yb[:, kc - 4, :]) for kc in range(8)],
                     P.Rbank[b], [Rc, Ran, yr])
            P.op('dve', lambda h, dc=dc, b=b: h.scalar_tensor_tensor(
                out=tt[:, dc, :], in0=xb[:, dc, :], scalar=ALPHA, in1=P.banks[b][:, :], op0=ALU.mult, op1=ALU.add),
                reads=[xr, P.Rbank[b]], writes=[Rtt[dc]])
            sq, Rsq = sq_r.next()
            act(P, sq, tt[:, dc, :], AF.Square, [Rtt[dc]], [Rsq])

            def stats(dc=dc, sq=sq, Rsq=Rsq):
                P.op('pe', lambda h: h.matmul(P.banks[S1][:, :], ONES, tt[:, dc, :], start=(dc == 0), stop=(dc == 7)),
                     reads=[Rc, Rtt[dc]], writes=[P.Rbank[S1]])
                P.op('pe', lambda h: h.matmul(P.banks[S2][:, :], ONES, sq, start=(dc == 0), stop=(dc == 7)),
                     reads=[Rc, Rsq], writes=[P.Rbank[S2]])
            if pend is not None:
                pend()
            pend = stats
        pend()
        hf, Rhf, hfch = hf_r.next()
        hb, Rhb, hbch = hb_r.next()

        def mk_out(dc, u1, Ru1):
            act(P, hf[:, dc, :], u1, AF.Identity, [Ru1, Rc], [Rhf], scale=lnfm[:, 0, dc:dc + 1], bias=lnfm[:, 1, dc:dc + 1])
            act(P, hb[:, dc, :], u1, AF.Identity, [Ru1, Rc], [Rhb], scale=lnfm[:, 0, dc:dc + 1], bias=lnfm[:, 1, dc:dc + 1])
        _ln_feature_major(P, C, tt, Rtt, 8, T, S1, S2, ONES, Rc, mk_out)
        P.dma('sp', hfch, lambda h: h.dma_start(
            out=C.H1F[s].rearrange("(c p) t -> p c t", p=128)[:, :, t0:t0 + T], in_=hf), reads=[Rhf])
        P.dma('sp', hbch, lambda h: h.dma_start(
            out=C.H1B[s].rearrange("(c p) t -> p c t", p=128)[:, :, t0:t0 + T], in_=hb), reads=[Rhb])
        return nxt_x

    ld = [None] * (len(tiles) + 2)
    ld[0] = load(*tiles[0])
    if len(tiles) > 1:
        ld[1] = load(*tiles[1])
    xs_next = stage_X(ld[0])
    for ti in range(len(tiles)):
        xs_cur = xs_next
        if ti + 2 < len(tiles):
            ld[ti + 2] = load(*tiles[ti + 2])
        inj = (lambda ti=ti: stage_X(ld[ti + 1])) if ti + 1 < len(tiles) else None
        xs_next = stage_YZ(ti, xs_cur, inj)
    P.bank_list = list(range(8))
    P.reset(m0)


def phase_C2(C):
    P, L, NS = C.P, C.L, C.NS
    T = 256
    NF = DFF // 128
    m0 = P.mark()
    P.bank_list = [0, 1, 2, 3, 4, 5]
    S1, S2 = 6, 7
    wg = P.alloc([128, 8, DFF], BF16)
    wu = P.alloc([128, 8, DFF], BF16)
    wd = P.alloc([128, NF, D], BF16)
    cst = P.alloc([128, 6, 128], F32)
    lnfm = P.alloc([128, 4, 8], F32)
    Rc = P.res("c2_const")
    chc = P.chan("c2_const")
    chw = [P.chan(f"c2w{i}") for i in range(3)]
    for c0 in (0, 1408):
        P.dma('pool', chw[0], lambda h, c0=c0: h.dma_start(
            out=wg[:, :, c0:c0 + 1408], in_=C.w_gate.rearrange("(kc p) c -> p kc c", p=128)[:, :, c0:c0 + 1408]), writes=[Rc])
        P.dma('pool', chw[1], lambda h, c0=c0: h.dma_start(
            out=wu[:, :, c0:c0 + 1408], in_=C.w_up.rearrange("(kc p) c -> p kc c", p=128)[:, :, c0:c0 + 1408]), writes=[Rc])
    P.dma('pool', chw[2], lambda h: h.dma_start(out=wd, in_=C.w_down.rearrange("(kc p) c -> p kc c", p=128)), writes=[Rc])
    P.dma('sp', chc, lambda h: h.dma_start(out=cst, in_=C.cst), writes=[Rc])
    P.dma('sp', chc, lambda h: h.dma_start(out=lnfm, in_=C.lnfm), writes=[Rc])
    ONES = cst[:, 4, :]
    hb_r = Ring(P, "c2hb", 2, [128, 8, T], BF16, chan=True)
    hf_r = Ring(P, "c2hf", 4, [128, T], F32, chan=True)
    hid = P.alloc([128, NF, T], BF16)
    Rhid = P.res("hid")
    sg_r = Ring(P, "c2sg", 3, [128, T], F32)
    sq_r = Ring(P, "c2sq", 4, [128, T], F32)
    tt = P.alloc([128, 8, T], F32)
    Rtt = [P.res(f"tt2_{i}") for i in range(8)]
    C.ln_mean = P.alloc([128, T], F32)
    C.ln_m2 = P.alloc([128, T], F32)
    C.ln_rstd = P.alloc([128, T], F32)
    C.ln_Rst = P.res("lnst2")
    C.ln_u = Ring(P, "lnu2", 3, [128, T], F32)
    yo_r = Ring(P, "c2yo", 4, [128, T], F32, chan=True)
    outs = []

    def load(s, t0):
        hb, hr, hch = hb_r.next()
        P.dma('sp', hch, lambda h: h.dma_start(out=hb, in_=C.H1B[s].rearrange("(c p) t -> p c t", p=128)[:, :, t0:t0 + T]), writes=[hr])
        return hb, hr

    tiles = [(s, t0) for s in range(NS) for t0 in range(0, L, T)]
    nxt = load(*tiles[0])
    for ti, (s, t0) in enumerate(tiles):
        hb, hr = nxt
        if ti + 1 < len(tiles):
            nxt = load(*tiles[ti + 1])
        for fc in range(NF):
            bg = P.next_bank()
            mm_group(P, P.banks[bg][:, 0:T], [(wg[:, kc, fc * 128:(fc + 1) * 128], hb[:, kc, :]) for kc in range(8)],
                     P.Rbank[bg], [Rc, hr])
            bu = P.next_bank()
            mm_group(P, P.banks[bu][:, 0:T], [(wu[:, kc, fc * 128:(fc + 1) * 128], hb[:, kc, :]) for kc in range(8)],
                     P.Rbank[bu], [Rc, hr])
            sg, Rsg = sg_r.next()
            act(P, sg, P.banks[bg][:, 0:T], AF.Silu, [P.Rbank[bg]], [Rsg])
            P.op('dve', lambda h, fc=fc, sg=sg, bu=bu: h.tensor_tensor(out=hid[:, fc, :], in0=sg, in1=P.banks[bu][:, 0:T], op=ALU.mult),
                 reads=[Rsg, P.Rbank[bu]], writes=[Rhid])
        pend = None
        for dc in range(8):
            hf, Rhf, hfch = hf_r.next()
            P.dma('sp', hfch, lambda h, hf=hf, s=s, t0=t0, dc=dc: h.dma_start(
                out=hf, in_=C.H1F[s][dc * 128:(dc + 1) * 128, t0:t0 + T]), writes=[Rhf])
            b = P.next_bank()
            mm_group(P, P.banks[b][:, 0:T], [(wd[:, fc, dc * 128:(dc + 1) * 128], hid[:, fc, :]) for fc in range(NF)],
                     P.Rbank[b], [Rc, Rhid])
            P.op('dve', lambda h, dc=dc, b=b, hf=hf: h.scalar_tensor_tensor(
                out=tt[:, dc, :], in0=hf, scalar=ALPHA, in1=P.banks[b][:, 0:T], op0=ALU.mult, op1=ALU.add),
                reads=[Rhf, P.Rbank[b]], writes=[Rtt[dc]])
            sq, Rsq = sq_r.next()
            act(P, sq, tt[:, dc, :], AF.Square, [Rtt[dc]], [Rsq])

            def stats(dc=dc, sq=sq, Rsq=Rsq):
                P.op('pe', lambda h: h.matmul(P.banks[S1][:, 0:T], ONES, tt[:, dc, :], start=(dc == 0), stop=(dc == 7)),
                     reads=[Rc, Rtt[dc]], writes=[P.Rbank[S1]])
                P.op('pe', lambda h: h.matmul(P.banks[S2][:, 0:T], ONES, sq, start=(dc == 0), stop=(dc == 7)),
                     reads=[Rc, Rsq], writes=[P.Rbank[S2]])
            if pend is not None:
                pend()
            pend = stats
        pend()
        pend = None

        def mk_out(dc, u1, Ru1, s=s, t0=t0):
            yo, Ryo, yoch = yo_r.next()
            act(P, yo, u1[:, 0:T], AF.Identity, [Ru1, Rc], [Ryo], scale=lnfm[:, 2, dc:dc + 1], bias=lnfm[:, 3, dc:dc + 1])
            outs.append(P.dma('sp', yoch, lambda h, yo=yo, dc=dc: h.dma_start(
                out=C.yT[s][dc * 128:(dc + 1) * 128, t0:t0 + T], in_=yo), reads=[Ryo]))
        _ln_feature_major(P, C, tt, Rtt, 8, T, S1, S2, ONES, Rc, mk_out)
    P.bank_list = list(range(8))
    P.reset(m0)
    return outs

def make_cst():
    i = np.arange(128)
    U = (i[:, None] <= i[None, :]).astype(np.float32)
    SL = (i[:, None] > i[None, :]).astype(np.float32)
    Lo = (i[:, None] >= i[None, :]).astype(np.float32)
    SU = (i[:, None] < i[None, :]).astype(np.float32)
    ones = np.ones((128, 128), np.float32)
    ident = np.eye(128, dtype=np.float32)
    return np.ascontiguousarray(np.stack([U, SL, Lo, SU, ones, ident], axis=1))


def shared_inputs(inp):
    f = np.float32
    g = lambda k: np.asarray(inp[k], dtype=f)
    bc = lambda v, n=128: np.ascontiguousarray(np.broadcast_to(v[None, :], (n, v.shape[0])))
    m = {}
    m["w_in"] = np.ascontiguousarray(g("w_in")[0])
    m["convw"] = np.ascontiguousarray(g("conv_w")[0].reshape(5, 8, 128).transpose(2, 1, 0))
    m["convb"] = np.ascontiguousarray(g("conv_b")[0].reshape(8, 128).T)
    m["dtb"] = bc(np.concatenate([g("dt_bias_fwd")[0], g("dt_bias_bwd")[0]]))
    m["alog"] = bc(np.concatenate([g("a_log_fwd")[0], g("a_log_bwd")[0]]))
    m["dsk"] = bc(g("d_skip")[0])
    m["ang"] = np.ascontiguousarray(g("attn_norm_g")[0].reshape(4, 128).T)
    m["sng"] = bc(g("ssd_norm_g")[0])
    m["relb"] = np.ascontiguousarray(g("rel_bias"))
    m["lnfm"] = np.ascontiguousarray(np.stack([g(k)[0].reshape(8, 128).T for k in ("ln1_g", "ln1_b", "ln2_g", "ln2_b")], axis=1))
    m["w_out"] = np.ascontiguousarray(g("w_out")[0])
    m["w_gate"] = np.ascontiguousarray(g("w_gate")[0])
    m["w_up"] = np.ascontiguousarray(g("w_up")[0])
    m["w_down"] = np.ascontiguousarray(g("w_down")[0])
    m["cst"] = make_cst()
    m["oh"], m["jmat"], m["sel"] = make_att_consts()
    return m


def t5_bucket(rel):
    half = 16
    max_exact = 8
    ret = (rel > 0).astype(np.int32) * half
    n = np.abs(rel)
    large = max_exact + (np.log(np.maximum(n, 1) / max_exact)
                         / math.log(1024 / max_exact) * (half - max_exact)).astype(np.int32)
    large = np.minimum(large, half - 1)
    return ret + np.where(n < max_exact, n, large)


def make_att_consts():
    oh = np.zeros((32, 6, 256), np.float32)
    i = np.arange(255)
    for bi, dl in enumerate(BRANCH_DIL):
        for ty in range(2):
            rel = (i - 63) if ty == 0 else (i - 191)
            bk = t5_bucket(rel * dl)
            oh[bk, bi * 2 + ty, i] = 1.0
    eb = np.ascontiguousarray(np.eye(128, dtype=np.float32)[::-1])
    sel = np.zeros((65, 64), np.float32)
    sel[64, :] = 1.0
    return oh, eb, sel


SEQ_LEN = 8192
N_CORES = 8
SLOTS = 2
_CACHE = {}


def kernel(**inputs):
    xp = np.asarray(inputs["x_prompt"], dtype=np.float32)
    xs = np.asarray(inputs["x_sample"], dtype=np.float32)
    seqs = [xp[i] for i in range(xp.shape[0])] + [xs[i] for i in range(xs.shape[0])]
    nseq = len(seqs)
    L = seqs[0].shape[0]
    shared = shared_inputs(inputs)
    in_maps = []
    for c in range(N_CORES):
        xT = np.zeros((SLOTS, D, L), np.float32)
        for sl in range(SLOTS):
            i = c * SLOTS + sl
            if i < nseq:
                xT[sl] = seqs[i].T
        m = dict(shared)
        m["xT"] = xT
        in_maps.append(m)
    key = (L, SLOTS)
    if key not in _CACHE:
        _CACHE[key] = build(L, SLOTS)
    nc, _ = _CACHE[key]
    res = run_bass_kernel_spmd(nc, in_maps, core_ids=list(range(N_CORES)))
    outs = []
    for i in range(nseq):
        c, sl = divmod(i, SLOTS)
        outs.append(np.ascontiguousarray(np.asarray(res.results[c]["yT"][sl]).T))
    y_prompt = np.stack(outs[:xp.shape[0]]).astype(np.float32)
    y_sample = np.stack(outs[xp.shape[0]:]).astype(np.float32)
    return (y_prompt, y_sample)
```

```python
import contextlib
import math
import numpy as np
import concourse.bass as bass
import concourse.mybir as mybir
from concourse.bass_utils import run_bass_kernel_spmd

F32 = mybir.dt.float32
BF16 = mybir.dt.bfloat16
U8 = mybir.dt.uint8
AF = mybir.ActivationFunctionType
ALU = mybir.AluOpType

D = 1024
DIN = 3088
DFF = 2816
NH = 8
COL_Q, COL_K, COL_V, COL_Z, COL_X, COL_DT = 0, 512, 1024, 1536, 2048, 3072
ALPHA = 2.0 ** 0.25
EPS = 1e-5
ENGS = ['pe', 'act', 'dve', 'pool', 'sp']
EPOCH = 20000
POOL_BYTES = 206 * 1024


def _dsize(dt):
    return {F32: 4, BF16: 2, U8: 1}[dt]


class Res:
    __slots__ = ('name', 'w', 'r')

    def __init__(self, name):
        self.name = name
        self.w = {}
        self.r = {}


class Chan:
    def __init__(self, name):
        self.name = name
        self.n = 0
        self.sem = None
        self.last = None


class Prog:
    def __init__(self, nc):
        self.nc = nc
        self.streams = {e: [] for e in ENGS}
        self.last_op = {e: None for e in ENGS}
        self.chans = []
        self.stack = contextlib.ExitStack()
        self.nres = 0
        self.nseq = 0
        self.pool = self.stack.enter_context(nc.sbuf_tensor("pool", [128, POOL_BYTES], U8))
        self.off = 0
        self.peak = 0
        self.banks = [self.stack.enter_context(nc.psum_tensor(f"bank{i}", [128, 512], F32))
                      for i in range(8)]
        self.Rbank = [self.res(f"bank{i}") for i in range(8)]

    def res(self, name=None):
        self.nres += 1
        return Res(name or f"r{self.nres}")

    def chan(self, name):
        c = Chan(name)
        self.chans.append(c)
        return c

    def alloc(self, shape, dtype):
        n = 1
        for s in shape[1:]:
            n *= s
        nbytes = n * _dsize(dtype)
        self.off = (self.off + 63) // 64 * 64
        assert self.off + nbytes <= POOL_BYTES, f"SBUF overflow {self.off + nbytes}"
        ap = self.pool[0:shape[0], self.off:self.off + nbytes].bitcast(dtype)
        self.off += nbytes
        self.peak = max(self.peak, self.off)
        if len(shape) > 2:
            names = [f"d{i}" for i in range(len(shape) - 1)]
            pat = "p (" + " ".join(names) + ") -> p " + " ".join(names)
            ap = ap.rearrange(pat, **{names[i]: shape[i + 1] for i in range(len(names))})
        return ap

    def mark(self):
        return self.off

    bank_list = list(range(8))

    def next_bank(self):
        self.bank_i = getattr(self, 'bank_i', -1) + 1
        return self.bank_list[self.bank_i % len(self.bank_list)]

    def reset(self, m):
        self.off = m

    def _deps(self, reads, writes):
        deps = {}
        for r in reads:
            for v in r.w.values():
                deps[id(v)] = v
        for w in writes:
            for v in w.w.values():
                deps[id(v)] = v
            for v in w.r.values():
                deps[id(v)] = v
        return list(deps.values())

    def op(self, eng, fn, reads=(), writes=()):
        ins = dict(fn=fn, deps=self._deps(reads, writes), signal=False, chan=None, eng=eng, seq=self.nseq)
        self.nseq += 1
        self.streams[eng].append(ins)
        self.last_op[eng] = ins
        for r in reads:
            r.r[id(ins)] = ins
        for w in writes:
            w.w = {eng: ins}
            w.r = {}
        return ins

    def dma(self, q, chan, fn, reads=(), writes=()):
        deps = self._deps(reads, writes)
        if chan.n > 0:
            deps.append(chan.last)
        ins = dict(fn=fn, deps=deps, signal=True, chan=chan, eng=q, seq=self.nseq, n=chan.n)
        self.nseq += 1
        self.streams[q].append(ins)
        chan.n += 1
        chan.last = ins
        for r in reads:
            r.r[id(ins)] = ins
        for w in writes:
            w.w = {chan: ins}
            w.r = {}
        return ins

    def barrier(self):
        evs = []
        for e in ENGS:
            if self.last_op[e] is not None:
                evs.append(self.last_op[e])
        for c in self.chans:
            if c.n > 0:
                evs.append(c.last)
        for e in ENGS:
            self.streams[e].append(dict(fn=None, deps=[d for d in evs if not (d['chan'] is None and d['eng'] == e)],
                                        signal=False, chan=None, eng=e, seq=self.nseq, barrier=True))
        self.nseq += 1

    def wait_all(self, eng, evs):
        self.streams[eng].append(dict(fn=None, deps=list(evs), signal=False, chan=None, eng=eng, seq=self.nseq,
                                      barrier=True))
        self.nseq += 1

    def _cost(self, ins):
        rec = _Fake()
        try:
            ins['fn'](rec)
        except Exception:
            return 300.0
        name, args, kw = rec.call
        try:
            if name == 'dma_start':
                o = kw.get('out')
                n = 1
                for d in o.shape:
                    n *= d
                ins['dma_ns'] = 2000.0 + n * _dsize(o.dtype) / 150.0
                return 60.0
            if name in ('matmul', 'transpose'):
                rhs = kw.get('rhs', args[2] if len(args) > 2 else None)
                n = 1
                for d in rhs.shape[1:]:
                    n *= d
                st = 1
                try:
                    st = max(1, abs(rhs.ap[-1][0]))
                except Exception:
                    pass
                c = max(64, n) / 2.4
                if rhs.dtype == F32 and name == 'matmul':
                    c *= 4
                elif st > 1:
                    c *= min(8, st) / 2.0 + 0.5
                return c + 8
            o = kw.get('out', args[0] if args else None)
            n = 1
            for d in o.shape[1:]:
                n *= d
            e = ins['eng']
            if e == 'act':
                return 70 + n * 0.96 + (90 if 'accum_out' in kw else 0)
            if e == 'dve':
                return 65 + n * 1.04
            return 110 + n * (0.6 if name == 'memset' else 2.3)
        except Exception:
            return 300.0

    def schedule(self, window=32):
        LAT = 120.0
        new_streams = {e: [] for e in ENGS}
        pos = {e: 0 for e in ENGS}
        free = {e: 0.0 for e in ENGS}
        tnow = 0.0
        while any(pos[e] < len(self.streams[e]) for e in ENGS):
            seg = {}
            for e in ENGS:
                st = self.streams[e]
                i = pos[e]
                j = i
                while j < len(st) and not st[j].get('barrier'):
                    j += 1
                seg[e] = st[i:j]
                bar = st[j] if j < len(st) else None
                pos[e] = j + 1 if j < len(st) else j
                seg[e + '_bar'] = bar
            for e in ENGS:
                for ins in seg[e]:
                    ins['cost'] = self._cost(ins)
                    ins['fin'] = None
                free[e] = tnow
            pend = {e: list(seg[e]) for e in ENGS}
            nleft = sum(len(v) for v in pend.values())
            while nleft:
                progressed = False
                for e in sorted(ENGS, key=lambda x: free[x]):
                    pl = pend[e]
                    if not pl:
                        continue
                    best = None
                    bstart = None
                    for k in range(min(window if e != 'sp' else 6, len(pl))):
                        ins = pl[k]
                        rd = ins.get('ready')
                        if rd is None:
                            rd = 0.0
                            ok = True
                            for d in ins['deps']:
                                f = d.get('fin', 0.0)
                                if f is None:
                                    ok = False
                                    break
                                if d['chan'] is not None:
                                    f = d.get('dfin', f)
                                if d['eng'] == e and d['chan'] is None and e == 'pe':
                                    f = f - d['cost'] * 0.5
                                else:
                                    f = f + LAT
                                if f > rd:
                                    rd = f
                            if not ok:
                                continue
                            ins['ready'] = rd
                        stt = rd if rd > free[e] else free[e]
                        if bstart is None or stt < bstart - 1e-9:
                            best, bstart = k, stt
                            if stt <= free[e]:
                                break
                    if best is None:
                        continue
                    ins = pl.pop(best)
                    ins['fin'] = bstart + ins['cost']
                    if ins['chan'] is not None:
                        ins['dfin'] = bstart + ins.get('dma_ns', 2000.0)
                    free[e] = ins['fin']
                    new_streams[e].append(ins)
                    nleft -= 1
                    progressed = True
                    break
                assert progressed, "scheduler deadlock"
            tnow = max(free.values())
            for e in ENGS:
                for ins in seg[e]:
                    if ins['chan'] is not None and ins.get('dfin', 0) > tnow:
                        tnow = ins['dfin']
            for e in ENGS:
                if seg[e + '_bar'] is not None:
                    seg[e + '_bar']['fin'] = tnow
                    new_streams[e].append(seg[e + '_bar'])
        self.streams = new_streams
        self.est_ns = tnow

    def emit(self, sched=True):
        nc = self.nc
        import os
        if sched and os.environ.get('NOSCHED') != '1':
            self.schedule()
        for e in ENGS:
            for ins in self.streams[e]:
                for d in ins['deps']:
                    if d['chan'] is None:
                        if d['eng'] == 'pe' and e == 'pe' and ins['chan'] is None and ins['fn'] is not None:
                            continue
                        d['signal'] = True
        nsig = {}
        for e in ENGS:
            c = 0
            for ins in self.streams[e]:
                if ins['chan'] is None and ins['signal'] and ins['fn'] is not None:
                    c += 1
                    ins['cnt'] = c
            nsig[e] = c
        sems = {}
        for e in ENGS:
            ne = max(1, -(-nsig[e] // EPOCH))
            sems[e] = [self.stack.enter_context(nc.semaphore(f"s_{e}{i}")) for i in range(ne)]
        for c in self.chans:
            c.sem = self.stack.enter_context(nc.semaphore(f"c_{c.name}"))
        self.stats = {e: [len(self.streams[e]), nsig[e], 0] for e in ENGS}

        def emit_stream(e, h):
            waited = {}
            nw = 0
            for ins in self.streams[e]:
                need = {}
                for d in ins['deps']:
                    if d['chan'] is None:
                        if d['eng'] == 'pe' and e == 'pe' and ins['chan'] is None and ins['fn'] is not None:
                            continue
                        cnt = d['cnt']
                        ep = (cnt - 1) // EPOCH
                        sem = sems[d['eng']][ep]
                        val = cnt - ep * EPOCH
                    else:
                        sem = d['chan'].sem
                        val = 16 * (d['n'] + 1)
                    k = id(sem)
                    if waited.get(k, 0) >= val:
                        continue
                    if k not in need or need[k][1] < val:
                        need[k] = (sem, val)
                for k, (sem, val) in need.items():
                    h.wait_ge(sem, val)
                    waited[k] = val
                    nw += 1
                if ins['fn'] is None:
                    continue
                r = ins['fn'](h)
                if ins['chan'] is not None:
                    r.then_inc(ins['chan'].sem, 16)
                elif ins['signal']:
                    cnt = ins['cnt']
                    ep = (cnt - 1) // EPOCH
                    r.then_inc(sems[e][ep], 1)
            self.stats[e][2] = nw

        with nc.Block() as block:
            @block.tensor
            def _(h):
                emit_stream('pe', h)

            @block.scalar
            def _(h):
                emit_stream('act', h)

            @block.vector
            def _(h):
                emit_stream('dve', h)

            @block.gpsimd
            def _(h):
                emit_stream('pool', h)

            @block.sync
            def _(h):
                emit_stream('sp', h)
        self.stack.close()


class _Fake:
    def __init__(self):
        self.call = None

    def __getattr__(self, name):
        def f(*a, **k):
            self.call = (name, a, k)
            return self
        return f


def mm_group(P, out_ap, pairs, Rout, reads):
    n = len(pairs)
    for i, (l, r) in enumerate(pairs):
        P.op('pe', lambda h, l=l, r=r, i=i: h.matmul(out_ap, l, r, start=(i == 0), stop=(i == n - 1)),
             reads=reads, writes=[Rout])


def act(P, out, in_, func, reads, writes, scale=None, bias=None):
    kw = {}
    if scale is not None:
        kw['scale'] = scale
    if bias is not None:
        kw['bias'] = bias
    return P.op('act', lambda h: h.activation(out=out, in_=in_, func=func, **kw), reads=reads, writes=writes)


class Ring:
    def __init__(self, P, name, n, shape, dtype, chan=False):
        self.bufs = [P.alloc(shape, dtype) for _ in range(n)]
        self.res = [P.res(f"{name}{i}") for i in range(n)]
        self.ch = [P.chan(f"{name}{i}") for i in range(n)] if chan else None
        self.i = -1
        self.n = n

    def next(self):
        self.i = (self.i + 1) % self.n
        if self.ch:
            return self.bufs[self.i], self.res[self.i], self.ch[self.i]
        return self.bufs[self.i], self.res[self.i]


class Ctx:
    pass


def build(L, NS, debug=False, phases=('A', 'S', 'T', 'C')):
    nc = bass.Bass("TRN2", target_bir_lowering=False)
    P = Prog(nc)
    C = Ctx()
    C.L, C.NS, C.P, C.nc = L, NS, P, nc
    NT = L // 512

    def din(name, shape, dt=F32):
        return nc.dram_tensor(name, list(shape), dt, kind="ExternalInput").ap()

    def dscr(name, shape, dt):
        return nc.dram_tensor(name, list(shape), dt,
                              kind="ExternalOutput" if debug else "Internal").ap()

    C.xT = din("xT", [NS, D, L])
    C.xT16 = din("xT16", [NS, D, L])
    C.w_in = din("w_in", [D, DIN])
    C.convw = din("convw", [128, 8, 5])
    C.convb = din("convb", [128, 8])
    C.dtb = din("dtb", [128, 16])
    C.alog = din("alog", [128, 16])
    C.dsk = din("dsk", [128, 8])
    C.ang = din("ang", [128, 4])
    C.sng = din("sng", [128, 512])
    C.relb = din("relb", [32, 8])
    C.w_out = din("w_out", [D, D])
    C.w_gate = din("w_gate", [D, DFF])
    C.w_up = din("w_up", [D, DFF])
    C.w_down = din("w_down", [DFF, D])
    C.cst = din("cst", [128, 6, 128])
    C.oh = din("oh", [32, 6, 256])
    C.jmat = din("jmat", [128, 128])
    C.sel = din("sel", [65, 64])
    C.negm = din("negm", [128, 2, 128])
    C.GV = dscr("GV", [6, 8, 256], F32)
    C.lnfm = din("lnfm", [128, 4, 8])
    C.yT = nc.dram_tensor("yT", [NS, D, L], F32, kind="ExternalOutput").ap()
    C.H1F = dscr("H1F", [NS, D, L], F32)
    C.H1B = dscr("H1B", [NS, D, L], BF16)

    C.QT = dscr("QT", [NS, 512, L], BF16)
    C.KT = dscr("KT", [NS, 512, L], BF16)
    C.V = dscr("V", [NS, L, 8 * 65], BF16)
    C.Z = dscr("Z", [NS, L, 512], F32)
    C.XS = dscr("XS", [NS, 512, L], F32)
    C.BC = dscr("BC", [NS, 512, L], BF16)
    C.DT = dscr("DT", [NS, L, 16], F32)

    outs = []
    C.YN = dscr("YN", [NS, 512, L], BF16)
    C.AT = dscr("AT", [NS, 512, L], F32)
    if 'A' in phases:
        phase_A(C)
        P.barrier()
    if 'S' in phases:
        phase_S(C)
        P.barrier()
    if 'T' in phases:
        phase_T(C)
        P.barrier()
    if 'C' in phases:
        phase_C1(C)
        P.barrier()
        outs = phase_C2(C)
        P.barrier()
    if debug:
        evs = [c.last for c in P.chans if c.n > 0]
        P.wait_all('sp', evs)
    else:
        P.wait_all('sp', outs)
    P.emit()
    return nc, P


def phase_A(C):
    P, L, NS = C.P, C.L, C.NS
    NT = L // 512
    m0 = P.mark()
    win = P.alloc([128, 8, DIN], BF16)
    Rwin = P.res("win")
    cw = P.alloc([128, 8, 5], F32)
    cb = P.alloc([128, 8], F32)
    dtb = P.alloc([128, 16], F32)
    Rsm = P.res("small")
    ch_w = P.chan("w")
    w_v = C.w_in.rearrange("(kc p) c -> p kc c", p=128)
    for c0 in (0, 1544):
        P.dma('pool', ch_w, lambda h, c0=c0: h.dma_start(out=win[:, :, c0:c0 + 1544], in_=w_v[:, :, c0:c0 + 1544]),
              writes=[Rwin])
    ch_s = P.chan("small")
    P.dma('sp', ch_s, lambda h: h.dma_start(out=cw, in_=C.convw), writes=[Rsm])
    P.dma('sp', ch_s, lambda h: h.dma_start(out=cb, in_=C.convb), writes=[Rsm])
    P.dma('sp', ch_s, lambda h: h.dma_start(out=dtb, in_=C.dtb), writes=[Rsm])

    xtb = Ring(P, "xtb", 2, [128, 8, 512], BF16, chan=True)
    xtb16 = Ring(P, "xtb16", 2, [128, 8, 512], BF16, chan=True)
    stq = Ring(P, "stq", 2, [128, 4, 512], BF16, chan=True)
    stk = Ring(P, "stk", 2, [128, 4, 512], BF16, chan=True)
    stx = Ring(P, "stx", 2, [128, 4, 512], F32, chan=True)
    stbc = Ring(P, "stbc", 2, [128, 4, 512], BF16, chan=True)
    stv = Ring(P, "stv", 2, [128, 4, 8, 65], BF16, chan=True)
    stz = Ring(P, "stz", 2, [128, 4, 512], F32, chan=True)
    stdt = Ring(P, "stdt", 2, [128, 4, 16], F32, chan=True)
    raw = [P.alloc([128, 520], F32) for _ in range(8)]
    Rraw = [P.res(f"raw{c}") for c in range(8)]
    acc = Ring(P, "acc", 4, [128, 512], F32)
    dtt = Ring(P, "dtt", 2, [128, 16], F32)
    for b, r in zip(stv.bufs, stv.res):
        P.op('pool', lambda h, b=b: h.memset(b, 1.0), writes=[r])
    fm_banks = [0, 1, 2]
    tm_banks = [3, 4, 5, 6]
    dt_bank = 7
    fmi = [0]
    tmi = [0]

    def next_fm():
        b = fm_banks[fmi[0] % len(fm_banks)]
        fmi[0] += 1
        return b

    def next_tm():
        b = tm_banks[tmi[0] % len(tm_banks)]
        tmi[0] += 1
        return b

    def load_x(s, T):
        buf, r, ch = xtb.next()
        src = C.xT[s].rearrange("(kc p) t -> p kc t", p=128)[:, :, T * 512:(T + 1) * 512]
        P.dma('pool', ch, lambda h: h.dma_start(out=buf, in_=src), writes=[r])
        buf2, r2, ch2 = xtb16.next()
        src2 = C.xT16[s].rearrange("(kc p) t -> p kc t", p=128)[:, :, T * 512:(T + 1) * 512]
        P.dma('pool', ch2, lambda h: h.dma_start(out=buf2, in_=src2), writes=[r2])
        return buf, r, buf2, r2

    def conv_chunk_ops(c, width, accb, Racc):
        ops = []
        rb = raw[c]
        ops.append(lambda: P.op('dve', lambda h: h.tensor_scalar(
            out=accb[:, 0:width], in0=rb[:, 0:width], scalar1=cw[:, c, 0:1], scalar2=cb[:, c:c + 1],
            op0=ALU.mult, op1=ALU.add), reads=[Rraw[c], Rsm], writes=[Racc]))
        for j in range(1, 5):
            ops.append(lambda j=j: P.op('dve', lambda h: h.scalar_tensor_tensor(
                out=accb[:, 0:width], in0=rb[:, j:j + width], scalar=cw[:, c, j:j + 1], in1=accb[:, 0:width],
                op0=ALU.mult, op1=ALU.add), reads=[Rraw[c], Rsm, Racc], writes=[Racc]))
        return ops

    def conv_finish(s, c, width, accb, Racc, xbuf, xr, bcbuf, bcr):
        if c < 4:
            act(P, xbuf[:, c, 0:width], accb[:, 0:width], AF.Silu, [Racc], [xr])
        else:
            act(P, bcbuf[:, c - 4, 0:width], accb[:, 0:width], AF.Silu, [Racc], [bcr])

    for s in range(NS):
        for c in range(8):
            P.op('pool', lambda h, c=c: h.memset(raw[c][:, 0:4], 0.0), writes=[Rraw[c]])
        nxt = load_x(s, 0)
        for T in range(NT):
            xb, xr_, xb16, xr16 = nxt
            if T + 1 < NT:
                nxt = load_x(s, T + 1)
            t0 = T * 512
            qb, qr, qch = stq.next()
            kb, kr, kch = stk.next()
            for c in range(4):
                b = next_fm()
                mm_group(P, P.banks[b][:, :], [(win[:, kc, COL_Q + c * 128:COL_Q + (c + 1) * 128], xb16[:, kc, :])
                                               for kc in range(8)], P.Rbank[b], [Rwin, xr16])
                act(P, qb[:, c, :], P.banks[b][:, :], AF.Identity, [P.Rbank[b]], [qr], scale=0.125)
            P.dma('sp', qch, lambda h, qb=qb, s=s, t0=t0: h.dma_start(
                out=C.QT[s].rearrange("(c p) t -> p c t", p=128)[:, :, t0:t0 + 512], in_=qb), reads=[qr])
            for c in range(4):
                b = next_fm()
                mm_group(P, P.banks[b][:, :], [(win[:, kc, COL_K + c * 128:COL_K + (c + 1) * 128], xb[:, kc, :])
                                               for kc in range(8)], P.Rbank[b], [Rwin, xr_])
                P.op('dve', lambda h, b=b, c=c, kb=kb: h.tensor_copy(out=kb[:, c, :], in_=P.banks[b][:, :]),
                     reads=[P.Rbank[b]], writes=[kr])
            P.dma('sp', kch, lambda h, kb=kb, s=s, t0=t0: h.dma_start(
                out=C.KT[s].rearrange("(c p) t -> p c t", p=128)[:, :, t0:t0 + 512], in_=kb), reads=[kr])
            sxb, sxr, sxch = stx.next()
            sbb, sbr, sbch = stbc.next()
            for cp in range(4):
                chains = []
                accs = []
                for c in (2 * cp, 2 * cp + 1):
                    b = next_fm()
                    mm_group(P, P.banks[b][:, :],
                             [(win[:, kc, COL_X + c * 128:COL_X + (c + 1) * 128], xb[:, kc, :]) for kc in range(8)],
                             P.Rbank[b], [Rwin, xr_])
                    act(P, raw[c][:, 4:516], P.banks[b][:, :], AF.Identity, [P.Rbank[b]], [Rraw[c]])
                    ab, ar = acc.next()
                    accs.append((c, ab, ar))
                    chains.append(conv_chunk_ops(c, 512, ab, ar))
                for j in range(5):
                    for ch_ in chains:
                        ch_[j]()
                for (c, ab, ar) in accs:
                    conv_finish(s, c, 512, ab, ar, sxb, sxr, sbb, sbr)
                    P.op('pool', lambda h, c=c: h.tensor_copy(out=raw[c][:, 0:4], in_=raw[c][:, 512:516]),
                         reads=[Rraw[c]], writes=[Rraw[c]])
            lo = 2 if T == 0 else 0
            P.dma('sp', sxch, lambda h, sxb=sxb, s=s, t0=t0, lo=lo: h.dma_start(
                out=C.XS[s].rearrange("(c p) t -> p c t", p=128)[:, :, t0 - 2 + lo:t0 + 510],
                in_=sxb[:, :, lo:512]), reads=[sxr])
            P.dma('sp', sbch, lambda h, sbb=sbb, s=s, t0=t0, lo=lo: h.dma_start(
                out=C.BC[s].rearrange("(c p) t -> p c t", p=128)[:, :, t0 - 2 + lo:t0 + 510],
                in_=sbb[:, :, lo:512]), reads=[sbr])
            vb, vr, vch = stv.next()
            zb, zr, zch = stz.next()
            db, dr, dch = stdt.next()
            for u in range(4):
                lw = [xb[:, kc, u * 128:(u + 1) * 128] for kc in range(8)]
                b = next_tm()
                mm_group(P, P.banks[b][:, :], [(lw[kc], win[:, kc, COL_V:COL_V + 512]) for kc in range(8)],
                         P.Rbank[b], [Rwin, xr_])
                act(P, vb[:, u, :, 0:64], P.banks[b][:, :].rearrange("p (h d) -> p h d", d=64), AF.Identity,
                    [P.Rbank[b]], [vr])
                b = next_tm()
                mm_group(P, P.banks[b][:, :], [(lw[kc], win[:, kc, COL_Z:COL_Z + 512]) for kc in range(8)],
                         P.Rbank[b], [Rwin, xr_])
                act(P, zb[:, u, :], P.banks[b][:, :], AF.Silu, [P.Rbank[b]], [zr])
                b = dt_bank
                mm_group(P, P.banks[b][:, 0:16], [(lw[kc], win[:, kc, COL_DT:COL_DT + 16]) for kc in range(8)],
                         P.Rbank[b], [Rwin, xr_])
                tb, tr = dtt.next()
                P.op('dve', lambda h, b=b, tb=tb: h.tensor_tensor(out=tb, in0=P.banks[b][:, 0:16], in1=dtb, op=ALU.add),
                     reads=[P.Rbank[b], Rsm], writes=[tr])
                act(P, tb, tb, AF.Exp, [tr], [tr])
                act(P, db[:, u, :], tb, AF.Ln, [tr], [dr], bias=1.0)
            P.dma('sp', vch, lambda h, vb=vb, s=s, t0=t0: h.dma_start(
                out=C.V[s][t0:t0 + 512, :].rearrange("(u p) f -> p u f", p=128),
                in_=vb.rearrange("p u h e -> p u (h e)")), reads=[vr])
            P.dma('sp', zch, lambda h, zb=zb, s=s, t0=t0: h.dma_start(
                out=C.Z[s][t0:t0 + 512, :].rearrange("(u p) f -> p u f", p=128), in_=zb), reads=[zr])
            P.dma('sp', dch, lambda h, db=db, s=s, t0=t0: h.dma_start(
                out=C.DT[s][t0:t0 + 512, :].rearrange("(u p) f -> p u f", p=128), in_=db), reads=[dr])
        sxb, sxr, sxch = stx.next()
        sbb, sbr, sbch = stbc.next()
        for c in range(8):
            P.op('pool', lambda h, c=c: h.memset(raw[c][:, 4:8], 0.0), reads=[Rraw[c]], writes=[Rraw[c]])
            ab, ar = acc.next()
            for o in conv_chunk_ops(c, 2, ab, ar):
                o()
            conv_finish(s, c, 2, ab, ar, sxb, sxr, sbb, sbr)
        P.dma('sp', sxch, lambda h, sxb=sxb, s=s: h.dma_start(
            out=C.XS[s].rearrange("(c p) t -> p c t", p=128)[:, :, L - 2:L], in_=sxb[:, :, 0:2]),
            reads=[sxr])
        P.dma('sp', sbch, lambda h, sbb=sbb, s=s: h.dma_start(
            out=C.BC[s].rearrange("(c p) t -> p c t", p=128)[:, :, L - 2:L], in_=sbb[:, :, 0:2]),
            reads=[sbr])
    P.reset(m0)


def phase_S(C):
    P, L, NS = C.P, C.L, C.NS
    NC = L // 128
    NG = NC // 4
    m0 = P.mark()
    cst = P.alloc([128, 6, 128], F32)
    idb = P.alloc([128, 128], BF16)
    dsk = P.alloc([128, 8], F32)
    Aneg = P.alloc([128, 16], F32)
    sng = P.alloc([128, 512], F32)
    Rc = P.res("s_const")
    chc = P.chan("s_const")
    P.dma('sp', chc, lambda h: h.dma_start(out=cst, in_=C.cst), writes=[Rc])
    P.dma('sp', chc, lambda h: h.dma_start(out=dsk, in_=C.dsk), writes=[Rc])
    P.dma('sp', chc, lambda h: h.dma_start(out=Aneg, in_=C.alog), writes=[Rc])
    P.dma('sp', chc, lambda h: h.dma_start(out=sng, in_=C.sng), writes=[Rc])
    act(P, Aneg, Aneg, AF.Exp, [Rc], [Rc])
    P.op('dve', lambda h: h.tensor_scalar(out=Aneg, in0=Aneg, scalar1=-1.0, scalar2=None, op0=ALU.mult),
         reads=[Rc], writes=[Rc])
    P.op('dve', lambda h: h.tensor_copy(out=idb, in_=cst[:, 5, :]), reads=[Rc], writes=[Rc])
    U_, SL_, LO_, SU_, ON_, ID_ = [cst[:, i, :] for i in range(6)]

    SbAll = P.alloc([128, NC, 512], BF16)
    RSb = [P.res(f"sb{c}") for c in range(NC)]
    gx = Ring(P, "gx", 3, [128, 4, 512], F32, chan=True)
    gbc = Ring(P, "gbc", 3, [128, 4, 512], BF16, chan=True)
    gdt = Ring(P, "gdt", 3, [128, 4, 16], F32, chan=True)
    gz = Ring(P, "gz", 3, [128, 4, 512], F32, chan=True)
    syn = Ring(P, "syn", 2, [128, 4, 512], BF16, chan=True)
    da_r = Ring(P, "da", 3, [128, 16], F32)
    ew_r = Ring(P, "ew", 4, [128, 64], F32)
    sc_r = Ring(P, "sc", 4, [128, 16], F32)
    xdt_r = Ring(P, "xdt", 6, [128, 512], BF16)
    xsd_r = Ring(P, "xsd", 3, [128, 512], F32)
    btok_r = Ring(P, "btok", 3, [128, 256], BF16)
    cbm_r = Ring(P, "cbm", 4, [128, 256], F32)
    L_r = Ring(P, "Lr", 2, [128, 8, 128], F32)
    dec_r = Ring(P, "dec", 2, [128, 8, 128], F32)
    M_r = Ring(P, "Mr", 4, [128, 8, 128], BF16)
    t_r = Ring(P, "tr", 4, [128, 512], F32)
    yn_r = Ring(P, "yn", 2, [128, 512], BF16)
    sm_r = Ring(P, "sm", 4, [128, 4], F32)
    Sf = P.alloc([128, 512], F32)
    Sfb = P.alloc([128, 512], BF16)
    Sb = P.alloc([128, 512], F32)
    RSf, RSfb, RSbr = P.res("Sf"), P.res("Sfb"), P.res("Sbr")

    def bc8(ap8):
        return ap8.unsqueeze(2).to_broadcast([128, 8, 64])

    def v3(ap):
        return ap.rearrange("p (h d) -> p h d", d=64)

    def load_group(s, g, with_z):
        t0 = g * 512
        xb, xr, xch = gx.next()
        P.dma('sp', xch, lambda h: h.dma_start(
            out=xb, in_=C.XS[s].rearrange("(c p) t -> p c t", p=128)[:, :, t0:t0 + 512]), writes=[xr])
        bb, br, bch = gbc.next()
        P.dma('sp', bch, lambda h: h.dma_start(
            out=bb, in_=C.BC[s].rearrange("(c p) t -> p c t", p=128)[:, :, t0:t0 + 512]), writes=[br])
        db, dr, dch = gdt.next()
        P.dma('sp', dch, lambda h: h.dma_start(
            out=db, in_=C.DT[s][t0:t0 + 512, :].rearrange("(u p) f -> p u f", p=128)), writes=[dr])
        zz = None
        if with_z:
            zb, zr, zch = gz.next()
            P.dma('sp', zch, lambda h: h.dma_start(
                out=zb, in_=C.Z[s][t0:t0 + 512, :].rearrange("(u p) f -> p u f", p=128)), writes=[zr])
            zz = (zb, zr)
        return (xb, xr), (bb, br), (db, dr), zz

    def small_mms(da, Rda, mats):
        b = P.next_bank()
        for i, m_ in enumerate(mats):
            P.op('pe', lambda h, i=i, m_=m_, b=b: h.matmul(P.banks[b][:, 16 * i:16 * i + 16], m_, da, start=True, stop=True),
                 reads=[Rc, Rda], writes=[P.Rbank[b]])
        ew, Rew = ew_r.next()
        n = 16 * len(mats)
        act(P, ew[:, 0:n], P.banks[b][:, 0:n], AF.Exp, [P.Rbank[b]], [Rew])
        return ew, Rew

    def xs_transpose(xb, xr, u):
        b = P.next_bank()
        for fc in range(4):
            P.op('pe', lambda h, fc=fc, b=b: h.transpose(P.banks[b][:, fc * 128:(fc + 1) * 128],
                                                        xb[:, fc, u * 128:(u + 1) * 128], ID_),
                 reads=[xr, Rc], writes=[P.Rbank[b]])
        return b

    def b_transpose(bb, br, u):
        b = P.next_bank()
        pb = P.banks[b][:, :].bitcast(BF16)
        for g in range(2):
            P.op('pe', lambda h, g=g, pb=pb: h.transpose(pb[:, g * 128:(g + 1) * 128],
                                                        bb[:, g, u * 128:(u + 1) * 128], idb),
                 reads=[br, Rc], writes=[P.Rbank[b]])
        bt, Rbt = btok_r.next()
        act(P, bt, pb[:, 0:256], AF.Identity, [P.Rbank[b]], [Rbt])
        return bt, Rbt

    def state_mm(bt, Rbt, xw, Rxw):
        b = P.next_bank()
        for g in range(2):
            P.op('pe', lambda h, g=g, b=b: h.matmul(P.banks[b][:, g * 256:(g + 1) * 256], bt[:, g * 128:(g + 1) * 128],
                                                   xw[:, g * 256:(g + 1) * 256], start=True, stop=True),
                 reads=[Rbt, Rxw], writes=[P.Rbank[b]])
        return b

    def s1_front(db, dr, xb, xr, bb, br, u):
        da, Rda = da_r.next()
        P.op('dve', lambda h: h.tensor_tensor(out=da, in0=db[:, u, :], in1=Aneg, op=ALU.mult),
             reads=[dr, Rc], writes=[Rda])
        ew, Rew = small_mms(da, Rda, [SU_, ON_])
        sc, Rsc = sc_r.next()
        P.op('dve', lambda h: h.tensor_tensor(out=sc[:, 0:8], in0=db[:, u, 8:16], in1=ew[:, 8:16], op=ALU.mult),
             reads=[dr, Rew], writes=[Rsc])
        bx = xs_transpose(xb, xr, u)
        xw, Rxw = xdt_r.next()
        P.op('dve', lambda h: h.tensor_tensor(out=v3(xw), in0=v3(P.banks[bx][:, :]), in1=bc8(sc[:, 0:8]), op=ALU.mult),
             reads=[P.Rbank[bx], Rsc], writes=[Rxw])
        bt, Rbt = b_transpose(bb, br, u)
        bs = state_mm(bt, Rbt, xw, Rxw)
        return ew, Rew, bs

    def s1_back(c, ew, Rew, bs):
        P.op('dve', lambda h: h.tensor_tensor(out=v3(Sb), in0=v3(Sb), in1=bc8(ew[:, 24:32]), op=ALU.mult),
             reads=[RSbr, Rew], writes=[RSbr])
        P.op('dve', lambda h: h.tensor_tensor(out=Sb, in0=Sb, in1=P.banks[bs][:, :], op=ALU.add),
             reads=[RSbr, P.Rbank[bs]], writes=[RSbr])
        act(P, SbAll[:, c - 1, :], Sb, AF.Identity, [RSbr], [RSb[c - 1]])

    def s2_front(db, dr, xb, xr, bb, br, u):
        da, Rda = da_r.next()
        P.op('dve', lambda h: h.tensor_tensor(out=da, in0=db[:, u, :], in1=Aneg, op=ALU.mult),
             reads=[dr, Rc], writes=[Rda])
        ew, Rew = small_mms(da, Rda, [U_, SL_, LO_, ON_])
        sc, Rsc = sc_r.next()
        P.op('dve', lambda h: h.tensor_tensor(out=sc[:, 0:8], in0=db[:, u, 0:8], in1=ew[:, 16:24], op=ALU.mult),
             reads=[dr, Rew], writes=[Rsc])
        Ms = []
        Ls = []
        for (tri_l, c0) in ((SL_, 0), (SU_, 8)):
            Lt, RLt = L_r.next()
            P.op('pool', lambda h, Lt=Lt, tri_l=tri_l, c0=c0: h.tensor_tensor(
                out=Lt, in0=tri_l.unsqueeze(1).to_broadcast([128, 8, 128]),
                in1=da[:, c0:c0 + 8].unsqueeze(2).to_broadcast([128, 8, 128]), op=ALU.mult),
                reads=[Rc, Rda], writes=[RLt])
            Ls.append((Lt, RLt))
        bx = xs_transpose(xb, xr, u)
        xsP = v3(P.banks[bx][:, :])
        xf, Rxf = xdt_r.next()
        xbw, Rxbw = xdt_r.next()
        xw, Rxw = xdt_r.next()
        xsd, Rxsd = xsd_r.next()
        P.op('dve', lambda h: h.tensor_tensor(out=v3(xf), in0=xsP, in1=bc8(db[:, u, 0:8]), op=ALU.mult),
             reads=[P.Rbank[bx], dr], writes=[Rxf])
        P.op('dve', lambda h: h.tensor_tensor(out=v3(xbw), in0=xsP, in1=bc8(db[:, u, 8:16]), op=ALU.mult),
             reads=[P.Rbank[bx], dr], writes=[Rxbw])
        P.op('dve', lambda h: h.tensor_tensor(out=v3(xw), in0=xsP, in1=bc8(sc[:, 0:8]), op=ALU.mult),
             reads=[P.Rbank[bx], Rsc], writes=[Rxw])
        P.op('dve', lambda h: h.tensor_tensor(out=v3(xsd), in0=xsP, in1=bc8(dsk), op=ALU.mult),
             reads=[P.Rbank[bx], Rc], writes=[Rxsd])
        bt, Rbt = b_transpose(bb, br, u)
        bcb = P.next_bank()
        for gg in range(2):
            P.op('pe', lambda h, gg=gg: h.matmul(
                P.banks[bcb][:, gg * 128:(gg + 1) * 128], bb[:, gg, u * 128:(u + 1) * 128],
                bb[:, 2 + gg, u * 128:(u + 1) * 128], start=True, stop=True), reads=[br], writes=[P.Rbank[bcb]])
        cbU, RcbU = cbm_r.next()
        cbL, RcbL = cbm_r.next()
        cbP = P.banks[bcb][:, 0:256].rearrange("p (g q) -> p g q", g=2)
        P.op('dve', lambda h: h.tensor_tensor(
            out=cbU.rearrange("p (g q) -> p g q", g=2), in0=cbP,
            in1=U_.unsqueeze(1).to_broadcast([128, 2, 128]), op=ALU.mult),
            reads=[P.Rbank[bcb], Rc], writes=[RcbU])
        P.op('dve', lambda h: h.tensor_tensor(
            out=cbL.rearrange("p (g q) -> p g q", g=2), in0=cbP,
            in1=LO_.unsqueeze(1).to_broadcast([128, 2, 128]), op=ALU.mult),
            reads=[P.Rbank[bcb], Rc], writes=[RcbL])
        for di, (tri_r, cbm, Rcbm) in enumerate(((U_, cbU, RcbU), (LO_, cbL, RcbL))):
            Lt, RLt = Ls[di]
            dec, Rdec = dec_r.next()
            for hh in range(2):
                b = P.next_bank()
                for h4 in range(4):
                    hd = hh * 4 + h4
                    P.op('pe', lambda h, b=b, h4=h4, hd=hd, Lt=Lt, tri_r=tri_r: h.matmul(
                        P.banks[b][:, h4 * 128:(h4 + 1) * 128], Lt[:, hd, :], tri_r, start=True, stop=True),
                        reads=[RLt, Rc], writes=[P.Rbank[b]])
                act(P, dec[:, hh * 4:(hh + 1) * 4, :], P.banks[b][:, :].rearrange("p (a q) -> p a q", a=4),
                    AF.Exp, [P.Rbank[b]], [Rdec])
            Mt, RMt = M_r.next()
            P.op('dve', lambda h, Mt=Mt, dec=dec, cbm=cbm: h.tensor_tensor(
                out=Mt.rearrange("p (g e) q -> p g e q", g=2), in0=dec.rearrange("p (g e) q -> p g e q", g=2),
                in1=cbm.rearrange("p (g q) -> p g q", g=2).unsqueeze(2).to_broadcast([128, 2, 4, 128]),
                op=ALU.mult), reads=[Rdec, Rcbm], writes=[RMt])
            Ms.append((Mt, RMt))
        return dict(ew=ew, Rew=Rew, xf=xf, Rxf=Rxf, xbw=xbw, Rxbw=Rxbw, xw=xw, Rxw=Rxw, xsd=xsd, Rxsd=Rxsd,
                    bt=bt, Rbt=Rbt, Ms=Ms)

    def s2_back(c, u, f, bb, br, zb, zr, yb, yr):
        ew, Rew, xf, Rxf, xbw, Rxbw, xw, Rxw = f['ew'], f['Rew'], f['xf'], f['Rxf'], f['xbw'], f['Rxbw'], f['xw'], f['Rxw']
        xsd, Rxsd, bt, Rbt, Ms = f['xsd'], f['Rxsd'], f['bt'], f['Rbt'], f['Ms']
        by = P.next_bank()
        for hd in range(8):
            P.op('pe', lambda h, hd=hd: h.matmul(
                P.banks[by][:, hd * 64:(hd + 1) * 64], Ms[0][0][:, hd, :], xf[:, hd * 64:(hd + 1) * 64],
                start=True, stop=False), reads=[Ms[0][1], Rxf], writes=[P.Rbank[by]])
            P.op('pe', lambda h, hd=hd: h.matmul(
                P.banks[by][:, hd * 64:(hd + 1) * 64], Ms[1][0][:, hd, :], xbw[:, hd * 64:(hd + 1) * 64],
                start=False, stop=True), reads=[Ms[1][1], Rxbw], writes=[P.Rbank[by]])
        bof = P.next_bank()
        bob = P.next_bank()
        for gg in range(2):
            P.op('pe', lambda h, gg=gg: h.matmul(
                P.banks[bof][:, gg * 256:(gg + 1) * 256], bb[:, 2 + gg, u * 128:(u + 1) * 128],
                Sfb[:, gg * 256:(gg + 1) * 256], start=True, stop=True), reads=[br, RSfb], writes=[P.Rbank[bof]])
        for gg in range(2):
            P.op('pe', lambda h, gg=gg: h.matmul(
                P.banks[bob][:, gg * 256:(gg + 1) * 256], bb[:, 2 + gg, u * 128:(u + 1) * 128],
                SbAll[:, c, gg * 256:(gg + 1) * 256], start=True, stop=True), reads=[br, RSb[c]], writes=[P.Rbank[bob]])
        bs = state_mm(bt, Rbt, xw, Rxw)
        P.op('dve', lambda h: h.tensor_tensor(out=v3(Sf), in0=v3(Sf), in1=bc8(ew[:, 48:56]), op=ALU.mult),
             reads=[RSf, Rew], writes=[RSf])
        P.op('dve', lambda h: h.tensor_tensor(out=Sf, in0=Sf, in1=P.banks[bs][:, :], op=ALU.add),
             reads=[RSf, P.Rbank[bs]], writes=[RSf])
        act(P, Sfb, Sf, AF.Identity, [RSf], [RSfb])
        t1, Rt1 = t_r.next()
        t2, Rt2 = t_r.next()
        P.op('dve', lambda h: h.tensor_tensor(out=v3(t1), in0=v3(P.banks[bof][:, :]), in1=bc8(ew[:, 0:8]), op=ALU.mult),
             reads=[P.Rbank[bof], Rew], writes=[Rt1])
        P.op('dve', lambda h: h.tensor_tensor(out=v3(t2), in0=v3(P.banks[bob][:, :]), in1=bc8(ew[:, 40:48]), op=ALU.mult),
             reads=[P.Rbank[bob], Rew], writes=[Rt2])
        P.op('pool', lambda h: h.tensor_tensor(out=t1, in0=t1, in1=t2, op=ALU.add), reads=[Rt1, Rt2], writes=[Rt1])
        P.op('pool', lambda h: h.tensor_tensor(out=t1, in0=t1, in1=xsd, op=ALU.add), reads=[Rt1, Rxsd], writes=[Rt1])
        P.op('dve', lambda h: h.tensor_tensor(out=t1, in0=t1, in1=P.banks[by][:, :], op=ALU.add),
             reads=[Rt1, P.Rbank[by]], writes=[Rt1])
        P.op('dve', lambda h: h.tensor_tensor(out=t1, in0=t1, in1=zb[:, u, :], op=ALU.mult),
             reads=[Rt1, zr], writes=[Rt1])
        sm, Rsm_ = sm_r.next()
        P.op('act', lambda h: h.activation(out=t2, in_=t1, func=AF.Square, accum_out=sm[:, 0:1]),
             reads=[Rt1, Rt2], writes=[Rt2, Rsm_])
        act(P, sm[:, 1:2], sm[:, 0:1], AF.Ln, [Rsm_], [Rsm_], scale=1.0 / 512, bias=EPS)
        act(P, sm[:, 2:3], sm[:, 1:2], AF.Exp, [Rsm_], [Rsm_], scale=-0.5)
        yn, Ryn = yn_r.next()
        P.op('dve', lambda h: h.scalar_tensor_tensor(
            out=yn, in0=t1, scalar=sm[:, 2:3], in1=sng, op0=ALU.mult, op1=ALU.mult),
            reads=[Rt1, Rsm_, Rc], writes=[Ryn])
        bt_ = P.next_bank()
        pbt = P.banks[bt_][:, :].bitcast(BF16)
        for fc in range(4):
            P.op('pe', lambda h, fc=fc: h.transpose(
                pbt[:, fc * 128:(fc + 1) * 128], yn[:, fc * 128:(fc + 1) * 128], idb),
                reads=[Ryn, Rc], writes=[P.Rbank[bt_]])
        act(P, yb[:, :, u * 128:(u + 1) * 128], pbt[:, 0:512].rearrange("p (c t) -> p c t", c=4), AF.Identity,
            [P.Rbank[bt_]], [yr])

    ybuf = {}

    def finish_back(p):
        c, u, g, fr, bb, br, zb, zr = p
        if u == 0:
            ybuf['cur'] = syn.next()
        yb, yr, ych = ybuf['cur']
        s2_back(c, u, fr, bb, br, zb, zr, yb, yr)
        if u == 3:
            s_ = ybuf['s']
            P.dma('sp', ych, lambda h: h.dma_start(
                out=C.YN[s_].rearrange("(c p) t -> p c t", p=128)[:, :, g * 512:(g + 1) * 512], in_=yb), reads=[yr])

    for s in range(NS):
        ybuf['s'] = s
        P.op('pool', lambda h: h.memset(Sb, 0.0), writes=[RSbr])
        P.op('pool', lambda h: h.memset(SbAll[:, NC - 1, :], 0.0), writes=[RSb[NC - 1]])
        groups = {}
        groups[NG - 1] = load_group(s, NG - 1, False)
        if NG > 1:
            groups[NG - 2] = load_group(s, NG - 2, False)
        pend = None
        for c in range(NC - 1, 0, -1):
            g, u = divmod(c, 4)
            (xb, xr), (bb, br), (db, dr), _ = groups[g]
            fr = s1_front(db, dr, xb, xr, bb, br, u)
            if pend is not None:
                s1_back(*pend)
            pend = (c,) + fr
            if u == 3 and g - 2 >= 0:
                groups[g - 2] = load_group(s, g - 2, False)
        if pend is not None:
            s1_back(*pend)
        P.op('pool', lambda h: h.memset(Sf, 0.0), writes=[RSf])
        P.op('pool', lambda h: h.memset(Sfb, 0.0), writes=[RSfb])
        groups = {0: load_group(s, 0, True)}
        if NG > 1:
            groups[1] = load_group(s, 1, True)
        ybs = {}
        pend = None
        for c in range(NC):
            g, u = divmod(c, 4)
            (xb, xr), (bb, br), (db, dr), (zb, zr) = groups[g]
            fr = s2_front(db, dr, xb, xr, bb, br, u)
            if pend is not None:
                finish_back(pend)
            pend = (c, u, g, fr, bb, br, zb, zr)
            if u == 0 and g + 2 < NG:
                groups[g + 2] = load_group(s, g + 2, True)
        finish_back(pend)
    P.reset(m0)


BRANCH_DIL = (1, 4, 16)


def phase_T(C):
    P, L, NS = C.P, C.L, C.NS
    m0 = P.mark()
    cst = P.alloc([128, 6, 128], F32)
    jmat = P.alloc([128, 128], F32)
    sel = P.alloc([65, 64], F32)
    LBh = P.alloc([128, 4, 3, 2, 256], BF16)
    LBl = P.alloc([128, 4, 3, 2, 256], BF16)
    negm = P.alloc([128, 2, 128], F32)
    idb = P.alloc([128, 128], BF16)
    Rc = P.res("t_const")
    REB = P.res("EB")
    chc = P.chan("t_const")
    P.dma('sp', chc, lambda h: h.dma_start(out=cst, in_=C.cst), writes=[Rc])
    P.dma('sp', chc, lambda h: h.dma_start(out=jmat, in_=C.jmat), writes=[Rc])
    P.dma('sp', chc, lambda h: h.dma_start(out=sel, in_=C.sel), writes=[Rc])
    P.dma('sp', chc, lambda h: h.dma_start(out=negm, in_=C.negm), writes=[Rc])
    P.op('dve', lambda h: h.tensor_copy(out=idb, in_=cst[:, 5, :]), reads=[Rc], writes=[Rc])
    U_, LO_ = cst[:, 0, :], cst[:, 2, :]
    m1 = P.mark()
    oh = P.alloc([32, 6, 256], F32)
    relb = P.alloc([32, 8], F32)
    P.dma('sp', chc, lambda h: h.dma_start(out=oh, in_=C.oh), writes=[Rc])
    P.dma('sp', chc, lambda h: h.dma_start(out=relb, in_=C.relb), writes=[Rc])
    gs_r = Ring(P, "gs", 2, [8, 256], F32, chan=True)
    hk_r = Ring(P, "hk", 4, [128, 128], F32, chan=True)
    tmp_r = Ring(P, "ebtmp", 3, [128, 128], F32)
    RGV = [P.res(f"gv{i}") for i in range(6)]
    for bt in range(6):
        b = P.next_bank()
        P.op('pe', lambda h, b=b, bt=bt: h.matmul(P.banks[b][0:8, 0:256], relb, oh[:, bt, :], start=True, stop=True),
             reads=[Rc], writes=[P.Rbank[b]])
        gs, Rgs, gch = gs_r.next()
        P.op('dve', lambda h, gs=gs, b=b: h.tensor_copy(out=gs, in_=P.banks[b][0:8, 0:256]), reads=[P.Rbank[b]], writes=[Rgs])
        P.dma('sp', gch, lambda h, gs=gs, bt=bt: h.dma_start(out=C.GV[bt], in_=gs), reads=[Rgs], writes=[RGV[bt]])
    for bt in range(6):
        bi, ty = bt // 2, bt % 2
        for hd in range(8):
            hk, Rhk, hch = hk_r.next()
            src = bass.AP(C.GV.tensor, (bt * 8 + hd) * 256, [[1, 128], [1, 128]])
            P.dma('sp', hch, lambda h, hk=hk, src=src: h.dma_start(out=hk, in_=src), reads=[RGV[bt]], writes=[Rhk])
            b = P.next_bank()
            P.op('pe', lambda h, b=b, hk=hk: h.matmul(P.banks[b][:, 0:128], hk, jmat, start=True, stop=True),
                 reads=[Rhk, Rc], writes=[P.Rbank[b]])
            tmp, Rtmp = tmp_r.next()
            msk = U_ if ty == 0 else LO_
            ngm = negm[:, ty, :]
            P.op('dve', lambda h, tmp=tmp, msk=msk, b=b: h.tensor_tensor(out=tmp, in0=P.banks[b][:, 0:128], in1=msk, op=ALU.mult),
                 reads=[P.Rbank[b], Rc], writes=[Rtmp])
            P.op('dve', lambda h, tmp=tmp, ngm=ngm: h.tensor_tensor(out=tmp, in0=tmp, in1=ngm, op=ALU.add),
                 reads=[Rtmp, Rc], writes=[Rtmp])
            G_ = 16 // BRANCH_DIL[bi]
            HW_ = 128 // G_
            eh = LBh[:, hd // 2, bi, hd % 2, :].rearrange("p (c w) -> p c w", c=G_)[:, :, ty * HW_:(ty + 1) * HW_]
            el = LBl[:, hd // 2, bi, hd % 2, :].rearrange("p (c w) -> p c w", c=G_)[:, :, ty * HW_:(ty + 1) * HW_]
            tv = tmp.rearrange("p (i c) -> p c i", c=G_)
            P.op('dve', lambda h, tv=tv, eh=eh: h.tensor_copy(out=eh, in_=tv), reads=[Rtmp], writes=[REB])
            P.op('dve', lambda h, tv=tv, eh=eh, el=el: h.tensor_tensor(out=el, in0=tv, in1=eh, op=ALU.subtract),
                 reads=[Rtmp, REB], writes=[REB])
    P.barrier()
    P.reset(m1)

    PADK = 1024
    Ld16 = L // 16
    Qbd = P.alloc([128, 2, L], BF16)
    Qv = Qbd.rearrange("p h (r i) -> p h r i", r=16)
    KTp = P.alloc([128, L + 2 * PADK], BF16)
    OT = P.alloc([128, 2, L], F32)
    NTmax = L // 128 + 16
    Vbs = [P.alloc([128, NTmax, 130], BF16) for _ in range(2)]
    RVs = [P.res("Vb0"), P.res("Vb1")]
    RQ, RK, ROT = P.res("Qbd"), P.res("KTp"), P.res("OT")
    chq, chk = P.chan("q"), P.chan("k")
    chv2 = [[P.chan(f"v{i}_{k}") for k in range(4)] for i in range(2)]
    P_r = Ring(P, "Pt", 6, [128, 512], BF16)
    rc_r = Ring(P, "rc", 2, [64, 512], F32)
    so_r = Ring(P, "so", 2, [64, 512], F32, chan=True)
    dummy = P.alloc([128, 16], F32)
    P.op('pool', lambda h: h.memset(Qbd, 0.0), writes=[RQ])
    P.op('pool', lambda h: h.memset(KTp, 0.0), writes=[RK])
    S_banks = [0, 1, 2, 3]
    si = [0]
    O_bank = {(0, 0): 4, (0, 1): 5, (1, 0): 6, (1, 1): 7}

    def load_V(s, hp, bi, slot):
        dl = BRANCH_DIL[bi]
        Ld = L // dl
        NJ = Ld // 128 + 1
        Vb, RV = Vbs[slot], RVs[slot]
        Vs = C.V[s]
        c0, c1 = hp * 130, (hp + 1) * 130
        for rho in range(dl):
            tb = rho * NJ
            P.op('pool', lambda h, tb=tb, Vb=Vb: h.memset(Vb[0:64, tb, :], 0.0), writes=[RV])
            P.op('pool', lambda h, tb=tb, NJ=NJ, Vb=Vb: h.memset(Vb[64:128, tb + NJ - 1, :], 0.0), writes=[RV])
            ch_ = chv2[slot][rho % 4]
            if NJ > 2:
                src = Vs[rho + dl * 64:rho + dl * 64 + dl * 128 * (NJ - 2):dl, c0:c1]
                P.dma('sp', ch_, lambda h, tb=tb, NJ=NJ, src=src, Vb=Vb: h.dma_start(
                    out=Vb[:, tb + 1:tb + NJ - 1, :], in_=src.rearrange("(j a) f -> a j f", a=128)), writes=[RV])
            r_first = Vs[rho:rho + dl * 63 + 1:dl, c0:c1]
            t_l = rho + dl * (Ld - 64)
            r_last = Vs[t_l:t_l + dl * 63 + 1:dl, c0:c1]
            P.dma('sp', ch_, lambda h, tb=tb, r_first=r_first, Vb=Vb: h.dma_start(out=Vb[64:128, tb, :], in_=r_first), writes=[RV])
            P.dma('sp', ch_, lambda h, tb=tb, NJ=NJ, r_last=r_last, Vb=Vb: h.dma_start(
                out=Vb[0:64, tb + NJ - 1, :], in_=r_last), writes=[RV])

    work = [(s, hp, bi) for s in range(NS) for hp in range(4) for bi in range(3)]
    load_V(*work[0], 0)

    def do_work(wi, s, hp, bi):
        slot = wi % 2
        Vb, RV = Vbs[slot], RVs[slot]
        dl = BRANCH_DIL[bi]
        G = 16 // dl
        W = 256 // G
        HW = 128 // G
        Ld = L // dl
        NQ = Ld // 128
        NJ = NQ + 1
        gs = min(4, NQ)
        if bi == 0:
            r0 = hp * 128
            P.dma('sp', chq, lambda h: h.dma_start(out=Qbd[0:64, 0, :], in_=C.QT[s][r0:r0 + 64, :]), writes=[RQ])
            P.dma('sp', chq, lambda h: h.dma_start(out=Qbd[64:128, 1, :], in_=C.QT[s][r0 + 64:r0 + 128, :]), writes=[RQ])
            P.dma('sp', chk, lambda h: h.dma_start(out=KTp[:, PADK:PADK + L], in_=C.KT[s][r0:r0 + 128, :]), writes=[RK])
            P.op('pool', lambda h: h.memset(OT, 0.0), writes=[ROT])
        else:
            P.op('pool', lambda h: h.memset(dummy, 0.0), writes=[ROT])
        if wi + 1 < len(work):
            load_V(*work[wi + 1], 1 - slot)

        def v4(ap512):
            return ap512.rearrange("p (h c w) -> p h c w", h=2, c=G)

        for rho in range(dl):
            prev = None
            for j in range(NJ):
                halves = [hf for hf in (0, 1) if 0 <= j - 1 + hf < NQ]
                h0, h1 = halves[0], halves[-1] + 1
                w0, w1 = h0 * HW, h1 * HW
                i0 = (128 * (j - 1) + 128 * h0) // G
                i1 = (128 * (j - 1) + 128 * h1) // G
                ks = PADK + rho + dl * (128 * j - 64)
                bS = S_banks[si[0] % 4]
                si[0] += 1
                Sv = v4(P.banks[bS][:, :])[:, :, :, w0:w1]
                qa = Qv[:, :, rho:16:dl, i0:i1]
                ka = KTp[:, ks:ks + dl * 127 + 1:dl]
                lh = v4(LBh[:, hp, bi, :, :].rearrange("p h q -> p (h q)"))[:, :, :, w0:w1]
                ll = v4(LBl[:, hp, bi, :, :].rearrange("p h q -> p (h q)"))[:, :, :, w0:w1]
                P.op('pe', lambda h, Sv=Sv, qa=qa, ka=ka: h.matmul(Sv, ka, qa, start=True, stop=False),
                     reads=[RK, RQ], writes=[P.Rbank[bS]])
                P.op('pe', lambda h, Sv=Sv, lh=lh: h.matmul(Sv, idb, lh, start=False, stop=False),
                     reads=[Rc, REB], writes=[P.Rbank[bS]])
                P.op('pe', lambda h, Sv=Sv, ll=ll: h.matmul(Sv, idb, ll, start=False, stop=True),
                     reads=[Rc, REB], writes=[P.Rbank[bS]])
                Pt, RPt = P_r.next()
                Pv = v4(Pt)
                act(P, Pv[:, :, :, w0:w1], Sv, AF.Exp, [P.Rbank[bS]], [RPt])
                tile = rho * NJ + j
                if j >= 1:
                    m = j - 1
                    pPv, pRPt, ptile = prev
                    for hd in range(2):
                        bO = O_bank[(hd, (m // gs) % 2)]
                        oc = (m % gs) * 128
                        Ov = P.banks[bO][0:65, oc:oc + 128].rearrange("p (c w) -> p c w", c=G)
                        P.op('pe', lambda h, Ov=Ov, hd=hd, pPv=pPv, ptile=ptile: h.matmul(
                            Ov, Vb[:, ptile, hd * 65:(hd + 1) * 65], pPv[:, hd, :, HW:2 * HW], start=True, stop=False),
                            reads=[RV, pRPt], writes=[P.Rbank[bO]])
                        P.op('pe', lambda h, Ov=Ov, hd=hd, Pv=Pv, tile=tile: h.matmul(
                            Ov, Vb[:, tile, hd * 65:(hd + 1) * 65], Pv[:, hd, :, 0:HW], start=False, stop=True),
                            reads=[RV, RPt], writes=[P.Rbank[bO]])
                        if (m + 1) % gs == 0:
                            m0_ = m + 1 - gs
                            t0 = rho + dl * 128 * m0_
                            ov = OT[0:65, hd, t0 - rho:t0 - rho + dl * 128 * gs].rearrange(
                                "p (m i r) -> p m i r", m=gs, r=16)[:, :, :, rho:16:dl].rearrange("p m i c -> p m c i")
                            src = P.banks[bO][0:65, 0:gs * 128].rearrange("p (m c i) -> p m c i", m=gs, c=G)
                            P.op('dve', lambda h, ov=ov, src=src: h.tensor_tensor(out=ov, in0=ov, in1=src, op=ALU.add),
                                 reads=[ROT, P.Rbank[bO]], writes=[])
                prev = (Pv, RPt, tile)
        if bi == 2:
            P.op('pool', lambda h: h.memset(dummy, 0.0), writes=[ROT])
            for hd in range(2):
                for ct in range(L // 512):
                    b = S_banks[si[0] % 4]
                    si[0] += 1
                    P.op('pe', lambda h, b=b, hd=hd, ct=ct: h.matmul(
                        P.banks[b][0:64, :], sel, OT[0:65, hd, ct * 512:(ct + 1) * 512], start=True, stop=True),
                        reads=[Rc, ROT], writes=[P.Rbank[b]])
                    rc, Rrc = rc_r.next()
                    act(P, rc, P.banks[b][0:64, :], AF.Ln, [P.Rbank[b]], [Rrc])
                    act(P, rc, rc, AF.Exp, [Rrc], [Rrc], scale=-1.0)
                    so, Rso, soch = so_r.next()
                    P.op('dve', lambda h, so=so, rc=rc, hd=hd, ct=ct: h.tensor_tensor(
                        out=so, in0=OT[0:64, hd, ct * 512:(ct + 1) * 512], in1=rc, op=ALU.mult),
                        reads=[ROT, Rrc], writes=[Rso])
                    rr = hp * 128 + hd * 64
                    P.dma('sp', soch, lambda h, so=so, rr=rr, ct=ct: h.dma_start(
                        out=C.AT[s][rr:rr + 64, ct * 512:(ct + 1) * 512], in_=so), reads=[Rso])

    for wi, (s, hp, bi) in enumerate(work):
        do_work(wi, s, hp, bi)
    P.reset(m0)


def _ln_feature_major(P, C, tt, Rtt, nd, T, S1, S2, cst_ones, Rc, mk_out):
    mean = C.ln_mean
    m2 = C.ln_m2
    rstd = C.ln_rstd
    Rst = C.ln_Rst
    act(P, mean[:, 0:T], P.banks[S1][:, 0:T], AF.Identity, [P.Rbank[S1]], [Rst], scale=1.0 / D)
    P.op('dve', lambda h: h.tensor_tensor(out=m2[:, 0:T], in0=mean[:, 0:T], in1=mean[:, 0:T], op=ALU.mult),
         reads=[Rst], writes=[Rst])
    P.op('dve', lambda h: h.scalar_tensor_tensor(out=m2[:, 0:T], in0=P.banks[S2][:, 0:T], scalar=1.0 / D,
                                                 in1=m2[:, 0:T], op0=ALU.mult, op1=ALU.subtract),
         reads=[P.Rbank[S2], Rst], writes=[Rst])
    act(P, m2[:, 0:T], m2[:, 0:T], AF.Ln, [Rst], [Rst], bias=EPS)
    act(P, rstd[:, 0:T], m2[:, 0:T], AF.Exp, [Rst], [Rst], scale=-0.5)
    for dc in range(nd):
        u1, Ru1 = C.ln_u.next()
        P.op('dve', lambda h, u1=u1, dc=dc: h.tensor_tensor(out=u1[:, 0:T], in0=tt[:, dc, 0:T], in1=mean[:, 0:T], op=ALU.subtract),
             reads=[Rtt[dc], Rst], writes=[Ru1])
        P.op('dve', lambda h, u1=u1: h.tensor_tensor(out=u1[:, 0:T], in0=u1[:, 0:T], in1=rstd[:, 0:T], op=ALU.mult),
             reads=[Ru1, Rst], writes=[Ru1])
        mk_out(dc, u1, Ru1)


def phase_C1(C):
    P, L, NS = C.P, C.L, C.NS
    T = 512
    m0 = P.mark()
    P.bank_list = [0, 1, 2, 3, 4]
    SA, S1, S2 = 5, 6, 7
    wout = P.alloc([128, 8, D], BF16)
    cst = P.alloc([128, 6, 128], F32)
    ang = P.alloc([128, 4], F32)
    lnfm = P.alloc([128, 4, 8], F32)
    Rc = P.res("c1_const")
    chc = P.chan("c1_const")
    P.dma('pool', chc, lambda h: h.dma_start(out=wout, in_=C.w_out.rearrange("(kc p) c -> p kc c", p=128)), writes=[Rc])
    P.dma('sp', chc, lambda h: h.dma_start(out=cst, in_=C.cst), writes=[Rc])
    P.dma('sp', chc, lambda h: h.dma_start(out=ang, in_=C.ang), writes=[Rc])
    P.dma('sp', chc, lambda h: h.dma_start(out=lnfm, in_=C.lnfm), writes=[Rc])
    ONES = cst[:, 4, :]
    xt_r = Ring(P, "c1x", 3, [128, 8, T], F32, chan=True)
    at_r = Ring(P, "c1a", 2, [128, 4, T], F32, chan=True)
    yn_r = Ring(P, "c1y", 3, [128, 4, T], BF16, chan=True)
    an_r = Ring(P, "c1an", 2, [128, 4, T], BF16)
    sq_r = Ring(P, "c1sq", 3, [128, T], F32)
    sqx_r = Ring(P, "c1sqx", 2, [128, T], F32)
    rsa_r = Ring(P, "c1rsa", 2, [128, T], F32)
    tts = [P.alloc([128, 8, T], F32) for _ in range(2)]
    Rtts = [[P.res(f"tt{k}_{i}") for i in range(8)] for k in range(2)]
    C.ln_mean = P.alloc([128, T], F32)
    C.ln_m2 = P.alloc([128, T], F32)
    C.ln_rstd = P.alloc([128, T], F32)
    C.ln_Rst = P.res("lnst")
    C.ln_u = Ring(P, "lnu", 3, [128, T], F32)
    hf_r = Ring(P, "c1hf", 1, [128, 8, T], F32, chan=True)
    hb_r = Ring(P, "c1hb", 1, [128, 8, T], BF16, chan=True)
    tiles = [(s, t0) for s in range(NS) for t0 in range(0, L, T)]

    def load(s, t0):
        xb, xr, xch = xt_r.next()
        P.dma('sp', xch, lambda h: h.dma_start(out=xb, in_=C.xT[s].rearrange("(c p) t -> p c t", p=128)[:, :, t0:t0 + T]), writes=[xr])
        ab, ar, ach = at_r.next()
        P.dma('sp', ach, lambda h: h.dma_start(out=ab, in_=C.AT[s].rearrange("(c p) t -> p c t", p=128)[:, :, t0:t0 + T]), writes=[ar])
        yb, yr, ych = yn_r.next()
        P.dma('sp', ych, lambda h: h.dma_start(out=yb, in_=C.YN[s].rearrange("(c p) t -> p c t", p=128)[:, :, t0:t0 + T]), writes=[yr])
        return (xb, xr), (ab, ar), (yb, yr)

    def stage_X(ld):
        (xb, xr), (ab, ar), (yb, yr) = ld
        for fc in range(4):
            sq, Rsq = sqx_r.next()
            act(P, sq, ab[:, fc, :], AF.Square, [ar], [Rsq])
            P.op('pe', lambda h, sq=sq, fc=fc: h.matmul(P.banks[SA][:, :], ONES, sq, start=(fc == 0), stop=(fc == 3)),
                 reads=[Rc, Rsq], writes=[P.Rbank[SA]])
        rsa, Rrsa = rsa_r.next()
        act(P, rsa, P.banks[SA][:, :], AF.Ln, [P.Rbank[SA]], [Rrsa], scale=1.0 / 512, bias=EPS)
        act(P, rsa, rsa, AF.Exp, [Rrsa], [Rrsa], scale=-0.5)
        an, Ran = an_r.next()
        for fc in range(4):
            P.op('dve', lambda h, fc=fc: h.scalar_tensor_tensor(
                out=an[:, fc, :], in0=ab[:, fc, :], scalar=ang[:, fc:fc + 1], in1=rsa, op0=ALU.mult, op1=ALU.mult),
                reads=[ar, Rc, Rrsa], writes=[Ran])
        return (xb, xr), (yb, yr), (an, Ran)

    def stage_YZ(ti, xs_, inject):
        s, t0 = tiles[ti]
        (xb, xr), (yb, yr), (an, Ran) = xs_
        tt, Rtt = tts[ti % 2], Rtts[ti % 2]
        pend = None
        nxt_x = None
        for dc in range(8):
            if dc == 4 and inject is not None:
                nxt_x = inject()
            b = P.next_bank()
            mm_group(P, P.banks[b][:, :],
                     [(wout[:, kc, dc * 128:(dc + 1) * 128], an[:, kc, :] if kc < 4 else yb[:, kc - 4, :]) for kc in range(8)],
                     P.Rbank[b], [Rc, Ran, yr])
            P.op('dve', lambda h, dc=dc, b=b: h.scalar_tensor_tensor(
                out=tt[:, dc, :], in0=xb[:, dc, :], scalar=ALPHA, in1=P.banks[b][:, :], op0=ALU.mult, op1=ALU.add),
                reads=[xr, P.Rbank[b]], writes=[Rtt[dc]])
            sq, Rsq = sq_r.next()
            act(P, sq, tt[:, dc, :], AF.Square, [Rtt[dc]], [Rsq])

            def stats(dc=dc, sq=sq, Rsq=Rsq):
                P.op('pe', lambda h: h.matmul(P.banks[S1][:, :], ONES, tt[:, dc, :], start=(dc == 0), stop=(dc == 7)),
                     reads=[Rc, Rtt[dc]], writes=[P.Rbank[S1]])
                P.op('pe', lambda h: h.matmul(P.banks[S2][:, :], ONES, sq, start=(dc == 0), stop=(dc == 7)),
                     reads=[Rc, Rsq], writes=[P.Rbank[S2]])
            if pend is not None:
                pend()
            pend = stats
        pend()
        hf, Rhf, hfch = hf_r.next()
        hb, Rhb, hbch = hb_r.next()

        def mk_out(dc, u1, Ru1):
            act(P, hf[:, dc, :], u1, AF.Identity, [Ru1, Rc], [Rhf], scale=lnfm[:, 0, dc:dc + 1], bias=lnfm[:, 1, dc:dc + 1])
            act(P, hb[:, dc, :], u1, AF.Identity, [Ru1, Rc], [Rhb], scale=lnfm[:, 0, dc:dc + 1], bias=lnfm[:, 1, dc:dc + 1])
        _ln_feature_major(P, C, tt, Rtt, 8, T, S1, S2, ONES, Rc, mk_out)
        P.dma('sp', hfch, lambda h: h.dma_start(
            out=C.H1F[s].rearrange("(c p) t -> p c t", p=128)[:, :, t0:t0 + T], in_=hf), reads=[Rhf])
        P.dma('sp', hbch, lambda h: h.dma_start(
            out=C.H1B[s].rearrange("(c p) t -> p c t", p=128)[:, :, t0:t0 + T], in_=hb), reads=[Rhb])
        return nxt_x

    ld = [None] * (len(tiles) + 2)
    ld[0] = load(*tiles[0])
    if len(tiles) > 1:
        ld[1] = load(*tiles[1])
    xs_next = stage_X(ld[0])
    for ti in range(len(tiles)):
        xs_cur = xs_next
        if ti + 2 < len(tiles):
            ld[ti + 2] = load(*tiles[ti + 2])
        inj = (lambda ti=ti: stage_X(ld[ti + 1])) if ti + 1 < len(tiles) else None
        xs_next = stage_YZ(ti, xs_cur, inj)
    P.bank_list = list(range(8))
    P.reset(m0)


def phase_C2(C):
    P, L, NS = C.P, C.L, C.NS
    T = 256
    NF = DFF // 128
    m0 = P.mark()
    P.bank_list = [0, 1, 2, 3, 4, 5]
    S1, S2 = 6, 7
    wg = P.alloc([128, 8, DFF], BF16)
    wu = P.alloc([128, 8, DFF], BF16)
    wd = P.alloc([128, NF, D], BF16)
    cst = P.alloc([128, 6, 128], F32)
    lnfm = P.alloc([128, 4, 8], F32)
    Rc = P.res("c2_const")
    chc = P.chan("c2_const")
    chw = [P.chan(f"c2w{i}") for i in range(3)]
    for c0 in (0, 1408):
        P.dma('pool', chw[0], lambda h, c0=c0: h.dma_start(
            out=wg[:, :, c0:c0 + 1408], in_=C.w_gate.rearrange("(kc p) c -> p kc c", p=128)[:, :, c0:c0 + 1408]), writes=[Rc])
        P.dma('pool', chw[1], lambda h, c0=c0: h.dma_start(
            out=wu[:, :, c0:c0 + 1408], in_=C.w_up.rearrange("(kc p) c -> p kc c", p=128)[:, :, c0:c0 + 1408]), writes=[Rc])
    P.dma('pool', chw[2], lambda h: h.dma_start(out=wd, in_=C.w_down.rearrange("(kc p) c -> p kc c", p=128)), writes=[Rc])
    P.dma('sp', chc, lambda h: h.dma_start(out=cst, in_=C.cst), writes=[Rc])
    P.dma('sp', chc, lambda h: h.dma_start(out=lnfm, in_=C.lnfm), writes=[Rc])
    ONES = cst[:, 4, :]
    hb_r = Ring(P, "c2hb", 2, [128, 8, T], BF16, chan=True)
    hf_r = Ring(P, "c2hf", 4, [128, T], F32, chan=True)
    hid = P.alloc([128, NF, T], BF16)
    Rhid = P.res("hid")
    sg_r = Ring(P, "c2sg", 3, [128, T], F32)
    sq_r = Ring(P, "c2sq", 4, [128, T], F32)
    tt = P.alloc([128, 8, T], F32)
    Rtt = [P.res(f"tt2_{i}") for i in range(8)]
    C.ln_mean = P.alloc([128, T], F32)
    C.ln_m2 = P.alloc([128, T], F32)
    C.ln_rstd = P.alloc([128, T], F32)
    C.ln_Rst = P.res("lnst2")
    C.ln_u = Ring(P, "lnu2", 3, [128, T], F32)
    yo_r = Ring(P, "c2yo", 4, [128, T], F32, chan=True)
    outs = []

    def load(s, t0):
        hb, hr, hch = hb_r.next()
        P.dma('sp', hch, lambda h: h.dma_start(out=hb, in_=C.H1B[s].rearrange("(c p) t -> p c t", p=128)[:, :, t0:t0 + T]), writes=[hr])
        return hb, hr

    tiles = [(s, t0) for s in range(NS) for t0 in range(0, L, T)]
    nxt = load(*tiles[0])
    for ti, (s, t0) in enumerate(tiles):
        hb, hr = nxt
        if ti + 1 < len(tiles):
            nxt = load(*tiles[ti + 1])
        for fc in range(NF):
            bg = P.next_bank()
            mm_group(P, P.banks[bg][:, 0:T], [(wg[:, kc, fc * 128:(fc + 1) * 128], hb[:, kc, :]) for kc in range(8)],
                     P.Rbank[bg], [Rc, hr])
            bu = P.next_bank()
            mm_group(P, P.banks[bu][:, 0:T], [(wu[:, kc, fc * 128:(fc + 1) * 128], hb[:, kc, :]) for kc in range(8)],
                     P.Rbank[bu], [Rc, hr])
            sg, Rsg = sg_r.next()
            act(P, sg, P.banks[bg][:, 0:T], AF.Silu, [P.Rbank[bg]], [Rsg])
            P.op('dve', lambda h, fc=fc, sg=sg, bu=bu: h.tensor_tensor(out=hid[:, fc, :], in0=sg, in1=P.banks[bu][:, 0:T], op=ALU.mult),
                 reads=[Rsg, P.Rbank[bu]], writes=[Rhid])
        pend = None
        for dc in range(8):
            hf, Rhf, hfch = hf_r.next()
            P.dma('sp', hfch, lambda h, hf=hf, s=s, t0=t0, dc=dc: h.dma_start(
                out=hf, in_=C.H1F[s][dc * 128:(dc + 1) * 128, t0:t0 + T]), writes=[Rhf])
            b = P.next_bank()
            mm_group(P, P.banks[b][:, 0:T], [(wd[:, fc, dc * 128:(dc + 1) * 128], hid[:, fc, :]) for fc in range(NF)],
                     P.Rbank[b], [Rc, Rhid])
            P.op('dve', lambda h, dc=dc, b=b, hf=hf: h.scalar_tensor_tensor(
                out=tt[:, dc, :], in0=hf, scalar=ALPHA, in1=P.banks[b][:, 0:T], op0=ALU.mult, op1=ALU.add),
                reads=[Rhf, P.Rbank[b]], writes=[Rtt[dc]])
            sq, Rsq = sq_r.next()
            act(P, sq, tt[:, dc, :], AF.Square, [Rtt[dc]], [Rsq])

            def stats(dc=dc, sq=sq, Rsq=Rsq):
                P.op('pe', lambda h: h.matmul(P.banks[S1][:, 0:T], ONES, tt[:, dc, :], start=(dc == 0), stop=(dc == 7)),
                     reads=[Rc, Rtt[dc]], writes=[P.Rbank[S1]])
                P.op('pe', lambda h: h.matmul(P.banks[S2][:, 0:T], ONES, sq, start=(dc == 0), stop=(dc == 7)),
                     reads=[Rc, Rsq], writes=[P.Rbank[S2]])
            if pend is not None:
                pend()
            pend = stats
        pend()
        pend = None

        def mk_out(dc, u1, Ru1, s=s, t0=t0):
            yo, Ryo, yoch = yo_r.next()
            act(P, yo, u1[:, 0:T], AF.Identity, [Ru1, Rc], [Ryo], scale=lnfm[:, 2, dc:dc + 1], bias=lnfm[:, 3, dc:dc + 1])
            outs.append(P.dma('sp', yoch, lambda h, yo=yo, dc=dc: h.dma_start(
                out=C.yT[s][dc * 128:(dc + 1) * 128, t0:t0 + T], in_=yo), reads=[Ryo]))
        _ln_feature_major(P, C, tt, Rtt, 8, T, S1, S2, ONES, Rc, mk_out)
    P.bank_list = list(range(8))
    P.reset(m0)
    return outs

def to_xT16(xT):
    sh = xT.shape
    L = sh[-1]
    return np.ascontiguousarray(xT.reshape(sh[:-1] + (L // 16, 16)).swapaxes(-1, -2).reshape(sh))


def make_cst():
    i = np.arange(128)
    U = (i[:, None] <= i[None, :]).astype(np.float32)
    SL = (i[:, None] > i[None, :]).astype(np.float32)
    Lo = (i[:, None] >= i[None, :]).astype(np.float32)
    SU = (i[:, None] < i[None, :]).astype(np.float32)
    ones = np.ones((128, 128), np.float32)
    ident = np.eye(128, dtype=np.float32)
    return np.ascontiguousarray(np.stack([U, SL, Lo, SU, ones, ident], axis=1))


def shared_inputs(inp):
    f = np.float32
    g = lambda k: np.asarray(inp[k], dtype=f)
    bc = lambda v, n=128: np.ascontiguousarray(np.broadcast_to(v[None, :], (n, v.shape[0])))
    m = {}
    m["w_in"] = np.ascontiguousarray(g("w_in")[0])
    m["convw"] = np.ascontiguousarray(g("conv_w")[0].reshape(5, 8, 128).transpose(2, 1, 0))
    m["convb"] = np.ascontiguousarray(g("conv_b")[0].reshape(8, 128).T)
    m["dtb"] = bc(np.concatenate([g("dt_bias_fwd")[0], g("dt_bias_bwd")[0]]))
    m["alog"] = bc(np.concatenate([g("a_log_fwd")[0], g("a_log_bwd")[0]]))
    m["dsk"] = bc(g("d_skip")[0])
    m["ang"] = np.ascontiguousarray(g("attn_norm_g")[0].reshape(4, 128).T)
    m["sng"] = bc(g("ssd_norm_g")[0])
    m["relb"] = np.ascontiguousarray(g("rel_bias"))
    m["lnfm"] = np.ascontiguousarray(np.stack([g(k)[0].reshape(8, 128).T for k in ("ln1_g", "ln1_b", "ln2_g", "ln2_b")], axis=1))
    m["w_out"] = np.ascontiguousarray(g("w_out")[0])
    m["w_gate"] = np.ascontiguousarray(g("w_gate")[0])
    m["w_up"] = np.ascontiguousarray(g("w_up")[0])
    m["w_down"] = np.ascontiguousarray(g("w_down")[0])
    m["cst"] = make_cst()
    m["oh"], m["jmat"], m["sel"] = make_att_consts()
    cst_ = m["cst"]
    m["negm"] = np.ascontiguousarray(np.stack([(cst_[:, 0, :] - 1.0) * 30000.0, (cst_[:, 2, :] - 1.0) * 30000.0], axis=1))
    return m


def t5_bucket(rel):
    half = 16
    max_exact = 8
    ret = (rel > 0).astype(np.int32) * half
    n = np.abs(rel)
    large = max_exact + (np.log(np.maximum(n, 1) / max_exact)
                         / math.log(1024 / max_exact) * (half - max_exact)).astype(np.int32)
    large = np.minimum(large, half - 1)
    return ret + np.where(n < max_exact, n, large)


def make_att_consts():
    oh = np.zeros((32, 6, 256), np.float32)
    i = np.arange(255)
    for bi, dl in enumerate(BRANCH_DIL):
        for ty in range(2):
            rel = (i - 63) if ty == 0 else (i - 191)
            bk = t5_bucket(rel * dl)
            oh[bk, bi * 2 + ty, i] = 1.0
    eb = np.ascontiguousarray(np.eye(128, dtype=np.float32)[::-1])
    sel = np.zeros((65, 64), np.float32)
    sel[64, :] = 1.0
    return oh, eb, sel


SEQ_LEN = 8192
N_CORES = 8
SLOTS = 2
_CACHE = {}


def kernel(**inputs):
    xp = np.asarray(inputs["x_prompt"], dtype=np.float32)
    xs = np.asarray(inputs["x_sample"], dtype=np.float32)
    seqs = [xp[i] for i in range(xp.shape[0])] + [xs[i] for i in range(xs.shape[0])]
    nseq = len(seqs)
    L = seqs[0].shape[0]
    shared = shared_inputs(inputs)
    in_maps = []
    for c in range(N_CORES):
        xT = np.zeros((SLOTS, D, L), np.float32)
        for sl in range(SLOTS):
            i = c * SLOTS + sl
            if i < nseq:
                xT[sl] = seqs[i].T
        m = dict(shared)
        m["xT"] = xT
        m["xT16"] = to_xT16(xT)
        in_maps.append(m)
    key = (L, SLOTS)
    if key not in _CACHE:
        _CACHE[key] = build(L, SLOTS)
    nc, _ = _CACHE[key]
    res = run_bass_kernel_spmd(nc, in_maps, core_ids=list(range(N_CORES)))
    outs = []
    for i in range(nseq):
        c, sl = divmod(i, SLOTS)
        outs.append(np.ascontiguousarray(np.asarray(res.results[c]["yT"][sl]).T))
    y_prompt = np.stack(outs[:xp.shape[0]]).astype(np.float32)
    y_sample = np.stack(outs[xp.shape[0]:]).astype(np.float32)
    return (y_prompt, y_sample)
```

```python
import contextlib
import math
import numpy as np
import concourse.bass as bass
import concourse.mybir as mybir
from concourse.bass_utils import run_bass_kernel_spmd

F32 = mybir.dt.float32
BF16 = mybir.dt.bfloat16
U8 = mybir.dt.uint8
AF = mybir.ActivationFunctionType
ALU = mybir.AluOpType

D = 1024
DIN = 3088
DFF = 2816
NH = 8
COL_Q, COL_K, COL_V, COL_Z, COL_X, COL_DT = 0, 512, 1024, 1536, 2048, 3072
ALPHA = 2.0 ** 0.25
EPS = 1e-5
ENGS = ['pe', 'act', 'dve', 'pool', 'sp']
EPOCH = 20000
POOL_BYTES = 206 * 1024


def _dsize(dt):
    return {F32: 4, BF16: 2, U8: 1}[dt]


class Res:
    __slots__ = ('name', 'w', 'r')

    def __init__(self, name):
        self.name = name
        self.w = {}
        self.r = {}


class Chan:
    def __init__(self, name):
        self.name = name
        self.n = 0
        self.sem = None
        self.last = None


class Prog:
    def __init__(self, nc):
        self.nc = nc
        self.streams = {e: [] for e in ENGS}
        self.last_op = {e: None for e in ENGS}
        self.chans = []
        self.stack = contextlib.ExitStack()
        self.nres = 0
        self.nseq = 0
        self.pool = self.stack.enter_context(nc.sbuf_tensor("pool", [128, POOL_BYTES], U8))
        self.off = 0
        self.peak = 0
        self.banks = [self.stack.enter_context(nc.psum_tensor(f"bank{i}", [128, 512], F32))
                      for i in range(8)]
        self.Rbank = [self.res(f"bank{i}") for i in range(8)]

    def res(self, name=None):
        self.nres += 1
        return Res(name or f"r{self.nres}")

    def chan(self, name):
        c = Chan(name)
        self.chans.append(c)
        return c

    def alloc(self, shape, dtype):
        n = 1
        for s in shape[1:]:
            n *= s
        nbytes = n * _dsize(dtype)
        self.off = (self.off + 63) // 64 * 64
        assert self.off + nbytes <= POOL_BYTES, f"SBUF overflow {self.off + nbytes}"
        ap = self.pool[0:shape[0], self.off:self.off + nbytes].bitcast(dtype)
        self.off += nbytes
        self.peak = max(self.peak, self.off)
        if len(shape) > 2:
            names = [f"d{i}" for i in range(len(shape) - 1)]
            pat = "p (" + " ".join(names) + ") -> p " + " ".join(names)
            ap = ap.rearrange(pat, **{names[i]: shape[i + 1] for i in range(len(names))})
        return ap

    def mark(self):
        return self.off

    bank_list = list(range(8))

    def next_bank(self):
        self.bank_i = getattr(self, 'bank_i', -1) + 1
        return self.bank_list[self.bank_i % len(self.bank_list)]

    def reset(self, m):
        self.off = m

    def _deps(self, reads, writes):
        deps = {}
        for r in reads:
            for v in r.w.values():
                deps[id(v)] = v
        for w in writes:
            for v in w.w.values():
                deps[id(v)] = v
            for v in w.r.values():
                deps[id(v)] = v
        return list(deps.values())

    def op(self, eng, fn, reads=(), writes=()):
        ins = dict(fn=fn, deps=self._deps(reads, writes), signal=False, chan=None, eng=eng, seq=self.nseq)
        self.nseq += 1
        self.streams[eng].append(ins)
        self.last_op[eng] = ins
        for r in reads:
            r.r[id(ins)] = ins
        for w in writes:
            w.w = {eng: ins}
            w.r = {}
        return ins

    def dma(self, q, chan, fn, reads=(), writes=()):
        deps = self._deps(reads, writes)
        if chan.n > 0:
            deps.append(chan.last)
        ins = dict(fn=fn, deps=deps, signal=True, chan=chan, eng=q, seq=self.nseq, n=chan.n)
        self.nseq += 1
        self.streams[q].append(ins)
        chan.n += 1
        chan.last = ins
        for r in reads:
            r.r[id(ins)] = ins
        for w in writes:
            w.w = {chan: ins}
            w.r = {}
        return ins

    def barrier(self):
        evs = []
        for e in ENGS:
            if self.last_op[e] is not None:
                evs.append(self.last_op[e])
        for c in self.chans:
            if c.n > 0:
                evs.append(c.last)
        for e in ENGS:
            self.streams[e].append(dict(fn=None, deps=[d for d in evs if not (d['chan'] is None and d['eng'] == e)],
                                        signal=False, chan=None, eng=e, seq=self.nseq, barrier=True))
        self.nseq += 1

    def wait_all(self, eng, evs):
        self.streams[eng].append(dict(fn=None, deps=list(evs), signal=False, chan=None, eng=eng, seq=self.nseq,
                                      barrier=True))
        self.nseq += 1

    def _cost(self, ins):
        rec = _Fake()
        try:
            ins['fn'](rec)
        except Exception:
            return 300.0
        name, args, kw = rec.call
        try:
            if name == 'dma_start':
                o = kw.get('out')
                n = 1
                for d in o.shape:
                    n *= d
                ins['dma_ns'] = 2000.0 + n * _dsize(o.dtype) / 150.0
                return 60.0
            if name in ('matmul', 'transpose'):
                rhs = kw.get('rhs', args[2] if len(args) > 2 else None)
                n = 1
                for d in rhs.shape[1:]:
                    n *= d
                st = 1
                try:
                    st = max(1, abs(rhs.ap[-1][0]))
                except Exception:
                    pass
                c = max(64, n) / 2.4
                if rhs.dtype == F32 and name == 'matmul':
                    c *= 4
                elif st > 1:
                    c *= min(8, st) / 2.0 + 0.5
                return c + 8
            o = kw.get('out', args[0] if args else None)
            n = 1
            for d in o.shape[1:]:
                n *= d
            e = ins['eng']
            if e == 'act':
                return 70 + n * 0.96 + (90 if 'accum_out' in kw else 0)
            if e == 'dve':
                return 65 + n * 1.04
            return 110 + n * (0.6 if name == 'memset' else 2.3)
        except Exception:
            return 300.0

    def schedule(self, window=32):
        import os
        LAT = float(os.environ.get('SCHED_LAT', '120'))
        new_streams = {e: [] for e in ENGS}
        pos = {e: 0 for e in ENGS}
        free = {e: 0.0 for e in ENGS}
        tnow = 0.0
        while any(pos[e] < len(self.streams[e]) for e in ENGS):
            seg = {}
            for e in ENGS:
                st = self.streams[e]
                i = pos[e]
                j = i
                while j < len(st) and not st[j].get('barrier'):
                    j += 1
                seg[e] = st[i:j]
                bar = st[j] if j < len(st) else None
                pos[e] = j + 1 if j < len(st) else j
                seg[e + '_bar'] = bar
            for e in ENGS:
                for ins in seg[e]:
                    ins['cost'] = self._cost(ins)
                    ins['fin'] = None
                free[e] = tnow
            pend = {e: list(seg[e]) for e in ENGS}
            nleft = sum(len(v) for v in pend.values())
            while nleft:
                progressed = False
                for e in sorted(ENGS, key=lambda x: free[x]):
                    pl = pend[e]
                    if not pl:
                        continue
                    best = None
                    bstart = None
                    for k in range(min(window if e != 'sp' else 6, len(pl))):
                        ins = pl[k]
                        rd = ins.get('ready')
                        if rd is None:
                            rd = 0.0
                            ok = True
                            for d in ins['deps']:
                                f = d.get('fin', 0.0)
                                if f is None:
                                    ok = False
                                    break
                                if d['chan'] is not None:
                                    f = d.get('dfin', f)
                                if d['eng'] == e and d['chan'] is None and e == 'pe':
                                    f = f - d['cost'] * 0.5
                                else:
                                    f = f + LAT
                                if f > rd:
                                    rd = f
                            if not ok:
                                continue
                            ins['ready'] = rd
                        stt = rd if rd > free[e] else free[e]
                        if bstart is None or stt < bstart - 1e-9:
                            best, bstart = k, stt
                            if stt <= free[e]:
                                break
                    if best is None:
                        continue
                    ins = pl.pop(best)
                    ins['fin'] = bstart + ins['cost']
                    if ins['chan'] is not None:
                        ins['dfin'] = bstart + ins.get('dma_ns', 2000.0)
                    free[e] = ins['fin']
                    new_streams[e].append(ins)
                    nleft -= 1
                    progressed = True
                    break
                assert progressed, "scheduler deadlock"
            tnow = max(free.values())
            for e in ENGS:
                for ins in seg[e]:
                    if ins['chan'] is not None and ins.get('dfin', 0) > tnow:
                        tnow = ins['dfin']
            for e in ENGS:
                if seg[e + '_bar'] is not None:
                    seg[e + '_bar']['fin'] = tnow
                    new_streams[e].append(seg[e + '_bar'])
        self.streams = new_streams
        self.est_ns = tnow

    def emit(self, sched=True):
        nc = self.nc
        import os
        if sched and os.environ.get('NOSCHED') != '1':
            self.schedule()
        for e in ENGS:
            for ins in self.streams[e]:
                for d in ins['deps']:
                    if d['chan'] is None:
                        if d['eng'] == 'pe' and e == 'pe' and ins['chan'] is None and ins['fn'] is not None:
                            continue
                        d['signal'] = True
        nsig = {}
        for e in ENGS:
            c = 0
            for ins in self.streams[e]:
                if ins['chan'] is None and ins['signal'] and ins['fn'] is not None:
                    c += 1
                    ins['cnt'] = c
            nsig[e] = c
        sems = {}
        for e in ENGS:
            ne = max(1, -(-nsig[e] // EPOCH))
            sems[e] = [self.stack.enter_context(nc.semaphore(f"s_{e}{i}")) for i in range(ne)]
        for c in self.chans:
            c.sem = self.stack.enter_context(nc.semaphore(f"c_{c.name}"))
        self.stats = {e: [len(self.streams[e]), nsig[e], 0] for e in ENGS}

        def emit_stream(e, h):
            waited = {}
            nw = 0
            for ins in self.streams[e]:
                need = {}
                for d in ins['deps']:
                    if d['chan'] is None:
                        if d['eng'] == 'pe' and e == 'pe' and ins['chan'] is None and ins['fn'] is not None:
                            continue
                        cnt = d['cnt']
                        ep = (cnt - 1) // EPOCH
                        sem = sems[d['eng']][ep]
                        val = cnt - ep * EPOCH
                    else:
                        sem = d['chan'].sem
                        val = 16 * (d['n'] + 1)
                    k = id(sem)
                    if waited.get(k, 0) >= val:
                        continue
                    if k not in need or need[k][1] < val:
                        need[k] = (sem, val)
                for k, (sem, val) in need.items():
                    h.wait_ge(sem, val)
                    waited[k] = val
                    nw += 1
                if ins['fn'] is None:
                    continue
                r = ins['fn'](h)
                if ins['chan'] is not None:
                    r.then_inc(ins['chan'].sem, 16)
                elif ins['signal']:
                    cnt = ins['cnt']
                    ep = (cnt - 1) // EPOCH
                    r.then_inc(sems[e][ep], 1)
            self.stats[e][2] = nw

        with nc.Block() as block:
            @block.tensor
            def _(h):
                emit_stream('pe', h)

            @block.scalar
            def _(h):
                emit_stream('act', h)

            @block.vector
            def _(h):
                emit_stream('dve', h)

            @block.gpsimd
            def _(h):
                emit_stream('pool', h)

            @block.sync
            def _(h):
                emit_stream('sp', h)
        self.stack.close()


class _Fake:
    def __init__(self):
        self.call = None

    def __getattr__(self, name):
        def f(*a, **k):
            self.call = (name, a, k)
            return self
        return f


def mm_group(P, out_ap, pairs, Rout, reads):
    n = len(pairs)
    for i, (l, r) in enumerate(pairs):
        P.op('pe', lambda h, l=l, r=r, i=i: h.matmul(out_ap, l, r, start=(i == 0), stop=(i == n - 1)),
             reads=reads, writes=[Rout])


def act(P, out, in_, func, reads, writes, scale=None, bias=None):
    kw = {}
    if scale is not None:
        kw['scale'] = scale
    if bias is not None:
        kw['bias'] = bias
    return P.op('act', lambda h: h.activation(out=out, in_=in_, func=func, **kw), reads=reads, writes=writes)


class Ring:
    def __init__(self, P, name, n, shape, dtype, chan=False):
        self.bufs = [P.alloc(shape, dtype) for _ in range(n)]
        self.res = [P.res(f"{name}{i}") for i in range(n)]
        self.ch = [P.chan(f"{name}{i}") for i in range(n)] if chan else None
        self.i = -1
        self.n = n

    def next(self):
        self.i = (self.i + 1) % self.n
        if self.ch:
            return self.bufs[self.i], self.res[self.i], self.ch[self.i]
        return self.bufs[self.i], self.res[self.i]


class Ctx:
    pass


def build(L, NS, debug=False, phases=('A', 'S', 'T', 'C')):
    nc = bass.Bass("TRN2", target_bir_lowering=False)
    P = Prog(nc)
    C = Ctx()
    C.L, C.NS, C.P, C.nc = L, NS, P, nc
    NT = L // 512

    def din(name, shape, dt=F32):
        return nc.dram_tensor(name, list(shape), dt, kind="ExternalInput").ap()

    def dscr(name, shape, dt):
        return nc.dram_tensor(name, list(shape), dt,
                              kind="ExternalOutput" if debug else "Internal").ap()

    C.xT = din("xT", [NS, D, L])
    C.xT16 = din("xT16", [NS, D, L])
    C.w_in = din("w_in", [D, DIN])
    C.convw = din("convw", [128, 8, 5])
    C.convb = din("convb", [128, 8])
    C.dtb = din("dtb", [128, 16])
    C.alog = din("alog", [128, 16])
    C.dsk = din("dsk", [128, 8])
    C.ang = din("ang", [128, 4])
    C.sng = din("sng", [128, 512])
    C.relb = din("relb", [32, 8])
    C.w_out = din("w_out", [D, D])
    C.w_gate = din("w_gate", [D, DFF])
    C.w_up = din("w_up", [D, DFF])
    C.w_down = din("w_down", [DFF, D])
    C.cst = din("cst", [128, 6, 128])
    C.oh = din("oh", [32, 6, 256])
    C.jmat = din("jmat", [128, 128])
    C.sel = din("sel", [65, 64])
    C.negm = din("negm", [128, 2, 128])
    C.GV = dscr("GV", [6, 8, 256], F32)
    C.lnfm = din("lnfm", [128, 4, 8])
    C.yT = nc.dram_tensor("yT", [NS, D, L], F32, kind="ExternalOutput").ap()
    C.H1F = dscr("H1F", [NS, D, L], F32)
    C.H1B = dscr("H1B", [NS, D, L], BF16)

    C.QT = dscr("QT", [NS, 512, L], BF16)
    C.KT = dscr("KT", [NS, 512, L], BF16)
    C.V = dscr("V", [NS, L, 8 * 65], BF16)
    C.Z = dscr("Z", [NS, L, 512], F32)
    C.XS = dscr("XS", [NS, 512, L], F32)
    C.BC = dscr("BC", [NS, 512, L], BF16)
    C.DT = dscr("DT", [NS, L, 16], F32)

    outs = []
    C.YN = dscr("YN", [NS, 512, L], BF16)
    C.AT = dscr("AT", [NS, 512, L], F32)
    if 'A' in phases:
        phase_A(C)
        P.barrier()
    if 'S' in phases:
        phase_S(C)
        P.barrier()
    if 'T' in phases:
        phase_T(C)
        P.barrier()
    if 'C' in phases:
        phase_C1(C)
        P.barrier()
        outs = phase_C2(C)
        P.barrier()
    if debug:
        evs = [c.last for c in P.chans if c.n > 0]
        P.wait_all('sp', evs)
    else:
        P.wait_all('sp', outs)
    P.emit()
    return nc, P


def phase_A(C):
    P, L, NS = C.P, C.L, C.NS
    NT = L // 512
    m0 = P.mark()
    win = P.alloc([128, 8, DIN], BF16)
    Rwin = P.res("win")
    cw = P.alloc([128, 8, 5], F32)
    cb = P.alloc([128, 8], F32)
    dtb = P.alloc([128, 16], F32)
    Rsm = P.res("small")
    ch_w = P.chan("w")
    w_v = C.w_in.rearrange("(kc p) c -> p kc c", p=128)
    for c0 in (0, 1544):
        P.dma('pool', ch_w, lambda h, c0=c0: h.dma_start(out=win[:, :, c0:c0 + 1544], in_=w_v[:, :, c0:c0 + 1544]),
              writes=[Rwin])
    ch_s = P.chan("small")
    P.dma('sp', ch_s, lambda h: h.dma_start(out=cw, in_=C.convw), writes=[Rsm])
    P.dma('sp', ch_s, lambda h: h.dma_start(out=cb, in_=C.convb), writes=[Rsm])
    P.dma('sp', ch_s, lambda h: h.dma_start(out=dtb, in_=C.dtb), writes=[Rsm])

    xtb = Ring(P, "xtb", 2, [128, 8, 512], BF16, chan=True)
    xtb16 = Ring(P, "xtb16", 2, [128, 8, 512], BF16, chan=True)
    stq = Ring(P, "stq", 2, [128, 4, 512], BF16, chan=True)
    stk = Ring(P, "stk", 2, [128, 4, 512], BF16, chan=True)
    stx = Ring(P, "stx", 2, [128, 4, 512], F32, chan=True)
    stbc = Ring(P, "stbc", 2, [128, 4, 512], BF16, chan=True)
    stv = Ring(P, "stv", 2, [128, 4, 8, 65], BF16, chan=True)
    stz = Ring(P, "stz", 2, [128, 4, 512], F32, chan=True)
    stdt = Ring(P, "stdt", 2, [128, 4, 16], F32, chan=True)
    raw = [P.alloc([128, 520], F32) for _ in range(8)]
    Rraw = [P.res(f"raw{c}") for c in range(8)]
    acc = Ring(P, "acc", 4, [128, 512], F32)
    dtt = Ring(P, "dtt", 2, [128, 16], F32)
    for b, r in zip(stv.bufs, stv.res):
        P.op('pool', lambda h, b=b: h.memset(b, 1.0), writes=[r])
    fm_banks = [0, 1, 2]
    tm_banks = [3, 4, 5, 6]
    dt_bank = 7
    fmi = [0]
    tmi = [0]

    def next_fm():
        b = fm_banks[fmi[0] % len(fm_banks)]
        fmi[0] += 1
        return b

    def next_tm():
        b = tm_banks[tmi[0] % len(tm_banks)]
        tmi[0] += 1
        return b

    def load_x(s, T):
        buf, r, ch = xtb.next()
        src = C.xT[s].rearrange("(kc p) t -> p kc t", p=128)[:, :, T * 512:(T + 1) * 512]
        P.dma('pool', ch, lambda h: h.dma_start(out=buf, in_=src), writes=[r])
        buf2, r2, ch2 = xtb16.next()
        src2 = C.xT16[s].rearrange("(kc p) t -> p kc t", p=128)[:, :, T * 512:(T + 1) * 512]
        P.dma('pool', ch2, lambda h: h.dma_start(out=buf2, in_=src2), writes=[r2])
        return buf, r, buf2, r2

    def conv_chunk_ops(c, width, accb, Racc):
        ops = []
        rb = raw[c]
        ops.append(lambda: P.op('dve', lambda h: h.tensor_scalar(
            out=accb[:, 0:width], in0=rb[:, 0:width], scalar1=cw[:, c, 0:1], scalar2=cb[:, c:c + 1],
            op0=ALU.mult, op1=ALU.add), reads=[Rraw[c], Rsm], writes=[Racc]))
        for j in range(1, 5):
            ops.append(lambda j=j: P.op('dve', lambda h: h.scalar_tensor_tensor(
                out=accb[:, 0:width], in0=rb[:, j:j + width], scalar=cw[:, c, j:j + 1], in1=accb[:, 0:width],
                op0=ALU.mult, op1=ALU.add), reads=[Rraw[c], Rsm, Racc], writes=[Racc]))
        return ops

    def conv_finish(s, c, width, accb, Racc, xbuf, xr, bcbuf, bcr):
        if c < 4:
            act(P, xbuf[:, c, 0:width], accb[:, 0:width], AF.Silu, [Racc], [xr])
        else:
            act(P, bcbuf[:, c - 4, 0:width], accb[:, 0:width], AF.Silu, [Racc], [bcr])

    for s in range(NS):
        for c in range(8):
            P.op('pool', lambda h, c=c: h.memset(raw[c][:, 0:4], 0.0), writes=[Rraw[c]])
        nxt = load_x(s, 0)
        for T in range(NT):
            xb, xr_, xb16, xr16 = nxt
            if T + 1 < NT:
                nxt = load_x(s, T + 1)
            t0 = T * 512
            qb, qr, qch = stq.next()
            kb, kr, kch = stk.next()
            for c in range(4):
                b = next_fm()
                mm_group(P, P.banks[b][:, :], [(win[:, kc, COL_Q + c * 128:COL_Q + (c + 1) * 128], xb16[:, kc, :])
                                               for kc in range(8)], P.Rbank[b], [Rwin, xr16])
                act(P, qb[:, c, :], P.banks[b][:, :], AF.Identity, [P.Rbank[b]], [qr], scale=0.125)
            P.dma('sp', qch, lambda h, qb=qb, s=s, t0=t0: h.dma_start(
                out=C.QT[s].rearrange("(c p) t -> p c t", p=128)[:, :, t0:t0 + 512], in_=qb), reads=[qr])
            for c in range(4):
                b = next_fm()
                mm_group(P, P.banks[b][:, :], [(win[:, kc, COL_K + c * 128:COL_K + (c + 1) * 128], xb[:, kc, :])
                                               for kc in range(8)], P.Rbank[b], [Rwin, xr_])
                P.op('dve', lambda h, b=b, c=c, kb=kb: h.tensor_copy(out=kb[:, c, :], in_=P.banks[b][:, :]),
                     reads=[P.Rbank[b]], writes=[kr])
            P.dma('sp', kch, lambda h, kb=kb, s=s, t0=t0: h.dma_start(
                out=C.KT[s].rearrange("(c p) t -> p c t", p=128)[:, :, t0:t0 + 512], in_=kb), reads=[kr])
            sxb, sxr, sxch = stx.next()
            sbb, sbr, sbch = stbc.next()
            for cp in range(4):
                chains = []
                accs = []
                for c in (2 * cp, 2 * cp + 1):
                    b = next_fm()
                    mm_group(P, P.banks[b][:, :],
                             [(win[:, kc, COL_X + c * 128:COL_X + (c + 1) * 128], xb[:, kc, :]) for kc in range(8)],
                             P.Rbank[b], [Rwin, xr_])
                    act(P, raw[c][:, 4:516], P.banks[b][:, :], AF.Identity, [P.Rbank[b]], [Rraw[c]])
                    ab, ar = acc.next()
                    accs.append((c, ab, ar))
                    chains.append(conv_chunk_ops(c, 512, ab, ar))
                for j in range(5):
                    for ch_ in chains:
                        ch_[j]()
                for (c, ab, ar) in accs:
                    conv_finish(s, c, 512, ab, ar, sxb, sxr, sbb, sbr)
                    P.op('pool', lambda h, c=c: h.tensor_copy(out=raw[c][:, 0:4], in_=raw[c][:, 512:516]),
                         reads=[Rraw[c]], writes=[Rraw[c]])
            lo = 2 if T == 0 else 0
            P.dma('sp', sxch, lambda h, sxb=sxb, s=s, t0=t0, lo=lo: h.dma_start(
                out=C.XS[s].rearrange("(c p) t -> p c t", p=128)[:, :, t0 - 2 + lo:t0 + 510],
                in_=sxb[:, :, lo:512]), reads=[sxr])
            P.dma('sp', sbch, lambda h, sbb=sbb, s=s, t0=t0, lo=lo: h.dma_start(
                out=C.BC[s].rearrange("(c p) t -> p c t", p=128)[:, :, t0 - 2 + lo:t0 + 510],
                in_=sbb[:, :, lo:512]), reads=[sbr])
            vb, vr, vch = stv.next()
            zb, zr, zch = stz.next()
            db, dr, dch = stdt.next()
            for u in range(4):
                lw = [xb[:, kc, u * 128:(u + 1) * 128] for kc in range(8)]
                b = next_tm()
                mm_group(P, P.banks[b][:, :], [(lw[kc], win[:, kc, COL_V:COL_V + 512]) for kc in range(8)],
                         P.Rbank[b], [Rwin, xr_])
                act(P, vb[:, u, :, 0:64], P.banks[b][:, :].rearrange("p (h d) -> p h d", d=64), AF.Identity,
                    [P.Rbank[b]], [vr])
                b = next_tm()
                mm_group(P, P.banks[b][:, :], [(lw[kc], win[:, kc, COL_Z:COL_Z + 512]) for kc in range(8)],
                         P.Rbank[b], [Rwin, xr_])
                act(P, zb[:, u, :], P.banks[b][:, :], AF.Silu, [P.Rbank[b]], [zr])
                b = dt_bank
                mm_group(P, P.banks[b][:, 0:16], [(lw[kc], win[:, kc, COL_DT:COL_DT + 16]) for kc in range(8)],
                         P.Rbank[b], [Rwin, xr_])
                tb, tr = dtt.next()
                P.op('dve', lambda h, b=b, tb=tb: h.tensor_tensor(out=tb, in0=P.banks[b][:, 0:16], in1=dtb, op=ALU.add),
                     reads=[P.Rbank[b], Rsm], writes=[tr])
                act(P, tb, tb, AF.Exp, [tr], [tr])
                act(P, db[:, u, :], tb, AF.Ln, [tr], [dr], bias=1.0)
            P.dma('sp', vch, lambda h, vb=vb, s=s, t0=t0: h.dma_start(
                out=C.V[s][t0:t0 + 512, :].rearrange("(u p) f -> p u f", p=128),
                in_=vb.rearrange("p u h e -> p u (h e)")), reads=[vr])
            P.dma('sp', zch, lambda h, zb=zb, s=s, t0=t0: h.dma_start(
                out=C.Z[s][t0:t0 + 512, :].rearrange("(u p) f -> p u f", p=128), in_=zb), reads=[zr])
            P.dma('sp', dch, lambda h, db=db, s=s, t0=t0: h.dma_start(
                out=C.DT[s][t0:t0 + 512, :].rearrange("(u p) f -> p u f", p=128), in_=db), reads=[dr])
        sxb, sxr, sxch = stx.next()
        sbb, sbr, sbch = stbc.next()
        for c in range(8):
            P.op('pool', lambda h, c=c: h.memset(raw[c][:, 4:8], 0.0), reads=[Rraw[c]], writes=[Rraw[c]])
            ab, ar = acc.next()
            for o in conv_chunk_ops(c, 2, ab, ar):
                o()
            conv_finish(s, c, 2, ab, ar, sxb, sxr, sbb, sbr)
        P.dma('sp', sxch, lambda h, sxb=sxb, s=s: h.dma_start(
            out=C.XS[s].rearrange("(c p) t -> p c t", p=128)[:, :, L - 2:L], in_=sxb[:, :, 0:2]),
            reads=[sxr])
        P.dma('sp', sbch, lambda h, sbb=sbb, s=s: h.dma_start(
            out=C.BC[s].rearrange("(c p) t -> p c t", p=128)[:, :, L - 2:L], in_=sbb[:, :, 0:2]),
            reads=[sbr])
    P.reset(m0)


def phase_S(C):
    P, L, NS = C.P, C.L, C.NS
    NC = L // 128
    NG = NC // 4
    m0 = P.mark()
    cst = P.alloc([128, 6, 128], F32)
    idb = P.alloc([128, 128], BF16)
    dsk = P.alloc([128, 8], F32)
    Aneg = P.alloc([128, 16], F32)
    sng = P.alloc([128, 512], F32)
    Rc = P.res("s_const")
    chc = P.chan("s_const")
    P.dma('sp', chc, lambda h: h.dma_start(out=cst, in_=C.cst), writes=[Rc])
    P.dma('sp', chc, lambda h: h.dma_start(out=dsk, in_=C.dsk), writes=[Rc])
    P.dma('sp', chc, lambda h: h.dma_start(out=Aneg, in_=C.alog), writes=[Rc])
    P.dma('sp', chc, lambda h: h.dma_start(out=sng, in_=C.sng), writes=[Rc])
    act(P, Aneg, Aneg, AF.Exp, [Rc], [Rc])
    P.op('dve', lambda h: h.tensor_scalar(out=Aneg, in0=Aneg, scalar1=-1.0, scalar2=None, op0=ALU.mult),
         reads=[Rc], writes=[Rc])
    P.op('dve', lambda h: h.tensor_copy(out=idb, in_=cst[:, 5, :]), reads=[Rc], writes=[Rc])
    U_, SL_, LO_, SU_, ON_, ID_ = [cst[:, i, :] for i in range(6)]

    SbAll = P.alloc([128, NC, 512], BF16)
    RSb = [P.res(f"sb{c}") for c in range(NC)]
    gx = Ring(P, "gx", 3, [128, 4, 512], F32, chan=True)
    gbc = Ring(P, "gbc", 3, [128, 4, 512], BF16, chan=True)
    gdt = Ring(P, "gdt", 3, [128, 4, 16], F32, chan=True)
    gz = Ring(P, "gz", 3, [128, 4, 512], F32, chan=True)
    syn = Ring(P, "syn", 2, [128, 4, 512], BF16, chan=True)
    da_r = Ring(P, "da", 3, [128, 16], F32)
    ew_r = Ring(P, "ew", 4, [128, 64], F32)
    sc_r = Ring(P, "sc", 4, [128, 16], F32)
    xdt_r = Ring(P, "xdt", 6, [128, 512], BF16)
    xsd_r = Ring(P, "xsd", 3, [128, 512], F32)
    btok_r = Ring(P, "btok", 3, [128, 256], BF16)
    cbm_r = Ring(P, "cbm", 4, [128, 256], F32)
    L_r = Ring(P, "Lr", 2, [128, 8, 128], F32)
    dec_r = Ring(P, "dec", 2, [128, 8, 128], F32)
    M_r = Ring(P, "Mr", 4, [128, 8, 128], BF16)
    t_r = Ring(P, "tr", 4, [128, 512], F32)
    yn_r = Ring(P, "yn", 2, [128, 512], BF16)
    sm_r = Ring(P, "sm", 4, [128, 4], F32)
    Sf = P.alloc([128, 512], F32)
    Sfb = P.alloc([128, 512], BF16)
    Sb = P.alloc([128, 512], F32)
    RSf, RSfb, RSbr = P.res("Sf"), P.res("Sfb"), P.res("Sbr")

    def bc8(ap8):
        return ap8.unsqueeze(2).to_broadcast([128, 8, 64])

    def v3(ap):
        return ap.rearrange("p (h d) -> p h d", d=64)

    def load_group(s, g, with_z):
        t0 = g * 512
        xb, xr, xch = gx.next()
        P.dma('sp', xch, lambda h: h.dma_start(
            out=xb, in_=C.XS[s].rearrange("(c p) t -> p c t", p=128)[:, :, t0:t0 + 512]), writes=[xr])
        bb, br, bch = gbc.next()
        P.dma('sp', bch, lambda h: h.dma_start(
            out=bb, in_=C.BC[s].rearrange("(c p) t -> p c t", p=128)[:, :, t0:t0 + 512]), writes=[br])
        db, dr, dch = gdt.next()
        P.dma('sp', dch, lambda h: h.dma_start(
            out=db, in_=C.DT[s][t0:t0 + 512, :].rearrange("(u p) f -> p u f", p=128)), writes=[dr])
        zz = None
        if with_z:
            zb, zr, zch = gz.next()
            P.dma('sp', zch, lambda h: h.dma_start(
                out=zb, in_=C.Z[s][t0:t0 + 512, :].rearrange("(u p) f -> p u f", p=128)), writes=[zr])
            zz = (zb, zr)
        return (xb, xr), (bb, br), (db, dr), zz

    def small_mms(da, Rda, mats):
        b = P.next_bank()
        for i, m_ in enumerate(mats):
            P.op('pe', lambda h, i=i, m_=m_, b=b: h.matmul(P.banks[b][:, 16 * i:16 * i + 16], m_, da, start=True, stop=True),
                 reads=[Rc, Rda], writes=[P.Rbank[b]])
        ew, Rew = ew_r.next()
        n = 16 * len(mats)
        act(P, ew[:, 0:n], P.banks[b][:, 0:n], AF.Exp, [P.Rbank[b]], [Rew])
        return ew, Rew

    def xs_transpose(xb, xr, u):
        b = P.next_bank()
        for fc in range(4):
            P.op('pe', lambda h, fc=fc, b=b: h.transpose(P.banks[b][:, fc * 128:(fc + 1) * 128],
                                                        xb[:, fc, u * 128:(u + 1) * 128], ID_),
                 reads=[xr, Rc], writes=[P.Rbank[b]])
        return b

    def b_transpose(bb, br, u):
        b = P.next_bank()
        pb = P.banks[b][:, :].bitcast(BF16)
        for g in range(2):
            P.op('pe', lambda h, g=g, pb=pb: h.transpose(pb[:, g * 128:(g + 1) * 128],
                                                        bb[:, g, u * 128:(u + 1) * 128], idb),
                 reads=[br, Rc], writes=[P.Rbank[b]])
        bt, Rbt = btok_r.next()
        act(P, bt, pb[:, 0:256], AF.Identity, [P.Rbank[b]], [Rbt])
        return bt, Rbt

    def state_mm(bt, Rbt, xw, Rxw):
        b = P.next_bank()
        for g in range(2):
            P.op('pe', lambda h, g=g, b=b: h.matmul(P.banks[b][:, g * 256:(g + 1) * 256], bt[:, g * 128:(g + 1) * 128],
                                                   xw[:, g * 256:(g + 1) * 256], start=True, stop=True),
                 reads=[Rbt, Rxw], writes=[P.Rbank[b]])
        return b

    def s1_front(db, dr, xb, xr, bb, br, u):
        da, Rda = da_r.next()
        P.op('dve', lambda h: h.tensor_tensor(out=da, in0=db[:, u, :], in1=Aneg, op=ALU.mult),
             reads=[dr, Rc], writes=[Rda])
        ew, Rew = small_mms(da, Rda, [SU_, ON_])
        sc, Rsc = sc_r.next()
        P.op('dve', lambda h: h.tensor_tensor(out=sc[:, 0:8], in0=db[:, u, 8:16], in1=ew[:, 8:16], op=ALU.mult),
             reads=[dr, Rew], writes=[Rsc])
        bx = xs_transpose(xb, xr, u)
        xw, Rxw = xdt_r.next()
        P.op('dve', lambda h: h.tensor_tensor(out=v3(xw), in0=v3(P.banks[bx][:, :]), in1=bc8(sc[:, 0:8]), op=ALU.mult),
             reads=[P.Rbank[bx], Rsc], writes=[Rxw])
        bt, Rbt = b_transpose(bb, br, u)
        bs = state_mm(bt, Rbt, xw, Rxw)
        return ew, Rew, bs

    def s1_back(c, ew, Rew, bs):
        P.op('dve', lambda h: h.tensor_tensor(out=v3(Sb), in0=v3(Sb), in1=bc8(ew[:, 24:32]), op=ALU.mult),
             reads=[RSbr, Rew], writes=[RSbr])
        P.op('dve', lambda h: h.tensor_tensor(out=Sb, in0=Sb, in1=P.banks[bs][:, :], op=ALU.add),
             reads=[RSbr, P.Rbank[bs]], writes=[RSbr])
        act(P, SbAll[:, c - 1, :], Sb, AF.Identity, [RSbr], [RSb[c - 1]])

    def s2_front(db, dr, xb, xr, bb, br, u):
        da, Rda = da_r.next()
        P.op('dve', lambda h: h.tensor_tensor(out=da, in0=db[:, u, :], in1=Aneg, op=ALU.mult),
             reads=[dr, Rc], writes=[Rda])
        ew, Rew = small_mms(da, Rda, [U_, SL_, LO_, ON_])
        sc, Rsc = sc_r.next()
        P.op('dve', lambda h: h.tensor_tensor(out=sc[:, 0:8], in0=db[:, u, 0:8], in1=ew[:, 16:24], op=ALU.mult),
             reads=[dr, Rew], writes=[Rsc])
        Ms = []
        Ls = []
        for (tri_l, c0) in ((SL_, 0), (SU_, 8)):
            Lt, RLt = L_r.next()
            P.op('pool', lambda h, Lt=Lt, tri_l=tri_l, c0=c0: h.tensor_tensor(
                out=Lt, in0=tri_l.unsqueeze(1).to_broadcast([128, 8, 128]),
                in1=da[:, c0:c0 + 8].unsqueeze(2).to_broadcast([128, 8, 128]), op=ALU.mult),
                reads=[Rc, Rda], writes=[RLt])
            Ls.append((Lt, RLt))
        bx = xs_transpose(xb, xr, u)
        xsP = v3(P.banks[bx][:, :])
        xf, Rxf = xdt_r.next()
        xbw, Rxbw = xdt_r.next()
        xw, Rxw = xdt_r.next()
        xsd, Rxsd = xsd_r.next()
        P.op('dve', lambda h: h.tensor_tensor(out=v3(xf), in0=xsP, in1=bc8(db[:, u, 0:8]), op=ALU.mult),
             reads=[P.Rbank[bx], dr], writes=[Rxf])
        P.op('dve', lambda h: h.tensor_tensor(out=v3(xbw), in0=xsP, in1=bc8(db[:, u, 8:16]), op=ALU.mult),
             reads=[P.Rbank[bx], dr], writes=[Rxbw])
        P.op('dve', lambda h: h.tensor_tensor(out=v3(xw), in0=xsP, in1=bc8(sc[:, 0:8]), op=ALU.mult),
             reads=[P.Rbank[bx], Rsc], writes=[Rxw])
        P.op('dve', lambda h: h.tensor_tensor(out=v3(xsd), in0=xsP, in1=bc8(dsk), op=ALU.mult),
             reads=[P.Rbank[bx], Rc], writes=[Rxsd])
        bt, Rbt = b_transpose(bb, br, u)
        bcb = P.next_bank()
        for gg in range(2):
            P.op('pe', lambda h, gg=gg: h.matmul(
                P.banks[bcb][:, gg * 128:(gg + 1) * 128], bb[:, gg, u * 128:(u + 1) * 128],
                bb[:, 2 + gg, u * 128:(u + 1) * 128], start=True, stop=True), reads=[br], writes=[P.Rbank[bcb]])
        cbU, RcbU = cbm_r.next()
        cbL, RcbL = cbm_r.next()
        cbP = P.banks[bcb][:, 0:256].rearrange("p (g q) -> p g q", g=2)
        P.op('dve', lambda h: h.tensor_tensor(
            out=cbU.rearrange("p (g q) -> p g q", g=2), in0=cbP,
            in1=U_.unsqueeze(1).to_broadcast([128, 2, 128]), op=ALU.mult),
            reads=[P.Rbank[bcb], Rc], writes=[RcbU])
        P.op('dve', lambda h: h.tensor_tensor(
            out=cbL.rearrange("p (g q) -> p g q", g=2), in0=cbP,
            in1=LO_.unsqueeze(1).to_broadcast([128, 2, 128]), op=ALU.mult),
            reads=[P.Rbank[bcb], Rc], writes=[RcbL])
        for di, (tri_r, cbm, Rcbm) in enumerate(((U_, cbU, RcbU), (LO_, cbL, RcbL))):
            Lt, RLt = Ls[di]
            dec, Rdec = dec_r.next()
            for hh in range(2):
                b = P.next_bank()
                for h4 in range(4):
                    hd = hh * 4 + h4
                    P.op('pe', lambda h, b=b, h4=h4, hd=hd, Lt=Lt, tri_r=tri_r: h.matmul(
                        P.banks[b][:, h4 * 128:(h4 + 1) * 128], Lt[:, hd, :], tri_r, start=True, stop=True),
                        reads=[RLt, Rc], writes=[P.Rbank[b]])
                act(P, dec[:, hh * 4:(hh + 1) * 4, :], P.banks[b][:, :].rearrange("p (a q) -> p a q", a=4),
                    AF.Exp, [P.Rbank[b]], [Rdec])
            Mt, RMt = M_r.next()
            P.op('dve', lambda h, Mt=Mt, dec=dec, cbm=cbm: h.tensor_tensor(
                out=Mt.rearrange("p (g e) q -> p g e q", g=2), in0=dec.rearrange("p (g e) q -> p g e q", g=2),
                in1=cbm.rearrange("p (g q) -> p g q", g=2).unsqueeze(2).to_broadcast([128, 2, 4, 128]),
                op=ALU.mult), reads=[Rdec, Rcbm], writes=[RMt])
            Ms.append((Mt, RMt))
        return dict(ew=ew, Rew=Rew, xf=xf, Rxf=Rxf, xbw=xbw, Rxbw=Rxbw, xw=xw, Rxw=Rxw, xsd=xsd, Rxsd=Rxsd,
                    bt=bt, Rbt=Rbt, Ms=Ms)

    def s2_back(c, u, f, bb, br, zb, zr, yb, yr):
        ew, Rew, xf, Rxf, xbw, Rxbw, xw, Rxw = f['ew'], f['Rew'], f['xf'], f['Rxf'], f['xbw'], f['Rxbw'], f['xw'], f['Rxw']
        xsd, Rxsd, bt, Rbt, Ms = f['xsd'], f['Rxsd'], f['bt'], f['Rbt'], f['Ms']
        by = P.next_bank()
        for hd in range(8):
            P.op('pe', lambda h, hd=hd: h.matmul(
                P.banks[by][:, hd * 64:(hd + 1) * 64], Ms[0][0][:, hd, :], xf[:, hd * 64:(hd + 1) * 64],
                start=True, stop=False), reads=[Ms[0][1], Rxf], writes=[P.Rbank[by]])
            P.op('pe', lambda h, hd=hd: h.matmul(
                P.banks[by][:, hd * 64:(hd + 1) * 64], Ms[1][0][:, hd, :], xbw[:, hd * 64:(hd + 1) * 64],
                start=False, stop=True), reads=[Ms[1][1], Rxbw], writes=[P.Rbank[by]])
        bof = P.next_bank()
        bob = P.next_bank()
        for gg in range(2):
            P.op('pe', lambda h, gg=gg: h.matmul(
                P.banks[bof][:, gg * 256:(gg + 1) * 256], bb[:, 2 + gg, u * 128:(u + 1) * 128],
                Sfb[:, gg * 256:(gg + 1) * 256], start=True, stop=True), reads=[br, RSfb], writes=[P.Rbank[bof]])
        for gg in range(2):
            P.op('pe', lambda h, gg=gg: h.matmul(
                P.banks[bob][:, gg * 256:(gg + 1) * 256], bb[:, 2 + gg, u * 128:(u + 1) * 128],
                SbAll[:, c, gg * 256:(gg + 1) * 256], start=True, stop=True), reads=[br, RSb[c]], writes=[P.Rbank[bob]])
        bs = state_mm(bt, Rbt, xw, Rxw)
        P.op('dve', lambda h: h.tensor_tensor(out=v3(Sf), in0=v3(Sf), in1=bc8(ew[:, 48:56]), op=ALU.mult),
             reads=[RSf, Rew], writes=[RSf])
        P.op('dve', lambda h: h.tensor_tensor(out=Sf, in0=Sf, in1=P.banks[bs][:, :], op=ALU.add),
             reads=[RSf, P.Rbank[bs]], writes=[RSf])
        act(P, Sfb, Sf, AF.Identity, [RSf], [RSfb])
        t1, Rt1 = t_r.next()
        t2, Rt2 = t_r.next()
        P.op('dve', lambda h: h.tensor_tensor(out=v3(t1), in0=v3(P.banks[bof][:, :]), in1=bc8(ew[:, 0:8]), op=ALU.mult),
             reads=[P.Rbank[bof], Rew], writes=[Rt1])
        P.op('dve', lambda h: h.tensor_tensor(out=v3(t2), in0=v3(P.banks[bob][:, :]), in1=bc8(ew[:, 40:48]), op=ALU.mult),
             reads=[P.Rbank[bob], Rew], writes=[Rt2])
        P.op('pool', lambda h: h.tensor_tensor(out=t1, in0=t1, in1=t2, op=ALU.add), reads=[Rt1, Rt2], writes=[Rt1])
        P.op('pool', lambda h: h.tensor_tensor(out=t1, in0=t1, in1=xsd, op=ALU.add), reads=[Rt1, Rxsd], writes=[Rt1])
        P.op('dve', lambda h: h.tensor_tensor(out=t1, in0=t1, in1=P.banks[by][:, :], op=ALU.add),
             reads=[Rt1, P.Rbank[by]], writes=[Rt1])
        P.op('dve', lambda h: h.tensor_tensor(out=t1, in0=t1, in1=zb[:, u, :], op=ALU.mult),
             reads=[Rt1, zr], writes=[Rt1])
        sm, Rsm_ = sm_r.next()
        P.op('act', lambda h: h.activation(out=t2, in_=t1, func=AF.Square, accum_out=sm[:, 0:1]),
             reads=[Rt1, Rt2], writes=[Rt2, Rsm_])
        act(P, sm[:, 1:2], sm[:, 0:1], AF.Ln, [Rsm_], [Rsm_], scale=1.0 / 512, bias=EPS)
        act(P, sm[:, 2:3], sm[:, 1:2], AF.Exp, [Rsm_], [Rsm_], scale=-0.5)
        yn, Ryn = yn_r.next()
        P.op('dve', lambda h: h.scalar_tensor_tensor(
            out=yn, in0=t1, scalar=sm[:, 2:3], in1=sng, op0=ALU.mult, op1=ALU.mult),
            reads=[Rt1, Rsm_, Rc], writes=[Ryn])
        bt_ = P.next_bank()
        pbt = P.banks[bt_][:, :].bitcast(BF16)
        for fc in range(4):
            P.op('pe', lambda h, fc=fc: h.transpose(
                pbt[:, fc * 128:(fc + 1) * 128], yn[:, fc * 128:(fc + 1) * 128], idb),
                reads=[Ryn, Rc], writes=[P.Rbank[bt_]])
        act(P, yb[:, :, u * 128:(u + 1) * 128], pbt[:, 0:512].rearrange("p (c t) -> p c t", c=4), AF.Identity,
            [P.Rbank[bt_]], [yr])

    ybuf = {}

    def finish_back(p):
        c, u, g, fr, bb, br, zb, zr = p
        if u == 0:
            ybuf['cur'] = syn.next()
        yb, yr, ych = ybuf['cur']
        s2_back(c, u, fr, bb, br, zb, zr, yb, yr)
        if u == 3:
            s_ = ybuf['s']
            P.dma('sp', ych, lambda h: h.dma_start(
                out=C.YN[s_].rearrange("(c p) t -> p c t", p=128)[:, :, g * 512:(g + 1) * 512], in_=yb), reads=[yr])

    for s in range(NS):
        ybuf['s'] = s
        P.op('pool', lambda h: h.memset(Sb, 0.0), writes=[RSbr])
        P.op('pool', lambda h: h.memset(SbAll[:, NC - 1, :], 0.0), writes=[RSb[NC - 1]])
        groups = {}
        groups[NG - 1] = load_group(s, NG - 1, False)
        if NG > 1:
            groups[NG - 2] = load_group(s, NG - 2, False)
        pend = None
        for c in range(NC - 1, 0, -1):
            g, u = divmod(c, 4)
            (xb, xr), (bb, br), (db, dr), _ = groups[g]
            fr = s1_front(db, dr, xb, xr, bb, br, u)
            if pend is not None:
                s1_back(*pend)
            pend = (c,) + fr
            if u == 3 and g - 2 >= 0:
                groups[g - 2] = load_group(s, g - 2, False)
        if pend is not None:
            s1_back(*pend)
        P.op('pool', lambda h: h.memset(Sf, 0.0), writes=[RSf])
        P.op('pool', lambda h: h.memset(Sfb, 0.0), writes=[RSfb])
        groups = {0: load_group(s, 0, True)}
        if NG > 1:
            groups[1] = load_group(s, 1, True)
        ybs = {}
        pend = None
        for c in range(NC):
            g, u = divmod(c, 4)
            (xb, xr), (bb, br), (db, dr), (zb, zr) = groups[g]
            fr = s2_front(db, dr, xb, xr, bb, br, u)
            if pend is not None:
                finish_back(pend)
            pend = (c, u, g, fr, bb, br, zb, zr)
            if u == 0 and g + 2 < NG:
                groups[g + 2] = load_group(s, g + 2, True)
        finish_back(pend)
    P.reset(m0)


BRANCH_DIL = (1, 4, 16)


def phase_T(C):
    P, L, NS = C.P, C.L, C.NS
    m0 = P.mark()
    cst = P.alloc([128, 6, 128], F32)
    jmat = P.alloc([128, 128], F32)
    sel = P.alloc([65, 64], F32)
    LBh = P.alloc([128, 4, 3, 2, 256], BF16)
    LBl = P.alloc([128, 4, 3, 2, 256], BF16)
    negm = P.alloc([128, 2, 128], F32)
    idb = P.alloc([128, 128], BF16)
    Rc = P.res("t_const")
    REB = P.res("EB")
    chc = P.chan("t_const")
    P.dma('sp', chc, lambda h: h.dma_start(out=cst, in_=C.cst), writes=[Rc])
    P.dma('sp', chc, lambda h: h.dma_start(out=jmat, in_=C.jmat), writes=[Rc])
    P.dma('sp', chc, lambda h: h.dma_start(out=sel, in_=C.sel), writes=[Rc])
    P.dma('sp', chc, lambda h: h.dma_start(out=negm, in_=C.negm), writes=[Rc])
    P.op('dve', lambda h: h.tensor_copy(out=idb, in_=cst[:, 5, :]), reads=[Rc], writes=[Rc])
    U_, LO_ = cst[:, 0, :], cst[:, 2, :]
    m1 = P.mark()
    oh = P.alloc([32, 6, 256], F32)
    relb = P.alloc([32, 8], F32)
    P.dma('sp', chc, lambda h: h.dma_start(out=oh, in_=C.oh), writes=[Rc])
    P.dma('sp', chc, lambda h: h.dma_start(out=relb, in_=C.relb), writes=[Rc])
    gs_r = Ring(P, "gs", 2, [8, 256], F32, chan=True)
    hk_r = Ring(P, "hk", 4, [128, 128], F32, chan=True)
    tmp_r = Ring(P, "ebtmp", 3, [128, 128], F32)
    RGV = [P.res(f"gv{i}") for i in range(6)]
    for bt in range(6):
        b = P.next_bank()
        P.op('pe', lambda h, b=b, bt=bt: h.matmul(P.banks[b][0:8, 0:256], relb, oh[:, bt, :], start=True, stop=True),
             reads=[Rc], writes=[P.Rbank[b]])
        gs, Rgs, gch = gs_r.next()
        P.op('dve', lambda h, gs=gs, b=b: h.tensor_copy(out=gs, in_=P.banks[b][0:8, 0:256]), reads=[P.Rbank[b]], writes=[Rgs])
        P.dma('sp', gch, lambda h, gs=gs, bt=bt: h.dma_start(out=C.GV[bt], in_=gs), reads=[Rgs], writes=[RGV[bt]])
    for bt in range(6):
        bi, ty = bt // 2, bt % 2
        for hd in range(8):
            hk, Rhk, hch = hk_r.next()
            src = bass.AP(C.GV.tensor, (bt * 8 + hd) * 256, [[1, 128], [1, 128]])
            P.dma('sp', hch, lambda h, hk=hk, src=src: h.dma_start(out=hk, in_=src), reads=[RGV[bt]], writes=[Rhk])
            b = P.next_bank()
            P.op('pe', lambda h, b=b, hk=hk: h.matmul(P.banks[b][:, 0:128], hk, jmat, start=True, stop=True),
                 reads=[Rhk, Rc], writes=[P.Rbank[b]])
            tmp, Rtmp = tmp_r.next()
            msk = U_ if ty == 0 else LO_
            ngm = negm[:, ty, :]
            P.op('dve', lambda h, tmp=tmp, msk=msk, b=b: h.tensor_tensor(out=tmp, in0=P.banks[b][:, 0:128], in1=msk, op=ALU.mult),
                 reads=[P.Rbank[b], Rc], writes=[Rtmp])
            P.op('dve', lambda h, tmp=tmp, ngm=ngm: h.tensor_tensor(out=tmp, in0=tmp, in1=ngm, op=ALU.add),
                 reads=[Rtmp, Rc], writes=[Rtmp])
            G_ = 16 // BRANCH_DIL[bi]
            HW_ = 128 // G_
            eh = LBh[:, hd // 2, bi, hd % 2, :].rearrange("p (c w) -> p c w", c=G_)[:, :, ty * HW_:(ty + 1) * HW_]
            el = LBl[:, hd // 2, bi, hd % 2, :].rearrange("p (c w) -> p c w", c=G_)[:, :, ty * HW_:(ty + 1) * HW_]
            tv = tmp.rearrange("p (i c) -> p c i", c=G_)
            P.op('dve', lambda h, tv=tv, eh=eh: h.tensor_copy(out=eh, in_=tv), reads=[Rtmp], writes=[REB])
            P.op('dve', lambda h, tv=tv, eh=eh, el=el: h.tensor_tensor(out=el, in0=tv, in1=eh, op=ALU.subtract),
                 reads=[Rtmp, REB], writes=[REB])
    P.barrier()
    P.reset(m1)

    PADK = 1024
    Ld16 = L // 16
    Qbd = P.alloc([128, 2, L], BF16)
    Qv = Qbd.rearrange("p h (r i) -> p h r i", r=16)
    KTp = P.alloc([128, L + 2 * PADK], BF16)
    OT = P.alloc([128, 2, L], F32)
    NTmax = L // 128 + 16
    Vbs = [P.alloc([128, NTmax, 130], BF16) for _ in range(2)]
    RVs = [[P.res(f"Vb{k}_{r}") for r in range(16)] for k in range(2)]
    RQ, RK, ROT = P.res("Qbd"), P.res("KTp"), P.res("OT")
    chq, chk = P.chan("q"), P.chan("k")
    chv2 = [[P.chan(f"v{i}_{k}") for k in range(4)] for i in range(2)]
    P_r = Ring(P, "Pt", 6, [128, 512], BF16)
    rc_r = Ring(P, "rc", 2, [64, 512], F32)
    so_r = Ring(P, "so", 2, [64, 512], F32, chan=True)
    dummy = P.alloc([128, 16], F32)
    dummy2 = P.alloc([128, 16], F32)
    P.op('pool', lambda h: h.memset(Qbd, 0.0), writes=[RQ])
    P.op('pool', lambda h: h.memset(KTp, 0.0), writes=[RK])
    S_banks = [0, 1, 2, 3]
    si = [0]
    O_bank = {(0, 0): 4, (0, 1): 5, (1, 0): 6, (1, 1): 7}

    def load_V(s, hp, bi, slot):
        dl = BRANCH_DIL[bi]
        Ld = L // dl
        NJ = Ld // 128 + 1
        Vb = Vbs[slot]
        Vs = C.V[s]
        c0, c1 = hp * 130, (hp + 1) * 130
        P.op('pool', lambda h: h.memset(dummy2, 0.0), writes=list(RVs[slot]))
        for rho in range(dl):
            RV = RVs[slot][rho]
            tb = rho * NJ
            P.op('pool', lambda h, tb=tb, Vb=Vb: h.memset(Vb[0:64, tb, :], 0.0), writes=[RV])
            P.op('pool', lambda h, tb=tb, NJ=NJ, Vb=Vb: h.memset(Vb[64:128, tb + NJ - 1, :], 0.0), writes=[RV])
            ch_ = chv2[slot][rho % 4]
            if NJ > 2:
                src = Vs[rho + dl * 64:rho + dl * 64 + dl * 128 * (NJ - 2):dl, c0:c1]
                P.dma('sp', ch_, lambda h, tb=tb, NJ=NJ, src=src, Vb=Vb: h.dma_start(
                    out=Vb[:, tb + 1:tb + NJ - 1, :], in_=src.rearrange("(j a) f -> a j f", a=128)), writes=[RV])
            r_first = Vs[rho:rho + dl * 63 + 1:dl, c0:c1]
            t_l = rho + dl * (Ld - 64)
            r_last = Vs[t_l:t_l + dl * 63 + 1:dl, c0:c1]
            P.dma('sp', ch_, lambda h, tb=tb, r_first=r_first, Vb=Vb: h.dma_start(out=Vb[64:128, tb, :], in_=r_first), writes=[RV])
            P.dma('sp', ch_, lambda h, tb=tb, NJ=NJ, r_last=r_last, Vb=Vb: h.dma_start(
                out=Vb[0:64, tb + NJ - 1, :], in_=r_last), writes=[RV])

    BORDER = (2, 1, 0)
    work = [(s, hp, bi) for s in range(NS) for hp in range(4) for bi in BORDER]
    load_V(*work[0], 0)

    def do_work(wi, s, hp, bi):
        slot = wi % 2
        Vb = Vbs[slot]
        dl = BRANCH_DIL[bi]
        G = 16 // dl
        W = 256 // G
        HW = 128 // G
        Ld = L // dl
        NQ = Ld // 128
        NJ = NQ + 1
        gs = min(4, NQ)
        first = (bi == BORDER[0])
        last = (bi == BORDER[-1])
        if first:
            r0 = hp * 128
            P.dma('sp', chq, lambda h: h.dma_start(out=Qbd[0:64, 0, :], in_=C.QT[s][r0:r0 + 64, :]), writes=[RQ])
            P.dma('sp', chq, lambda h: h.dma_start(out=Qbd[64:128, 1, :], in_=C.QT[s][r0 + 64:r0 + 128, :]), writes=[RQ])
            P.dma('sp', chk, lambda h: h.dma_start(out=KTp[:, PADK:PADK + L], in_=C.KT[s][r0:r0 + 128, :]), writes=[RK])
        P.op('pool', lambda h: h.memset(dummy, 0.0), writes=[ROT])
        if wi + 1 < len(work):
            load_V(*work[wi + 1], 1 - slot)

        def v4(ap512):
            return ap512.rearrange("p (h c w) -> p h c w", h=2, c=G)

        for rho in range(dl):
            prev = None
            RV = RVs[slot][rho]
            for j in range(NJ):
                halves = [hf for hf in (0, 1) if 0 <= j - 1 + hf < NQ]
                h0, h1 = halves[0], halves[-1] + 1
                w0, w1 = h0 * HW, h1 * HW
                i0 = (128 * (j - 1) + 128 * h0) // G
                i1 = (128 * (j - 1) + 128 * h1) // G
                ks = PADK + rho + dl * (128 * j - 64)
                bS = S_banks[si[0] % 4]
                si[0] += 1
                Sv = v4(P.banks[bS][:, :])[:, :, :, w0:w1]
                qa = Qv[:, :, rho:16:dl, i0:i1]
                ka = KTp[:, ks:ks + dl * 127 + 1:dl]
                lh = v4(LBh[:, hp, bi, :, :].rearrange("p h q -> p (h q)"))[:, :, :, w0:w1]
                ll = v4(LBl[:, hp, bi, :, :].rearrange("p h q -> p (h q)"))[:, :, :, w0:w1]
                P.op('pe', lambda h, Sv=Sv, qa=qa, ka=ka: h.matmul(Sv, ka, qa, start=True, stop=False),
                     reads=[RK, RQ], writes=[P.Rbank[bS]])
                P.op('pe', lambda h, Sv=Sv, lh=lh: h.matmul(Sv, idb, lh, start=False, stop=False),
                     reads=[Rc, REB], writes=[P.Rbank[bS]])
                P.op('pe', lambda h, Sv=Sv, ll=ll: h.matmul(Sv, idb, ll, start=False, stop=True),
                     reads=[Rc, REB], writes=[P.Rbank[bS]])
                Pt, RPt = P_r.next()
                Pv = v4(Pt)
                act(P, Pv[:, :, :, w0:w1], Sv, AF.Exp, [P.Rbank[bS]], [RPt])
                tile = rho * NJ + j
                if j >= 1:
                    m = j - 1
                    pPv, pRPt, ptile = prev
                    for hd in range(2):
                        bO = O_bank[(hd, (m // gs) % 2)]
                        oc = (m % gs) * 128
                        Ov = P.banks[bO][0:65, oc:oc + 128].rearrange("p (c w) -> p c w", c=G)
                        P.op('pe', lambda h, Ov=Ov, hd=hd, pPv=pPv, ptile=ptile: h.matmul(
                            Ov, Vb[:, ptile, hd * 65:(hd + 1) * 65], pPv[:, hd, :, HW:2 * HW], start=True, stop=False),
                            reads=[RV, pRPt], writes=[P.Rbank[bO]])
                        P.op('pe', lambda h, Ov=Ov, hd=hd, Pv=Pv, tile=tile: h.matmul(
                            Ov, Vb[:, tile, hd * 65:(hd + 1) * 65], Pv[:, hd, :, 0:HW], start=False, stop=True),
                            reads=[RV, RPt], writes=[P.Rbank[bO]])
                        if (m + 1) % gs == 0:
                            m0_ = m + 1 - gs
                            t0 = rho + dl * 128 * m0_
                            ov = OT[0:65, hd, t0 - rho:t0 - rho + dl * 128 * gs].rearrange(
                                "p (m i r) -> p m i r", m=gs, r=16)[:, :, :, rho:16:dl].rearrange("p m i c -> p m c i")
                            src = P.banks[bO][0:65, 0:gs * 128].rearrange("p (m c i) -> p m c i", m=gs, c=G)
                            if first:
                                act(P, ov, src, AF.Identity, [ROT, P.Rbank[bO]], [])
                            elif not last:
                                P.op('dve', lambda h, ov=ov, src=src: h.tensor_tensor(out=ov, in0=ov, in1=src, op=ALU.add),
                                     reads=[ROT, P.Rbank[bO]], writes=[])
                            else:
                                Rfin = P.res()
                                P.op('dve', lambda h, ov=ov, src=src: h.tensor_tensor(out=ov, in0=ov, in1=src, op=ALU.add),
                                     reads=[ROT, P.Rbank[bO]], writes=[Rfin])
                                ncol = 128 * gs
                                b = S_banks[si[0] % 4]
                                si[0] += 1
                                P.op('pe', lambda h, b=b, hd=hd, t0=t0, ncol=ncol: h.matmul(
                                    P.banks[b][0:64, 0:ncol], sel, OT[0:65, hd, t0:t0 + ncol], start=True, stop=True),
                                    reads=[Rc, Rfin, ROT], writes=[P.Rbank[b]])
                                rc, Rrc = rc_r.next()
                                act(P, rc[:, 0:ncol], P.banks[b][0:64, 0:ncol], AF.Ln, [P.Rbank[b]], [Rrc])
                                act(P, rc[:, 0:ncol], rc[:, 0:ncol], AF.Exp, [Rrc], [Rrc], scale=-1.0)
                                so, Rso, soch = so_r.next()
                                P.op('dve', lambda h, so=so, rc=rc, hd=hd, t0=t0, ncol=ncol: h.tensor_tensor(
                                    out=so[:, 0:ncol], in0=OT[0:64, hd, t0:t0 + ncol], in1=rc[:, 0:ncol], op=ALU.mult),
                                    reads=[ROT, Rfin, Rrc], writes=[Rso])
                                rr = hp * 128 + hd * 64
                                P.dma('sp', soch, lambda h, so=so, rr=rr, t0=t0, ncol=ncol: h.dma_start(
                                    out=C.AT[s][rr:rr + 64, t0:t0 + ncol], in_=so[:, 0:ncol]), reads=[Rso])
                prev = (Pv, RPt, tile)

    for wi, (s, hp, bi) in enumerate(work):
        do_work(wi, s, hp, bi)
    P.reset(m0)


def _ln_feature_major(P, C, tt, Rtt, nd, T, S1, S2, cst_ones, Rc, mk_out):
    mean = C.ln_mean
    m2 = C.ln_m2
    rstd = C.ln_rstd
    Rst = C.ln_Rst
    act(P, mean[:, 0:T], P.banks[S1][:, 0:T], AF.Identity, [P.Rbank[S1]], [Rst], scale=1.0 / D)
    P.op('dve', lambda h: h.tensor_tensor(out=m2[:, 0:T], in0=mean[:, 0:T], in1=mean[:, 0:T], op=ALU.mult),
         reads=[Rst], writes=[Rst])
    P.op('dve', lambda h: h.scalar_tensor_tensor(out=m2[:, 0:T], in0=P.banks[S2][:, 0:T], scalar=1.0 / D,
                                                 in1=m2[:, 0:T], op0=ALU.mult, op1=ALU.subtract),
         reads=[P.Rbank[S2], Rst], writes=[Rst])
    act(P, m2[:, 0:T], m2[:, 0:T], AF.Ln, [Rst], [Rst], bias=EPS)
    act(P, rstd[:, 0:T], m2[:, 0:T], AF.Exp, [Rst], [Rst], scale=-0.5)
    for dc in range(nd):
        u1, Ru1 = C.ln_u.next()
        P.op('dve', lambda h, u1=u1, dc=dc: h.tensor_tensor(out=u1[:, 0:T], in0=tt[:, dc, 0:T], in1=mean[:, 0:T], op=ALU.subtract),
             reads=[Rtt[dc], Rst], writes=[Ru1])
        P.op('dve', lambda h, u1=u1: h.tensor_tensor(out=u1[:, 0:T], in0=u1[:, 0:T], in1=rstd[:, 0:T], op=ALU.mult),
             reads=[Ru1, Rst], writes=[Ru1])
        mk_out(dc, u1, Ru1)


def phase_C1(C):
    P, L, NS = C.P, C.L, C.NS
    T = 512
    m0 = P.mark()
    P.bank_list = [0, 1, 2, 3, 4]
    SA, S1, S2 = 5, 6, 7
    wout = P.alloc([128, 8, D], BF16)
    cst = P.alloc([128, 6, 128], F32)
    ang = P.alloc([128, 4], F32)
    lnfm = P.alloc([128, 4, 8], F32)
    Rc = P.res("c1_const")
    chc = P.chan("c1_const")
    P.dma('pool', chc, lambda h: h.dma_start(out=wout, in_=C.w_out.rearrange("(kc p) c -> p kc c", p=128)), writes=[Rc])
    P.dma('sp', chc, lambda h: h.dma_start(out=cst, in_=C.cst), writes=[Rc])
    P.dma('sp', chc, lambda h: h.dma_start(out=ang, in_=C.ang), writes=[Rc])
    P.dma('sp', chc, lambda h: h.dma_start(out=lnfm, in_=C.lnfm), writes=[Rc])
    ONES = cst[:, 4, :]
    xt_r = Ring(P, "c1x", 3, [128, 8, T], F32, chan=True)
    at_r = Ring(P, "c1a", 2, [128, 4, T], F32, chan=True)
    yn_r = Ring(P, "c1y", 3, [128, 4, T], BF16, chan=True)
    an_r = Ring(P, "c1an", 2, [128, 4, T], BF16)
    sq_r = Ring(P, "c1sq", 3, [128, T], F32)
    sqx_r = Ring(P, "c1sqx", 2, [128, T], F32)
    rsa_r = Ring(P, "c1rsa", 2, [128, T], F32)
    tts = [P.alloc([128, 8, T], F32) for _ in range(2)]
    Rtts = [[P.res(f"tt{k}_{i}") for i in range(8)] for k in range(2)]
    C.ln_mean = P.alloc([128, T], F32)
    C.ln_m2 = P.alloc([128, T], F32)
    C.ln_rstd = P.alloc([128, T], F32)
    C.ln_Rst = P.res("lnst")
    C.ln_u = Ring(P, "lnu", 3, [128, T], F32)
    hf_r = Ring(P, "c1hf", 1, [128, 8, T], F32, chan=True)
    hb_r = Ring(P, "c1hb", 1, [128, 8, T], BF16, chan=True)
    tiles = [(s, t0) for s in range(NS) for t0 in range(0, L, T)]

    def load(s, t0):
        xb, xr, xch = xt_r.next()
        P.dma('sp', xch, lambda h: h.dma_start(out=xb, in_=C.xT[s].rearrange("(c p) t -> p c t", p=128)[:, :, t0:t0 + T]), writes=[xr])
        ab, ar, ach = at_r.next()
        P.dma('sp', ach, lambda h: h.dma_start(out=ab, in_=C.AT[s].rearrange("(c p) t -> p c t", p=128)[:, :, t0:t0 + T]), writes=[ar])
        yb, yr, ych = yn_r.next()
        P.dma('sp', ych, lambda h: h.dma_start(out=yb, in_=C.YN[s].rearrange("(c p) t -> p c t", p=128)[:, :, t0:t0 + T]), writes=[yr])
        return (xb, xr), (ab, ar), (yb, yr)

    def stage_X(ld):
        (xb, xr), (ab, ar), (yb, yr) = ld
        for fc in range(4):
            sq, Rsq = sqx_r.next()
            act(P, sq, ab[:, fc, :], AF.Square, [ar], [Rsq])
            P.op('pe', lambda h, sq=sq, fc=fc: h.matmul(P.banks[SA][:, :], ONES, sq, start=(fc == 0), stop=(fc == 3)),
                 reads=[Rc, Rsq], writes=[P.Rbank[SA]])
        rsa, Rrsa = rsa_r.next()
        act(P, rsa, P.banks[SA][:, :], AF.Ln, [P.Rbank[SA]], [Rrsa], scale=1.0 / 512, bias=EPS)
        act(P, rsa, rsa, AF.Exp, [Rrsa], [Rrsa], scale=-0.5)
        an, Ran = an_r.next()
        for fc in range(4):
            P.op('dve', lambda h, fc=fc: h.scalar_tensor_tensor(
                out=an[:, fc, :], in0=ab[:, fc, :], scalar=ang[:, fc:fc + 1], in1=rsa, op0=ALU.mult, op1=ALU.mult),
                reads=[ar, Rc, Rrsa], writes=[Ran])
        return (xb, xr), (yb, yr), (an, Ran)

    def stage_YZ(ti, xs_, inject):
        s, t0 = tiles[ti]
        (xb, xr), (yb, yr), (an, Ran) = xs_
        tt, Rtt = tts[ti % 2], Rtts[ti % 2]
        pend = None
        nxt_x = None
        for dc in range(8):
            if dc == 4 and inject is not None:
                nxt_x = inject()
            b = P.next_bank()
            mm_group(P, P.banks[b][:, :],
                     [(wout[:, kc, dc * 128:(dc + 1) * 128], an[:, kc, :] if kc < 4 else yb[:, kc - 4, :]) for kc in range(8)],
                     P.Rbank[b], [Rc, Ran, yr])
            P.op('dve', lambda h, dc=dc, b=b: h.scalar_tensor_tensor(
                out=tt[:, dc, :], in0=xb[:, dc, :], scalar=ALPHA, in1=P.banks[b][:, :], op0=ALU.mult, op1=ALU.add),
                reads=[xr, P.Rbank[b]], writes=[Rtt[dc]])
            sq, Rsq = sq_r.next()
            act(P, sq, tt[:, dc, :], AF.Square, [Rtt[dc]], [Rsq])

            def stats(dc=dc, sq=sq, Rsq=Rsq):
                P.op('pe', lambda h: h.matmul(P.banks[S1][:, :], ONES, tt[:, dc, :], start=(dc == 0), stop=(dc == 7)),
                     reads=[Rc, Rtt[dc]], writes=[P.Rbank[S1]])
                P.op('pe', lambda h: h.matmul(P.banks[S2][:, :], ONES, sq, start=(dc == 0), stop=(dc == 7)),
                     reads=[Rc, Rsq], writes=[P.Rbank[S2]])
            if pend is not None:
                pend()
            pend = stats
        pend()
        hf, Rhf, hfch = hf_r.next()
        hb, Rhb, hbch = hb_r.next()

        def mk_out(dc, u1, Ru1):
            act(P, hf[:, dc, :], u1, AF.Identity, [Ru1, Rc], [Rhf], scale=lnfm[:, 0, dc:dc + 1], bias=lnfm[:, 1, dc:dc + 1])
            act(P, hb[:, dc, :], u1, AF.Identity, [Ru1, Rc], [Rhb], scale=lnfm[:, 0, dc:dc + 1], bias=lnfm[:, 1, dc:dc + 1])
        _ln_feature_major(P, C, tt, Rtt, 8, T, S1, S2, ONES, Rc, mk_out)
        P.dma('sp', hfch, lambda h: h.dma_start(
            out=C.H1F[s].rearrange("(c p) t -> p c t", p=128)[:, :, t0:t0 + T], in_=hf), reads=[Rhf])
        P.dma('sp', hbch, lambda h: h.dma_start(
            out=C.H1B[s].rearrange("(c p) t -> p c t", p=128)[:, :, t0:t0 + T], in_=hb), reads=[Rhb])
        return nxt_x

    ld = [None] * (len(tiles) + 2)
    ld[0] = load(*tiles[0])
    if len(tiles) > 1:
        ld[1] = load(*tiles[1])
    xs_next = stage_X(ld[0])
    for ti in range(len(tiles)):
        xs_cur = xs_next
        if ti + 2 < len(tiles):
            ld[ti + 2] = load(*tiles[ti + 2])
        inj = (lambda ti=ti: stage_X(ld[ti + 1])) if ti + 1 < len(tiles) else None
        xs_next = stage_YZ(ti, xs_cur, inj)
    P.bank_list = list(range(8))
    P.reset(m0)


def phase_C2(C):
    P, L, NS = C.P, C.L, C.NS
    T = 256
    NF = DFF // 128
    m0 = P.mark()
    P.bank_list = [0, 1, 2, 3, 4, 5]
    S1, S2 = 6, 7
    wg = P.alloc([128, 8, DFF], BF16)
    wu = P.alloc([128, 8, DFF], BF16)
    wd = P.alloc([128, NF, D], BF16)
    cst = P.alloc([128, 6, 128], F32)
    lnfm = P.alloc([128, 4, 8], F32)
    Rc = P.res("c2_const")
    chc = P.chan("c2_const")
    chw = [P.chan(f"c2w{i}") for i in range(3)]
    for c0 in (0, 1408):
        P.dma('pool', chw[0], lambda h, c0=c0: h.dma_start(
            out=wg[:, :, c0:c0 + 1408], in_=C.w_gate.rearrange("(kc p) c -> p kc c", p=128)[:, :, c0:c0 + 1408]), writes=[Rc])
        P.dma('pool', chw[1], lambda h, c0=c0: h.dma_start(
            out=wu[:, :, c0:c0 + 1408], in_=C.w_up.rearrange("(kc p) c -> p kc c", p=128)[:, :, c0:c0 + 1408]), writes=[Rc])
    P.dma('pool', chw[2], lambda h: h.dma_start(out=wd, in_=C.w_down.rearrange("(kc p) c -> p kc c", p=128)), writes=[Rc])
    P.dma('sp', chc, lambda h: h.dma_start(out=cst, in_=C.cst), writes=[Rc])
    P.dma('sp', chc, lambda h: h.dma_start(out=lnfm, in_=C.lnfm), writes=[Rc])
    ONES = cst[:, 4, :]
    hb_r = Ring(P, "c2hb", 2, [128, 8, T], BF16, chan=True)
    hf_r = Ring(P, "c2hf", 4, [128, T], F32, chan=True)
    hid = P.alloc([128, NF, T], BF16)
    Rhid = P.res("hid")
    sg_r = Ring(P, "c2sg", 3, [128, T], F32)
    sq_r = Ring(P, "c2sq", 4, [128, T], F32)
    tt = P.alloc([128, 8, T], F32)
    Rtt = [P.res(f"tt2_{i}") for i in range(8)]
    C.ln_mean = P.alloc([128, T], F32)
    C.ln_m2 = P.alloc([128, T], F32)
    C.ln_rstd = P.alloc([128, T], F32)
    C.ln_Rst = P.res("lnst2")
    C.ln_u = Ring(P, "lnu2", 3, [128, T], F32)
    yo_r = Ring(P, "c2yo", 4, [128, T], F32, chan=True)
    outs = []

    def load(s, t0):
        hb, hr, hch = hb_r.next()
        P.dma('sp', hch, lambda h: h.dma_start(out=hb, in_=C.H1B[s].rearrange("(c p) t -> p c t", p=128)[:, :, t0:t0 + T]), writes=[hr])
        return hb, hr

    tiles = [(s, t0) for s in range(NS) for t0 in range(0, L, T)]
    nxt = load(*tiles[0])
    for ti, (s, t0) in enumerate(tiles):
        hb, hr = nxt
        if ti + 1 < len(tiles):
            nxt = load(*tiles[ti + 1])
        for fc in range(NF):
            bg = P.next_bank()
            mm_group(P, P.banks[bg][:, 0:T], [(wg[:, kc, fc * 128:(fc + 1) * 128], hb[:, kc, :]) for kc in range(8)],
                     P.Rbank[bg], [Rc, hr])
            bu = P.next_bank()
            mm_group(P, P.banks[bu][:, 0:T], [(wu[:, kc, fc * 128:(fc + 1) * 128], hb[:, kc, :]) for kc in range(8)],
                     P.Rbank[bu], [Rc, hr])
            sg, Rsg = sg_r.next()
            act(P, sg, P.banks[bg][:, 0:T], AF.Silu, [P.Rbank[bg]], [Rsg])
            P.op('dve', lambda h, fc=fc, sg=sg, bu=bu: h.tensor_tensor(out=hid[:, fc, :], in0=sg, in1=P.banks[bu][:, 0:T], op=ALU.mult),
                 reads=[Rsg, P.Rbank[bu]], writes=[Rhid])
        pend = None
        for dc in range(8):
            hf, Rhf, hfch = hf_r.next()
            P.dma('sp', hfch, lambda h, hf=hf, s=s, t0=t0, dc=dc: h.dma_start(
                out=hf, in_=C.H1F[s][dc * 128:(dc + 1) * 128, t0:t0 + T]), writes=[Rhf])
            b = P.next_bank()
            mm_group(P, P.banks[b][:, 0:T], [(wd[:, fc, dc * 128:(dc + 1) * 128], hid[:, fc, :]) for fc in range(NF)],
                     P.Rbank[b], [Rc, Rhid])
            P.op('dve', lambda h, dc=dc, b=b, hf=hf: h.scalar_tensor_tensor(
                out=tt[:, dc, :], in0=hf, scalar=ALPHA, in1=P.banks[b][:, 0:T], op0=ALU.mult, op1=ALU.add),
                reads=[Rhf, P.Rbank[b]], writes=[Rtt[dc]])
            sq, Rsq = sq_r.next()
            act(P, sq, tt[:, dc, :], AF.Square, [Rtt[dc]], [Rsq])

            def stats(dc=dc, sq=sq, Rsq=Rsq):
                P.op('pe', lambda h: h.matmul(P.banks[S1][:, 0:T], ONES, tt[:, dc, :], start=(dc == 0), stop=(dc == 7)),
                     reads=[Rc, Rtt[dc]], writes=[P.Rbank[S1]])
                P.op('pe', lambda h: h.matmul(P.banks[S2][:, 0:T], ONES, sq, start=(dc == 0), stop=(dc == 7)),
                     reads=[Rc, Rsq], writes=[P.Rbank[S2]])
            if pend is not None:
                pend()
            pend = stats
        pend()
        pend = None

        def mk_out(dc, u1, Ru1, s=s, t0=t0):
            yo, Ryo, yoch = yo_r.next()
            act(P, yo, u1[:, 0:T], AF.Identity, [Ru1, Rc], [Ryo], scale=lnfm[:, 2, dc:dc + 1], bias=lnfm[:, 3, dc:dc + 1])
            outs.append(P.dma('sp', yoch, lambda h, yo=yo, dc=dc: h.dma_start(
                out=C.yT[s][dc * 128:(dc + 1) * 128, t0:t0 + T], in_=yo), reads=[Ryo]))
        _ln_feature_major(P, C, tt, Rtt, 8, T, S1, S2, ONES, Rc, mk_out)
    P.bank_list = list(range(8))
    P.reset(m0)
    return outs

def to_xT16(xT):
    sh = xT.shape
    L = sh[-1]
    return np.ascontiguousarray(xT.reshape(sh[:-1] + (L // 16, 16)).swapaxes(-1, -2).reshape(sh))


def make_cst():
    i = np.arange(128)
    U = (i[:, None] <= i[None, :]).astype(np.float32)
    SL = (i[:, None] > i[None, :]).astype(np.float32)
    Lo = (i[:, None] >= i[None, :]).astype(np.float32)
    SU = (i[:, None] < i[None, :]).astype(np.float32)
    ones = np.ones((128, 128), np.float32)
    ident = np.eye(128, dtype=np.float32)
    return np.ascontiguousarray(np.stack([U, SL, Lo, SU, ones, ident], axis=1))


def shared_inputs(inp):
    f = np.float32
    g = lambda k: np.asarray(inp[k], dtype=f)
    bc = lambda v, n=128: np.ascontiguousarray(np.broadcast_to(v[None, :], (n, v.shape[0])))
    m = {}
    m["w_in"] = np.ascontiguousarray(g("w_in")[0])
    m["convw"] = np.ascontiguousarray(g("conv_w")[0].reshape(5, 8, 128).transpose(2, 1, 0))
    m["convb"] = np.ascontiguousarray(g("conv_b")[0].reshape(8, 128).T)
    m["dtb"] = bc(np.concatenate([g("dt_bias_fwd")[0], g("dt_bias_bwd")[0]]))
    m["alog"] = bc(np.concatenate([g("a_log_fwd")[0], g("a_log_bwd")[0]]))
    m["dsk"] = bc(g("d_skip")[0])
    m["ang"] = np.ascontiguousarray(g("attn_norm_g")[0].reshape(4, 128).T)
    m["sng"] = bc(g("ssd_norm_g")[0])
    m["relb"] = np.ascontiguousarray(g("rel_bias"))
    m["lnfm"] = np.ascontiguousarray(np.stack([g(k)[0].reshape(8, 128).T for k in ("ln1_g", "ln1_b", "ln2_g", "ln2_b")], axis=1))
    m["w_out"] = np.ascontiguousarray(g("w_out")[0])
    m["w_gate"] = np.ascontiguousarray(g("w_gate")[0])
    m["w_up"] = np.ascontiguousarray(g("w_up")[0])
    m["w_down"] = np.ascontiguousarray(g("w_down")[0])
    m["cst"] = make_cst()
    m["oh"], m["jmat"], m["sel"] = make_att_consts()
    cst_ = m["cst"]
    m["negm"] = np.ascontiguousarray(np.stack([(cst_[:, 0, :] - 1.0) * 30000.0, (cst_[:, 2, :] - 1.0) * 30000.0], axis=1))
    return m


def t5_bucket(rel):
    half = 16
    max_exact = 8
    ret = (rel > 0).astype(np.int32) * half
    n = np.abs(rel)
    large = max_exact + (np.log(np.maximum(n, 1) / max_exact)
                         / math.log(1024 / max_exact) * (half - max_exact)).astype(np.int32)
    large = np.minimum(large, half - 1)
    return ret + np.where(n < max_exact, n, large)


def make_att_consts():
    oh = np.zeros((32, 6, 256), np.float32)
    i = np.arange(255)
    for bi, dl in enumerate(BRANCH_DIL):
        for ty in range(2):
            rel = (i - 63) if ty == 0 else (i - 191)
            bk = t5_bucket(rel * dl)
            oh[bk, bi * 2 + ty, i] = 1.0
    eb = np.ascontiguousarray(np.eye(128, dtype=np.float32)[::-1])
    sel = np.zeros((65, 64), np.float32)
    sel[64, :] = 1.0
    return oh, eb, sel


SEQ_LEN = 8192
N_CORES = 8
SLOTS = 2
_CACHE = {}


def kernel(**inputs):
    xp = np.asarray(inputs["x_prompt"], dtype=np.float32)
    xs = np.asarray(inputs["x_sample"], dtype=np.float32)
    seqs = [xp[i] for i in range(xp.shape[0])] + [xs[i] for i in range(xs.shape[0])]
    nseq = len(seqs)
    L = seqs[0].shape[0]
    shared = shared_inputs(inputs)
    in_maps = []
    for c in range(N_CORES):
        xT = np.zeros((SLOTS, D, L), np.float32)
        for sl in range(SLOTS):
            i = c * SLOTS + sl
            if i < nseq:
                xT[sl] = seqs[i].T
        m = dict(shared)
        m["xT"] = xT
        m["xT16"] = to_xT16(xT)
        in_maps.append(m)
    key = (L, SLOTS)
    if key not in _CACHE:
        _CACHE[key] = build(L, SLOTS)
    nc, _ = _CACHE[key]
    res = run_bass_kernel_spmd(nc, in_maps, core_ids=list(range(N_CORES)))
    outs = []
    for i in range(nseq):
        c, sl = divmod(i, SLOTS)
        outs.append(np.ascontiguousarray(np.asarray(res.results[c]["yT"][sl]).T))
    y_prompt = np.stack(outs[:xp.shape[0]]).astype(np.float32)
    y_sample = np.stack(outs[xp.shape[0]:]).astype(np.float32)
    return (y_prompt, y_sample)
```

```python
import contextlib
import math
import numpy as np
import concourse.bass as bass
import concourse.mybir as mybir
from concourse.bass_utils import run_bass_kernel_spmd

F32 = mybir.dt.float32
BF16 = mybir.dt.bfloat16
U8 = mybir.dt.uint8
AF = mybir.ActivationFunctionType
ALU = mybir.AluOpType

D = 1024
DIN = 3088
DFF = 2816
NH = 8
COL_Q, COL_K, COL_V, COL_Z, COL_X, COL_DT = 0, 512, 1024, 1536, 2048, 3072
ALPHA = 2.0 ** 0.25
EPS = 1e-5
ENGS = ['pe', 'act', 'dve', 'pool', 'sp']
EPOCH = 20000
POOL_BYTES = 206 * 1024


def _dsize(dt):
    return {F32: 4, BF16: 2, U8: 1}[dt]


class Res:
    __slots__ = ('name', 'w', 'r')

    def __init__(self, name):
        self.name = name
        self.w = {}
        self.r = {}


class Chan:
    def __init__(self, name):
        self.name = name
        self.n = 0
        self.sem = None
        self.last = None


class Prog:
    def __init__(self, nc):
        self.nc = nc
        self.streams = {e: [] for e in ENGS}
        self.last_op = {e: None for e in ENGS}
        self.chans = []
        self.stack = contextlib.ExitStack()
        self.nres = 0
        self.nseq = 0
        self.pool = self.stack.enter_context(nc.sbuf_tensor("pool", [128, POOL_BYTES], U8))
        self.off = 0
        self.peak = 0
        self.banks = [self.stack.enter_context(nc.psum_tensor(f"bank{i}", [128, 512], F32))
                      for i in range(8)]
        self.Rbank = [self.res(f"bank{i}") for i in range(8)]

    def res(self, name=None):
        self.nres += 1
        return Res(name or f"r{self.nres}")

    def chan(self, name):
        c = Chan(name)
        self.chans.append(c)
        return c

    def alloc(self, shape, dtype):
        n = 1
        for s in shape[1:]:
            n *= s
        nbytes = n * _dsize(dtype)
        self.off = (self.off + 63) // 64 * 64
        assert self.off + nbytes <= POOL_BYTES, f"SBUF overflow {self.off + nbytes}"
        ap = self.pool[0:shape[0], self.off:self.off + nbytes].bitcast(dtype)
        self.off += nbytes
        self.peak = max(self.peak, self.off)
        if len(shape) > 2:
            names = [f"d{i}" for i in range(len(shape) - 1)]
            pat = "p (" + " ".join(names) + ") -> p " + " ".join(names)
            ap = ap.rearrange(pat, **{names[i]: shape[i + 1] for i in range(len(names))})
        return ap

    def mark(self):
        return self.off

    bank_list = list(range(8))

    def next_bank(self):
        self.bank_i = getattr(self, 'bank_i', -1) + 1
        return self.bank_list[self.bank_i % len(self.bank_list)]

    def reset(self, m):
        self.off = m

    def _deps(self, reads, writes):
        deps = {}
        for r in reads:
            for v in r.w.values():
                deps[id(v)] = v
        for w in writes:
            for v in w.w.values():
                deps[id(v)] = v
            for v in w.r.values():
                deps[id(v)] = v
        return list(deps.values())

    def op(self, eng, fn, reads=(), writes=()):
        ins = dict(fn=fn, deps=self._deps(reads, writes), signal=False, chan=None, eng=eng, seq=self.nseq)
        self.nseq += 1
        self.streams[eng].append(ins)
        self.last_op[eng] = ins
        for r in reads:
            r.r[id(ins)] = ins
        for w in writes:
            w.w = {eng: ins}
            w.r = {}
        return ins

    def dma(self, q, chan, fn, reads=(), writes=()):
        deps = self._deps(reads, writes)
        if chan.n > 0:
            deps.append(chan.last)
        ins = dict(fn=fn, deps=deps, signal=True, chan=chan, eng=q, seq=self.nseq, n=chan.n)
        self.nseq += 1
        self.streams[q].append(ins)
        chan.n += 1
        chan.last = ins
        for r in reads:
            r.r[id(ins)] = ins
        for w in writes:
            w.w = {chan: ins}
            w.r = {}
        return ins

    def barrier(self):
        evs = []
        for e in ENGS:
            if self.last_op[e] is not None:
                evs.append(self.last_op[e])
        for c in self.chans:
            if c.n > 0:
                evs.append(c.last)
        for e in ENGS:
            self.streams[e].append(dict(fn=None, deps=[d for d in evs if not (d['chan'] is None and d['eng'] == e)],
                                        signal=False, chan=None, eng=e, seq=self.nseq, barrier=True))
        self.nseq += 1

    def wait_all(self, eng, evs):
        self.streams[eng].append(dict(fn=None, deps=list(evs), signal=False, chan=None, eng=eng, seq=self.nseq,
                                      barrier=True))
        self.nseq += 1

    def _cost(self, ins):
        rec = _Fake()
        try:
            ins['fn'](rec)
        except Exception:
            return 300.0
        name, args, kw = rec.call
        try:
            if name == 'dma_start':
                o = kw.get('out')
                n = 1
                for d in o.shape:
                    n *= d
                ins['dma_ns'] = 2000.0 + n * _dsize(o.dtype) / 150.0
                return 60.0
            if name in ('matmul', 'transpose'):
                rhs = kw.get('rhs', args[2] if len(args) > 2 else None)
                n = 1
                for d in rhs.shape[1:]:
                    n *= d
                st = 1
                try:
                    st = max(1, abs(rhs.ap[-1][0]))
                except Exception:
                    pass
                c = max(64, n) / 2.4
                if rhs.dtype == F32 and name == 'matmul':
                    c *= 4
                elif st > 1:
                    c *= min(8, st) / 2.0 + 0.5
                return c + 8
            o = kw.get('out', args[0] if args else None)
            n = 1
            for d in o.shape[1:]:
                n *= d
            e = ins['eng']
            if e == 'act':
                return 70 + n * 0.96 + (90 if 'accum_out' in kw else 0)
            if e == 'dve':
                return 65 + n * 1.04
            return 110 + n * (0.6 if name == 'memset' else 2.3)
        except Exception:
            return 300.0

    def schedule(self, window=32):
        import os
        LAT = float(os.environ.get('SCHED_LAT', '120'))
        new_streams = {e: [] for e in ENGS}
        pos = {e: 0 for e in ENGS}
        free = {e: 0.0 for e in ENGS}
        tnow = 0.0
        while any(pos[e] < len(self.streams[e]) for e in ENGS):
            seg = {}
            for e in ENGS:
                st = self.streams[e]
                i = pos[e]
                j = i
                while j < len(st) and not st[j].get('barrier'):
                    j += 1
                seg[e] = st[i:j]
                bar = st[j] if j < len(st) else None
                pos[e] = j + 1 if j < len(st) else j
                seg[e + '_bar'] = bar
            for e in ENGS:
                for ins in seg[e]:
                    ins['cost'] = self._cost(ins)
                    ins['fin'] = None
                free[e] = tnow
            pend = {e: list(seg[e]) for e in ENGS}
            nleft = sum(len(v) for v in pend.values())
            while nleft:
                progressed = False
                for e in sorted(ENGS, key=lambda x: free[x]):
                    pl = pend[e]
                    if not pl:
                        continue
                    best = None
                    bstart = None
                    for k in range(min(window if e != 'sp' else 6, len(pl))):
                        ins = pl[k]
                        rd = ins.get('ready')
                        if rd is None:
                            rd = 0.0
                            ok = True
                            for d in ins['deps']:
                                f = d.get('fin', 0.0)
                                if f is None:
                                    ok = False
                                    break
                                if d['chan'] is not None:
                                    f = d.get('dfin', f)
                                if d['eng'] == e and d['chan'] is None and e == 'pe':
                                    f = f - d['cost'] * 0.5
                                else:
                                    f = f + LAT
                                if f > rd:
                                    rd = f
                            if not ok:
                                continue
                            ins['ready'] = rd
                        stt = rd if rd > free[e] else free[e]
                        if bstart is None or stt < bstart - 1e-9:
                            best, bstart = k, stt
                            if stt <= free[e]:
                                break
                    if best is None:
                        continue
                    ins = pl.pop(best)
                    ins['fin'] = bstart + ins['cost']
                    if ins['chan'] is not None:
                        ins['dfin'] = bstart + ins.get('dma_ns', 2000.0)
                    free[e] = ins['fin']
                    new_streams[e].append(ins)
                    nleft -= 1
                    progressed = True
                    break
                assert progressed, "scheduler deadlock"
            tnow = max(free.values())
            for e in ENGS:
                for ins in seg[e]:
                    if ins['chan'] is not None and ins.get('dfin', 0) > tnow:
                        tnow = ins['dfin']
            for e in ENGS:
                if seg[e + '_bar'] is not None:
                    seg[e + '_bar']['fin'] = tnow
                    new_streams[e].append(seg[e + '_bar'])
        self.streams = new_streams
        self.est_ns = tnow

    def emit(self, sched=True):
        nc = self.nc
        import os
        if sched and os.environ.get('NOSCHED') != '1':
            self.schedule()
        for e in ENGS:
            for ins in self.streams[e]:
                for d in ins['deps']:
                    if d['chan'] is None:
                        if d['eng'] == 'pe' and e == 'pe' and ins['chan'] is None and ins['fn'] is not None:
                            continue
                        d['signal'] = True
        nsig = {}
        for e in ENGS:
            c = 0
            for ins in self.streams[e]:
                if ins['chan'] is None and ins['signal'] and ins['fn'] is not None:
                    c += 1
                    ins['cnt'] = c
            nsig[e] = c
        sems = {}
        for e in ENGS:
            ne = max(1, -(-nsig[e] // EPOCH))
            sems[e] = [self.stack.enter_context(nc.semaphore(f"s_{e}{i}")) for i in range(ne)]
        for c in self.chans:
            c.sem = self.stack.enter_context(nc.semaphore(f"c_{c.name}"))
        self.stats = {e: [len(self.streams[e]), nsig[e], 0] for e in ENGS}

        def emit_stream(e, h):
            waited = {}
            nw = 0
            for ins in self.streams[e]:
                need = {}
                for d in ins['deps']:
                    if d['chan'] is None:
                        if d['eng'] == 'pe' and e == 'pe' and ins['chan'] is None and ins['fn'] is not None:
                            continue
                        cnt = d['cnt']
                        ep = (cnt - 1) // EPOCH
                        sem = sems[d['eng']][ep]
                        val = cnt - ep * EPOCH
                    else:
                        sem = d['chan'].sem
                        val = 16 * (d['n'] + 1)
                    k = id(sem)
                    if waited.get(k, 0) >= val:
                        continue
                    if k not in need or need[k][1] < val:
                        need[k] = (sem, val)
                for k, (sem, val) in need.items():
                    h.wait_ge(sem, val)
                    waited[k] = val
                    nw += 1
                if ins['fn'] is None:
                    continue
                r = ins['fn'](h)
                if ins['chan'] is not None:
                    r.then_inc(ins['chan'].sem, 16)
                elif ins['signal']:
                    cnt = ins['cnt']
                    ep = (cnt - 1) // EPOCH
                    r.then_inc(sems[e][ep], 1)
            self.stats[e][2] = nw

        with nc.Block() as block:
            @block.tensor
            def _(h):
                emit_stream('pe', h)

            @block.scalar
            def _(h):
                emit_stream('act', h)

            @block.vector
            def _(h):
                emit_stream('dve', h)

            @block.gpsimd
            def _(h):
                emit_stream('pool', h)

            @block.sync
            def _(h):
                emit_stream('sp', h)
        self.stack.close()


class _Fake:
    def __init__(self):
        self.call = None

    def __getattr__(self, name):
        def f(*a, **k):
            self.call = (name, a, k)
            return self
        return f


def mm_group(P, out_ap, pairs, Rout, reads):
    n = len(pairs)
    for i, (l, r) in enumerate(pairs):
        P.op('pe', lambda h, l=l, r=r, i=i: h.matmul(out_ap, l, r, start=(i == 0), stop=(i == n - 1)),
             reads=reads, writes=[Rout])


def act(P, out, in_, func, reads, writes, scale=None, bias=None):
    kw = {}
    if scale is not None:
        kw['scale'] = scale
    if bias is not None:
        kw['bias'] = bias
    return P.op('act', lambda h: h.activation(out=out, in_=in_, func=func, **kw), reads=reads, writes=writes)


class Ring:
    def __init__(self, P, name, n, shape, dtype, chan=False):
        self.bufs = [P.alloc(shape, dtype) for _ in range(n)]
        self.res = [P.res(f"{name}{i}") for i in range(n)]
        self.ch = [P.chan(f"{name}{i}") for i in range(n)] if chan else None
        self.i = -1
        self.n = n

    def next(self):
        self.i = (self.i + 1) % self.n
        if self.ch:
            return self.bufs[self.i], self.res[self.i], self.ch[self.i]
        return self.bufs[self.i], self.res[self.i]


class Ctx:
    pass


def build(L, NS, debug=False, phases=('A', 'S', 'T', 'C')):
    nc = bass.Bass("TRN2", target_bir_lowering=False)
    P = Prog(nc)
    C = Ctx()
    C.L, C.NS, C.P, C.nc = L, NS, P, nc
    NT = L // 512

    def din(name, shape, dt=F32):
        return nc.dram_tensor(name, list(shape), dt, kind="ExternalInput").ap()

    def dscr(name, shape, dt):
        return nc.dram_tensor(name, list(shape), dt,
                              kind="ExternalOutput" if debug else "Internal").ap()

    C.xT = din("xT", [NS, D, L])
    C.xT16 = din("xT16", [NS, D, L])
    C.w_in = din("w_in", [D, DIN])
    C.convw = din("convw", [128, 8, 5])
    C.convb = din("convb", [128, 8])
    C.dtb = din("dtb", [128, 16])
    C.alog = din("alog", [128, 16])
    C.dsk = din("dsk", [128, 8])
    C.ang = din("ang", [128, 4])
    C.sng = din("sng", [128, 512])
    C.relb = din("relb", [32, 8])
    C.w_out = din("w_out", [D, D])
    C.w_gate = din("w_gate", [D, DFF])
    C.w_up = din("w_up", [D, DFF])
    C.w_down = din("w_down", [DFF, D])
    C.cst = din("cst", [128, 6, 128])
    C.oh = din("oh", [32, 6, 256])
    C.jmat = din("jmat", [128, 128])
    C.sel = din("sel", [65, 64])
    C.negm = din("negm", [128, 2, 128])
    C.GV = dscr("GV", [6, 8, 256], F32)
    C.lnfm = din("lnfm", [128, 4, 8])
    C.yT = nc.dram_tensor("yT", [NS, D, L], F32, kind="ExternalOutput").ap()
    C.H1F = dscr("H1F", [NS, D, L], F32)
    C.H1B = dscr("H1B", [NS, D, L], BF16)

    C.QT = dscr("QT", [NS, 512, L], BF16)
    C.KT = dscr("KT", [NS, 512, L], BF16)
    C.V = dscr("V", [NS, L, 8 * 65], BF16)
    C.Z = dscr("Z", [NS, L, 512], F32)
    C.XS = dscr("XS", [NS, 512, L], F32)
    C.BC = dscr("BC", [NS, 512, L], BF16)
    C.DT = dscr("DT", [NS, L, 16], F32)

    outs = []
    C.YN = dscr("YN", [NS, 512, L], BF16)
    C.AT = dscr("AT", [NS, 512, L], F32)
    if 'A' in phases:
        phase_A(C)
        P.barrier()
    if 'S' in phases:
        phase_S(C)
        P.barrier()
    if 'T' in phases:
        phase_T(C)
        P.barrier()
    if 'C' in phases:
        phase_C1(C)
        P.barrier()
        outs = phase_C2(C)
        P.barrier()
    if debug:
        evs = [c.last for c in P.chans if c.n > 0]
        P.wait_all('sp', evs)
    else:
        P.wait_all('sp', outs)
    P.emit()
    return nc, P


def phase_A(C):
    P, L, NS = C.P, C.L, C.NS
    NT = L // 512
    m0 = P.mark()
    win = P.alloc([128, 8, DIN], BF16)
    Rwin = P.res("win")
    cw = P.alloc([128, 8, 5], F32)
    cb = P.alloc([128, 8], F32)
    dtb = P.alloc([128, 16], F32)
    Rsm = P.res("small")
    ch_w = P.chan("w")
    w_v = C.w_in.rearrange("(kc p) c -> p kc c", p=128)
    for c0 in (0, 1544):
        P.dma('pool', ch_w, lambda h, c0=c0: h.dma_start(out=win[:, :, c0:c0 + 1544], in_=w_v[:, :, c0:c0 + 1544]),
              writes=[Rwin])
    ch_s = P.chan("small")
    P.dma('sp', ch_s, lambda h: h.dma_start(out=cw, in_=C.convw), writes=[Rsm])
    P.dma('sp', ch_s, lambda h: h.dma_start(out=cb, in_=C.convb), writes=[Rsm])
    P.dma('sp', ch_s, lambda h: h.dma_start(out=dtb, in_=C.dtb), writes=[Rsm])

    xtb = Ring(P, "xtb", 2, [128, 8, 512], BF16, chan=True)
    xtb16 = Ring(P, "xtb16", 2, [128, 8, 512], BF16, chan=True)
    stq = Ring(P, "stq", 2, [128, 4, 512], BF16, chan=True)
    stk = Ring(P, "stk", 2, [128, 4, 512], BF16, chan=True)
    stx = Ring(P, "stx", 2, [128, 4, 512], F32, chan=True)
    stbc = Ring(P, "stbc", 2, [128, 4, 512], BF16, chan=True)
    stv = Ring(P, "stv", 2, [128, 4, 8, 65], BF16, chan=True)
    stz = Ring(P, "stz", 2, [128, 4, 512], F32, chan=True)
    stdt = Ring(P, "stdt", 2, [128, 4, 16], F32, chan=True)
    raw = [P.alloc([128, 520], F32) for _ in range(8)]
    Rraw = [P.res(f"raw{c}") for c in range(8)]
    acc = Ring(P, "acc", 4, [128, 512], F32)
    dtt = Ring(P, "dtt", 2, [128, 16], F32)
    for b, r in zip(stv.bufs, stv.res):
        P.op('pool', lambda h, b=b: h.memset(b, 1.0), writes=[r])
    fm_banks = [0, 1, 2]
    tm_banks = [3, 4, 5, 6]
    dt_bank = 7
    fmi = [0]
    tmi = [0]

    def next_fm():
        b = fm_banks[fmi[0] % len(fm_banks)]
        fmi[0] += 1
        return b

    def next_tm():
        b = tm_banks[tmi[0] % len(tm_banks)]
        tmi[0] += 1
        return b

    def load_x(s, T):
        buf, r, ch = xtb.next()
        src = C.xT[s].rearrange("(kc p) t -> p kc t", p=128)[:, :, T * 512:(T + 1) * 512]
        P.dma('pool', ch, lambda h: h.dma_start(out=buf, in_=src), writes=[r])
        buf2, r2, ch2 = xtb16.next()
        src2 = C.xT16[s].rearrange("(kc p) t -> p kc t", p=128)[:, :, T * 512:(T + 1) * 512]
        P.dma('pool', ch2, lambda h: h.dma_start(out=buf2, in_=src2), writes=[r2])
        return buf, r, buf2, r2

    def conv_chunk_ops(c, width, accb, Racc):
        ops = []
        rb = raw[c]
        ops.append(lambda: P.op('dve', lambda h: h.tensor_scalar(
            out=accb[:, 0:width], in0=rb[:, 0:width], scalar1=cw[:, c, 0:1], scalar2=cb[:, c:c + 1],
            op0=ALU.mult, op1=ALU.add), reads=[Rraw[c], Rsm], writes=[Racc]))
        for j in range(1, 5):
            ops.append(lambda j=j: P.op('dve', lambda h: h.scalar_tensor_tensor(
                out=accb[:, 0:width], in0=rb[:, j:j + width], scalar=cw[:, c, j:j + 1], in1=accb[:, 0:width],
                op0=ALU.mult, op1=ALU.add), reads=[Rraw[c], Rsm, Racc], writes=[Racc]))
        return ops

    def conv_finish(s, c, width, accb, Racc, xbuf, xr, bcbuf, bcr):
        if c < 4:
            act(P, xbuf[:, c, 0:width], accb[:, 0:width], AF.Silu, [Racc], [xr])
        else:
            act(P, bcbuf[:, c - 4, 0:width], accb[:, 0:width], AF.Silu, [Racc], [bcr])

    for s in range(NS):
        for c in range(8):
            P.op('pool', lambda h, c=c: h.memset(raw[c][:, 0:4], 0.0), writes=[Rraw[c]])
        nxt = load_x(s, 0)
        for T in range(NT):
            xb, xr_, xb16, xr16 = nxt
            if T + 1 < NT:
                nxt = load_x(s, T + 1)
            t0 = T * 512
            qb, qr, qch = stq.next()
            kb, kr, kch = stk.next()
            for c in range(4):
                b = next_fm()
                mm_group(P, P.banks[b][:, :], [(win[:, kc, COL_Q + c * 128:COL_Q + (c + 1) * 128], xb16[:, kc, :])
                                               for kc in range(8)], P.Rbank[b], [Rwin, xr16])
                act(P, qb[:, c, :], P.banks[b][:, :], AF.Identity, [P.Rbank[b]], [qr], scale=0.125)
            P.dma('sp', qch, lambda h, qb=qb, s=s, t0=t0: h.dma_start(
                out=C.QT[s].rearrange("(c p) t -> p c t", p=128)[:, :, t0:t0 + 512], in_=qb), reads=[qr])
            for c in range(4):
                b = next_fm()
                mm_group(P, P.banks[b][:, :], [(win[:, kc, COL_K + c * 128:COL_K + (c + 1) * 128], xb[:, kc, :])
                                               for kc in range(8)], P.Rbank[b], [Rwin, xr_])
                P.op('dve', lambda h, b=b, c=c, kb=kb: h.tensor_copy(out=kb[:, c, :], in_=P.banks[b][:, :]),
                     reads=[P.Rbank[b]], writes=[kr])
            P.dma('sp', kch, lambda h, kb=kb, s=s, t0=t0: h.dma_start(
                out=C.KT[s].rearrange("(c p) t -> p c t", p=128)[:, :, t0:t0 + 512], in_=kb), reads=[kr])
            sxb, sxr, sxch = stx.next()
            sbb, sbr, sbch = stbc.next()
            for cp in range(4):
                chains = []
                accs = []
                for c in (2 * cp, 2 * cp + 1):
                    b = next_fm()
                    mm_group(P, P.banks[b][:, :],
                             [(win[:, kc, COL_X + c * 128:COL_X + (c + 1) * 128], xb[:, kc, :]) for kc in range(8)],
                             P.Rbank[b], [Rwin, xr_])
                    act(P, raw[c][:, 4:516], P.banks[b][:, :], AF.Identity, [P.Rbank[b]], [Rraw[c]])
                    ab, ar = acc.next()
                    accs.append((c, ab, ar))
                    chains.append(conv_chunk_ops(c, 512, ab, ar))
                for j in range(5):
                    for ch_ in chains:
                        ch_[j]()
                for (c, ab, ar) in accs:
                    conv_finish(s, c, 512, ab, ar, sxb, sxr, sbb, sbr)
                    P.op('pool', lambda h, c=c: h.tensor_copy(out=raw[c][:, 0:4], in_=raw[c][:, 512:516]),
                         reads=[Rraw[c]], writes=[Rraw[c]])
            lo = 2 if T == 0 else 0
            P.dma('sp', sxch, lambda h, sxb=sxb, s=s, t0=t0, lo=lo: h.dma_start(
                out=C.XS[s].rearrange("(c p) t -> p c t", p=128)[:, :, t0 - 2 + lo:t0 + 510],
                in_=sxb[:, :, lo:512]), reads=[sxr])
            P.dma('sp', sbch, lambda h, sbb=sbb, s=s, t0=t0, lo=lo: h.dma_start(
                out=C.BC[s].rearrange("(c p) t -> p c t", p=128)[:, :, t0 - 2 + lo:t0 + 510],
                in_=sbb[:, :, lo:512]), reads=[sbr])
            vb, vr, vch = stv.next()
            zb, zr, zch = stz.next()
            db, dr, dch = stdt.next()
            for u in range(4):
                lw = [xb[:, kc, u * 128:(u + 1) * 128] for kc in range(8)]
                b = next_tm()
                mm_group(P, P.banks[b][:, :], [(lw[kc], win[:, kc, COL_V:COL_V + 512]) for kc in range(8)],
                         P.Rbank[b], [Rwin, xr_])
                act(P, vb[:, u, :, 0:64], P.banks[b][:, :].rearrange("p (h d) -> p h d", d=64), AF.Identity,
                    [P.Rbank[b]], [vr])
                b = next_tm()
                mm_group(P, P.banks[b][:, :], [(lw[kc], win[:, kc, COL_Z:COL_Z + 512]) for kc in range(8)],
                         P.Rbank[b], [Rwin, xr_])
                act(P, zb[:, u, :], P.banks[b][:, :], AF.Silu, [P.Rbank[b]], [zr])
                b = dt_bank
                mm_group(P, P.banks[b][:, 0:16], [(lw[kc], win[:, kc, COL_DT:COL_DT + 16]) for kc in range(8)],
                         P.Rbank[b], [Rwin, xr_])
                tb, tr = dtt.next()
                P.op('dve', lambda h, b=b, tb=tb: h.tensor_tensor(out=tb, in0=P.banks[b][:, 0:16], in1=dtb, op=ALU.add),
                     reads=[P.Rbank[b], Rsm], writes=[tr])
                act(P, tb, tb, AF.Exp, [tr], [tr])
                act(P, db[:, u, :], tb, AF.Ln, [tr], [dr], bias=1.0)
            P.dma('sp', vch, lambda h, vb=vb, s=s, t0=t0: h.dma_start(
                out=C.V[s][t0:t0 + 512, :].rearrange("(u p) f -> p u f", p=128),
                in_=vb.rearrange("p u h e -> p u (h e)")), reads=[vr])
            P.dma('sp', zch, lambda h, zb=zb, s=s, t0=t0: h.dma_start(
                out=C.Z[s][t0:t0 + 512, :].rearrange("(u p) f -> p u f", p=128), in_=zb), reads=[zr])
            P.dma('sp', dch, lambda h, db=db, s=s, t0=t0: h.dma_start(
                out=C.DT[s][t0:t0 + 512, :].rearrange("(u p) f -> p u f", p=128), in_=db), reads=[dr])
        sxb, sxr, sxch = stx.next()
        sbb, sbr, sbch = stbc.next()
        for c in range(8):
            P.op('pool', lambda h, c=c: h.memset(raw[c][:, 4:8], 0.0), reads=[Rraw[c]], writes=[Rraw[c]])
            ab, ar = acc.next()
            for o in conv_chunk_ops(c, 2, ab, ar):
                o()
            conv_finish(s, c, 2, ab, ar, sxb, sxr, sbb, sbr)
        P.dma('sp', sxch, lambda h, sxb=sxb, s=s: h.dma_start(
            out=C.XS[s].rearrange("(c p) t -> p c t", p=128)[:, :, L - 2:L], in_=sxb[:, :, 0:2]),
            reads=[sxr])
        P.dma('sp', sbch, lambda h, sbb=sbb, s=s: h.dma_start(
            out=C.BC[s].rearrange("(c p) t -> p c t", p=128)[:, :, L - 2:L], in_=sbb[:, :, 0:2]),
            reads=[sbr])
    P.reset(m0)


def phase_S(C):
    P, L, NS = C.P, C.L, C.NS
    NC = L // 128
    NG = NC // 4
    m0 = P.mark()
    cst = P.alloc([128, 6, 128], F32)
    idb = P.alloc([128, 128], BF16)
    dsk = P.alloc([128, 8], F32)
    Aneg = P.alloc([128, 16], F32)
    sng = P.alloc([128, 512], F32)
    Rc = P.res("s_const")
    chc = P.chan("s_const")
    P.dma('sp', chc, lambda h: h.dma_start(out=cst, in_=C.cst), writes=[Rc])
    P.dma('sp', chc, lambda h: h.dma_start(out=dsk, in_=C.dsk), writes=[Rc])
    P.dma('sp', chc, lambda h: h.dma_start(out=Aneg, in_=C.alog), writes=[Rc])
    P.dma('sp', chc, lambda h: h.dma_start(out=sng, in_=C.sng), writes=[Rc])
    act(P, Aneg, Aneg, AF.Exp, [Rc], [Rc])
    P.op('dve', lambda h: h.tensor_scalar(out=Aneg, in0=Aneg, scalar1=-1.0, scalar2=None, op0=ALU.mult),
         reads=[Rc], writes=[Rc])
    P.op('dve', lambda h: h.tensor_copy(out=idb, in_=cst[:, 5, :]), reads=[Rc], writes=[Rc])
    U_, SL_, LO_, SU_, ON_, ID_ = [cst[:, i, :] for i in range(6)]

    SbAll = P.alloc([128, NC, 512], BF16)
    RSb = [P.res(f"sb{c}") for c in range(NC)]
    gx = Ring(P, "gx", 3, [128, 4, 512], F32, chan=True)
    gbc = Ring(P, "gbc", 3, [128, 4, 512], BF16, chan=True)
    gdt = Ring(P, "gdt", 3, [128, 4, 16], F32, chan=True)
    gz = Ring(P, "gz", 3, [128, 4, 512], F32, chan=True)
    syn = Ring(P, "syn", 2, [128, 4, 512], BF16, chan=True)
    da_r = Ring(P, "da", 3, [128, 16], F32)
    ew_r = Ring(P, "ew", 4, [128, 64], F32)
    sc_r = Ring(P, "sc", 4, [128, 16], F32)
    xdt_r = Ring(P, "xdt", 6, [128, 512], BF16)
    xsd_r = Ring(P, "xsd", 3, [128, 512], F32)
    btok_r = Ring(P, "btok", 3, [128, 256], BF16)
    cbm_r = Ring(P, "cbm", 4, [128, 256], F32)
    L_r = Ring(P, "Lr", 2, [128, 8, 128], F32)
    dec_r = Ring(P, "dec", 2, [128, 8, 128], F32)
    M_r = Ring(P, "Mr", 4, [128, 8, 128], BF16)
    t_r = Ring(P, "tr", 4, [128, 512], F32)
    yn_r = Ring(P, "yn", 2, [128, 512], BF16)
    sm_r = Ring(P, "sm", 4, [128, 4], F32)
    Sf = P.alloc([128, 512], F32)
    Sfb = P.alloc([128, 512], BF16)
    Sb = P.alloc([128, 512], F32)
    RSf, RSfb, RSbr = P.res("Sf"), P.res("Sfb"), P.res("Sbr")

    def bc8(ap8):
        return ap8.unsqueeze(2).to_broadcast([128, 8, 64])

    def v3(ap):
        return ap.rearrange("p (h d) -> p h d", d=64)

    def load_group(s, g, with_z):
        t0 = g * 512
        xb, xr, xch = gx.next()
        P.dma('sp', xch, lambda h: h.dma_start(
            out=xb, in_=C.XS[s].rearrange("(c p) t -> p c t", p=128)[:, :, t0:t0 + 512]), writes=[xr])
        bb, br, bch = gbc.next()
        P.dma('sp', bch, lambda h: h.dma_start(
            out=bb, in_=C.BC[s].rearrange("(c p) t -> p c t", p=128)[:, :, t0:t0 + 512]), writes=[br])
        db, dr, dch = gdt.next()
        P.dma('sp', dch, lambda h: h.dma_start(
            out=db, in_=C.DT[s][t0:t0 + 512, :].rearrange("(u p) f -> p u f", p=128)), writes=[dr])
        zz = None
        if with_z:
            zb, zr, zch = gz.next()
            P.dma('sp', zch, lambda h: h.dma_start(
                out=zb, in_=C.Z[s][t0:t0 + 512, :].rearrange("(u p) f -> p u f", p=128)), writes=[zr])
            zz = (zb, zr)
        return (xb, xr), (bb, br), (db, dr), zz

    def small_mms(da, Rda, mats):
        b = P.next_bank()
        for i, m_ in enumerate(mats):
            P.op('pe', lambda h, i=i, m_=m_, b=b: h.matmul(P.banks[b][:, 16 * i:16 * i + 16], m_, da, start=True, stop=True),
                 reads=[Rc, Rda], writes=[P.Rbank[b]])
        ew, Rew = ew_r.next()
        n = 16 * len(mats)
        act(P, ew[:, 0:n], P.banks[b][:, 0:n], AF.Exp, [P.Rbank[b]], [Rew])
        return ew, Rew

    def xs_transpose(xb, xr, u):
        b = P.next_bank()
        for fc in range(4):
            P.op('pe', lambda h, fc=fc, b=b: h.transpose(P.banks[b][:, fc * 128:(fc + 1) * 128],
                                                        xb[:, fc, u * 128:(u + 1) * 128], ID_),
                 reads=[xr, Rc], writes=[P.Rbank[b]])
        return b

    def b_transpose(bb, br, u):
        b = P.next_bank()
        pb = P.banks[b][:, :].bitcast(BF16)
        for g in range(2):
            P.op('pe', lambda h, g=g, pb=pb: h.transpose(pb[:, g * 128:(g + 1) * 128],
                                                        bb[:, g, u * 128:(u + 1) * 128], idb),
                 reads=[br, Rc], writes=[P.Rbank[b]])
        bt, Rbt = btok_r.next()
        act(P, bt, pb[:, 0:256], AF.Identity, [P.Rbank[b]], [Rbt])
        return bt, Rbt

    def state_mm(bt, Rbt, xw, Rxw):
        b = P.next_bank()
        for g in range(2):
            P.op('pe', lambda h, g=g, b=b: h.matmul(P.banks[b][:, g * 256:(g + 1) * 256], bt[:, g * 128:(g + 1) * 128],
                                                   xw[:, g * 256:(g + 1) * 256], start=True, stop=True),
                 reads=[Rbt, Rxw], writes=[P.Rbank[b]])
        return b

    def s1_front(db, dr, xb, xr, bb, br, u):
        da, Rda = da_r.next()
        P.op('dve', lambda h: h.tensor_tensor(out=da, in0=db[:, u, :], in1=Aneg, op=ALU.mult),
             reads=[dr, Rc], writes=[Rda])
        ew, Rew = small_mms(da, Rda, [SU_, ON_])
        sc, Rsc = sc_r.next()
        P.op('dve', lambda h: h.tensor_tensor(out=sc[:, 0:8], in0=db[:, u, 8:16], in1=ew[:, 8:16], op=ALU.mult),
             reads=[dr, Rew], writes=[Rsc])
        bx = xs_transpose(xb, xr, u)
        xw, Rxw = xdt_r.next()
        P.op('dve', lambda h: h.tensor_tensor(out=v3(xw), in0=v3(P.banks[bx][:, :]), in1=bc8(sc[:, 0:8]), op=ALU.mult),
             reads=[P.Rbank[bx], Rsc], writes=[Rxw])
        bt, Rbt = b_transpose(bb, br, u)
        bs = state_mm(bt, Rbt, xw, Rxw)
        return ew, Rew, bs

    def s1_back(c, ew, Rew, bs):
        P.op('dve', lambda h: h.tensor_tensor(out=v3(Sb), in0=v3(Sb), in1=bc8(ew[:, 24:32]), op=ALU.mult),
             reads=[RSbr, Rew], writes=[RSbr])
        P.op('dve', lambda h: h.tensor_tensor(out=Sb, in0=Sb, in1=P.banks[bs][:, :], op=ALU.add),
             reads=[RSbr, P.Rbank[bs]], writes=[RSbr])
        act(P, SbAll[:, c - 1, :], Sb, AF.Identity, [RSbr], [RSb[c - 1]])

    def s2_front(db, dr, xb, xr, bb, br, u):
        da, Rda = da_r.next()
        P.op('dve', lambda h: h.tensor_tensor(out=da, in0=db[:, u, :], in1=Aneg, op=ALU.mult),
             reads=[dr, Rc], writes=[Rda])
        ew, Rew = small_mms(da, Rda, [U_, SL_, LO_, ON_])
        sc, Rsc = sc_r.next()
        P.op('dve', lambda h: h.tensor_tensor(out=sc[:, 0:8], in0=db[:, u, 0:8], in1=ew[:, 16:24], op=ALU.mult),
             reads=[dr, Rew], writes=[Rsc])
        Ms = []
        Ls = []
        for (tri_l, c0) in ((SL_, 0), (SU_, 8)):
            Lt, RLt = L_r.next()
            P.op('pool', lambda h, Lt=Lt, tri_l=tri_l, c0=c0: h.tensor_tensor(
                out=Lt, in0=tri_l.unsqueeze(1).to_broadcast([128, 8, 128]),
                in1=da[:, c0:c0 + 8].unsqueeze(2).to_broadcast([128, 8, 128]), op=ALU.mult),
                reads=[Rc, Rda], writes=[RLt])
            Ls.append((Lt, RLt))
        bx = xs_transpose(xb, xr, u)
        xsP = v3(P.banks[bx][:, :])
        xf, Rxf = xdt_r.next()
        xbw, Rxbw = xdt_r.next()
        xw, Rxw = xdt_r.next()
        xsd, Rxsd = xsd_r.next()
        P.op('dve', lambda h: h.tensor_tensor(out=v3(xf), in0=xsP, in1=bc8(db[:, u, 0:8]), op=ALU.mult),
             reads=[P.Rbank[bx], dr], writes=[Rxf])
        P.op('dve', lambda h: h.tensor_tensor(out=v3(xbw), in0=xsP, in1=bc8(db[:, u, 8:16]), op=ALU.mult),
             reads=[P.Rbank[bx], dr], writes=[Rxbw])
        P.op('dve', lambda h: h.tensor_tensor(out=v3(xw), in0=xsP, in1=bc8(sc[:, 0:8]), op=ALU.mult),
             reads=[P.Rbank[bx], Rsc], writes=[Rxw])
        P.op('dve', lambda h: h.tensor_tensor(out=v3(xsd), in0=xsP, in1=bc8(dsk), op=ALU.mult),
             reads=[P.Rbank[bx], Rc], writes=[Rxsd])
        bt, Rbt = b_transpose(bb, br, u)
        bcb = P.next_bank()
        for gg in range(2):
            P.op('pe', lambda h, gg=gg: h.matmul(
                P.banks[bcb][:, gg * 128:(gg + 1) * 128], bb[:, gg, u * 128:(u + 1) * 128],
                bb[:, 2 + gg, u * 128:(u + 1) * 128], start=True, stop=True), reads=[br], writes=[P.Rbank[bcb]])
        cbU, RcbU = cbm_r.next()
        cbL, RcbL = cbm_r.next()
        cbP = P.banks[bcb][:, 0:256].rearrange("p (g q) -> p g q", g=2)
        P.op('dve', lambda h: h.tensor_tensor(
            out=cbU.rearrange("p (g q) -> p g q", g=2), in0=cbP,
            in1=U_.unsqueeze(1).to_broadcast([128, 2, 128]), op=ALU.mult),
            reads=[P.Rbank[bcb], Rc], writes=[RcbU])
        P.op('dve', lambda h: h.tensor_tensor(
            out=cbL.rearrange("p (g q) -> p g q", g=2), in0=cbP,
            in1=LO_.unsqueeze(1).to_broadcast([128, 2, 128]), op=ALU.mult),
            reads=[P.Rbank[bcb], Rc], writes=[RcbL])
        for di, (tri_r, cbm, Rcbm) in enumerate(((U_, cbU, RcbU), (LO_, cbL, RcbL))):
            Lt, RLt = Ls[di]
            dec, Rdec = dec_r.next()
            for hh in range(2):
                b = P.next_bank()
                for h4 in range(4):
                    hd = hh * 4 + h4
                    P.op('pe', lambda h, b=b, h4=h4, hd=hd, Lt=Lt, tri_r=tri_r: h.matmul(
                        P.banks[b][:, h4 * 128:(h4 + 1) * 128], Lt[:, hd, :], tri_r, start=True, stop=True),
                        reads=[RLt, Rc], writes=[P.Rbank[b]])
                act(P, dec[:, hh * 4:(hh + 1) * 4, :], P.banks[b][:, :].rearrange("p (a q) -> p a q", a=4),
                    AF.Exp, [P.Rbank[b]], [Rdec])
            Mt, RMt = M_r.next()
            P.op('dve', lambda h, Mt=Mt, dec=dec, cbm=cbm: h.tensor_tensor(
                out=Mt.rearrange("p (g e) q -> p g e q", g=2), in0=dec.rearrange("p (g e) q -> p g e q", g=2),
                in1=cbm.rearrange("p (g q) -> p g q", g=2).unsqueeze(2).to_broadcast([128, 2, 4, 128]),
                op=ALU.mult), reads=[Rdec, Rcbm], writes=[RMt])
            Ms.append((Mt, RMt))
        return dict(ew=ew, Rew=Rew, xf=xf, Rxf=Rxf, xbw=xbw, Rxbw=Rxbw, xw=xw, Rxw=Rxw, xsd=xsd, Rxsd=Rxsd,
                    bt=bt, Rbt=Rbt, Ms=Ms)

    def s2_back(c, u, f, bb, br, zb, zr, yb, yr):
        ew, Rew, xf, Rxf, xbw, Rxbw, xw, Rxw = f['ew'], f['Rew'], f['xf'], f['Rxf'], f['xbw'], f['Rxbw'], f['xw'], f['Rxw']
        xsd, Rxsd, bt, Rbt, Ms = f['xsd'], f['Rxsd'], f['bt'], f['Rbt'], f['Ms']
        by = P.next_bank()
        for hd in range(8):
            P.op('pe', lambda h, hd=hd: h.matmul(
                P.banks[by][:, hd * 64:(hd + 1) * 64], Ms[0][0][:, hd, :], xf[:, hd * 64:(hd + 1) * 64],
                start=True, stop=False), reads=[Ms[0][1], Rxf], writes=[P.Rbank[by]])
            P.op('pe', lambda h, hd=hd: h.matmul(
                P.banks[by][:, hd * 64:(hd + 1) * 64], Ms[1][0][:, hd, :], xbw[:, hd * 64:(hd + 1) * 64],
                start=False, stop=True), reads=[Ms[1][1], Rxbw], writes=[P.Rbank[by]])
        bof = P.next_bank()
        bob = P.next_bank()
        for gg in range(2):
            P.op('pe', lambda h, gg=gg: h.matmul(
                P.banks[bof][:, gg * 256:(gg + 1) * 256], bb[:, 2 + gg, u * 128:(u + 1) * 128],
                Sfb[:, gg * 256:(gg + 1) * 256], start=True, stop=True), reads=[br, RSfb], writes=[P.Rbank[bof]])
        for gg in range(2):
            P.op('pe', lambda h, gg=gg: h.matmul(
                P.banks[bob][:, gg * 256:(gg + 1) * 256], bb[:, 2 + gg, u * 128:(u + 1) * 128],
                SbAll[:, c, gg * 256:(gg + 1) * 256], start=True, stop=True), reads=[br, RSb[c]], writes=[P.Rbank[bob]])
        bs = state_mm(bt, Rbt, xw, Rxw)
        P.op('dve', lambda h: h.tensor_tensor(out=v3(Sf), in0=v3(Sf), in1=bc8(ew[:, 48:56]), op=ALU.mult),
             reads=[RSf, Rew], writes=[RSf])
        P.op('dve', lambda h: h.tensor_tensor(out=Sf, in0=Sf, in1=P.banks[bs][:, :], op=ALU.add),
             reads=[RSf, P.Rbank[bs]], writes=[RSf])
        act(P, Sfb, Sf, AF.Identity, [RSf], [RSfb])
        t1, Rt1 = t_r.next()
        t2, Rt2 = t_r.next()
        P.op('dve', lambda h: h.tensor_tensor(out=v3(t1), in0=v3(P.banks[bof][:, :]), in1=bc8(ew[:, 0:8]), op=ALU.mult),
             reads=[P.Rbank[bof], Rew], writes=[Rt1])
        P.op('dve', lambda h: h.tensor_tensor(out=v3(t2), in0=v3(P.banks[bob][:, :]), in1=bc8(ew[:, 40:48]), op=ALU.mult),
             reads=[P.Rbank[bob], Rew], writes=[Rt2])
        P.op('pool', lambda h: h.tensor_tensor(out=t1, in0=t1, in1=t2, op=ALU.add), reads=[Rt1, Rt2], writes=[Rt1])
        P.op('pool', lambda h: h.tensor_tensor(out=t1, in0=t1, in1=xsd, op=ALU.add), reads=[Rt1, Rxsd], writes=[Rt1])
        P.op('dve', lambda h: h.tensor_tensor(out=t1, in0=t1, in1=P.banks[by][:, :], op=ALU.add),
             reads=[Rt1, P.Rbank[by]], writes=[Rt1])
        P.op('dve', lambda h: h.tensor_tensor(out=t1, in0=t1, in1=zb[:, u, :], op=ALU.mult),
             reads=[Rt1, zr], writes=[Rt1])
        sm, Rsm_ = sm_r.next()
        P.op('act', lambda h: h.activation(out=t2, in_=t1, func=AF.Square, accum_out=sm[:, 0:1]),
             reads=[Rt1, Rt2], writes=[Rt2, Rsm_])
        act(P, sm[:, 1:2], sm[:, 0:1], AF.Ln, [Rsm_], [Rsm_], scale=1.0 / 512, bias=EPS)
        act(P, sm[:, 2:3], sm[:, 1:2], AF.Exp, [Rsm_], [Rsm_], scale=-0.5)
        yn, Ryn = yn_r.next()
        P.op('dve', lambda h: h.scalar_tensor_tensor(
            out=yn, in0=t1, scalar=sm[:, 2:3], in1=sng, op0=ALU.mult, op1=ALU.mult),
            reads=[Rt1, Rsm_, Rc], writes=[Ryn])
        bt_ = P.next_bank()
        pbt = P.banks[bt_][:, :].bitcast(BF16)
        for fc in range(4):
            P.op('pe', lambda h, fc=fc: h.transpose(
                pbt[:, fc * 128:(fc + 1) * 128], yn[:, fc * 128:(fc + 1) * 128], idb),
                reads=[Ryn, Rc], writes=[P.Rbank[bt_]])
        act(P, yb[:, :, u * 128:(u + 1) * 128], pbt[:, 0:512].rearrange("p (c t) -> p c t", c=4), AF.Identity,
            [P.Rbank[bt_]], [yr])

    ybuf = {}

    def finish_back(p):
        c, u, g, fr, bb, br, zb, zr = p
        if u == 0:
            ybuf['cur'] = syn.next()
        yb, yr, ych = ybuf['cur']
        s2_back(c, u, fr, bb, br, zb, zr, yb, yr)
        if u == 3:
            s_ = ybuf['s']
            P.dma('sp', ych, lambda h: h.dma_start(
                out=C.YN[s_].rearrange("(c p) t -> p c t", p=128)[:, :, g * 512:(g + 1) * 512], in_=yb), reads=[yr])

    for s in range(NS):
        ybuf['s'] = s
        P.op('pool', lambda h: h.memset(Sb, 0.0), writes=[RSbr])
        P.op('pool', lambda h: h.memset(SbAll[:, NC - 1, :], 0.0), writes=[RSb[NC - 1]])
        groups = {}
        groups[NG - 1] = load_group(s, NG - 1, False)
        if NG > 1:
            groups[NG - 2] = load_group(s, NG - 2, False)
        pend = None
        for c in range(NC - 1, 0, -1):
            g, u = divmod(c, 4)
            (xb, xr), (bb, br), (db, dr), _ = groups[g]
            fr = s1_front(db, dr, xb, xr, bb, br, u)
            if pend is not None:
                s1_back(*pend)
            pend = (c,) + fr
            if u == 3 and g - 2 >= 0:
                groups[g - 2] = load_group(s, g - 2, False)
        if pend is not None:
            s1_back(*pend)
        P.op('pool', lambda h: h.memset(Sf, 0.0), writes=[RSf])
        P.op('pool', lambda h: h.memset(Sfb, 0.0), writes=[RSfb])
        groups = {0: load_group(s, 0, True)}
        if NG > 1:
            groups[1] = load_group(s, 1, True)
        ybs = {}
        pend = None
        for c in range(NC):
            g, u = divmod(c, 4)
            (xb, xr), (bb, br), (db, dr), (zb, zr) = groups[g]
            fr = s2_front(db, dr, xb, xr, bb, br, u)
            if pend is not None:
                finish_back(pend)
            pend = (c, u, g, fr, bb, br, zb, zr)
            if u == 0 and g + 2 < NG:
                groups[g + 2] = load_group(s, g + 2, True)
        finish_back(pend)
    P.reset(m0)


BRANCH_DIL = (1, 4, 16)


def phase_T(C):
    P, L, NS = C.P, C.L, C.NS
    m0 = P.mark()
    cst = P.alloc([128, 6, 128], F32)
    jmat = P.alloc([128, 128], F32)
    sel = P.alloc([65, 64], F32)
    LBh = P.alloc([128, 4, 3, 2, 256], BF16)
    LBl = P.alloc([128, 4, 3, 2, 256], BF16)
    negm = P.alloc([128, 2, 128], F32)
    idb = P.alloc([128, 128], BF16)
    Rc = P.res("t_const")
    REB = P.res("EB")
    chc = P.chan("t_const")
    P.dma('sp', chc, lambda h: h.dma_start(out=cst, in_=C.cst), writes=[Rc])
    P.dma('sp', chc, lambda h: h.dma_start(out=jmat, in_=C.jmat), writes=[Rc])
    P.dma('sp', chc, lambda h: h.dma_start(out=sel, in_=C.sel), writes=[Rc])
    P.dma('sp', chc, lambda h: h.dma_start(out=negm, in_=C.negm), writes=[Rc])
    P.op('dve', lambda h: h.tensor_copy(out=idb, in_=cst[:, 5, :]), reads=[Rc], writes=[Rc])
    U_, LO_ = cst[:, 0, :], cst[:, 2, :]
    m1 = P.mark()
    oh = P.alloc([32, 6, 256], F32)
    relb = P.alloc([32, 8], F32)
    P.dma('sp', chc, lambda h: h.dma_start(out=oh, in_=C.oh), writes=[Rc])
    P.dma('sp', chc, lambda h: h.dma_start(out=relb, in_=C.relb), writes=[Rc])
    gs_r = Ring(P, "gs", 2, [8, 256], F32, chan=True)
    hk_r = Ring(P, "hk", 4, [128, 128], F32, chan=True)
    tmp_r = Ring(P, "ebtmp", 3, [128, 128], F32)
    RGV = [P.res(f"gv{i}") for i in range(6)]
    for bt in range(6):
        b = P.next_bank()
        P.op('pe', lambda h, b=b, bt=bt: h.matmul(P.banks[b][0:8, 0:256], relb, oh[:, bt, :], start=True, stop=True),
             reads=[Rc], writes=[P.Rbank[b]])
        gs, Rgs, gch = gs_r.next()
        P.op('dve', lambda h, gs=gs, b=b: h.tensor_copy(out=gs, in_=P.banks[b][0:8, 0:256]), reads=[P.Rbank[b]], writes=[Rgs])
        P.dma('sp', gch, lambda h, gs=gs, bt=bt: h.dma_start(out=C.GV[bt], in_=gs), reads=[Rgs], writes=[RGV[bt]])
    for bt in range(6):
        bi, ty = bt // 2, bt % 2
        for hd in range(8):
            hk, Rhk, hch = hk_r.next()
            src = bass.AP(C.GV.tensor, (bt * 8 + hd) * 256, [[1, 128], [1, 128]])
            P.dma('sp', hch, lambda h, hk=hk, src=src: h.dma_start(out=hk, in_=src), reads=[RGV[bt]], writes=[Rhk])
            b = P.next_bank()
            P.op('pe', lambda h, b=b, hk=hk: h.matmul(P.banks[b][:, 0:128], hk, jmat, start=True, stop=True),
                 reads=[Rhk, Rc], writes=[P.Rbank[b]])
            tmp, Rtmp = tmp_r.next()
            msk = U_ if ty == 0 else LO_
            ngm = negm[:, ty, :]
            P.op('dve', lambda h, tmp=tmp, msk=msk, b=b: h.tensor_tensor(out=tmp, in0=P.banks[b][:, 0:128], in1=msk, op=ALU.mult),
                 reads=[P.Rbank[b], Rc], writes=[Rtmp])
            P.op('dve', lambda h, tmp=tmp, ngm=ngm: h.tensor_tensor(out=tmp, in0=tmp, in1=ngm, op=ALU.add),
                 reads=[Rtmp, Rc], writes=[Rtmp])
            G_ = 16 // BRANCH_DIL[bi]
            HW_ = 128 // G_
            eh = LBh[:, hd // 2, bi, hd % 2, :].rearrange("p (c w) -> p c w", c=G_)[:, :, ty * HW_:(ty + 1) * HW_]
            el = LBl[:, hd // 2, bi, hd % 2, :].rearrange("p (c w) -> p c w", c=G_)[:, :, ty * HW_:(ty + 1) * HW_]
            tv = tmp.rearrange("p (i c) -> p c i", c=G_)
            P.op('dve', lambda h, tv=tv, eh=eh: h.tensor_copy(out=eh, in_=tv), reads=[Rtmp], writes=[REB])
            P.op('dve', lambda h, tv=tv, eh=eh, el=el: h.tensor_tensor(out=el, in0=tv, in1=eh, op=ALU.subtract),
                 reads=[Rtmp, REB], writes=[REB])
    P.barrier()
    P.reset(m1)

    PADK = 1024
    Ld16 = L // 16
    Qbd = P.alloc([128, 2, L], BF16)
    Qv = Qbd.rearrange("p h (r i) -> p h r i", r=16)
    KTp = P.alloc([128, L + 2 * PADK], BF16)
    OT = P.alloc([128, 2, L], F32)
    NTmax = L // 128 + 16
    Vbs = [P.alloc([128, NTmax, 130], BF16) for _ in range(2)]
    RVs = [[P.res(f"Vb{k}_{r}") for r in range(16)] for k in range(2)]
    RQ, RK, ROT = P.res("Qbd"), P.res("KTp"), P.res("OT")
    chq, chk = P.chan("q"), P.chan("k")
    chv2 = [[P.chan(f"v{i}_{k}") for k in range(4)] for i in range(2)]
    P_r = Ring(P, "Pt", 6, [128, 512], BF16)
    rc_r = Ring(P, "rc", 2, [64, 512], F32)
    so_r = Ring(P, "so", 2, [64, 512], F32, chan=True)
    dummy = P.alloc([128, 16], F32)
    dummy2 = P.alloc([128, 16], F32)
    P.op('pool', lambda h: h.memset(Qbd, 0.0), writes=[RQ])
    P.op('pool', lambda h: h.memset(KTp, 0.0), writes=[RK])
    S_banks = [0, 1, 2, 3]
    si = [0]
    O_bank = {(0, 0): 4, (0, 1): 5, (1, 0): 6, (1, 1): 7}

    def load_V(s, hp, bi, slot):
        dl = BRANCH_DIL[bi]
        Ld = L // dl
        NJ = Ld // 128 + 1
        Vb = Vbs[slot]
        Vs = C.V[s]
        c0, c1 = hp * 130, (hp + 1) * 130
        P.op('pool', lambda h: h.memset(dummy2, 0.0), writes=list(RVs[slot]))
        for rho in range(dl):
            RV = RVs[slot][rho]
            tb = rho * NJ
            P.op('pool', lambda h, tb=tb, Vb=Vb: h.memset(Vb[0:64, tb, :], 0.0), writes=[RV])
            P.op('pool', lambda h, tb=tb, NJ=NJ, Vb=Vb: h.memset(Vb[64:128, tb + NJ - 1, :], 0.0), writes=[RV])
            ch_ = chv2[slot][rho % 4]
            if NJ > 2:
                src = Vs[rho + dl * 64:rho + dl * 64 + dl * 128 * (NJ - 2):dl, c0:c1]
                P.dma('sp', ch_, lambda h, tb=tb, NJ=NJ, src=src, Vb=Vb: h.dma_start(
                    out=Vb[:, tb + 1:tb + NJ - 1, :], in_=src.rearrange("(j a) f -> a j f", a=128)), writes=[RV])
            r_first = Vs[rho:rho + dl * 63 + 1:dl, c0:c1]
            t_l = rho + dl * (Ld - 64)
            r_last = Vs[t_l:t_l + dl * 63 + 1:dl, c0:c1]
            P.dma('sp', ch_, lambda h, tb=tb, r_first=r_first, Vb=Vb: h.dma_start(out=Vb[64:128, tb, :], in_=r_first), writes=[RV])
            P.dma('sp', ch_, lambda h, tb=tb, NJ=NJ, r_last=r_last, Vb=Vb: h.dma_start(
                out=Vb[0:64, tb + NJ - 1, :], in_=r_last), writes=[RV])

    BORDER = (2, 1, 0)
    work = [(s, hp, bi) for s in range(NS) for hp in range(4) for bi in BORDER]
    load_V(*work[0], 0)

    def do_work(wi, s, hp, bi):
        slot = wi % 2
        Vb = Vbs[slot]
        dl = BRANCH_DIL[bi]
        G = 16 // dl
        W = 256 // G
        HW = 128 // G
        Ld = L // dl
        NQ = Ld // 128
        NJ = NQ + 1
        gs = min(4, NQ)
        first = (bi == BORDER[0])
        last = (bi == BORDER[-1])
        if first:
            r0 = hp * 128
            P.dma('sp', chq, lambda h: h.dma_start(out=Qbd[0:64, 0, :], in_=C.QT[s][r0:r0 + 64, :]), writes=[RQ])
            P.dma('sp', chq, lambda h: h.dma_start(out=Qbd[64:128, 1, :], in_=C.QT[s][r0 + 64:r0 + 128, :]), writes=[RQ])
            P.dma('sp', chk, lambda h: h.dma_start(out=KTp[:, PADK:PADK + L], in_=C.KT[s][r0:r0 + 128, :]), writes=[RK])
        P.op('pool', lambda h: h.memset(dummy, 0.0), writes=[ROT])
        if wi + 1 < len(work):
            load_V(*work[wi + 1], 1 - slot)

        def v4(ap512):
            return ap512.rearrange("p (h c w) -> p h c w", h=2, c=G)

        for rho in range(dl):
            prev = None
            RV = RVs[slot][rho]
            for j in range(NJ):
                halves = [hf for hf in (0, 1) if 0 <= j - 1 + hf < NQ]
                h0, h1 = halves[0], halves[-1] + 1
                w0, w1 = h0 * HW, h1 * HW
                i0 = (128 * (j - 1) + 128 * h0) // G
                i1 = (128 * (j - 1) + 128 * h1) // G
                ks = PADK + rho + dl * (128 * j - 64)
                bS = S_banks[si[0] % 4]
                si[0] += 1
                Sv = v4(P.banks[bS][:, :])[:, :, :, w0:w1]
                qa = Qv[:, :, rho:16:dl, i0:i1]
                ka = KTp[:, ks:ks + dl * 127 + 1:dl]
                lh = v4(LBh[:, hp, bi, :, :].rearrange("p h q -> p (h q)"))[:, :, :, w0:w1]
                ll = v4(LBl[:, hp, bi, :, :].rearrange("p h q -> p (h q)"))[:, :, :, w0:w1]
                P.op('pe', lambda h, Sv=Sv, qa=qa, ka=ka: h.matmul(Sv, ka, qa, start=True, stop=False),
                     reads=[RK, RQ], writes=[P.Rbank[bS]])
                P.op('pe', lambda h, Sv=Sv, lh=lh: h.matmul(Sv, idb, lh, start=False, stop=False),
                     reads=[Rc, REB], writes=[P.Rbank[bS]])
                P.op('pe', lambda h, Sv=Sv, ll=ll: h.matmul(Sv, idb, ll, start=False, stop=True),
                     reads=[Rc, REB], writes=[P.Rbank[bS]])
                Pt, RPt = P_r.next()
                Pv = v4(Pt)
                act(P, Pv[:, :, :, w0:w1], Sv, AF.Exp, [P.Rbank[bS]], [RPt])
                tile = rho * NJ + j
                if j >= 1:
                    m = j - 1
                    pPv, pRPt, ptile = prev
                    for hd in range(2):
                        bO = O_bank[(hd, (m // gs) % 2)]
                        oc = (m % gs) * 128
                        Ov = P.banks[bO][0:65, oc:oc + 128].rearrange("p (c w) -> p c w", c=G)
                        P.op('pe', lambda h, Ov=Ov, hd=hd, pPv=pPv, ptile=ptile: h.matmul(
                            Ov, Vb[:, ptile, hd * 65:(hd + 1) * 65], pPv[:, hd, :, HW:2 * HW], start=True, stop=False),
                            reads=[RV, pRPt], writes=[P.Rbank[bO]])
                        P.op('pe', lambda h, Ov=Ov, hd=hd, Pv=Pv, tile=tile: h.matmul(
                            Ov, Vb[:, tile, hd * 65:(hd + 1) * 65], Pv[:, hd, :, 0:HW], start=False, stop=True),
                            reads=[RV, RPt], writes=[P.Rbank[bO]])
                        if (m + 1) % gs == 0:
                            m0_ = m + 1 - gs
                            t0 = rho + dl * 128 * m0_
                            ov = OT[0:65, hd, t0 - rho:t0 - rho + dl * 128 * gs].rearrange(
                                "p (m i r) -> p m i r", m=gs, r=16)[:, :, :, rho:16:dl].rearrange("p m i c -> p m c i")
                            src = P.banks[bO][0:65, 0:gs * 128].rearrange("p (m c i) -> p m c i", m=gs, c=G)
                            if first:
                                act(P, ov, src, AF.Identity, [ROT, P.Rbank[bO]], [])
                            elif not last:
                                P.op('dve', lambda h, ov=ov, src=src: h.tensor_tensor(out=ov, in0=ov, in1=src, op=ALU.add),
                                     reads=[ROT, P.Rbank[bO]], writes=[])
                            else:
                                Rfin = P.res()
                                P.op('dve', lambda h, ov=ov, src=src: h.tensor_tensor(out=ov, in0=ov, in1=src, op=ALU.add),
                                     reads=[ROT, P.Rbank[bO]], writes=[Rfin])
                                ncol = 128 * gs
                                b = S_banks[si[0] % 4]
                                si[0] += 1
                                P.op('pe', lambda h, b=b, hd=hd, t0=t0, ncol=ncol: h.matmul(
                                    P.banks[b][0:64, 0:ncol], sel, OT[0:65, hd, t0:t0 + ncol], start=True, stop=True),
                                    reads=[Rc, Rfin, ROT], writes=[P.Rbank[b]])
                                rc, Rrc = rc_r.next()
                                act(P, rc[:, 0:ncol], P.banks[b][0:64, 0:ncol], AF.Ln, [P.Rbank[b]], [Rrc])
                                act(P, rc[:, 0:ncol], rc[:, 0:ncol], AF.Exp, [Rrc], [Rrc], scale=-1.0)
                                so, Rso, soch = so_r.next()
                                P.op('dve', lambda h, so=so, rc=rc, hd=hd, t0=t0, ncol=ncol: h.tensor_tensor(
                                    out=so[:, 0:ncol], in0=OT[0:64, hd, t0:t0 + ncol], in1=rc[:, 0:ncol], op=ALU.mult),
                                    reads=[ROT, Rfin, Rrc], writes=[Rso])
                                rr = hp * 128 + hd * 64
                                P.dma('sp', soch, lambda h, so=so, rr=rr, t0=t0, ncol=ncol: h.dma_start(
                                    out=C.AT[s][rr:rr + 64, t0:t0 + ncol], in_=so[:, 0:ncol]), reads=[Rso])
                prev = (Pv, RPt, tile)

    for wi, (s, hp, bi) in enumerate(work):
        do_work(wi, s, hp, bi)
    P.reset(m0)


def _ln_feature_major(P, C, tt, Rtt, nd, T, S1, S2, cst_ones, Rc, mk_out):
    mean = C.ln_mean
    m2 = C.ln_m2
    rstd = C.ln_rstd
    Rst = C.ln_Rst
    act(P, mean[:, 0:T], P.banks[S1][:, 0:T], AF.Identity, [P.Rbank[S1]], [Rst], scale=1.0 / D)
    P.op('dve', lambda h: h.tensor_tensor(out=m2[:, 0:T], in0=mean[:, 0:T], in1=mean[:, 0:T], op=ALU.mult),
         reads=[Rst], writes=[Rst])
    P.op('dve', lambda h: h.scalar_tensor_tensor(out=m2[:, 0:T], in0=P.banks[S2][:, 0:T], scalar=1.0 / D,
                                                 in1=m2[:, 0:T], op0=ALU.mult, op1=ALU.subtract),
         reads=[P.Rbank[S2], Rst], writes=[Rst])
    act(P, m2[:, 0:T], m2[:, 0:T], AF.Ln, [Rst], [Rst], bias=EPS)
    act(P, rstd[:, 0:T], m2[:, 0:T], AF.Exp, [Rst], [Rst], scale=-0.5)
    for dc in range(nd):
        u1, Ru1 = C.ln_u.next()
        P.op('dve', lambda h, u1=u1, dc=dc: h.tensor_tensor(out=u1[:, 0:T], in0=tt[:, dc, 0:T], in1=mean[:, 0:T], op=ALU.subtract),
             reads=[Rtt[dc], Rst], writes=[Ru1])
        P.op('dve', lambda h, u1=u1: h.tensor_tensor(out=u1[:, 0:T], in0=u1[:, 0:T], in1=rstd[:, 0:T], op=ALU.mult),
             reads=[Ru1, Rst], writes=[Ru1])
        mk_out(dc, u1, Ru1)


def phase_C1(C):
    P, L, NS = C.P, C.L, C.NS
    T = 512
    m0 = P.mark()
    P.bank_list = [0, 1, 2, 3, 4]
    SA, S1, S2 = 5, 6, 7
    wout = P.alloc([128, 8, D], BF16)
    cst = P.alloc([128, 6, 128], F32)
    ang = P.alloc([128, 4], F32)
    lnfm = P.alloc([128, 4, 8], F32)
    Rc = P.res("c1_const")
    chc = P.chan("c1_const")
    P.dma('pool', chc, lambda h: h.dma_start(out=wout, in_=C.w_out.rearrange("(kc p) c -> p kc c", p=128)), writes=[Rc])
    P.dma('sp', chc, lambda h: h.dma_start(out=cst, in_=C.cst), writes=[Rc])
    P.dma('sp', chc, lambda h: h.dma_start(out=ang, in_=C.ang), writes=[Rc])
    P.dma('sp', chc, lambda h: h.dma_start(out=lnfm, in_=C.lnfm), writes=[Rc])
    ONES = cst[:, 4, :]
    xt_r = Ring(P, "c1x", 3, [128, 8, T], F32, chan=True)
    at_r = Ring(P, "c1a", 2, [128, 4, T], F32, chan=True)
    yn_r = Ring(P, "c1y", 3, [128, 4, T], BF16, chan=True)
    an_r = Ring(P, "c1an", 2, [128, 4, T], BF16)
    sq_r = Ring(P, "c1sq", 3, [128, T], F32)
    sqx_r = Ring(P, "c1sqx", 2, [128, T], F32)
    hl_r = Ring(P, "c1hl", 3, [128, 4, T], BF16)
    ONESB = P.alloc([128, 128], BF16)
    P.op('dve', lambda h: h.tensor_copy(out=ONESB, in_=ONES), reads=[Rc], writes=[Rc])
    rsa_r = Ring(P, "c1rsa", 2, [128, T], F32)
    tts = [P.alloc([128, 8, T], F32) for _ in range(2)]
    Rtts = [[P.res(f"tt{k}_{i}") for i in range(8)] for k in range(2)]
    C.ln_mean = P.alloc([128, T], F32)
    C.ln_m2 = P.alloc([128, T], F32)
    C.ln_rstd = P.alloc([128, T], F32)
    C.ln_Rst = P.res("lnst")
    C.ln_u = Ring(P, "lnu", 3, [128, T], F32)
    hf_r = Ring(P, "c1hf", 1, [128, 8, T], F32, chan=True)
    hb_r = Ring(P, "c1hb", 1, [128, 8, T], BF16, chan=True)
    tiles = [(s, t0) for s in range(NS) for t0 in range(0, L, T)]

    def load(s, t0):
        xb, xr, xch = xt_r.next()
        P.dma('sp', xch, lambda h: h.dma_start(out=xb, in_=C.xT[s].rearrange("(c p) t -> p c t", p=128)[:, :, t0:t0 + T]), writes=[xr])
        ab, ar, ach = at_r.next()
        P.dma('sp', ach, lambda h: h.dma_start(out=ab, in_=C.AT[s].rearrange("(c p) t -> p c t", p=128)[:, :, t0:t0 + T]), writes=[ar])
        yb, yr, ych = yn_r.next()
        P.dma('sp', ych, lambda h: h.dma_start(out=yb, in_=C.YN[s].rearrange("(c p) t -> p c t", p=128)[:, :, t0:t0 + T]), writes=[yr])
        return (xb, xr), (ab, ar), (yb, yr)

    def stage_X(ld):
        (xb, xr), (ab, ar), (yb, yr) = ld
        for fc in range(4):
            sq, Rsq = sqx_r.next()
            act(P, sq, ab[:, fc, :], AF.Square, [ar], [Rsq])
            P.op('pe', lambda h, sq=sq, fc=fc: h.matmul(P.banks[SA][:, :], ONES, sq, start=(fc == 0), stop=(fc == 3)),
                 reads=[Rc, Rsq], writes=[P.Rbank[SA]])
        rsa, Rrsa = rsa_r.next()
        act(P, rsa, P.banks[SA][:, :], AF.Ln, [P.Rbank[SA]], [Rrsa], scale=1.0 / 512, bias=EPS)
        act(P, rsa, rsa, AF.Exp, [Rrsa], [Rrsa], scale=-0.5)
        an, Ran = an_r.next()
        for fc in range(4):
            P.op('dve', lambda h, fc=fc: h.scalar_tensor_tensor(
                out=an[:, fc, :], in0=ab[:, fc, :], scalar=ang[:, fc:fc + 1], in1=rsa, op0=ALU.mult, op1=ALU.mult),
                reads=[ar, Rc, Rrsa], writes=[Ran])
        return (xb, xr), (yb, yr), (an, Ran)

    def stage_YZ(ti, xs_, inject):
        s, t0 = tiles[ti]
        (xb, xr), (yb, yr), (an, Ran) = xs_
        tt, Rtt = tts[ti % 2], Rtts[ti % 2]
        pend = None
        nxt_x = None
        for dc in range(8):
            if dc == 4 and inject is not None:
                nxt_x = inject()
            b = P.next_bank()
            mm_group(P, P.banks[b][:, :],
                     [(wout[:, kc, dc * 128:(dc + 1) * 128], an[:, kc, :] if kc < 4 else yb[:, kc - 4, :]) for kc in range(8)],
                     P.Rbank[b], [Rc, Ran, yr])
            P.op('dve', lambda h, dc=dc, b=b: h.scalar_tensor_tensor(
                out=tt[:, dc, :], in0=xb[:, dc, :], scalar=ALPHA, in1=P.banks[b][:, :], op0=ALU.mult, op1=ALU.add),
                reads=[xr, P.Rbank[b]], writes=[Rtt[dc]])
            sq, Rsq = sq_r.next()
            act(P, sq, tt[:, dc, :], AF.Square, [Rtt[dc]], [Rsq])
            hl, Rhl = hl_r.next()
            act(P, hl[:, 0, :], tt[:, dc, :], AF.Identity, [Rtt[dc]], [Rhl])
            act(P, hl[:, 2, :], sq, AF.Identity, [Rsq], [Rhl])
            P.op('pool', lambda h, hl=hl, dc=dc: h.tensor_tensor(out=hl[:, 1, :], in0=tt[:, dc, :], in1=hl[:, 0, :], op=ALU.subtract),
                 reads=[Rtt[dc], Rhl], writes=[Rhl])
            P.op('pool', lambda h, hl=hl, sq=sq: h.tensor_tensor(out=hl[:, 3, :], in0=sq, in1=hl[:, 2, :], op=ALU.subtract),
                 reads=[Rsq, Rhl], writes=[Rhl])

            def stats(dc=dc, hl=hl, Rhl=Rhl):
                for k, bk in ((0, S1), (1, S1), (2, S2), (3, S2)):
                    P.op('pe', lambda h, k=k, bk=bk: h.matmul(P.banks[bk][:, :], ONESB, hl[:, k, :],
                                                              start=(dc == 0 and k % 2 == 0), stop=(dc == 7 and k % 2 == 1)),
                         reads=[Rc, Rhl], writes=[P.Rbank[bk]])
            if pend is not None:
                pend()
            pend = stats
        pend()
        hf, Rhf, hfch = hf_r.next()
        hb, Rhb, hbch = hb_r.next()

        def mk_out(dc, u1, Ru1):
            act(P, hf[:, dc, :], u1, AF.Identity, [Ru1, Rc], [Rhf], scale=lnfm[:, 0, dc:dc + 1], bias=lnfm[:, 1, dc:dc + 1])
            act(P, hb[:, dc, :], u1, AF.Identity, [Ru1, Rc], [Rhb], scale=lnfm[:, 0, dc:dc + 1], bias=lnfm[:, 1, dc:dc + 1])
        _ln_feature_major(P, C, tt, Rtt, 8, T, S1, S2, ONES, Rc, mk_out)
        P.dma('sp', hfch, lambda h: h.dma_start(
            out=C.H1F[s].rearrange("(c p) t -> p c t", p=128)[:, :, t0:t0 + T], in_=hf), reads=[Rhf])
        P.dma('sp', hbch, lambda h: h.dma_start(
            out=C.H1B[s].rearrange("(c p) t -> p c t", p=128)[:, :, t0:t0 + T], in_=hb), reads=[Rhb])
        return nxt_x

    ld = [None] * (len(tiles) + 2)
    ld[0] = load(*tiles[0])
    if len(tiles) > 1:
        ld[1] = load(*tiles[1])
    xs_next = stage_X(ld[0])
    for ti in range(len(tiles)):
        xs_cur = xs_next
        if ti + 2 < len(tiles):
            ld[ti + 2] = load(*tiles[ti + 2])
        inj = (lambda ti=ti: stage_X(ld[ti + 1])) if ti + 1 < len(tiles) else None
        xs_next = stage_YZ(ti, xs_cur, inj)
    P.bank_list = list(range(8))
    P.reset(m0)


def phase_C2(C):
    P, L, NS = C.P, C.L, C.NS
    T = 256
    NF = DFF // 128
    m0 = P.mark()
    P.bank_list = [0, 1, 2, 3, 4, 5]
    S1, S2 = 6, 7
    wg = P.alloc([128, 8, DFF], BF16)
    wu = P.alloc([128, 8, DFF], BF16)
    wd = P.alloc([128, NF, D], BF16)
    cst = P.alloc([128, 6, 128], F32)
    lnfm = P.alloc([128, 4, 8], F32)
    Rc = P.res("c2_const")
    chc = P.chan("c2_const")
    chw = [P.chan(f"c2w{i}") for i in range(3)]
    for c0 in (0, 1408):
        P.dma('pool', chw[0], lambda h, c0=c0: h.dma_start(
            out=wg[:, :, c0:c0 + 1408], in_=C.w_gate.rearrange("(kc p) c -> p kc c", p=128)[:, :, c0:c0 + 1408]), writes=[Rc])
        P.dma('pool', chw[1], lambda h, c0=c0: h.dma_start(
            out=wu[:, :, c0:c0 + 1408], in_=C.w_up.rearrange("(kc p) c -> p kc c", p=128)[:, :, c0:c0 + 1408]), writes=[Rc])
    P.dma('pool', chw[2], lambda h: h.dma_start(out=wd, in_=C.w_down.rearrange("(kc p) c -> p kc c", p=128)), writes=[Rc])
    P.dma('sp', chc, lambda h: h.dma_start(out=cst, in_=C.cst), writes=[Rc])
    P.dma('sp', chc, lambda h: h.dma_start(out=lnfm, in_=C.lnfm), writes=[Rc])
    ONES = cst[:, 4, :]
    hb_r = Ring(P, "c2hb", 2, [128, 8, T], BF16, chan=True)
    hf_r = Ring(P, "c2hf", 4, [128, T], F32, chan=True)
    hid = P.alloc([128, NF, T], BF16)
    Rhid = P.res("hid")
    sg_r = Ring(P, "c2sg", 3, [128, T], F32)
    sq_r = Ring(P, "c2sq", 4, [128, T], F32)
    hl_r = Ring(P, "c2hl", 3, [128, 4, T], BF16)
    ONESB = P.alloc([128, 128], BF16)
    P.op('dve', lambda h: h.tensor_copy(out=ONESB, in_=ONES), reads=[Rc], writes=[Rc])
    tt = P.alloc([128, 8, T], F32)
    Rtt = [P.res(f"tt2_{i}") for i in range(8)]
    C.ln_mean = P.alloc([128, T], F32)
    C.ln_m2 = P.alloc([128, T], F32)
    C.ln_rstd = P.alloc([128, T], F32)
    C.ln_Rst = P.res("lnst2")
    C.ln_u = Ring(P, "lnu2", 3, [128, T], F32)
    yo_r = Ring(P, "c2yo", 4, [128, T], F32, chan=True)
    outs = []

    def load(s, t0):
        hb, hr, hch = hb_r.next()
        P.dma('sp', hch, lambda h: h.dma_start(out=hb, in_=C.H1B[s].rearrange("(c p) t -> p c t", p=128)[:, :, t0:t0 + T]), writes=[hr])
        return hb, hr

    tiles = [(s, t0) for s in range(NS) for t0 in range(0, L, T)]
    nxt = load(*tiles[0])
    for ti, (s, t0) in enumerate(tiles):
        hb, hr = nxt
        if ti + 1 < len(tiles):
            nxt = load(*tiles[ti + 1])
        for fc in range(NF):
            bg = P.next_bank()
            mm_group(P, P.banks[bg][:, 0:T], [(wg[:, kc, fc * 128:(fc + 1) * 128], hb[:, kc, :]) for kc in range(8)],
                     P.Rbank[bg], [Rc, hr])
            bu = P.next_bank()
            mm_group(P, P.banks[bu][:, 0:T], [(wu[:, kc, fc * 128:(fc + 1) * 128], hb[:, kc, :]) for kc in range(8)],
                     P.Rbank[bu], [Rc, hr])
            sg, Rsg = sg_r.next()
            act(P, sg, P.banks[bg][:, 0:T], AF.Silu, [P.Rbank[bg]], [Rsg])
            P.op('dve', lambda h, fc=fc, sg=sg, bu=bu: h.tensor_tensor(out=hid[:, fc, :], in0=sg, in1=P.banks[bu][:, 0:T], op=ALU.mult),
                 reads=[Rsg, P.Rbank[bu]], writes=[Rhid])
        pend = None
        for dc in range(8):
            hf, Rhf, hfch = hf_r.next()
            P.dma('sp', hfch, lambda h, hf=hf, s=s, t0=t0, dc=dc: h.dma_start(
                out=hf, in_=C.H1F[s][dc * 128:(dc + 1) * 128, t0:t0 + T]), writes=[Rhf])
            b = P.next_bank()
            mm_group(P, P.banks[b][:, 0:T], [(wd[:, fc, dc * 128:(dc + 1) * 128], hid[:, fc, :]) for fc in range(NF)],
                     P.Rbank[b], [Rc, Rhid])
            P.op('dve', lambda h, dc=dc, b=b, hf=hf: h.scalar_tensor_tensor(
                out=tt[:, dc, :], in0=hf, scalar=ALPHA, in1=P.banks[b][:, 0:T], op0=ALU.mult, op1=ALU.add),
                reads=[Rhf, P.Rbank[b]], writes=[Rtt[dc]])
            sq, Rsq = sq_r.next()
            act(P, sq, tt[:, dc, :], AF.Square, [Rtt[dc]], [Rsq])
            hl, Rhl = hl_r.next()
            act(P, hl[:, 0, :], tt[:, dc, :], AF.Identity, [Rtt[dc]], [Rhl])
            act(P, hl[:, 2, :], sq, AF.Identity, [Rsq], [Rhl])
            P.op('pool', lambda h, hl=hl, dc=dc: h.tensor_tensor(out=hl[:, 1, :], in0=tt[:, dc, :], in1=hl[:, 0, :], op=ALU.subtract),
                 reads=[Rtt[dc], Rhl], writes=[Rhl])
            P.op('pool', lambda h, hl=hl, sq=sq: h.tensor_tensor(out=hl[:, 3, :], in0=sq, in1=hl[:, 2, :], op=ALU.subtract),
                 reads=[Rsq, Rhl], writes=[Rhl])

            def stats(dc=dc, hl=hl, Rhl=Rhl):
                for k, bk in ((0, S1), (1, S1), (2, S2), (3, S2)):
                    P.op('pe', lambda h, k=k, bk=bk: h.matmul(P.banks[bk][:, 0:T], ONESB, hl[:, k, :],
                                                              start=(dc == 0 and k % 2 == 0), stop=(dc == 7 and k % 2 == 1)),
                         reads=[Rc, Rhl], writes=[P.Rbank[bk]])
            if pend is not None:
                pend()
            pend = stats
        pend()
        pend = None

        def mk_out(dc, u1, Ru1, s=s, t0=t0):
            yo, Ryo, yoch = yo_r.next()
            act(P, yo, u1[:, 0:T], AF.Identity, [Ru1, Rc], [Ryo], scale=lnfm[:, 2, dc:dc + 1], bias=lnfm[:, 3, dc:dc + 1])
            outs.append(P.dma('sp', yoch, lambda h, yo=yo, dc=dc: h.dma_start(
                out=C.yT[s][dc * 128:(dc + 1) * 128, t0:t0 + T], in_=yo), reads=[Ryo]))
        _ln_feature_major(P, C, tt, Rtt, 8, T, S1, S2, ONES, Rc, mk_out)
    P.bank_list = list(range(8))
    P.reset(m0)
    return outs

def to_xT16(xT):
    sh = xT.shape
    L = sh[-1]
    return np.ascontiguousarray(xT.reshape(sh[:-1] + (L // 16, 16)).swapaxes(-1, -2).reshape(sh))


def make_cst():
    i = np.arange(128)
    U = (i[:, None] <= i[None, :]).astype(np.float32)
    SL = (i[:, None] > i[None, :]).astype(np.float32)
    Lo = (i[:, None] >= i[None, :]).astype(np.float32)
    SU = (i[:, None] < i[None, :]).astype(np.float32)
    ones = np.ones((128, 128), np.float32)
    ident = np.eye(128, dtype=np.float32)
    return np.ascontiguousarray(np.stack([U, SL, Lo, SU, ones, ident], axis=1))


def shared_inputs(inp):
    f = np.float32
    g = lambda k: np.asarray(inp[k], dtype=f)
    bc = lambda v, n=128: np.ascontiguousarray(np.broadcast_to(v[None, :], (n, v.shape[0])))
    m = {}
    m["w_in"] = np.ascontiguousarray(g("w_in")[0])
    m["convw"] = np.ascontiguousarray(g("conv_w")[0].reshape(5, 8, 128).transpose(2, 1, 0))
    m["convb"] = np.ascontiguousarray(g("conv_b")[0].reshape(8, 128).T)
    m["dtb"] = bc(np.concatenate([g("dt_bias_fwd")[0], g("dt_bias_bwd")[0]]))
    m["alog"] = bc(np.concatenate([g("a_log_fwd")[0], g("a_log_bwd")[0]]))
    m["dsk"] = bc(g("d_skip")[0])
    m["ang"] = np.ascontiguousarray(g("attn_norm_g")[0].reshape(4, 128).T)
    m["sng"] = bc(g("ssd_norm_g")[0])
    m["relb"] = np.ascontiguousarray(g("rel_bias"))
    m["lnfm"] = np.ascontiguousarray(np.stack([g(k)[0].reshape(8, 128).T for k in ("ln1_g", "ln1_b", "ln2_g", "ln2_b")], axis=1))
    m["w_out"] = np.ascontiguousarray(g("w_out")[0])
    m["w_gate"] = np.ascontiguousarray(g("w_gate")[0])
    m["w_up"] = np.ascontiguousarray(g("w_up")[0])
    m["w_down"] = np.ascontiguousarray(g("w_down")[0])
    m["cst"] = make_cst()
    m["oh"], m["jmat"], m["sel"] = make_att_consts()
    cst_ = m["cst"]
    m["negm"] = np.ascontiguousarray(np.stack([(cst_[:, 0, :] - 1.0) * 30000.0, (cst_[:, 2, :] - 1.0) * 30000.0], axis=1))
    return m


def t5_bucket(rel):
    half = 16
    max_exact = 8
    ret = (rel > 0).astype(np.int32) * half
    n = np.abs(rel)
    large = max_exact + (np.log(np.maximum(n, 1) / max_exact)
                         / math.log(1024 / max_exact) * (half - max_exact)).astype(np.int32)
    large = np.minimum(large, half - 1)
    return ret + np.where(n < max_exact, n, large)


def make_att_consts():
    oh = np.zeros((32, 6, 256), np.float32)
    i = np.arange(255)
    for bi, dl in enumerate(BRANCH_DIL):
        for ty in range(2):
            rel = (i - 63) if ty == 0 else (i - 191)
            bk = t5_bucket(rel * dl)
            oh[bk, bi * 2 + ty, i] = 1.0
    eb = np.ascontiguousarray(np.eye(128, dtype=np.float32)[::-1])
    sel = np.zeros((65, 64), np.float32)
    sel[64, :] = 1.0
    return oh, eb, sel


SEQ_LEN = 8192
N_CORES = 8
SLOTS = 2
_CACHE = {}


def kernel(**inputs):
    xp = np.asarray(inputs["x_prompt"], dtype=np.float32)
    xs = np.asarray(inputs["x_sample"], dtype=np.float32)
    seqs = [xp[i] for i in range(xp.shape[0])] + [xs[i] for i in range(xs.shape[0])]
    nseq = len(seqs)
    L = seqs[0].shape[0]
    shared = shared_inputs(inputs)
    in_maps = []
    for c in range(N_CORES):
        xT = np.zeros((SLOTS, D, L), np.float32)
        for sl in range(SLOTS):
            i = c * SLOTS + sl
            if i < nseq:
                xT[sl] = seqs[i].T
        m = dict(shared)
        m["xT"] = xT
        m["xT16"] = to_xT16(xT)
        in_maps.append(m)
    key = (L, SLOTS)
    if key not in _CACHE:
        _CACHE[key] = build(L, SLOTS)
    nc, _ = _CACHE[key]
    res = run_bass_kernel_spmd(nc, in_maps, core_ids=list(range(N_CORES)))
    outs = []
    for i in range(nseq):
        c, sl = divmod(i, SLOTS)
        outs.append(np.ascontiguousarray(np.asarray(res.results[c]["yT"][sl]).T))
    y_prompt = np.stack(outs[:xp.shape[0]]).astype(np.float32)
    y_sample = np.stack(outs[xp.shape[0]:]).astype(np.float32)
    return (y_prompt, y_sample)
```

```python
import contextlib
import math
import numpy as np
import concourse.bass as bass
import concourse.mybir as mybir
from concourse.bass_utils import run_bass_kernel_spmd

F32 = mybir.dt.float32
BF16 = mybir.dt.bfloat16
U8 = mybir.dt.uint8
AF = mybir.ActivationFunctionType
ALU = mybir.AluOpType

D = 1024
DIN = 3088
DFF = 2816
NH = 8
COL_Q, COL_K, COL_V, COL_Z, COL_X, COL_DT = 0, 512, 1024, 1536, 2048, 3072
ALPHA = 2.0 ** 0.25
EPS = 1e-5
ENGS = ['pe', 'act', 'dve', 'pool', 'sp']
EPOCH = 20000
POOL_BYTES = 206 * 1024


def _dsize(dt):
    return {F32: 4, BF16: 2, U8: 1}[dt]


class Res:
    __slots__ = ('name', 'w', 'r')

    def __init__(self, name):
        self.name = name
        self.w = {}
        self.r = {}


class Chan:
    def __init__(self, name):
        self.name = name
        self.n = 0
        self.sem = None
        self.last = None


class Prog:
    def __init__(self, nc):
        self.nc = nc
        self.streams = {e: [] for e in ENGS}
        self.last_op = {e: None for e in ENGS}
        self.chans = []
        self.stack = contextlib.ExitStack()
        self.nres = 0
        self.nseq = 0
        self.pool = self.stack.enter_context(nc.sbuf_tensor("pool", [128, POOL_BYTES], U8))
        self.off = 0
        self.peak = 0
        self.banks = [self.stack.enter_context(nc.psum_tensor(f"bank{i}", [128, 512], F32))
                      for i in range(8)]
        self.Rbank = [self.res(f"bank{i}") for i in range(8)]

    def res(self, name=None):
        self.nres += 1
        return Res(name or f"r{self.nres}")

    def chan(self, name):
        c = Chan(name)
        self.chans.append(c)
        return c

    def alloc(self, shape, dtype):
        n = 1
        for s in shape[1:]:
            n *= s
        nbytes = n * _dsize(dtype)
        self.off = (self.off + 63) // 64 * 64
        assert self.off + nbytes <= POOL_BYTES, f"SBUF overflow {self.off + nbytes}"
        ap = self.pool[0:shape[0], self.off:self.off + nbytes].bitcast(dtype)
        self.off += nbytes
        self.peak = max(self.peak, self.off)
        if len(shape) > 2:
            names = [f"d{i}" for i in range(len(shape) - 1)]
            pat = "p (" + " ".join(names) + ") -> p " + " ".join(names)
            ap = ap.rearrange(pat, **{names[i]: shape[i + 1] for i in range(len(names))})
        return ap

    def mark(self):
        return self.off

    bank_list = list(range(8))

    def next_bank(self):
        self.bank_i = getattr(self, 'bank_i', -1) + 1
        return self.bank_list[self.bank_i % len(self.bank_list)]

    def reset(self, m):
        self.off = m

    def _deps(self, reads, writes):
        deps = {}
        for r in reads:
            for v in r.w.values():
                deps[id(v)] = v
        for w in writes:
            for v in w.w.values():
                deps[id(v)] = v
            for v in w.r.values():
                deps[id(v)] = v
        return list(deps.values())

    def op(self, eng, fn, reads=(), writes=()):
        ins = dict(fn=fn, deps=self._deps(reads, writes), signal=False, chan=None, eng=eng, seq=self.nseq)
        self.nseq += 1
        self.streams[eng].append(ins)
        self.last_op[eng] = ins
        for r in reads:
            r.r[id(ins)] = ins
        for w in writes:
            w.w = {eng: ins}
            w.r = {}
        return ins

    def dma(self, q, chan, fn, reads=(), writes=()):
        deps = self._deps(reads, writes)
        if chan.n > 0:
            deps.append(chan.last)
        ins = dict(fn=fn, deps=deps, signal=True, chan=chan, eng=q, seq=self.nseq, n=chan.n)
        self.nseq += 1
        self.streams[q].append(ins)
        chan.n += 1
        chan.last = ins
        for r in reads:
            r.r[id(ins)] = ins
        for w in writes:
            w.w = {chan: ins}
            w.r = {}
        return ins

    def barrier(self):
        evs = []
        for e in ENGS:
            if self.last_op[e] is not None:
                evs.append(self.last_op[e])
        for c in self.chans:
            if c.n > 0:
                evs.append(c.last)
        for e in ENGS:
            self.streams[e].append(dict(fn=None, deps=[d for d in evs if not (d['chan'] is None and d['eng'] == e)],
                                        signal=False, chan=None, eng=e, seq=self.nseq, barrier=True))
        self.nseq += 1

    def wait_all(self, eng, evs):
        self.streams[eng].append(dict(fn=None, deps=list(evs), signal=False, chan=None, eng=eng, seq=self.nseq,
                                      barrier=True))
        self.nseq += 1

    def _cost(self, ins):
        rec = _Fake()
        try:
            ins['fn'](rec)
        except Exception:
            return 300.0
        name, args, kw = rec.call
        try:
            if name == 'dma_start':
                o = kw.get('out')
                n = 1
                for d in o.shape:
                    n *= d
                ins['dma_ns'] = 2000.0 + n * _dsize(o.dtype) / 150.0
                return 60.0
            if name in ('matmul', 'transpose'):
                rhs = kw.get('rhs', args[2] if len(args) > 2 else None)
                n = 1
                for d in rhs.shape[1:]:
                    n *= d
                st = 1
                try:
                    st = max(1, abs(rhs.ap[-1][0]))
                except Exception:
                    pass
                c = max(64, n) / 2.4
                if rhs.dtype == F32 and name == 'matmul':
                    c *= 4
                elif st > 1:
                    c *= min(8, st) / 2.0 + 0.5
                return c + 8
            o = kw.get('out', args[0] if args else None)
            n = 1
            for d in o.shape[1:]:
                n *= d
            e = ins['eng']
            if e == 'act':
                return 70 + n * 0.96 + (90 if 'accum_out' in kw else 0)
            if e == 'dve':
                return 65 + n * 1.04
            return 110 + n * (0.6 if name == 'memset' else 2.3)
        except Exception:
            return 300.0

    def schedule(self, window=32):
        import os
        LAT = float(os.environ.get('SCHED_LAT', '400'))
        new_streams = {e: [] for e in ENGS}
        pos = {e: 0 for e in ENGS}
        free = {e: 0.0 for e in ENGS}
        tnow = 0.0
        while any(pos[e] < len(self.streams[e]) for e in ENGS):
            seg = {}
            for e in ENGS:
                st = self.streams[e]
                i = pos[e]
                j = i
                while j < len(st) and not st[j].get('barrier'):
                    j += 1
                seg[e] = st[i:j]
                bar = st[j] if j < len(st) else None
                pos[e] = j + 1 if j < len(st) else j
                seg[e + '_bar'] = bar
            for e in ENGS:
                for ins in seg[e]:
                    ins['cost'] = self._cost(ins)
                    ins['fin'] = None
                free[e] = tnow
            pend = {e: list(seg[e]) for e in ENGS}
            nleft = sum(len(v) for v in pend.values())
            while nleft:
                progressed = False
                for e in sorted(ENGS, key=lambda x: free[x]):
                    pl = pend[e]
                    if not pl:
                        continue
                    best = None
                    bstart = None
                    for k in range(min(window if e != 'sp' else 6, len(pl))):
                        ins = pl[k]
                        rd = ins.get('ready')
                        if rd is None:
                            rd = 0.0
                            ok = True
                            for d in ins['deps']:
                                f = d.get('fin', 0.0)
                                if f is None:
                                    ok = False
                                    break
                                if d['chan'] is not None:
                                    f = d.get('dfin', f)
                                if d['eng'] == e and d['chan'] is None and e == 'pe':
                                    f = f - d['cost'] * 0.5
                                else:
                                    f = f + LAT
                                if f > rd:
                                    rd = f
                            if not ok:
                                continue
                            ins['ready'] = rd
                        stt = rd if rd > free[e] else free[e]
                        if bstart is None or stt < bstart - 1e-9:
                            best, bstart = k, stt
                            if stt <= free[e]:
                                break
                    if best is None:
                        continue
                    ins = pl.pop(best)
                    ins['fin'] = bstart + ins['cost']
                    if ins['chan'] is not None:
                        ins['dfin'] = bstart + ins.get('dma_ns', 2000.0)
                    free[e] = ins['fin']
                    new_streams[e].append(ins)
                    nleft -= 1
                    progressed = True
                    break
                assert progressed, "scheduler deadlock"
            tnow = max(free.values())
            for e in ENGS:
                for ins in seg[e]:
                    if ins['chan'] is not None and ins.get('dfin', 0) > tnow:
                        tnow = ins['dfin']
            for e in ENGS:
                if seg[e + '_bar'] is not None:
                    seg[e + '_bar']['fin'] = tnow
                    new_streams[e].append(seg[e + '_bar'])
        self.streams = new_streams
        self.est_ns = tnow

    def emit(self, sched=True):
        nc = self.nc
        import os
        if sched and os.environ.get('NOSCHED') != '1':
            self.schedule()
        for e in ENGS:
            for ins in self.streams[e]:
                for d in ins['deps']:
                    if d['chan'] is None:
                        if d['eng'] == 'pe' and e == 'pe' and ins['chan'] is None and ins['fn'] is not None:
                            continue
                        d['signal'] = True
        nsig = {}
        for e in ENGS:
            c = 0
            for ins in self.streams[e]:
                if ins['chan'] is None and ins['signal'] and ins['fn'] is not None:
                    c += 1
                    ins['cnt'] = c
            nsig[e] = c
        sems = {}
        for e in ENGS:
            ne = max(1, -(-nsig[e] // EPOCH))
            sems[e] = [self.stack.enter_context(nc.semaphore(f"s_{e}{i}")) for i in range(ne)]
        for c in self.chans:
            c.sem = self.stack.enter_context(nc.semaphore(f"c_{c.name}"))
        self.stats = {e: [len(self.streams[e]), nsig[e], 0] for e in ENGS}

        def emit_stream(e, h):
            waited = {}
            nw = 0
            for ins in self.streams[e]:
                need = {}
                for d in ins['deps']:
                    if d['chan'] is None:
                        if d['eng'] == 'pe' and e == 'pe' and ins['chan'] is None and ins['fn'] is not None:
                            continue
                        cnt = d['cnt']
                        ep = (cnt - 1) // EPOCH
                        sem = sems[d['eng']][ep]
                        val = cnt - ep * EPOCH
                    else:
                        sem = d['chan'].sem
                        val = 16 * (d['n'] + 1)
                    k = id(sem)
                    if waited.get(k, 0) >= val:
                        continue
                    if k not in need or need[k][1] < val:
                        need[k] = (sem, val)
                for k, (sem, val) in need.items():
                    h.wait_ge(sem, val)
                    waited[k] = val
                    nw += 1
                if ins['fn'] is None:
                    continue
                r = ins['fn'](h)
                if ins['chan'] is not None:
                    r.then_inc(ins['chan'].sem, 16)
                elif ins['signal']:
                    cnt = ins['cnt']
                    ep = (cnt - 1) // EPOCH
                    r.then_inc(sems[e][ep], 1)
            self.stats[e][2] = nw

        with nc.Block() as block:
            @block.tensor
            def _(h):
                emit_stream('pe', h)

            @block.scalar
            def _(h):
                emit_stream('act', h)

            @block.vector
            def _(h):
                emit_stream('dve', h)

            @block.gpsimd
            def _(h):
                emit_stream('pool', h)

            @block.sync
            def _(h):
                emit_stream('sp', h)
        self.stack.close()


class _Fake:
    def __init__(self):
        self.call = None

    def __getattr__(self, name):
        def f(*a, **k):
            self.call = (name, a, k)
            return self
        return f


def mm_group(P, out_ap, pairs, Rout, reads):
    n = len(pairs)
    for i, (l, r) in enumerate(pairs):
        P.op('pe', lambda h, l=l, r=r, i=i: h.matmul(out_ap, l, r, start=(i == 0), stop=(i == n - 1)),
             reads=reads, writes=[Rout])


def act(P, out, in_, func, reads, writes, scale=None, bias=None):
    kw = {}
    if scale is not None:
        kw['scale'] = scale
    if bias is not None:
        kw['bias'] = bias
    return P.op('act', lambda h: h.activation(out=out, in_=in_, func=func, **kw), reads=reads, writes=writes)


class Ring:
    def __init__(self, P, name, n, shape, dtype, chan=False):
        self.bufs = [P.alloc(shape, dtype) for _ in range(n)]
        self.res = [P.res(f"{name}{i}") for i in range(n)]
        self.ch = [P.chan(f"{name}{i}") for i in range(n)] if chan else None
        self.i = -1
        self.n = n

    def next(self):
        self.i = (self.i + 1) % self.n
        if self.ch:
            return self.bufs[self.i], self.res[self.i], self.ch[self.i]
        return self.bufs[self.i], self.res[self.i]


class Ctx:
    pass


def build(L, NS, debug=False, phases=('A', 'S', 'T', 'C')):
    nc = bass.Bass("TRN2", target_bir_lowering=False)
    P = Prog(nc)
    C = Ctx()
    C.L, C.NS, C.P, C.nc = L, NS, P, nc
    NT = L // 512

    def din(name, shape, dt=F32):
        return nc.dram_tensor(name, list(shape), dt, kind="ExternalInput").ap()

    def dscr(name, shape, dt):
        return nc.dram_tensor(name, list(shape), dt,
                              kind="ExternalOutput" if debug else "Internal").ap()

    C.xT = din("xT", [NS, D, L])
    C.xT16 = din("xT16", [NS, D, L])
    C.w_in = din("w_in", [D, DIN])
    C.convw = din("convw", [128, 8, 5])
    C.convb = din("convb", [128, 8])
    C.dtb = din("dtb", [128, 16])
    C.alog = din("alog", [128, 16])
    C.dsk = din("dsk", [128, 8])
    C.ang = din("ang", [128, 4])
    C.sng = din("sng", [128, 512])
    C.relb = din("relb", [32, 8])
    C.w_out = din("w_out", [D, D])
    C.w_gate = din("w_gate", [D, DFF])
    C.w_up = din("w_up", [D, DFF])
    C.w_down = din("w_down", [DFF, D])
    C.cst = din("cst", [128, 6, 128])
    C.oh = din("oh", [32, 6, 256])
    C.jmat = din("jmat", [128, 128])
    C.sel = din("sel", [65, 64])
    C.negm = din("negm", [128, 2, 128])
    C.GV = dscr("GV", [6, 8, 256], F32)
    C.lnfm = din("lnfm", [128, 4, 8])
    C.yT = nc.dram_tensor("yT", [NS, D, L], F32, kind="ExternalOutput").ap()
    C.H1F = dscr("H1F", [NS, D, L], F32)
    C.H1B = dscr("H1B", [NS, D, L], BF16)

    C.QT = dscr("QT", [NS, 512, L], BF16)
    C.KT = dscr("KT", [NS, 512, L], BF16)
    C.V = dscr("V", [NS, L, 8 * 65], BF16)
    C.Z = dscr("Z", [NS, L, 512], F32)
    C.XS = dscr("XS", [NS, 512, L], F32)
    C.BC = dscr("BC", [NS, 512, L], BF16)
    C.DT = dscr("DT", [NS, L, 16], F32)

    outs = []
    C.YN = dscr("YN", [NS, 512, L], BF16)
    C.AT = dscr("AT", [NS, 512, L], F32)
    if 'A' in phases:
        phase_A(C)
        P.barrier()
    if 'S' in phases:
        phase_S(C)
        P.barrier()
    if 'T' in phases:
        phase_T(C)
        P.barrier()
    if 'C' in phases:
        phase_C1(C)
        P.barrier()
        outs = phase_C2(C)
        P.barrier()
    if debug:
        evs = [c.last for c in P.chans if c.n > 0]
        P.wait_all('sp', evs)
    else:
        P.wait_all('sp', outs)
    P.emit()
    return nc, P


def phase_A(C):
    P, L, NS = C.P, C.L, C.NS
    NT = L // 512
    m0 = P.mark()
    win = P.alloc([128, 8, DIN], BF16)
    Rwin = P.res("win")
    cw = P.alloc([128, 8, 5], F32)
    cb = P.alloc([128, 8], F32)
    dtb = P.alloc([128, 16], F32)
    Rsm = P.res("small")
    ch_w = P.chan("w")
    w_v = C.w_in.rearrange("(kc p) c -> p kc c", p=128)
    for c0 in (0, 1544):
        P.dma('pool', ch_w, lambda h, c0=c0: h.dma_start(out=win[:, :, c0:c0 + 1544], in_=w_v[:, :, c0:c0 + 1544]),
              writes=[Rwin])
    ch_s = P.chan("small")
    P.dma('sp', ch_s, lambda h: h.dma_start(out=cw, in_=C.convw), writes=[Rsm])
    P.dma('sp', ch_s, lambda h: h.dma_start(out=cb, in_=C.convb), writes=[Rsm])
    P.dma('sp', ch_s, lambda h: h.dma_start(out=dtb, in_=C.dtb), writes=[Rsm])

    xtb = Ring(P, "xtb", 2, [128, 8, 512], BF16, chan=True)
    xtb16 = Ring(P, "xtb16", 2, [128, 8, 512], BF16, chan=True)
    stq = Ring(P, "stq", 2, [128, 4, 512], BF16, chan=True)
    stk = Ring(P, "stk", 2, [128, 4, 512], BF16, chan=True)
    stx = Ring(P, "stx", 2, [128, 4, 512], F32, chan=True)
    stbc = Ring(P, "stbc", 2, [128, 4, 512], BF16, chan=True)
    stv = Ring(P, "stv", 2, [128, 4, 8, 65], BF16, chan=True)
    stz = Ring(P, "stz", 2, [128, 4, 512], F32, chan=True)
    stdt = Ring(P, "stdt", 2, [128, 4, 16], F32, chan=True)
    raw = [P.alloc([128, 520], F32) for _ in range(8)]
    Rraw = [P.res(f"raw{c}") for c in range(8)]
    acc = Ring(P, "acc", 4, [128, 512], F32)
    dtt = Ring(P, "dtt", 2, [128, 16], F32)
    for b, r in zip(stv.bufs, stv.res):
        P.op('pool', lambda h, b=b: h.memset(b, 1.0), writes=[r])
    fm_banks = [0, 1, 2]
    tm_banks = [3, 4, 5, 6]
    dt_bank = 7
    fmi = [0]
    tmi = [0]

    def next_fm():
        b = fm_banks[fmi[0] % len(fm_banks)]
        fmi[0] += 1
        return b

    def next_tm():
        b = tm_banks[tmi[0] % len(tm_banks)]
        tmi[0] += 1
        return b

    def load_x(s, T):
        buf, r, ch = xtb.next()
        src = C.xT[s].rearrange("(kc p) t -> p kc t", p=128)[:, :, T * 512:(T + 1) * 512]
        P.dma('pool', ch, lambda h: h.dma_start(out=buf, in_=src), writes=[r])
        buf2, r2, ch2 = xtb16.next()
        src2 = C.xT16[s].rearrange("(kc p) t -> p kc t", p=128)[:, :, T * 512:(T + 1) * 512]
        P.dma('pool', ch2, lambda h: h.dma_start(out=buf2, in_=src2), writes=[r2])
        return buf, r, buf2, r2

    def conv_chunk_ops(c, width, accb, Racc):
        ops = []
        rb = raw[c]
        ops.append(lambda: P.op('dve', lambda h: h.tensor_scalar(
            out=accb[:, 0:width], in0=rb[:, 0:width], scalar1=cw[:, c, 0:1], scalar2=cb[:, c:c + 1],
            op0=ALU.mult, op1=ALU.add), reads=[Rraw[c], Rsm], writes=[Racc]))
        for j in range(1, 5):
            ops.append(lambda j=j: P.op('dve', lambda h: h.scalar_tensor_tensor(
                out=accb[:, 0:width], in0=rb[:, j:j + width], scalar=cw[:, c, j:j + 1], in1=accb[:, 0:width],
                op0=ALU.mult, op1=ALU.add), reads=[Rraw[c], Rsm, Racc], writes=[Racc]))
        return ops

    def conv_finish(s, c, width, accb, Racc, xbuf, xr, bcbuf, bcr):
        if c < 4:
            act(P, xbuf[:, c, 0:width], accb[:, 0:width], AF.Silu, [Racc], [xr])
        else:
            act(P, bcbuf[:, c - 4, 0:width], accb[:, 0:width], AF.Silu, [Racc], [bcr])

    for s in range(NS):
        for c in range(8):
            P.op('pool', lambda h, c=c: h.memset(raw[c][:, 0:4], 0.0), writes=[Rraw[c]])
        nxt = load_x(s, 0)
        for T in range(NT):
            xb, xr_, xb16, xr16 = nxt
            if T + 1 < NT:
                nxt = load_x(s, T + 1)
            t0 = T * 512
            qb, qr, qch = stq.next()
            kb, kr, kch = stk.next()
            for c in range(4):
                b = next_fm()
                mm_group(P, P.banks[b][:, :], [(win[:, kc, COL_Q + c * 128:COL_Q + (c + 1) * 128], xb16[:, kc, :])
                                               for kc in range(8)], P.Rbank[b], [Rwin, xr16])
                act(P, qb[:, c, :], P.banks[b][:, :], AF.Identity, [P.Rbank[b]], [qr], scale=0.125)
            P.dma('sp', qch, lambda h, qb=qb, s=s, t0=t0: h.dma_start(
                out=C.QT[s].rearrange("(c p) t -> p c t", p=128)[:, :, t0:t0 + 512], in_=qb), reads=[qr])
            for c in range(4):
                b = next_fm()
                mm_group(P, P.banks[b][:, :], [(win[:, kc, COL_K + c * 128:COL_K + (c + 1) * 128], xb[:, kc, :])
                                               for kc in range(8)], P.Rbank[b], [Rwin, xr_])
                P.op('dve', lambda h, b=b, c=c, kb=kb: h.tensor_copy(out=kb[:, c, :], in_=P.banks[b][:, :]),
                     reads=[P.Rbank[b]], writes=[kr])
            P.dma('sp', kch, lambda h, kb=kb, s=s, t0=t0: h.dma_start(
                out=C.KT[s].rearrange("(c p) t -> p c t", p=128)[:, :, t0:t0 + 512], in_=kb), reads=[kr])
            sxb, sxr, sxch = stx.next()
            sbb, sbr, sbch = stbc.next()
            for cp in range(4):
                chains = []
                accs = []
                for c in (2 * cp, 2 * cp + 1):
                    b = next_fm()
                    mm_group(P, P.banks[b][:, :],
                             [(win[:, kc, COL_X + c * 128:COL_X + (c + 1) * 128], xb[:, kc, :]) for kc in range(8)],
                             P.Rbank[b], [Rwin, xr_])
                    act(P, raw[c][:, 4:516], P.banks[b][:, :], AF.Identity, [P.Rbank[b]], [Rraw[c]])
                    ab, ar = acc.next()
                    accs.append((c, ab, ar))
                    chains.append(conv_chunk_ops(c, 512, ab, ar))
                for j in range(5):
                    for ch_ in chains:
                        ch_[j]()
                for (c, ab, ar) in accs:
                    conv_finish(s, c, 512, ab, ar, sxb, sxr, sbb, sbr)
                    P.op('pool', lambda h, c=c: h.tensor_copy(out=raw[c][:, 0:4], in_=raw[c][:, 512:516]),
                         reads=[Rraw[c]], writes=[Rraw[c]])
            lo = 2 if T == 0 else 0
            P.dma('sp', sxch, lambda h, sxb=sxb, s=s, t0=t0, lo=lo: h.dma_start(
                out=C.XS[s].rearrange("(c p) t -> p c t", p=128)[:, :, t0 - 2 + lo:t0 + 510],
                in_=sxb[:, :, lo:512]), reads=[sxr])
            P.dma('sp', sbch, lambda h, sbb=sbb, s=s, t0=t0, lo=lo: h.dma_start(
                out=C.BC[s].rearrange("(c p) t -> p c t", p=128)[:, :, t0 - 2 + lo:t0 + 510],
                in_=sbb[:, :, lo:512]), reads=[sbr])
            vb, vr, vch = stv.next()
            zb, zr, zch = stz.next()
            db, dr, dch = stdt.next()
            for u in range(4):
                lw = [xb[:, kc, u * 128:(u + 1) * 128] for kc in range(8)]
                b = next_tm()
                mm_group(P, P.banks[b][:, :], [(lw[kc], win[:, kc, COL_V:COL_V + 512]) for kc in range(8)],
                         P.Rbank[b], [Rwin, xr_])
                act(P, vb[:, u, :, 0:64], P.banks[b][:, :].rearrange("p (h d) -> p h d", d=64), AF.Identity,
                    [P.Rbank[b]], [vr])
                b = next_tm()
                mm_group(P, P.banks[b][:, :], [(lw[kc], win[:, kc, COL_Z:COL_Z + 512]) for kc in range(8)],
                         P.Rbank[b], [Rwin, xr_])
                act(P, zb[:, u, :], P.banks[b][:, :], AF.Silu, [P.Rbank[b]], [zr])
                b = dt_bank
                mm_group(P, P.banks[b][:, 0:16], [(lw[kc], win[:, kc, COL_DT:COL_DT + 16]) for kc in range(8)],
                         P.Rbank[b], [Rwin, xr_])
                tb, tr = dtt.next()
                P.op('dve', lambda h, b=b, tb=tb: h.tensor_tensor(out=tb, in0=P.banks[b][:, 0:16], in1=dtb, op=ALU.add),
                     reads=[P.Rbank[b], Rsm], writes=[tr])
                act(P, tb, tb, AF.Exp, [tr], [tr])
                act(P, db[:, u, :], tb, AF.Ln, [tr], [dr], bias=1.0)
            P.dma('sp', vch, lambda h, vb=vb, s=s, t0=t0: h.dma_start(
                out=C.V[s][t0:t0 + 512, :].rearrange("(u p) f -> p u f", p=128),
                in_=vb.rearrange("p u h e -> p u (h e)")), reads=[vr])
            P.dma('sp', zch, lambda h, zb=zb, s=s, t0=t0: h.dma_start(
                out=C.Z[s][t0:t0 + 512, :].rearrange("(u p) f -> p u f", p=128), in_=zb), reads=[zr])
            P.dma('sp', dch, lambda h, db=db, s=s, t0=t0: h.dma_start(
                out=C.DT[s][t0:t0 + 512, :].rearrange("(u p) f -> p u f", p=128), in_=db), reads=[dr])
        sxb, sxr, sxch = stx.next()
        sbb, sbr, sbch = stbc.next()
        for c in range(8):
            P.op('pool', lambda h, c=c: h.memset(raw[c][:, 4:8], 0.0), reads=[Rraw[c]], writes=[Rraw[c]])
            ab, ar = acc.next()
            for o in conv_chunk_ops(c, 2, ab, ar):
                o()
            conv_finish(s, c, 2, ab, ar, sxb, sxr, sbb, sbr)
        P.dma('sp', sxch, lambda h, sxb=sxb, s=s: h.dma_start(
            out=C.XS[s].rearrange("(c p) t -> p c t", p=128)[:, :, L - 2:L], in_=sxb[:, :, 0:2]),
            reads=[sxr])
        P.dma('sp', sbch, lambda h, sbb=sbb, s=s: h.dma_start(
            out=C.BC[s].rearrange("(c p) t -> p c t", p=128)[:, :, L - 2:L], in_=sbb[:, :, 0:2]),
            reads=[sbr])
    P.reset(m0)


def phase_S(C):
    P, L, NS = C.P, C.L, C.NS
    NC = L // 128
    NG = NC // 4
    m0 = P.mark()
    cst = P.alloc([128, 6, 128], F32)
    idb = P.alloc([128, 128], BF16)
    dsk = P.alloc([128, 8], F32)
    Aneg = P.alloc([128, 16], F32)
    sng = P.alloc([128, 512], F32)
    Rc = P.res("s_const")
    chc = P.chan("s_const")
    P.dma('sp', chc, lambda h: h.dma_start(out=cst, in_=C.cst), writes=[Rc])
    P.dma('sp', chc, lambda h: h.dma_start(out=dsk, in_=C.dsk), writes=[Rc])
    P.dma('sp', chc, lambda h: h.dma_start(out=Aneg, in_=C.alog), writes=[Rc])
    P.dma('sp', chc, lambda h: h.dma_start(out=sng, in_=C.sng), writes=[Rc])
    act(P, Aneg, Aneg, AF.Exp, [Rc], [Rc])
    P.op('dve', lambda h: h.tensor_scalar(out=Aneg, in0=Aneg, scalar1=-1.0, scalar2=None, op0=ALU.mult),
         reads=[Rc], writes=[Rc])
    P.op('dve', lambda h: h.tensor_copy(out=idb, in_=cst[:, 5, :]), reads=[Rc], writes=[Rc])
    U_, SL_, LO_, SU_, ON_, ID_ = [cst[:, i, :] for i in range(6)]

    SbAll = P.alloc([128, NC, 512], BF16)
    RSb = [P.res(f"sb{c}") for c in range(NC)]
    gx = Ring(P, "gx", 3, [128, 4, 512], F32, chan=True)
    gbc = Ring(P, "gbc", 3, [128, 4, 512], BF16, chan=True)
    gdt = Ring(P, "gdt", 3, [128, 4, 16], F32, chan=True)
    gz = Ring(P, "gz", 3, [128, 4, 512], F32, chan=True)
    syn = Ring(P, "syn", 2, [128, 4, 512], BF16, chan=True)
    da_r = Ring(P, "da", 3, [128, 16], F32)
    ew_r = Ring(P, "ew", 4, [128, 64], F32)
    sc_r = Ring(P, "sc", 4, [128, 16], F32)
    xdt_r = Ring(P, "xdt", 6, [128, 512], BF16)
    xsd_r = Ring(P, "xsd", 3, [128, 512], F32)
    btok_r = Ring(P, "btok", 3, [128, 256], BF16)
    cbm_r = Ring(P, "cbm", 4, [128, 256], F32)
    L_r = Ring(P, "Lr", 2, [128, 8, 128], F32)
    RLhs = [[P.res(f"L{k}_{hd}") for hd in range(8)] for k in range(2)]
    dec_r = Ring(P, "dec", 2, [128, 8, 128], F32)
    M_r = Ring(P, "Mr", 4, [128, 8, 128], BF16)
    t_r = Ring(P, "tr", 4, [128, 512], F32)
    yn_r = Ring(P, "yn", 2, [128, 512], BF16)
    sm_r = Ring(P, "sm", 4, [128, 4], F32)
    Sf = P.alloc([128, 512], F32)
    Sfb = P.alloc([128, 512], BF16)
    Sb = P.alloc([128, 512], F32)
    RSf, RSfb, RSbr = P.res("Sf"), P.res("Sfb"), P.res("Sbr")

    def bc8(ap8):
        return ap8.unsqueeze(2).to_broadcast([128, 8, 64])

    def v3(ap):
        return ap.rearrange("p (h d) -> p h d", d=64)

    def load_group(s, g, with_z):
        t0 = g * 512
        xb, xr, xch = gx.next()
        P.dma('sp', xch, lambda h: h.dma_start(
            out=xb, in_=C.XS[s].rearrange("(c p) t -> p c t", p=128)[:, :, t0:t0 + 512]), writes=[xr])
        bb, br, bch = gbc.next()
        P.dma('sp', bch, lambda h: h.dma_start(
            out=bb, in_=C.BC[s].rearrange("(c p) t -> p c t", p=128)[:, :, t0:t0 + 512]), writes=[br])
        db, dr, dch = gdt.next()
        P.dma('sp', dch, lambda h: h.dma_start(
            out=db, in_=C.DT[s][t0:t0 + 512, :].rearrange("(u p) f -> p u f", p=128)), writes=[dr])
        zz = None
        if with_z:
            zb, zr, zch = gz.next()
            P.dma('sp', zch, lambda h: h.dma_start(
                out=zb, in_=C.Z[s][t0:t0 + 512, :].rearrange("(u p) f -> p u f", p=128)), writes=[zr])
            zz = (zb, zr)
        return (xb, xr), (bb, br), (db, dr), zz

    def small_mms(da, Rda, mats):
        b = P.next_bank()
        for i, m_ in enumerate(mats):
            P.op('pe', lambda h, i=i, m_=m_, b=b: h.matmul(P.banks[b][:, 16 * i:16 * i + 16], m_, da, start=True, stop=True),
                 reads=[Rc, Rda], writes=[P.Rbank[b]])
        ew, Rew = ew_r.next()
        n = 16 * len(mats)
        act(P, ew[:, 0:n], P.banks[b][:, 0:n], AF.Exp, [P.Rbank[b]], [Rew])
        return ew, Rew

    def xs_transpose(xb, xr, u):
        b = P.next_bank()
        for fc in range(4):
            P.op('pe', lambda h, fc=fc, b=b: h.transpose(P.banks[b][:, fc * 128:(fc + 1) * 128],
                                                        xb[:, fc, u * 128:(u + 1) * 128], ID_),
                 reads=[xr, Rc], writes=[P.Rbank[b]])
        return b

    def b_transpose(bb, br, u):
        b = P.next_bank()
        pb = P.banks[b][:, :].bitcast(BF16)
        for g in range(2):
            P.op('pe', lambda h, g=g, pb=pb: h.transpose(pb[:, g * 128:(g + 1) * 128],
                                                        bb[:, g, u * 128:(u + 1) * 128], idb),
                 reads=[br, Rc], writes=[P.Rbank[b]])
        bt, Rbt = btok_r.next()
        act(P, bt, pb[:, 0:256], AF.Identity, [P.Rbank[b]], [Rbt])
        return bt, Rbt

    def state_mm(bt, Rbt, xw, Rxw):
        b = P.next_bank()
        for g in range(2):
            P.op('pe', lambda h, g=g, b=b: h.matmul(P.banks[b][:, g * 256:(g + 1) * 256], bt[:, g * 128:(g + 1) * 128],
                                                   xw[:, g * 256:(g + 1) * 256], start=True, stop=True),
                 reads=[Rbt, Rxw], writes=[P.Rbank[b]])
        return b

    def s1_front(db, dr, xb, xr, bb, br, u):
        da, Rda = da_r.next()
        P.op('dve', lambda h: h.tensor_tensor(out=da, in0=db[:, u, :], in1=Aneg, op=ALU.mult),
             reads=[dr, Rc], writes=[Rda])
        ew, Rew = small_mms(da, Rda, [SU_, ON_])
        sc, Rsc = sc_r.next()
        P.op('dve', lambda h: h.tensor_tensor(out=sc[:, 0:8], in0=db[:, u, 8:16], in1=ew[:, 8:16], op=ALU.mult),
             reads=[dr, Rew], writes=[Rsc])
        bx = xs_transpose(xb, xr, u)
        xw, Rxw = xdt_r.next()
        P.op('dve', lambda h: h.tensor_tensor(out=v3(xw), in0=v3(P.banks[bx][:, :]), in1=bc8(sc[:, 0:8]), op=ALU.mult),
             reads=[P.Rbank[bx], Rsc], writes=[Rxw])
        bt, Rbt = b_transpose(bb, br, u)
        bs = state_mm(bt, Rbt, xw, Rxw)
        return ew, Rew, bs

    def s1_back(c, ew, Rew, bs):
        P.op('dve', lambda h: h.tensor_tensor(out=v3(Sb), in0=v3(Sb), in1=bc8(ew[:, 24:32]), op=ALU.mult),
             reads=[RSbr, Rew], writes=[RSbr])
        P.op('dve', lambda h: h.tensor_tensor(out=Sb, in0=Sb, in1=P.banks[bs][:, :], op=ALU.add),
             reads=[RSbr, P.Rbank[bs]], writes=[RSbr])
        act(P, SbAll[:, c - 1, :], Sb, AF.Identity, [RSbr], [RSb[c - 1]])

    def s2_front(db, dr, xb, xr, bb, br, u):
        da, Rda = da_r.next()
        P.op('dve', lambda h: h.tensor_tensor(out=da, in0=db[:, u, :], in1=Aneg, op=ALU.mult),
             reads=[dr, Rc], writes=[Rda])
        ew, Rew = small_mms(da, Rda, [U_, SL_, LO_, ON_])
        sc, Rsc = sc_r.next()
        P.op('dve', lambda h: h.tensor_tensor(out=sc[:, 0:8], in0=db[:, u, 0:8], in1=ew[:, 16:24], op=ALU.mult),
             reads=[dr, Rew], writes=[Rsc])
        Ms = []
        Ls = []
        for (tri_l, c0) in ((SL_, 0), (SU_, 8)):
            Lt, _ = L_r.next()
            RLh = RLhs[L_r.i]
            for hd in range(8):
                act(P, Lt[:, hd, :], tri_l, AF.Identity, [Rc, Rda], [RLh[hd]], scale=da[:, c0 + hd:c0 + hd + 1])
            Ls.append((Lt, RLh))
        bx = xs_transpose(xb, xr, u)
        xsP = v3(P.banks[bx][:, :])
        xf, Rxf = xdt_r.next()
        xbw, Rxbw = xdt_r.next()
        xw, Rxw = xdt_r.next()
        xsd, Rxsd = xsd_r.next()
        P.op('dve', lambda h: h.tensor_tensor(out=v3(xf), in0=xsP, in1=bc8(db[:, u, 0:8]), op=ALU.mult),
             reads=[P.Rbank[bx], dr], writes=[Rxf])
        P.op('dve', lambda h: h.tensor_tensor(out=v3(xbw), in0=xsP, in1=bc8(db[:, u, 8:16]), op=ALU.mult),
             reads=[P.Rbank[bx], dr], writes=[Rxbw])
        P.op('dve', lambda h: h.tensor_tensor(out=v3(xw), in0=xsP, in1=bc8(sc[:, 0:8]), op=ALU.mult),
             reads=[P.Rbank[bx], Rsc], writes=[Rxw])
        P.op('dve', lambda h: h.tensor_tensor(out=v3(xsd), in0=xsP, in1=bc8(dsk), op=ALU.mult),
             reads=[P.Rbank[bx], Rc], writes=[Rxsd])
        bt, Rbt = b_transpose(bb, br, u)
        bcb = P.next_bank()
        for gg in range(2):
            P.op('pe', lambda h, gg=gg: h.matmul(
                P.banks[bcb][:, gg * 128:(gg + 1) * 128], bb[:, gg, u * 128:(u + 1) * 128],
                bb[:, 2 + gg, u * 128:(u + 1) * 128], start=True, stop=True), reads=[br], writes=[P.Rbank[bcb]])
        cbU, RcbU = cbm_r.next()
        cbL, RcbL = cbm_r.next()
        cbP = P.banks[bcb][:, 0:256].rearrange("p (g q) -> p g q", g=2)
        P.op('dve', lambda h: h.tensor_tensor(
            out=cbU.rearrange("p (g q) -> p g q", g=2), in0=cbP,
            in1=U_.unsqueeze(1).to_broadcast([128, 2, 128]), op=ALU.mult),
            reads=[P.Rbank[bcb], Rc], writes=[RcbU])
        P.op('dve', lambda h: h.tensor_tensor(
            out=cbL.rearrange("p (g q) -> p g q", g=2), in0=cbP,
            in1=LO_.unsqueeze(1).to_broadcast([128, 2, 128]), op=ALU.mult),
            reads=[P.Rbank[bcb], Rc], writes=[RcbL])
        for di, (tri_r, cbm, Rcbm) in enumerate(((U_, cbU, RcbU), (LO_, cbL, RcbL))):
            Lt, RLt = Ls[di]
            dec, Rdec = dec_r.next()
            for hh in range(2):
                b = P.next_bank()
                for h4 in range(4):
                    hd = hh * 4 + h4
                    P.op('pe', lambda h, b=b, h4=h4, hd=hd, Lt=Lt, tri_r=tri_r: h.matmul(
                        P.banks[b][:, h4 * 128:(h4 + 1) * 128], Lt[:, hd, :], tri_r, start=True, stop=True),
                        reads=[RLt[hd], Rc], writes=[P.Rbank[b]])
                act(P, dec[:, hh * 4:(hh + 1) * 4, :], P.banks[b][:, :].rearrange("p (a q) -> p a q", a=4),
                    AF.Exp, [P.Rbank[b]], [Rdec])
            Mt, RMt = M_r.next()
            P.op('dve', lambda h, Mt=Mt, dec=dec, cbm=cbm: h.tensor_tensor(
                out=Mt.rearrange("p (g e) q -> p g e q", g=2), in0=dec.rearrange("p (g e) q -> p g e q", g=2),
                in1=cbm.rearrange("p (g q) -> p g q", g=2).unsqueeze(2).to_broadcast([128, 2, 4, 128]),
                op=ALU.mult), reads=[Rdec, Rcbm], writes=[RMt])
            Ms.append((Mt, RMt))
        return dict(ew=ew, Rew=Rew, xf=xf, Rxf=Rxf, xbw=xbw, Rxbw=Rxbw, xw=xw, Rxw=Rxw, xsd=xsd, Rxsd=Rxsd,
                    bt=bt, Rbt=Rbt, Ms=Ms)

    def s2_back(c, u, f, bb, br, zb, zr, yb, yr):
        ew, Rew, xf, Rxf, xbw, Rxbw, xw, Rxw = f['ew'], f['Rew'], f['xf'], f['Rxf'], f['xbw'], f['Rxbw'], f['xw'], f['Rxw']
        xsd, Rxsd, bt, Rbt, Ms = f['xsd'], f['Rxsd'], f['bt'], f['Rbt'], f['Ms']
        by = P.next_bank()
        for hd in range(8):
            P.op('pe', lambda h, hd=hd: h.matmul(
                P.banks[by][:, hd * 64:(hd + 1) * 64], Ms[0][0][:, hd, :], xf[:, hd * 64:(hd + 1) * 64],
                start=True, stop=False), reads=[Ms[0][1], Rxf], writes=[P.Rbank[by]])
            P.op('pe', lambda h, hd=hd: h.matmul(
                P.banks[by][:, hd * 64:(hd + 1) * 64], Ms[1][0][:, hd, :], xbw[:, hd * 64:(hd + 1) * 64],
                start=False, stop=True), reads=[Ms[1][1], Rxbw], writes=[P.Rbank[by]])
        bof = P.next_bank()
        bob = P.next_bank()
        for gg in range(2):
            P.op('pe', lambda h, gg=gg: h.matmul(
                P.banks[bof][:, gg * 256:(gg + 1) * 256], bb[:, 2 + gg, u * 128:(u + 1) * 128],
                Sfb[:, gg * 256:(gg + 1) * 256], start=True, stop=True), reads=[br, RSfb], writes=[P.Rbank[bof]])
        for gg in range(2):
            P.op('pe', lambda h, gg=gg: h.matmul(
                P.banks[bob][:, gg * 256:(gg + 1) * 256], bb[:, 2 + gg, u * 128:(u + 1) * 128],
                SbAll[:, c, gg * 256:(gg + 1) * 256], start=True, stop=True), reads=[br, RSb[c]], writes=[P.Rbank[bob]])
        bs = state_mm(bt, Rbt, xw, Rxw)
        P.op('dve', lambda h: h.tensor_tensor(out=v3(Sf), in0=v3(Sf), in1=bc8(ew[:, 48:56]), op=ALU.mult),
             reads=[RSf, Rew], writes=[RSf])
        P.op('dve', lambda h: h.tensor_tensor(out=Sf, in0=Sf, in1=P.banks[bs][:, :], op=ALU.add),
             reads=[RSf, P.Rbank[bs]], writes=[RSf])
        act(P, Sfb, Sf, AF.Identity, [RSf], [RSfb])
        t1, Rt1 = t_r.next()
        t2, Rt2 = t_r.next()
        P.op('dve', lambda h: h.tensor_tensor(out=v3(t1), in0=v3(P.banks[bof][:, :]), in1=bc8(ew[:, 0:8]), op=ALU.mult),
             reads=[P.Rbank[bof], Rew], writes=[Rt1])
        P.op('dve', lambda h: h.tensor_tensor(out=v3(t2), in0=v3(P.banks[bob][:, :]), in1=bc8(ew[:, 40:48]), op=ALU.mult),
             reads=[P.Rbank[bob], Rew], writes=[Rt2])
        P.op('pool', lambda h: h.tensor_tensor(out=t1, in0=t1, in1=t2, op=ALU.add), reads=[Rt1, Rt2], writes=[Rt1])
        P.op('pool', lambda h: h.tensor_tensor(out=t1, in0=t1, in1=xsd, op=ALU.add), reads=[Rt1, Rxsd], writes=[Rt1])
        P.op('dve', lambda h: h.tensor_tensor(out=t1, in0=t1, in1=P.banks[by][:, :], op=ALU.add),
             reads=[Rt1, P.Rbank[by]], writes=[Rt1])
        P.op('dve', lambda h: h.tensor_tensor(out=t1, in0=t1, in1=zb[:, u, :], op=ALU.mult),
             reads=[Rt1, zr], writes=[Rt1])
        sm, Rsm_ = sm_r.next()
        P.op('act', lambda h: h.activation(out=t2, in_=t1, func=AF.Square, accum_out=sm[:, 0:1]),
             reads=[Rt1, Rt2], writes=[Rt2, Rsm_])
        act(P, sm[:, 1:2], sm[:, 0:1], AF.Ln, [Rsm_], [Rsm_], scale=1.0 / 512, bias=EPS)
        act(P, sm[:, 2:3], sm[:, 1:2], AF.Exp, [Rsm_], [Rsm_], scale=-0.5)
        yn, Ryn = yn_r.next()
        P.op('dve', lambda h: h.scalar_tensor_tensor(
            out=yn, in0=t1, scalar=sm[:, 2:3], in1=sng, op0=ALU.mult, op1=ALU.mult),
            reads=[Rt1, Rsm_, Rc], writes=[Ryn])
        bt_ = P.next_bank()
        pbt = P.banks[bt_][:, :].bitcast(BF16)
        for fc in range(4):
            P.op('pe', lambda h, fc=fc: h.transpose(
                pbt[:, fc * 128:(fc + 1) * 128], yn[:, fc * 128:(fc + 1) * 128], idb),
                reads=[Ryn, Rc], writes=[P.Rbank[bt_]])
        act(P, yb[:, :, u * 128:(u + 1) * 128], pbt[:, 0:512].rearrange("p (c t) -> p c t", c=4), AF.Identity,
            [P.Rbank[bt_]], [yr])

    ybuf = {}

    def finish_back(p):
        c, u, g, fr, bb, br, zb, zr = p
        if u == 0:
            ybuf['cur'] = syn.next()
        yb, yr, ych = ybuf['cur']
        s2_back(c, u, fr, bb, br, zb, zr, yb, yr)
        if u == 3:
            s_ = ybuf['s']
            P.dma('sp', ych, lambda h: h.dma_start(
                out=C.YN[s_].rearrange("(c p) t -> p c t", p=128)[:, :, g * 512:(g + 1) * 512], in_=yb), reads=[yr])

    for s in range(NS):
        ybuf['s'] = s
        P.op('pool', lambda h: h.memset(Sb, 0.0), writes=[RSbr])
        P.op('pool', lambda h: h.memset(SbAll[:, NC - 1, :], 0.0), writes=[RSb[NC - 1]])
        groups = {}
        groups[NG - 1] = load_group(s, NG - 1, False)
        if NG > 1:
            groups[NG - 2] = load_group(s, NG - 2, False)
        pend = None
        for c in range(NC - 1, 0, -1):
            g, u = divmod(c, 4)
            (xb, xr), (bb, br), (db, dr), _ = groups[g]
            fr = s1_front(db, dr, xb, xr, bb, br, u)
            if pend is not None:
                s1_back(*pend)
            pend = (c,) + fr
            if u == 3 and g - 2 >= 0:
                groups[g - 2] = load_group(s, g - 2, False)
        if pend is not None:
            s1_back(*pend)
        P.op('pool', lambda h: h.memset(Sf, 0.0), writes=[RSf])
        P.op('pool', lambda h: h.memset(Sfb, 0.0), writes=[RSfb])
        groups = {0: load_group(s, 0, True)}
        if NG > 1:
            groups[1] = load_group(s, 1, True)
        ybs = {}
        pend = None
        for c in range(NC):
            g, u = divmod(c, 4)
            (xb, xr), (bb, br), (db, dr), (zb, zr) = groups[g]
            fr = s2_front(db, dr, xb, xr, bb, br, u)
            if pend is not None:
                finish_back(pend)
            pend = (c, u, g, fr, bb, br, zb, zr)
            if u == 0 and g + 2 < NG:
                groups[g + 2] = load_group(s, g + 2, True)
        finish_back(pend)
    P.reset(m0)


BRANCH_DIL = (1, 4, 16)


def phase_T(C):
    P, L, NS = C.P, C.L, C.NS
    m0 = P.mark()
    cst = P.alloc([128, 6, 128], F32)
    jmat = P.alloc([128, 128], F32)
    sel = P.alloc([65, 64], F32)
    LBh = P.alloc([128, 4, 3, 2, 256], BF16)
    LBl = P.alloc([128, 4, 3, 2, 256], BF16)
    negm = P.alloc([128, 2, 128], F32)
    idb = P.alloc([128, 128], BF16)
    Rc = P.res("t_const")
    REB = P.res("EB")
    chc = P.chan("t_const")
    P.dma('sp', chc, lambda h: h.dma_start(out=cst, in_=C.cst), writes=[Rc])
    P.dma('sp', chc, lambda h: h.dma_start(out=jmat, in_=C.jmat), writes=[Rc])
    P.dma('sp', chc, lambda h: h.dma_start(out=sel, in_=C.sel), writes=[Rc])
    P.dma('sp', chc, lambda h: h.dma_start(out=negm, in_=C.negm), writes=[Rc])
    P.op('dve', lambda h: h.tensor_copy(out=idb, in_=cst[:, 5, :]), reads=[Rc], writes=[Rc])
    U_, LO_ = cst[:, 0, :], cst[:, 2, :]
    m1 = P.mark()
    oh = P.alloc([32, 6, 256], F32)
    relb = P.alloc([32, 8], F32)
    P.dma('sp', chc, lambda h: h.dma_start(out=oh, in_=C.oh), writes=[Rc])
    P.dma('sp', chc, lambda h: h.dma_start(out=relb, in_=C.relb), writes=[Rc])
    gs_r = Ring(P, "gs", 2, [8, 256], F32, chan=True)
    hk_r = Ring(P, "hk", 4, [128, 128], F32, chan=True)
    tmp_r = Ring(P, "ebtmp", 3, [128, 128], F32)
    RGV = [P.res(f"gv{i}") for i in range(6)]
    for bt in range(6):
        b = P.next_bank()
        P.op('pe', lambda h, b=b, bt=bt: h.matmul(P.banks[b][0:8, 0:256], relb, oh[:, bt, :], start=True, stop=True),
             reads=[Rc], writes=[P.Rbank[b]])
        gs, Rgs, gch = gs_r.next()
        P.op('dve', lambda h, gs=gs, b=b: h.tensor_copy(out=gs, in_=P.banks[b][0:8, 0:256]), reads=[P.Rbank[b]], writes=[Rgs])
        P.dma('sp', gch, lambda h, gs=gs, bt=bt: h.dma_start(out=C.GV[bt], in_=gs), reads=[Rgs], writes=[RGV[bt]])
    for bt in range(6):
        bi, ty = bt // 2, bt % 2
        for hd in range(8):
            hk, Rhk, hch = hk_r.next()
            src = bass.AP(C.GV.tensor, (bt * 8 + hd) * 256, [[1, 128], [1, 128]])
            P.dma('sp', hch, lambda h, hk=hk, src=src: h.dma_start(out=hk, in_=src), reads=[RGV[bt]], writes=[Rhk])
            b = P.next_bank()
            P.op('pe', lambda h, b=b, hk=hk: h.matmul(P.banks[b][:, 0:128], hk, jmat, start=True, stop=True),
                 reads=[Rhk, Rc], writes=[P.Rbank[b]])
            tmp, Rtmp = tmp_r.next()
            msk = U_ if ty == 0 else LO_
            ngm = negm[:, ty, :]
            P.op('dve', lambda h, tmp=tmp, msk=msk, b=b: h.tensor_tensor(out=tmp, in0=P.banks[b][:, 0:128], in1=msk, op=ALU.mult),
                 reads=[P.Rbank[b], Rc], writes=[Rtmp])
            P.op('dve', lambda h, tmp=tmp, ngm=ngm: h.tensor_tensor(out=tmp, in0=tmp, in1=ngm, op=ALU.add),
                 reads=[Rtmp, Rc], writes=[Rtmp])
            G_ = 16 // BRANCH_DIL[bi]
            HW_ = 128 // G_
            eh = LBh[:, hd // 2, bi, hd % 2, :].rearrange("p (c w) -> p c w", c=G_)[:, :, ty * HW_:(ty + 1) * HW_]
            el = LBl[:, hd // 2, bi, hd % 2, :].rearrange("p (c w) -> p c w", c=G_)[:, :, ty * HW_:(ty + 1) * HW_]
            tv = tmp.rearrange("p (i c) -> p c i", c=G_)
            P.op('dve', lambda h, tv=tv, eh=eh: h.tensor_copy(out=eh, in_=tv), reads=[Rtmp], writes=[REB])
            P.op('dve', lambda h, tv=tv, eh=eh, el=el: h.tensor_tensor(out=el, in0=tv, in1=eh, op=ALU.subtract),
                 reads=[Rtmp, REB], writes=[REB])
    P.barrier()
    P.reset(m1)

    PADK = 1024
    Ld16 = L // 16
    Qbd = P.alloc([128, 2, L], BF16)
    Qv = Qbd.rearrange("p h (r i) -> p h r i", r=16)
    KTp = P.alloc([128, L + 2 * PADK], BF16)
    OT = P.alloc([128, 2, L], F32)
    NTmax = L // 128 + 16
    Vbs = [P.alloc([128, NTmax, 130], BF16) for _ in range(2)]
    RVs = [[P.res(f"Vb{k}_{r}") for r in range(16)] for k in range(2)]
    RQ, RK, ROT = P.res("Qbd"), P.res("KTp"), P.res("OT")
    chq, chk = P.chan("q"), P.chan("k")
    chv2 = [[P.chan(f"v{i}_{k}") for k in range(4)] for i in range(2)]
    P_r = Ring(P, "Pt", 6, [128, 512], BF16)
    rc_r = Ring(P, "rc", 2, [64, 512], F32)
    so_r = Ring(P, "so", 2, [64, 512], F32, chan=True)
    dummy = P.alloc([128, 16], F32)
    dummy2 = P.alloc([128, 16], F32)
    P.op('pool', lambda h: h.memset(Qbd, 0.0), writes=[RQ])
    P.op('pool', lambda h: h.memset(KTp, 0.0), writes=[RK])
    S_banks = [0, 1, 2, 3]
    si = [0]
    O_bank = {(0, 0): 4, (0, 1): 5, (1, 0): 6, (1, 1): 7}

    def load_V(s, hp, bi, slot):
        dl = BRANCH_DIL[bi]
        Ld = L // dl
        NJ = Ld // 128 + 1
        Vb = Vbs[slot]
        Vs = C.V[s]
        c0, c1 = hp * 130, (hp + 1) * 130
        P.op('pool', lambda h: h.memset(dummy2, 0.0), writes=list(RVs[slot]))
        for rho in range(dl):
            RV = RVs[slot][rho]
            tb = rho * NJ
            P.op('pool', lambda h, tb=tb, Vb=Vb: h.memset(Vb[0:64, tb, :], 0.0), writes=[RV])
            P.op('pool', lambda h, tb=tb, NJ=NJ, Vb=Vb: h.memset(Vb[64:128, tb + NJ - 1, :], 0.0), writes=[RV])
            ch_ = chv2[slot][rho % 4]
            if NJ > 2:
                src = Vs[rho + dl * 64:rho + dl * 64 + dl * 128 * (NJ - 2):dl, c0:c1]
                P.dma('sp', ch_, lambda h, tb=tb, NJ=NJ, src=src, Vb=Vb: h.dma_start(
                    out=Vb[:, tb + 1:tb + NJ - 1, :], in_=src.rearrange("(j a) f -> a j f", a=128)), writes=[RV])
            r_first = Vs[rho:rho + dl * 63 + 1:dl, c0:c1]
            t_l = rho + dl * (Ld - 64)
            r_last = Vs[t_l:t_l + dl * 63 + 1:dl, c0:c1]
            P.dma('sp', ch_, lambda h, tb=tb, r_first=r_first, Vb=Vb: h.dma_start(out=Vb[64:128, tb, :], in_=r_first), writes=[RV])
            P.dma('sp', ch_, lambda h, tb=tb, NJ=NJ, r_last=r_last, Vb=Vb: h.dma_start(
                out=Vb[0:64, tb + NJ - 1, :], in_=r_last), writes=[RV])

    BORDER = (2, 1, 0)
    work = [(s, hp, bi) for s in range(NS) for hp in range(4) for bi in BORDER]
    load_V(*work[0], 0)

    def do_work(wi, s, hp, bi):
        slot = wi % 2
        Vb = Vbs[slot]
        dl = BRANCH_DIL[bi]
        G = 16 // dl
        W = 256 // G
        HW = 128 // G
        Ld = L // dl
        NQ = Ld // 128
        NJ = NQ + 1
        gs = min(4, NQ)
        first = (bi == BORDER[0])
        last = (bi == BORDER[-1])
        if first:
            r0 = hp * 128
            P.dma('sp', chq, lambda h: h.dma_start(out=Qbd[0:64, 0, :], in_=C.QT[s][r0:r0 + 64, :]), writes=[RQ])
            P.dma('sp', chq, lambda h: h.dma_start(out=Qbd[64:128, 1, :], in_=C.QT[s][r0 + 64:r0 + 128, :]), writes=[RQ])
            P.dma('sp', chk, lambda h: h.dma_start(out=KTp[:, PADK:PADK + L], in_=C.KT[s][r0:r0 + 128, :]), writes=[RK])
        P.op('pool', lambda h: h.memset(dummy, 0.0), writes=[ROT])
        if wi + 1 < len(work):
            load_V(*work[wi + 1], 1 - slot)

        def v4(ap512):
            return ap512.rearrange("p (h c w) -> p h c w", h=2, c=G)

        for rho in range(dl):
            prev = None
            RV = RVs[slot][rho]
            for j in range(NJ):
                halves = [hf for hf in (0, 1) if 0 <= j - 1 + hf < NQ]
                h0, h1 = halves[0], halves[-1] + 1
                w0, w1 = h0 * HW, h1 * HW
                i0 = (128 * (j - 1) + 128 * h0) // G
                i1 = (128 * (j - 1) + 128 * h1) // G
                ks = PADK + rho + dl * (128 * j - 64)
                bS = S_banks[si[0] % 4]
                si[0] += 1
                Sv = v4(P.banks[bS][:, :])[:, :, :, w0:w1]
                qa = Qv[:, :, rho:16:dl, i0:i1]
                ka = KTp[:, ks:ks + dl * 127 + 1:dl]
                lh = v4(LBh[:, hp, bi, :, :].rearrange("p h q -> p (h q)"))[:, :, :, w0:w1]
                ll = v4(LBl[:, hp, bi, :, :].rearrange("p h q -> p (h q)"))[:, :, :, w0:w1]
                P.op('pe', lambda h, Sv=Sv, qa=qa, ka=ka: h.matmul(Sv, ka, qa, start=True, stop=False),
                     reads=[RK, RQ], writes=[P.Rbank[bS]])
                P.op('pe', lambda h, Sv=Sv, lh=lh: h.matmul(Sv, idb, lh, start=False, stop=False),
                     reads=[Rc, REB], writes=[P.Rbank[bS]])
                P.op('pe', lambda h, Sv=Sv, ll=ll: h.matmul(Sv, idb, ll, start=False, stop=True),
                     reads=[Rc, REB], writes=[P.Rbank[bS]])
                Pt, RPt = P_r.next()
                Pv = v4(Pt)
                act(P, Pv[:, :, :, w0:w1], Sv, AF.Exp, [P.Rbank[bS]], [RPt])
                tile = rho * NJ + j
                if j >= 1:
                    m = j - 1
                    pPv, pRPt, ptile = prev
                    for hd in range(2):
                        bO = O_bank[(hd, (m // gs) % 2)]
                        oc = (m % gs) * 128
                        Ov = P.banks[bO][0:65, oc:oc + 128].rearrange("p (c w) -> p c w", c=G)
                        P.op('pe', lambda h, Ov=Ov, hd=hd, pPv=pPv, ptile=ptile: h.matmul(
                            Ov, Vb[:, ptile, hd * 65:(hd + 1) * 65], pPv[:, hd, :, HW:2 * HW], start=True, stop=False),
                            reads=[RV, pRPt], writes=[P.Rbank[bO]])
                        P.op('pe', lambda h, Ov=Ov, hd=hd, Pv=Pv, tile=tile: h.matmul(
                            Ov, Vb[:, tile, hd * 65:(hd + 1) * 65], Pv[:, hd, :, 0:HW], start=False, stop=True),
                            reads=[RV, RPt], writes=[P.Rbank[bO]])
                        if (m + 1) % gs == 0:
                            m0_ = m + 1 - gs
                            t0 = rho + dl * 128 * m0_
                            ov = OT[0:65, hd, t0 - rho:t0 - rho + dl * 128 * gs].rearrange(
                                "p (m i r) -> p m i r", m=gs, r=16)[:, :, :, rho:16:dl].rearrange("p m i c -> p m c i")
                            src = P.banks[bO][0:65, 0:gs * 128].rearrange("p (m c i) -> p m c i", m=gs, c=G)
                            if first:
                                act(P, ov, src, AF.Identity, [ROT, P.Rbank[bO]], [])
                            elif not last:
                                P.op('dve', lambda h, ov=ov, src=src: h.tensor_tensor(out=ov, in0=ov, in1=src, op=ALU.add),
                                     reads=[ROT, P.Rbank[bO]], writes=[])
                            else:
                                Rfin = P.res()
                                P.op('dve', lambda h, ov=ov, src=src: h.tensor_tensor(out=ov, in0=ov, in1=src, op=ALU.add),
                                     reads=[ROT, P.Rbank[bO]], writes=[Rfin])
                                ncol = 128 * gs
                                b = S_banks[si[0] % 4]
                                si[0] += 1
                                P.op('pe', lambda h, b=b, hd=hd, t0=t0, ncol=ncol: h.matmul(
                                    P.banks[b][0:64, 0:ncol], sel, OT[0:65, hd, t0:t0 + ncol], start=True, stop=True),
                                    reads=[Rc, Rfin, ROT], writes=[P.Rbank[b]])
                                rc, Rrc = rc_r.next()
                                act(P, rc[:, 0:ncol], P.banks[b][0:64, 0:ncol], AF.Ln, [P.Rbank[b]], [Rrc])
                                act(P, rc[:, 0:ncol], rc[:, 0:ncol], AF.Exp, [Rrc], [Rrc], scale=-1.0)
                                so, Rso, soch = so_r.next()
                                P.op('dve', lambda h, so=so, rc=rc, hd=hd, t0=t0, ncol=ncol: h.tensor_tensor(
                                    out=so[:, 0:ncol], in0=OT[0:64, hd, t0:t0 + ncol], in1=rc[:, 0:ncol], op=ALU.mult),
                                    reads=[ROT, Rfin, Rrc], writes=[Rso])
                                rr = hp * 128 + hd * 64
                                P.dma('sp', soch, lambda h, so=so, rr=rr, t0=t0, ncol=ncol: h.dma_start(
                                    out=C.AT[s][rr:rr + 64, t0:t0 + ncol], in_=so[:, 0:ncol]), reads=[Rso])
                prev = (Pv, RPt, tile)

    for wi, (s, hp, bi) in enumerate(work):
        do_work(wi, s, hp, bi)
    P.reset(m0)


def _ln_feature_major(P, C, tt, Rtt, nd, T, S1, S2, cst_ones, Rc, mk_out):
    mean = C.ln_mean
    m2 = C.ln_m2
    rstd = C.ln_rstd
    Rst = C.ln_Rst
    act(P, mean[:, 0:T], P.banks[S1][:, 0:T], AF.Identity, [P.Rbank[S1]], [Rst], scale=1.0 / D)
    P.op('dve', lambda h: h.tensor_tensor(out=m2[:, 0:T], in0=mean[:, 0:T], in1=mean[:, 0:T], op=ALU.mult),
         reads=[Rst], writes=[Rst])
    P.op('dve', lambda h: h.scalar_tensor_tensor(out=m2[:, 0:T], in0=P.banks[S2][:, 0:T], scalar=1.0 / D,
                                                 in1=m2[:, 0:T], op0=ALU.mult, op1=ALU.subtract),
         reads=[P.Rbank[S2], Rst], writes=[Rst])
    act(P, m2[:, 0:T], m2[:, 0:T], AF.Ln, [Rst], [Rst], bias=EPS)
    act(P, rstd[:, 0:T], m2[:, 0:T], AF.Exp, [Rst], [Rst], scale=-0.5)
    for dc in range(nd):
        u1, Ru1 = C.ln_u.next()
        P.op('dve', lambda h, u1=u1, dc=dc: h.tensor_tensor(out=u1[:, 0:T], in0=tt[:, dc, 0:T], in1=mean[:, 0:T], op=ALU.subtract),
             reads=[Rtt[dc], Rst], writes=[Ru1])
        P.op('dve', lambda h, u1=u1: h.tensor_tensor(out=u1[:, 0:T], in0=u1[:, 0:T], in1=rstd[:, 0:T], op=ALU.mult),
             reads=[Ru1, Rst], writes=[Ru1])
        mk_out(dc, u1, Ru1)


def phase_C1(C):
    P, L, NS = C.P, C.L, C.NS
    T = 512
    m0 = P.mark()
    P.bank_list = [0, 1, 2, 3, 4]
    SA, S1, S2 = 5, 6, 7
    wout = P.alloc([128, 8, D], BF16)
    cst = P.alloc([128, 6, 128], F32)
    ang = P.alloc([128, 4], F32)
    lnfm = P.alloc([128, 4, 8], F32)
    Rc = P.res("c1_const")
    chc = P.chan("c1_const")
    P.dma('pool', chc, lambda h: h.dma_start(out=wout, in_=C.w_out.rearrange("(kc p) c -> p kc c", p=128)), writes=[Rc])
    P.dma('sp', chc, lambda h: h.dma_start(out=cst, in_=C.cst), writes=[Rc])
    P.dma('sp', chc, lambda h: h.dma_start(out=ang, in_=C.ang), writes=[Rc])
    P.dma('sp', chc, lambda h: h.dma_start(out=lnfm, in_=C.lnfm), writes=[Rc])
    ONES = cst[:, 4, :]
    xt_r = Ring(P, "c1x", 3, [128, 8, T], F32, chan=True)
    at_r = Ring(P, "c1a", 2, [128, 4, T], F32, chan=True)
    yn_r = Ring(P, "c1y", 3, [128, 4, T], BF16, chan=True)
    an_r = Ring(P, "c1an", 2, [128, 4, T], BF16)
    sq_r = Ring(P, "c1sq", 3, [128, T], F32)
    sqx_r = Ring(P, "c1sqx", 2, [128, T], F32)
    hl_r = Ring(P, "c1hl", 3, [128, 4, T], BF16)
    ONESB = P.alloc([128, 128], BF16)
    P.op('dve', lambda h: h.tensor_copy(out=ONESB, in_=ONES), reads=[Rc], writes=[Rc])
    rsa_r = Ring(P, "c1rsa", 2, [128, T], F32)
    tts = [P.alloc([128, 8, T], F32) for _ in range(2)]
    Rtts = [[P.res(f"tt{k}_{i}") for i in range(8)] for k in range(2)]
    C.ln_mean = P.alloc([128, T], F32)
    C.ln_m2 = P.alloc([128, T], F32)
    C.ln_rstd = P.alloc([128, T], F32)
    C.ln_Rst = P.res("lnst")
    C.ln_u = Ring(P, "lnu", 3, [128, T], F32)
    hf_r = Ring(P, "c1hf", 1, [128, 8, T], F32, chan=True)
    hb_r = Ring(P, "c1hb", 1, [128, 8, T], BF16, chan=True)
    tiles = [(s, t0) for s in range(NS) for t0 in range(0, L, T)]

    def load(s, t0):
        xb, xr, xch = xt_r.next()
        P.dma('sp', xch, lambda h: h.dma_start(out=xb, in_=C.xT[s].rearrange("(c p) t -> p c t", p=128)[:, :, t0:t0 + T]), writes=[xr])
        ab, ar, ach = at_r.next()
        P.dma('sp', ach, lambda h: h.dma_start(out=ab, in_=C.AT[s].rearrange("(c p) t -> p c t", p=128)[:, :, t0:t0 + T]), writes=[ar])
        yb, yr, ych = yn_r.next()
        P.dma('sp', ych, lambda h: h.dma_start(out=yb, in_=C.YN[s].rearrange("(c p) t -> p c t", p=128)[:, :, t0:t0 + T]), writes=[yr])
        return (xb, xr), (ab, ar), (yb, yr)

    def stage_X(ld):
        (xb, xr), (ab, ar), (yb, yr) = ld
        for fc in range(4):
            sq, Rsq = sqx_r.next()
            act(P, sq, ab[:, fc, :], AF.Square, [ar], [Rsq])
            P.op('pe', lambda h, sq=sq, fc=fc: h.matmul(P.banks[SA][:, :], ONES, sq, start=(fc == 0), stop=(fc == 3)),
                 reads=[Rc, Rsq], writes=[P.Rbank[SA]])
        rsa, Rrsa = rsa_r.next()
        act(P, rsa, P.banks[SA][:, :], AF.Ln, [P.Rbank[SA]], [Rrsa], scale=1.0 / 512, bias=EPS)
        act(P, rsa, rsa, AF.Exp, [Rrsa], [Rrsa], scale=-0.5)
        an, Ran = an_r.next()
        for fc in range(4):
            P.op('dve', lambda h, fc=fc: h.scalar_tensor_tensor(
                out=an[:, fc, :], in0=ab[:, fc, :], scalar=ang[:, fc:fc + 1], in1=rsa, op0=ALU.mult, op1=ALU.mult),
                reads=[ar, Rc, Rrsa], writes=[Ran])
        return (xb, xr), (yb, yr), (an, Ran)

    def stage_YZ(ti, xs_, inject):
        s, t0 = tiles[ti]
        (xb, xr), (yb, yr), (an, Ran) = xs_
        tt, Rtt = tts[ti % 2], Rtts[ti % 2]
        pend = None
        nxt_x = None
        for dc in range(8):
            if dc == 4 and inject is not None:
                nxt_x = inject()
            b = P.next_bank()
            mm_group(P, P.banks[b][:, :],
                     [(wout[:, kc, dc * 128:(dc + 1) * 128], an[:, kc, :] if kc < 4 else yb[:, kc - 4, :]) for kc in range(8)],
                     P.Rbank[b], [Rc, Ran, yr])
            P.op('dve', lambda h, dc=dc, b=b: h.scalar_tensor_tensor(
                out=tt[:, dc, :], in0=xb[:, dc, :], scalar=ALPHA, in1=P.banks[b][:, :], op0=ALU.mult, op1=ALU.add),
                reads=[xr, P.Rbank[b]], writes=[Rtt[dc]])
            sq, Rsq = sq_r.next()
            act(P, sq, tt[:, dc, :], AF.Square, [Rtt[dc]], [Rsq])
            hl, Rhl = hl_r.next()
            act(P, hl[:, 0, :], tt[:, dc, :], AF.Identity, [Rtt[dc]], [Rhl])
            act(P, hl[:, 2, :], sq, AF.Identity, [Rsq], [Rhl])
            P.op('pool', lambda h, hl=hl, dc=dc: h.tensor_tensor(out=hl[:, 1, :], in0=tt[:, dc, :], in1=hl[:, 0, :], op=ALU.subtract),
                 reads=[Rtt[dc], Rhl], writes=[Rhl])
            P.op('pool', lambda h, hl=hl, sq=sq: h.tensor_tensor(out=hl[:, 3, :], in0=sq, in1=hl[:, 2, :], op=ALU.subtract),
                 reads=[Rsq, Rhl], writes=[Rhl])

            def stats(dc=dc, hl=hl, Rhl=Rhl):
                for k, bk in ((0, S1), (1, S1), (2, S2), (3, S2)):
                    P.op('pe', lambda h, k=k, bk=bk: h.matmul(P.banks[bk][:, :], ONESB, hl[:, k, :],
                                                              start=(dc == 0 and k % 2 == 0), stop=(dc == 7 and k % 2 == 1)),
                         reads=[Rc, Rhl], writes=[P.Rbank[bk]])
            if pend is not None:
                pend()
            pend = stats
        pend()
        hf, Rhf, hfch = hf_r.next()
        hb, Rhb, hbch = hb_r.next()

        def mk_out(dc, u1, Ru1):
            act(P, hf[:, dc, :], u1, AF.Identity, [Ru1, Rc], [Rhf], scale=lnfm[:, 0, dc:dc + 1], bias=lnfm[:, 1, dc:dc + 1])
            act(P, hb[:, dc, :], u1, AF.Identity, [Ru1, Rc], [Rhb], scale=lnfm[:, 0, dc:dc + 1], bias=lnfm[:, 1, dc:dc + 1])
        _ln_feature_major(P, C, tt, Rtt, 8, T, S1, S2, ONES, Rc, mk_out)
        P.dma('sp', hfch, lambda h: h.dma_start(
            out=C.H1F[s].rearrange("(c p) t -> p c t", p=128)[:, :, t0:t0 + T], in_=hf), reads=[Rhf])
        P.dma('sp', hbch, lambda h: h.dma_start(
            out=C.H1B[s].rearrange("(c p) t -> p c t", p=128)[:, :, t0:t0 + T], in_=hb), reads=[Rhb])
        return nxt_x

    ld = [None] * (len(tiles) + 2)
    ld[0] = load(*tiles[0])
    if len(tiles) > 1:
        ld[1] = load(*tiles[1])
    xs_next = stage_X(ld[0])
    for ti in range(len(tiles)):
        xs_cur = xs_next
        if ti + 2 < len(tiles):
            ld[ti + 2] = load(*tiles[ti + 2])
        inj = (lambda ti=ti: stage_X(ld[ti + 1])) if ti + 1 < len(tiles) else None
        xs_next = stage_YZ(ti, xs_cur, inj)
    P.bank_list = list(range(8))
    P.reset(m0)


def phase_C2(C):
    P, L, NS = C.P, C.L, C.NS
    T = 256
    NF = DFF // 128
    m0 = P.mark()
    P.bank_list = [0, 1, 2, 3, 4, 5]
    S1, S2 = 6, 7
    wg = P.alloc([128, 8, DFF], BF16)
    wu = P.alloc([128, 8, DFF], BF16)
    wd = P.alloc([128, NF, D], BF16)
    cst = P.alloc([128, 6, 128], F32)
    lnfm = P.alloc([128, 4, 8], F32)
    Rc = P.res("c2_const")
    chc = P.chan("c2_const")
    chw = [P.chan(f"c2w{i}") for i in range(3)]
    for c0 in (0, 1408):
        P.dma('pool', chw[0], lambda h, c0=c0: h.dma_start(
            out=wg[:, :, c0:c0 + 1408], in_=C.w_gate.rearrange("(kc p) c -> p kc c", p=128)[:, :, c0:c0 + 1408]), writes=[Rc])
        P.dma('pool', chw[1], lambda h, c0=c0: h.dma_start(
            out=wu[:, :, c0:c0 + 1408], in_=C.w_up.rearrange("(kc p) c -> p kc c", p=128)[:, :, c0:c0 + 1408]), writes=[Rc])
    P.dma('pool', chw[2], lambda h: h.dma_start(out=wd, in_=C.w_down.rearrange("(kc p) c -> p kc c", p=128)), writes=[Rc])
    P.dma('sp', chc, lambda h: h.dma_start(out=cst, in_=C.cst), writes=[Rc])
    P.dma('sp', chc, lambda h: h.dma_start(out=lnfm, in_=C.lnfm), writes=[Rc])
    ONES = cst[:, 4, :]
    hb_r = Ring(P, "c2hb", 2, [128, 8, T], BF16, chan=True)
    hf_r = Ring(P, "c2hf", 4, [128, T], F32, chan=True)
    hid = P.alloc([128, NF, T], BF16)
    Rhid = P.res("hid")
    sg_r = Ring(P, "c2sg", 3, [128, T], F32)
    sq_r = Ring(P, "c2sq", 4, [128, T], F32)
    hl_r = Ring(P, "c2hl", 3, [128, 4, T], BF16)
    ONESB = P.alloc([128, 128], BF16)
    P.op('dve', lambda h: h.tensor_copy(out=ONESB, in_=ONES), reads=[Rc], writes=[Rc])
    tt = P.alloc([128, 8, T], F32)
    Rtt = [P.res(f"tt2_{i}") for i in range(8)]
    C.ln_mean = P.alloc([128, T], F32)
    C.ln_m2 = P.alloc([128, T], F32)
    C.ln_rstd = P.alloc([128, T], F32)
    C.ln_Rst = P.res("lnst2")
    C.ln_u = Ring(P, "lnu2", 3, [128, T], F32)
    yo_r = Ring(P, "c2yo", 4, [128, T], F32, chan=True)
    outs = []

    def load(s, t0):
        hb, hr, hch = hb_r.next()
        P.dma('sp', hch, lambda h: h.dma_start(out=hb, in_=C.H1B[s].rearrange("(c p) t -> p c t", p=128)[:, :, t0:t0 + T]), writes=[hr])
        return hb, hr

    tiles = [(s, t0) for s in range(NS) for t0 in range(0, L, T)]
    nxt = load(*tiles[0])
    for ti, (s, t0) in enumerate(tiles):
        hb, hr = nxt
        if ti + 1 < len(tiles):
            nxt = load(*tiles[ti + 1])
        for fc in range(NF):
            bg = P.next_bank()
            mm_group(P, P.banks[bg][:, 0:T], [(wg[:, kc, fc * 128:(fc + 1) * 128], hb[:, kc, :]) for kc in range(8)],
                     P.Rbank[bg], [Rc, hr])
            bu = P.next_bank()
            mm_group(P, P.banks[bu][:, 0:T], [(wu[:, kc, fc * 128:(fc + 1) * 128], hb[:, kc, :]) for kc in range(8)],
                     P.Rbank[bu], [Rc, hr])
            sg, Rsg = sg_r.next()
            act(P, sg, P.banks[bg][:, 0:T], AF.Silu, [P.Rbank[bg]], [Rsg])
            P.op('dve', lambda h, fc=fc, sg=sg, bu=bu: h.tensor_tensor(out=hid[:, fc, :], in0=sg, in1=P.banks[bu][:, 0:T], op=ALU.mult),
                 reads=[Rsg, P.Rbank[bu]], writes=[Rhid])
        pend = None
        for dc in range(8):
            hf, Rhf, hfch = hf_r.next()
            P.dma('sp', hfch, lambda h, hf=hf, s=s, t0=t0, dc=dc: h.dma_start(
                out=hf, in_=C.H1F[s][dc * 128:(dc + 1) * 128, t0:t0 + T]), writes=[Rhf])
            b = P.next_bank()
            mm_group(P, P.banks[b][:, 0:T], [(wd[:, fc, dc * 128:(dc + 1) * 128], hid[:, fc, :]) for fc in range(NF)],
                     P.Rbank[b], [Rc, Rhid])
            P.op('dve', lambda h, dc=dc, b=b, hf=hf: h.scalar_tensor_tensor(
                out=tt[:, dc, :], in0=hf, scalar=ALPHA, in1=P.banks[b][:, 0:T], op0=ALU.mult, op1=ALU.add),
                reads=[Rhf, P.Rbank[b]], writes=[Rtt[dc]])
            sq, Rsq = sq_r.next()
            act(P, sq, tt[:, dc, :], AF.Square, [Rtt[dc]], [Rsq])
            hl, Rhl = hl_r.next()
            act(P, hl[:, 0, :], tt[:, dc, :], AF.Identity, [Rtt[dc]], [Rhl])
            act(P, hl[:, 2, :], sq, AF.Identity, [Rsq], [Rhl])
            P.op('pool', lambda h, hl=hl, dc=dc: h.tensor_tensor(out=hl[:, 1, :], in0=tt[:, dc, :], in1=hl[:, 0, :], op=ALU.subtract),
                 reads=[Rtt[dc], Rhl], writes=[Rhl])
            P.op('pool', lambda h, hl=hl, sq=sq: h.tensor_tensor(out=hl[:, 3, :], in0=sq, in1=hl[:, 2, :], op=ALU.subtract),
                 reads=[Rsq, Rhl], writes=[Rhl])

            def stats(dc=dc, hl=hl, Rhl=Rhl):
                for k, bk in ((0, S1), (1, S1), (2, S2), (3, S2)):
                    P.op('pe', lambda h, k=k, bk=bk: h.matmul(P.banks[bk][:, 0:T], ONESB, hl[:, k, :],
                                                              start=(dc == 0 and k % 2 == 0), stop=(dc == 7 and k % 2 == 1)),
                         reads=[Rc, Rhl], writes=[P.Rbank[bk]])
            if pend is not None:
                pend()
            pend = stats
        pend()
        pend = None

        def mk_out(dc, u1, Ru1, s=s, t0=t0):
            yo, Ryo, yoch = yo_r.next()
            act(P, yo, u1[:, 0:T], AF.Identity, [Ru1, Rc], [Ryo], scale=lnfm[:, 2, dc:dc + 1], bias=lnfm[:, 3, dc:dc + 1])
            outs.append(P.dma('sp', yoch, lambda h, yo=yo, dc=dc: h.dma_start(
                out=C.yT[s][dc * 128:(dc + 1) * 128, t0:t0 + T], in_=yo), reads=[Ryo]))
        _ln_feature_major(P, C, tt, Rtt, 8, T, S1, S2, ONES, Rc, mk_out)
    P.bank_list = list(range(8))
    P.reset(m0)
    return outs

def to_xT16(xT):
    sh = xT.shape
    L = sh[-1]
    return np.ascontiguousarray(xT.reshape(sh[:-1] + (L // 16, 16)).swapaxes(-1, -2).reshape(sh))


def make_cst():
    i = np.arange(128)
    U = (i[:, None] <= i[None, :]).astype(np.float32)
    SL = (i[:, None] > i[None, :]).astype(np.float32)
    Lo = (i[:, None] >= i[None, :]).astype(np.float32)
    SU = (i[:, None] < i[None, :]).astype(np.float32)
    ones = np.ones((128, 128), np.float32)
    ident = np.eye(128, dtype=np.float32)
    return np.ascontiguousarray(np.stack([U, SL, Lo, SU, ones, ident], axis=1))


def shared_inputs(inp):
    f = np.float32
    g = lambda k: np.asarray(inp[k], dtype=f)
    bc = lambda v, n=128: np.ascontiguousarray(np.broadcast_to(v[None, :], (n, v.shape[0])))
    m = {}
    m["w_in"] = np.ascontiguousarray(g("w_in")[0])
    m["convw"] = np.ascontiguousarray(g("conv_w")[0].reshape(5, 8, 128).transpose(2, 1, 0))
    m["convb"] = np.ascontiguousarray(g("conv_b")[0].reshape(8, 128).T)
    m["dtb"] = bc(np.concatenate([g("dt_bias_fwd")[0], g("dt_bias_bwd")[0]]))
    m["alog"] = bc(np.concatenate([g("a_log_fwd")[0], g("a_log_bwd")[0]]))
    m["dsk"] = bc(g("d_skip")[0])
    m["ang"] = np.ascontiguousarray(g("attn_norm_g")[0].reshape(4, 128).T)
    m["sng"] = bc(g("ssd_norm_g")[0])
    m["relb"] = np.ascontiguousarray(g("rel_bias"))
    m["lnfm"] = np.ascontiguousarray(np.stack([g(k)[0].reshape(8, 128).T for k in ("ln1_g", "ln1_b", "ln2_g", "ln2_b")], axis=1))
    m["w_out"] = np.ascontiguousarray(g("w_out")[0])
    m["w_gate"] = np.ascontiguousarray(g("w_gate")[0])
    m["w_up"] = np.ascontiguousarray(g("w_up")[0])
    m["w_down"] = np.ascontiguousarray(g("w_down")[0])
    m["cst"] = make_cst()
    m["oh"], m["jmat"], m["sel"] = make_att_consts()
    cst_ = m["cst"]
    m["negm"] = np.ascontiguousarray(np.stack([(cst_[:, 0, :] - 1.0) * 30000.0, (cst_[:, 2, :] - 1.0) * 30000.0], axis=1))
    return m


def t5_bucket(rel):
    half = 16
    max_exact = 8
    ret = (rel > 0).astype(np.int32) * half
    n = np.abs(rel)
    large = max_exact + (np.log(np.maximum(n, 1) / max_exact)
                         / math.log(1024 / max_exact) * (half - max_exact)).astype(np.int32)
    large = np.minimum(large, half - 1)
    return ret + np.where(n < max_exact, n, large)


def make_att_consts():
    oh = np.zeros((32, 6, 256), np.float32)
    i = np.arange(255)
    for bi, dl in enumerate(BRANCH_DIL):
        for ty in range(2):
            rel = (i - 63) if ty == 0 else (i - 191)
            bk = t5_bucket(rel * dl)
            oh[bk, bi * 2 + ty, i] = 1.0
    eb = np.ascontiguousarray(np.eye(128, dtype=np.float32)[::-1])
    sel = np.zeros((65, 64), np.float32)
    sel[64, :] = 1.0
    return oh, eb, sel


SEQ_LEN = 8192
N_CORES = 8
SLOTS = 2
_CACHE = {}


def kernel(**inputs):
    xp = np.asarray(inputs["x_prompt"], dtype=np.float32)
    xs = np.asarray(inputs["x_sample"], dtype=np.float32)
    seqs = [xp[i] for i in range(xp.shape[0])] + [xs[i] for i in range(xs.shape[0])]
    nseq = len(seqs)
    L = seqs[0].shape[0]
    shared = shared_inputs(inputs)
    in_maps = []
    for c in range(N_CORES):
        xT = np.zeros((SLOTS, D, L), np.float32)
        for sl in range(SLOTS):
            i = c * SLOTS + sl
            if i < nseq:
                xT[sl] = seqs[i].T
        m = dict(shared)
        m["xT"] = xT
        m["xT16"] = to_xT16(xT)
        in_maps.append(m)
    key = (L, SLOTS)
    if key not in _CACHE:
        _CACHE[key] = build(L, SLOTS)
    nc, _ = _CACHE[key]
    res = run_bass_kernel_spmd(nc, in_maps, core_ids=list(range(N_CORES)))
    outs = []
    for i in range(nseq):
        c, sl = divmod(i, SLOTS)
        outs.append(np.ascontiguousarray(np.asarray(res.results[c]["yT"][sl]).T))
    y_prompt = np.stack(outs[:xp.shape[0]]).astype(np.float32)
    y_sample = np.stack(outs[xp.shape[0]:]).astype(np.float32)
    return (y_prompt, y_sample)
```
